# Optimizing a Trainium2 kernel written in Bass

```python
import math
import jax
import jax.numpy as jnp
from jax import lax
import numpy as np

D_MODEL = 1024
BATCH = 4
SEQ = 8192
DEPTH = 4

CHUNK = 64
N_EVEN = (DEPTH + 1) // 2
N_ODD = DEPTH // 2
RMS_EPS = 1e-6

DIFF_HEADS = 4
DIFF_HEAD_DIM = 64
DIFF_V_DIM = 2 * DIFF_HEAD_DIM
DIFF_WIDTH = DIFF_HEADS * DIFF_V_DIM
Q_BLOCK = 128

POOL_WINDOWS = (2, 4, 8, 16)
POOL_GROUPS = len(POOL_WINDOWS)
POOL_WIDTH = D_MODEL // 2
POOL_GROUP_DIM = POOL_WIDTH // POOL_GROUPS
AB_IN_WIDTH = 3 * DIFF_WIDTH + POOL_WIDTH
AB_OUT_IN = DIFF_WIDTH + POOL_WIDTH

GLA_HEADS = 4
GLA_KEY_DIM = D_MODEL // 2
GLA_VAL_DIM = D_MODEL
GLA_HK = GLA_KEY_DIM // GLA_HEADS
GLA_HV = GLA_VAL_DIM // GLA_HEADS
GLA_GATE_RANK = 16
GLA_GATE_TEMP = 16.0
GLA_IN_WIDTH = 2 * GLA_KEY_DIM + 2 * GLA_VAL_DIM + GLA_GATE_RANK

D_FF = 4 * D_MODEL

kernel_name = 'hybrid_diffattn_pool_gla_encoder'


def _rmsnorm(x, gain):
    xf = x.astype(jnp.float32)
    y = xf * lax.rsqrt(jnp.mean(xf * xf, axis=-1, keepdims=True) + RMS_EPS)
    return (y * gain.astype(jnp.float32)).astype(x.dtype)


def _lambda_init(layer_idx):
    return 0.8 - 0.6 * math.exp(-0.3 * layer_idx)


def _diff_attention(q, k, v, lam_params, subln_gain, layer_idx):
    bsz, seq, _ = q.shape
    q = q.reshape(bsz, seq, DIFF_HEADS, 2, DIFF_HEAD_DIM)
    k = k.reshape(bsz, seq, DIFF_HEADS, 2, DIFF_HEAD_DIM)
    q1 = q[:, :, :, 0].transpose(0, 2, 1, 3)
    q2 = q[:, :, :, 1].transpose(0, 2, 1, 3)
    k1 = k[:, :, :, 0].transpose(0, 2, 1, 3)
    k2 = k[:, :, :, 1].transpose(0, 2, 1, 3)
    v = v.reshape(bsz, seq, DIFF_HEADS, DIFF_V_DIM).transpose(0, 2, 1, 3)
    lam_p = lam_params.astype(jnp.float32)
    lam_init = _lambda_init(layer_idx)
    lam = jnp.exp(jnp.sum(lam_p[0] * lam_p[1])) - jnp.exp(jnp.sum(lam_p[2] * lam_p[3])) + lam_init
    n_blk = seq // Q_BLOCK

    def to_blocks(t):
        return t.reshape(bsz, DIFF_HEADS, n_blk, Q_BLOCK, DIFF_HEAD_DIM).transpose(2, 0, 1, 3, 4)

    key_chunk = jnp.arange(seq) // CHUNK
    scale = DIFF_HEAD_DIM ** -0.5

    def one_block(args):
        q1_b, q2_b, blk = args
        query_chunk = (blk * Q_BLOCK + jnp.arange(Q_BLOCK)) // CHUNK
        mask = key_chunk[None, :] <= query_chunk[:, None]
        s1 = jnp.einsum('bhqd,bhkd->bhqk', q1_b, k1).astype(jnp.float32) * scale
        s2 = jnp.einsum('bhqd,bhkd->bhqk', q2_b, k2).astype(jnp.float32) * scale
        p1 = jax.nn.softmax(jnp.where(mask, s1, -jnp.inf), axis=-1)
        p2 = jax.nn.softmax(jnp.where(mask, s2, -jnp.inf), axis=-1)
        p = p1 - lam * p2
        return jnp.einsum('bhqk,bhkv->bhqv', p.astype(v.dtype), v)

    o = lax.map(one_block, (to_blocks(q1), to_blocks(q2), jnp.arange(n_blk)))
    o = o.transpose(1, 2, 0, 3, 4).reshape(bsz, DIFF_HEADS, seq, DIFF_V_DIM)
    o = _rmsnorm(o, subln_gain) * (1.0 - lam_init)
    return o.transpose(0, 2, 1, 3).reshape(bsz, seq, DIFF_WIDTH)


def _pool_mixer(u, pool_w, pool_scale):
    bsz, seq, _ = u.shape
    u = u.reshape(bsz, seq, POOL_GROUPS, POOL_GROUP_DIM)
    pos = jnp.arange(seq)
    outs = []
    for g, w in enumerate(POOL_WINDOWS):
        ug = u[:, :, g].astype(jnp.float32)
        cs = jnp.cumsum(ug, axis=1)
        lagged = jnp.pad(cs, ((0, 0), (w, 0), (0, 0)))[:, :seq]
        count = jnp.minimum(pos + 1, w).astype(jnp.float32)
        outs.append((cs - lagged) / count[None, :, None] - ug)
    r = jnp.stack(outs, axis=2).astype(u.dtype)
    y = jnp.einsum('bsgc,gcd->bsgd', r, pool_w) * pool_scale.reshape(POOL_GROUPS, POOL_GROUP_DIM)
    return y.reshape(bsz, seq, POOL_WIDTH)


def _gla_mixer(h, w_in, w_gk_up, b_gk, norm_gain, w_out):
    bsz, seq, _ = h.shape
    n_chunk = seq // CHUNK
    proj = h @ w_in
    q, k, v, g_out, gk_low = jnp.split(
        proj,
        [GLA_KEY_DIM, 2 * GLA_KEY_DIM, 2 * GLA_KEY_DIM + GLA_VAL_DIM, 2 * GLA_KEY_DIM + 2 * GLA_VAL_DIM],
        axis=-1)
    log_a = jax.nn.log_sigmoid((gk_low @ w_gk_up + b_gk).astype(jnp.float32)) / GLA_GATE_TEMP

    def to_chunks(t, d):
        return t.reshape(bsz, n_chunk, CHUNK, GLA_HEADS, d).transpose(1, 0, 3, 2, 4)

    qc = to_chunks(q * (GLA_HK ** -0.5), GLA_HK)
    kc = to_chunks(k, GLA_HK)
    vc = to_chunks(v, GLA_HV)
    gc = to_chunks(log_a, GLA_HK)
    causal = jnp.tril(jnp.ones((CHUNK, CHUNK), dtype=bool))

    def step(state, inp):
        qi, ki, vi, gi = inp
        qi = qi.astype(jnp.float32)
        ki = ki.astype(jnp.float32)
        vi = vi.astype(jnp.float32)
        b = jnp.cumsum(gi, axis=2)
        o_inter = jnp.einsum('bhlk,bhkv->bhlv', qi * jnp.exp(b), state)
        diff = b[:, :, :, None, :] - b[:, :, None, :, :]
        decay = jnp.exp(jnp.where(causal[:, :, None], diff, -jnp.inf))
        attn = jnp.einsum('bhijk,bhjk->bhij', qi[:, :, :, None, :] * decay, ki)
        o_intra = jnp.einsum('bhij,bhjv->bhiv', attn, vi)
        b_last = b[:, :, -1:, :]
        new_state = (jnp.exp(b_last[:, :, 0, :])[..., None] * state
                     + jnp.einsum('bhlk,bhlv->bhkv', ki * jnp.exp(b_last - b), vi))
        return new_state, o_inter + o_intra

    state0 = jnp.zeros((bsz, GLA_HEADS, GLA_HK, GLA_HV), jnp.float32)
    _, o = lax.scan(step, state0, (qc, kc, vc, gc))
    o = o.transpose(1, 0, 3, 2, 4).reshape(bsz, seq, GLA_HEADS, GLA_HV).astype(h.dtype)
    gate = jax.nn.silu(g_out).reshape(bsz, seq, GLA_HEADS, GLA_HV)
    o = _rmsnorm(o, norm_gain) * gate
    return o.reshape(bsz, seq, GLA_VAL_DIM) @ w_out


def setup_inputs(seed: int = 0) -> dict:
    key = jax.random.key(seed)
    ks = jax.random.split(key, 17)
    f32 = jnp.float32

    def nrm(k, shape, scale):
        return jax.random.normal(k, shape, f32) * scale

    return {
        'x': nrm(ks[0], (BATCH, SEQ, D_MODEL), 1.0),
        'norm_mix': 1.0 + nrm(ks[1], (DEPTH, D_MODEL), 0.02),
        'norm_ffn': 1.0 + nrm(ks[2], (DEPTH, D_MODEL), 0.02),
        'norm_final': 1.0 + nrm(ks[3], (D_MODEL,), 0.02),
        'ab_w_in': nrm(ks[4], (N_EVEN, D_MODEL, AB_IN_WIDTH), D_MODEL ** -0.5),
        'ab_lambda': nrm(ks[5], (N_EVEN, 4, DIFF_HEAD_DIM), 0.1),
        'ab_subln': 1.0 + nrm(ks[6], (N_EVEN, DIFF_V_DIM), 0.02),
        'pool_w': nrm(ks[7], (N_EVEN, POOL_GROUPS, POOL_GROUP_DIM, POOL_GROUP_DIM), POOL_GROUP_DIM ** -0.5),
        'pool_scale': 1.0 + nrm(ks[8], (N_EVEN, POOL_WIDTH), 0.1),
        'ab_w_out': nrm(ks[9], (N_EVEN, AB_OUT_IN, D_MODEL), AB_OUT_IN ** -0.5),
        'gla_w_in': nrm(ks[10], (N_ODD, D_MODEL, GLA_IN_WIDTH), D_MODEL ** -0.5),
        'gla_w_gk_up': nrm(ks[11], (N_ODD, GLA_GATE_RANK, GLA_KEY_DIM), GLA_GATE_RANK ** -0.5),
        'gla_b_gk': nrm(ks[12], (N_ODD, GLA_KEY_DIM), 0.1),
        'gla_norm': 1.0 + nrm(ks[13], (N_ODD, GLA_HEADS, GLA_HV), 0.02),
        'gla_w_out': nrm(ks[14], (N_ODD, GLA_VAL_DIM, D_MODEL), GLA_VAL_DIM ** -0.5),
        'ffn_w1': nrm(ks[15], (DEPTH, D_MODEL, D_FF), D_MODEL ** -0.5),
        'ffn_w2': nrm(ks[16], (DEPTH, D_FF, D_MODEL), D_FF ** -0.5),
    }


def reference(x, norm_mix, norm_ffn, norm_final, ab_w_in, ab_lambda, ab_subln, pool_w, pool_scale,
              ab_w_out, gla_w_in, gla_w_gk_up, gla_b_gk, gla_norm, gla_w_out, ffn_w1, ffn_w2):
    for i in range(DEPTH):
        h = _rmsnorm(x, norm_mix[i])
        if i % 2 == 0:
            e = i // 2
            proj = h @ ab_w_in[e]
            q, k, v, u = jnp.split(proj, [DIFF_WIDTH, 2 * DIFF_WIDTH, 3 * DIFF_WIDTH], axis=-1)
            a_out = _diff_attention(q, k, v, ab_lambda[e], ab_subln[e], i)
            b_out = _pool_mixer(u, pool_w[e], pool_scale[e])
            x = x + jnp.concatenate([a_out, b_out], axis=-1) @ ab_w_out[e]
        else:
            o = i // 2
            x = x + _gla_mixer(h, gla_w_in[o], gla_w_gk_up[o], gla_b_gk[o], gla_norm[o], gla_w_out[o])
        h = _rmsnorm(x, norm_ffn[i])
        x = x + jnp.square(jax.nn.relu(h @ ffn_w1[i])) @ ffn_w2[i]
    return _rmsnorm(x, norm_final)
```

```python
import math
from contextlib import ExitStack

import numpy as np
import concourse.bass as bass
import concourse.mybir as mybir
from concourse.bass_utils import run_bass_kernel_spmd

F32 = mybir.dt.float32
BF16 = mybir.dt.bfloat16
ALU = mybir.AluOpType
AF = mybir.ActivationFunctionType
AX = mybir.AxisListType

D = 1024
DFF = 4096
EPS = 1e-6
DEPTH = 4
NCORES = 8

ENGS = ("pe", "act", "dve", "pool", "sp")
NDMA_SEMS = 8


class Res:
    __slots__ = ("writer", "readers")

    def __init__(self):
        self.writer = None
        self.readers = []


class Op:
    __slots__ = ("eng", "fn", "deps", "is_dma", "sig", "count", "dsem", "dval", "waits", "snap", "done")

    def __init__(self, eng, fn, is_dma):
        self.eng = eng
        self.fn = fn
        self.is_dma = is_dma
        self.deps = []
        self.sig = False
        self.count = 0
        self.dsem = None
        self.dval = 0
        self.waits = []
        self.snap = None
        self.done = False


class Sched:
    def __init__(self, nc, stack):
        self.nc = nc
        self.ops = {e: [] for e in ENGS}
        self.cnt = {e: 0 for e in ENGS}
        self.ndma = {e: 0 for e in ENGS}
        self.dma_hist = {e: [] for e in ENGS}
        self.known = {e: [0] * len(ENGS) for e in ENGS}
        self.kdma = {e: {} for e in ENGS}
        self.esem = {e: stack.enter_context(nc.semaphore(f"s_{e}")) for e in ENGS if e != "sp"}
        self.dsem = {}
        for e in ("sp", "pool", "act"):
            for k in range(NDMA_SEMS):
                self.dsem[(e, k)] = stack.enter_context(nc.semaphore(f"d_{e}{k}"))
        self.ninst = 0

    def res(self):
        return Res()

    def add(self, eng, fn, reads=(), writes=(), is_dma=False):
        op = Op(eng, fn, is_dma)
        deps = []
        for r in reads:
            if r.writer is not None:
                deps.append(r.writer)
            r.readers.append(op)
        for w in writes:
            if w.writer is not None:
                deps.append(w.writer)
            deps.extend(w.readers)
            w.writer = op
            w.readers = []
        seen = set()
        for d in deps:
            if d is op or d.done or id(d) in seen:
                continue
            seen.add(id(d))
            op.deps.append(d)
        self.ops[eng].append(op)
        return op

    def dma(self, q, out, in_, reads=(), writes=()):
        return self.add(q, lambda e: e.dma_start(out=out, in_=in_), reads, writes, is_dma=True)

    def barrier(self):
        last = []
        for e in ENGS:
            for o in reversed(self.ops[e]):
                if o.fn is not None and not o.is_dma:
                    last.append(o)
                    break
        rec = []
        for e in ENGS:
            dm = [o for o in self.ops[e] if o.is_dma][-NDMA_SEMS:]
            rec.extend(dm)
        for e in ENGS:
            op = Op(e, None, False)
            op.deps = list(last) + list(rec)
            self.ops[e].append(op)

    def flush(self):
        self.barrier()
        for e in ENGS:
            for op in self.ops[e]:
                for d in op.deps:
                    d.sig = True
        for e in ENGS:
            for op in self.ops[e]:
                if op.is_dma:
                    n = self.ndma[e]
                    op.dsem = (e, n % NDMA_SEMS)
                    op.dval = 16 * (n // NDMA_SEMS + 1)
                    self.dma_hist[e].append(op)
                    self.ndma[e] += 1
                elif op.sig:
                    self.cnt[e] += 1
                    op.count = self.cnt[e]
        eidx = {e: i for i, e in enumerate(ENGS)}
        for e in ENGS:
            known = self.known[e]
            kdma = self.kdma[e]
            hist = self.dma_hist[e]
            nd = len(hist) - sum(1 for o in self.ops[e] if o.is_dma)
            for op in self.ops[e]:
                deps = list(op.deps)
                if op.is_dma:
                    if nd >= NDMA_SEMS:
                        deps.append(hist[nd - NDMA_SEMS])
                    nd += 1
                best = {}
                for d in deps:
                    if d.is_dma:
                        if kdma.get(d.dsem, 0) >= d.dval:
                            continue
                        kdma[d.dsem] = d.dval
                        best[("dma", d.dsem)] = d.dval
                    else:
                        if d.eng == "pe" and e == "pe":
                            continue
                        j = eidx[d.eng]
                        if known[j] >= d.count:
                            continue
                        known[j] = d.count
                        best[("eng", d.eng)] = max(best.get(("eng", d.eng), 0), d.count)
                        if d.snap is not None:
                            for k in range(len(ENGS)):
                                if d.snap[k] > known[k]:
                                    known[k] = d.snap[k]
                op.waits = [(k[0], k[1], v) for k, v in best.items()]
                op.snap = tuple(known)
        nc = self.nc
        ops = self.ops
        esem, dsem = self.esem, self.dsem

        def run(eng_name):
            lst = ops[eng_name]

            def body(eng):
                for op in lst:
                    for kind, key, val in op.waits:
                        eng.wait_ge(dsem[key] if kind == "dma" else esem[key], val)
                    if op.fn is None:
                        continue
                    ins = op.fn(eng)
                    if op.is_dma:
                        ins.then_inc(dsem[op.dsem], 16)
                    elif op.sig:
                        ins.then_inc(esem[eng_name], 1)
            return body

        with nc.Block() as block:
            block.tensor(run("pe"))
            block.scalar(run("act"))
            block.vector(run("dve"))
            block.gpsimd(run("pool"))
            block.sync(run("sp"))
        for e in ENGS:
            self.ninst += len(self.ops[e])
            for op in self.ops[e]:
                op.done = True
                op.fn = None
                op.deps = []
            self.ops[e] = []


C_ONES = 0
C_TRI = 128
C_SU = 256
C_TRI4 = 384
C_INVC = 896
NCST = 960


def make_cst():
    c = np.zeros((128, NCST), np.float32)
    c[:, C_ONES:C_ONES + 128] = 1.0
    s = np.arange(128)
    tri = (s[:, None] <= s[None, :]).astype(np.float32)
    c[:, C_TRI:C_TRI + 128] = tri
    c[:, C_SU:C_SU + 128] = (s[:, None] > s[None, :]).astype(np.float32)
    for h in range(4):
        c[:, C_TRI4 + h * 128:C_TRI4 + (h + 1) * 128] = tri
    for g, w in enumerate((2, 4, 8, 16)):
        t = np.arange(16)
        c[:, C_INVC + g * 16:C_INVC + (g + 1) * 16] = 1.0 / np.minimum(t + 1, w)
    return c


V_NMIX = 0
V_NFFN = 32
V_NFIN = 64
V_PSCALE = 72
V_SUBLN = 80
V_GNORM = 82
NVEC = 98


def make_vecs(inp):
    v = np.zeros((128, NVEC), np.float32)

    def put(col, arr):
        a = np.asarray(arr, np.float32).reshape(-1, 128).T
        v[:, col:col + a.shape[1]] = a

    for i in range(DEPTH):
        put(V_NMIX + 8 * i, inp["norm_mix"][i])
        put(V_NFFN + 8 * i, inp["norm_ffn"][i])
    put(V_NFIN, inp["norm_final"])
    for e in range(2):
        put(V_PSCALE + 4 * e, inp["pool_scale"][e])
        put(V_SUBLN + e, inp["ab_subln"][e])
        put(V_GNORM + 8 * e, inp["gla_norm"][e].reshape(-1))
    return v


class Builder:
    def __init__(self, T, layers=(0, 1, 2, 3), final=True, dbg=False):
        self.T = T
        self.layers = tuple(layers)
        self.final = final
        nc = self.nc = bass.Bass("TRN2", target_bir_lowering=False)
        dt = nc.dram_tensor
        self.x_in = dt("xT", [D, T], F32, kind="ExternalInput").ap()
        self.y_out = dt("yT", [D, T], F32, kind="ExternalOutput").ap()
        self.cst_d = dt("cst", [128, NCST], F32, kind="ExternalInput").ap()
        self.vecs_d = dt("vecs", [128, NVEC], F32, kind="ExternalInput").ap()
        self.w = {}
        for name, shape in (("ab_w_in", [2, D, 2048]), ("ab_lambda", [2, 256]), ("pool_w", [2, 4, 128, 128]),
                            ("ab_w_out", [2, D, D]), ("gla_w_in", [2, D, 3088]), ("gla_w_gk_up", [2, 16, 512]),
                            ("gla_b_gk", [2, 1, 512]), ("gla_w_out", [2, D, D]), ("ffn_w1", [4, D, DFF]),
                            ("ffn_w2", [4, DFF, D])):
            self.w[name] = dt(name, shape, F32, kind="ExternalInput").ap()
        kw = {"kind": "ExternalOutput"} if dbg else {}
        self.xa = dt("xa", [D, T], F32, **kw).ap()
        self.xb = dt("xb", [D, T], F32, **kw).ap()
        self.qT = dt("qTs", [4, 128, T], BF16, **kw).ap()
        self.kT = dt("kTs", [4, 128, T], BF16, **kw).ap()
        self.Vs = dt("Vs", [4, 128, T // 128, 128], BF16, **kw).ap()
        self.mT = dt("mTs", [4, 128, T], BF16, **kw).ap()
        self.r_x = {}
        self.nsb = 0

    def sb(self, st, shape, dtp):
        self.nsb += 1
        return st.enter_context(self.nc.sbuf_tensor(f"sb{self.nsb}", shape, dtp))

    def psum(self, st):
        self.nsb += 1
        return [st.enter_context(self.nc.psum_tensor(f"ps{self.nsb}_{i}", [128, 512], F32)) for i in range(8)]

    def load_consts(self, S, st):
        cst = self.sb(st, [128, NCST], F32)
        vecs = self.sb(st, [128, NVEC], F32)
        r = S.res()
        S.dma("sp", cst[:], self.cst_d, writes=[r])
        S.dma("sp", vecs[:], self.vecs_d, writes=[r])
        return cst, vecs, r

    def load_weight(self, S, st_tmp, dst, src_rows, ncols, rres, wres, scale_cols=None, cw=1024):
        stg = [self.sb(st_tmp, [128, cw], F32) for _ in range(3)]
        r_stg = [S.res() for _ in range(3)]
        engs = ["pool", "dve", "act"]
        ci = 0
        for r, src in enumerate(src_rows):
            for c0 in range(0, ncols, cw):
                c1 = min(ncols, c0 + cw)
                b = ci % 3
                S.dma("sp", stg[b][:, :c1 - c0], src[:, c0:c1], writes=[r_stg[b]])
                out = dst[:, r, c0:c1]
                in_ = stg[b][:, :c1 - c0]
                eng = engs[ci % 3]
                sc = None if scale_cols is None else scale_cols[r]
                if sc is None:
                    if eng == "act":
                        S.add("act", (lambda o, i: lambda e: e.copy(o, i))(out, in_), reads=[r_stg[b]] + rres, writes=[wres])
                    else:
                        S.add(eng, (lambda o, i: lambda e: e.tensor_copy(o, i))(out, in_), reads=[r_stg[b]] + rres, writes=[wres])
                else:
                    if eng == "act":
                        S.add("act", (lambda o, i, s: lambda e: e.activation(o, i, AF.Copy, scale=s))(out, in_, sc),
                              reads=[r_stg[b]] + rres, writes=[wres])
                    else:
                        S.add(eng, (lambda o, i, s: lambda e: e.tensor_scalar(o, i, s, None, ALU.mult))(out, in_, sc),
                              reads=[r_stg[b]] + rres, writes=[wres])
                ci += 1

    def rmsnorm_tile(self, S, xt_c, r_xt, NT, sq, r_sq, ssum, r_ssum, rstd, r_rstd, psb, r_psb, ones, r_c, h, r_h, nfeat_inv=1.0 / D):
        S.add("act", lambda e: e.activation(sq[:], xt_c[:], AF.Square), reads=[r_xt], writes=[r_sq])
        S.add("dve", lambda e: e.tensor_reduce(ssum[:], sq[:].rearrange("p c t -> p t c"), AX.X, ALU.add),
              reads=[r_sq], writes=[r_ssum])
        S.add("pe", lambda e: e.matmul(psb[:, :NT], ones, ssum[:], start=True, stop=True),
              reads=[r_c, r_ssum], writes=[r_psb])
        S.add("act", lambda e: e.activation(ssum[:], psb[:, :NT], AF.Sqrt, bias=float(EPS), scale=nfeat_inv),
              reads=[r_psb], writes=[r_ssum])
        S.add("dve", lambda e: e.reciprocal(rstd[:], ssum[:]), reads=[r_ssum], writes=[r_rstd])
        for c in range(8):
            eng = "dve" if c % 2 == 0 else "pool"
            S.add(eng, (lambda c: lambda e: e.tensor_tensor(h[:, c, :], xt_c[:, c, :], rstd[:], ALU.mult))(c),
                  reads=[r_xt, r_rstd], writes=[r_h[c]])

    def ffn_sweep(self, S, li, xin, r_xin, xout, r_xout, final):
        T = self.T
        FNT = 256
        nc = self.nc
        with ExitStack() as st:
            cst, vecs, r_c = self.load_consts(S, st)
            ones = cst[:, C_ONES:C_ONES + 128]
            w1sb = self.sb(st, [128, 8, DFF], BF16)
            w2sb = self.sb(st, [128, 32, D], BF16)
            r_w1, r_w2 = S.res(), S.res()
            with ExitStack() as st2:
                w1v = self.w["ffn_w1"][li].rearrange("(kc p) n -> p kc n", p=128)
                w2v = self.w["ffn_w2"][li].rearrange("(hc p) n -> p hc n", p=128)
                self.load_weight(S, st2, w1sb, [w1v[:, kc, :] for kc in range(8)], DFF, [r_c], r_w1,
                                 scale_cols=[vecs[:, V_NFFN + 8 * li + kc:V_NFFN + 8 * li + kc + 1] for kc in range(8)])
                self.load_weight(S, st2, w2sb, [w2v[:, hc, :] for hc in range(32)], D, [r_c], r_w2)
                S.flush()
            xt = [self.sb(st, [128, 8, FNT], F32) for _ in range(2)]
            xo = self.sb(st, [128, 8, FNT], F32)
            sq = self.sb(st, [128, 8, FNT], F32)
            ssum = self.sb(st, [128, FNT], F32)
            rstd = self.sb(st, [128, FNT], F32)
            h = self.sb(st, [128, 8, FNT], BF16)
            hid = self.sb(st, [128, 32, FNT], BF16)
            rl = [self.sb(st, [128, FNT], F32) for _ in range(4)]
            pb = self.psum(st)
            R = S.res
            r_xt = [R(), R()]; r_xo = R(); r_sq = R(); r_ssum = R(); r_rstd = R()
            r_h = [R() for _ in range(8)]; r_hid = [R() for _ in range(32)]
            r_pb = [R() for _ in range(8)]; r_rl = [R() for _ in range(4)]
            xv = xin.rearrange("(c p) t -> p c t", p=128)
            yv = xout.rearrange("(c p) t -> p c t", p=128)
            ntile = T // FNT

            def load(n):
                S.dma("sp", xt[n % 2][:], xv[:, :, n * FNT:(n + 1) * FNT], reads=[r_xin], writes=[r_xt[n % 2]])

            def norm(n):
                b = n % 2
                self.rmsnorm_tile(S, xt[b], r_xt[b], FNT, sq, r_sq, ssum, r_ssum, rstd, r_rstd, pb[7], r_pb[7],
                                  ones, r_c, h, r_h)

            def up(n):
                for j in range(32):
                    bk = j % 4
                    for kc in range(8):
                        S.add("pe", (lambda j, kc, bk: lambda e: e.matmul(
                            pb[bk][:, :FNT], w1sb[:, kc, j * 128:(j + 1) * 128], h[:, kc, :],
                            start=(kc == 0), stop=(kc == 7)))(j, kc, bk), reads=[r_w1, r_h[kc]], writes=[r_pb[bk]])
                    S.add("act", (lambda j, bk: lambda e: e.activation(rl[j % 4][:], pb[bk][:, :FNT], AF.Relu))(j, bk),
                          reads=[r_pb[bk]], writes=[r_rl[j % 4]])
                    S.add("pool", (lambda j: lambda e: e.tensor_tensor(hid[:, j, :], rl[j % 4][:], rl[j % 4][:], ALU.mult))(j),
                          reads=[r_rl[j % 4]], writes=[r_hid[j]])

            def down(n):
                b = n % 2
                for oc in range(8):
                    bk = 4 + oc % 2
                    for hc in range(32):
                        S.add("pe", (lambda oc, hc, bk: lambda e: e.matmul(
                            pb[bk][:, :FNT], w2sb[:, hc, oc * 128:(oc + 1) * 128], hid[:, hc, :],
                            start=(hc == 0), stop=(hc == 31)))(oc, hc, bk), reads=[r_w2, r_hid[hc]], writes=[r_pb[bk]])
                    S.add("dve", (lambda oc, bk: lambda e: e.tensor_tensor(
                        xo[:, oc, :], pb[bk][:, :FNT], xt[b][:, oc, :], ALU.add))(oc, bk),
                        reads=[r_pb[bk], r_xt[b]], writes=[r_xo])
                if final:
                    S.add("act", lambda e: e.activation(sq[:], xo[:], AF.Square), reads=[r_xo], writes=[r_sq])
                    S.add("dve", lambda e: e.tensor_reduce(ssum[:], sq[:].rearrange("p c t -> p t c"), AX.X, ALU.add),
                          reads=[r_sq], writes=[r_ssum])
                    S.add("pe", lambda e: e.matmul(pb[6][:, :FNT], ones, ssum[:], start=True, stop=True),
                          reads=[r_c, r_ssum], writes=[r_pb[6]])
                    S.add("act", lambda e: e.activation(ssum[:], pb[6][:, :FNT], AF.Sqrt, bias=float(EPS), scale=1.0 / D),
                          reads=[r_pb[6]], writes=[r_ssum])
                    S.add("dve", lambda e: e.reciprocal(rstd[:], ssum[:]), reads=[r_ssum], writes=[r_rstd])
                    for c in range(8):
                        S.add("dve", (lambda c: lambda e: e.scalar_tensor_tensor(
                            sq[:, c, :], xo[:, c, :], vecs[:, V_NFIN + c:V_NFIN + c + 1], rstd[:], ALU.mult, ALU.mult))(c),
                            reads=[r_xo, r_rstd, r_c], writes=[r_sq])
                    S.dma("pool", yv[:, :, n * FNT:(n + 1) * FNT], sq[:], reads=[r_sq], writes=[r_xout])
                else:
                    S.dma("pool", yv[:, :, n * FNT:(n + 1) * FNT], xo[:], reads=[r_xo], writes=[r_xout])

            load(0)
            if ntile > 1:
                load(1)
            norm(0)
            for n in range(ntile):
                up(n)
                if n + 1 < ntile and not final:
                    norm(n + 1)
                down(n)
                if n + 1 < ntile and final:
                    norm(n + 1)
                if n + 2 < ntile:
                    load(n + 2)
            S.flush()

    def a1_sweep(self, S, li, xin, r_xin):
        T = self.T
        NT = 512
        e_ = li // 2
        with ExitStack() as st:
            cst, vecs, r_c = self.load_consts(S, st)
            ones = cst[:, C_ONES:C_ONES + 128]
            win = self.sb(st, [128, 8, 2048], BF16)
            pw = self.sb(st, [128, 4, 128], BF16)
            r_win, r_pw = S.res(), S.res()
            with ExitStack() as st2:
                wv = self.w["ab_w_in"][e_].rearrange("(kc p) n -> p kc n", p=128)
                self.load_weight(S, st2, win, [wv[:, kc, :] for kc in range(8)], 2048, [r_c], r_win,
                                 scale_cols=[vecs[:, V_NMIX + 8 * li + kc:V_NMIX + 8 * li + kc + 1] for kc in range(8)])
                pwv = self.w["pool_w"][e_].rearrange("g c d -> c g d")
                self.load_weight(S, st2, pw, [pwv[:, g, :] for g in range(4)], 128, [r_c], r_pw, cw=128)
                S.flush()
            xt = [self.sb(st, [128, 8, NT], F32) for _ in range(2)]
            sq = self.sb(st, [128, 8, NT], F32)
            ssum = self.sb(st, [128, NT], F32)
            rstd = self.sb(st, [128, NT], F32)
            h = self.sb(st, [128, 8, NT], BF16)
            qk = [self.sb(st, [128, 8, NT], BF16) for _ in range(2)]
            vt = [self.sb(st, [128, 4, 512], BF16) for _ in range(2)]
            uext = [self.sb(st, [128, 4, 16 + NT], F32) for _ in range(2)]
            ta = self.sb(st, [128, 16 + NT], F32)
            tb = self.sb(st, [128, 16 + NT], F32)
            rr = self.sb(st, [128, 4, NT], BF16)
            mo = [self.sb(st, [128, 4, NT], BF16) for _ in range(2)]
            pb = self.psum(st)
            R = S.res
            r_xt = [R(), R()]; r_sq = R(); r_ssum = R(); r_rstd = R()
            r_h = [R() for _ in range(8)]; r_pb = [R() for _ in range(8)]
            r_qk = [R(), R()]; r_vt = [R(), R()]; r_ue = [[R() for _ in range(4)] for _ in range(2)]
            r_ta, r_tb = R(), R(); r_rr = [R() for _ in range(4)]; r_mo = [R(), R()]
            r_sc = self.r_scr
            xv = xin.rearrange("(c p) t -> p c t", p=128)
            ntile = T // NT
            qTv = self.qT.rearrange("h p t -> p h t")
            kTv = self.kT.rearrange("h p t -> p h t")
            mTv = self.mT.rearrange("g p t -> p g t")
            Vv = self.Vs.rearrange("h p k v -> p h k v")

            def load(n):
                S.dma("sp", xt[n % 2][:], xv[:, :, n * NT:(n + 1) * NT], reads=[r_xin], writes=[r_xt[n % 2]])

            def norm(n):
                b = n % 2
                self.rmsnorm_tile(S, xt[b], r_xt[b], NT, sq, r_sq, ssum, r_ssum, rstd, r_rstd, pb[7], r_pb[7],
                                  ones, r_c, h, r_h)

            for g in range(4):
                S.add("pool", (lambda g: lambda e: e.memset(uext[0][:, g, 0:16], 0.0))(g), writes=[r_ue[0][g]])

            def proj(n):
                b = n % 2
                t0 = n * NT
                for oc in range(8):
                    bk = oc % 3
                    for kc in range(8):
                        S.add("pe", (lambda oc, kc, bk: lambda e: e.matmul(
                            pb[bk][:, :NT], win[:, kc, oc * 128:(oc + 1) * 128], h[:, kc, :],
                            start=(kc == 0), stop=(kc == 7)))(oc, kc, bk), reads=[r_win, r_h[kc]], writes=[r_pb[bk]])
                    S.add("act", (lambda oc, bk: lambda e: e.copy(qk[b][:, oc, :], pb[bk][:, :NT]))(oc, bk),
                          reads=[r_pb[bk]], writes=[r_qk[b]])
                S.dma("pool", qTv[:, :, t0:t0 + NT], qk[b][:, 0:4, :], reads=[r_qk[b]], writes=[r_sc])
                S.dma("pool", kTv[:, :, t0:t0 + NT], qk[b][:, 4:8, :], reads=[r_qk[b]], writes=[r_sc])
                for s in range(4):
                    bk = 3 + s % 2
                    for kc in range(8):
                        S.add("pe", (lambda s, kc, bk: lambda e: e.matmul(
                            pb[bk][:, :512], h[:, kc, s * 128:(s + 1) * 128], win[:, kc, 1024:1536],
                            start=(kc == 0), stop=(kc == 7)))(s, kc, bk), reads=[r_win, r_h[kc]], writes=[r_pb[bk]])
                    S.add("dve", (lambda s, bk: lambda e: e.tensor_copy(vt[b][:, s, :], pb[bk][:, :512]))(s, bk),
                          reads=[r_pb[bk]], writes=[r_vt[b]])
                for hh in range(4):
                    S.dma("pool", self.Vs[hh, :, 4 * n:4 * n + 4, :], vt[b][:, :, hh * 128:(hh + 1) * 128],
                          reads=[r_vt[b]], writes=[r_sc])
                for g in range(4):
                    bk = 5 + g % 2
                    oc = 12 + g
                    for kc in range(8):
                        S.add("pe", (lambda oc, kc, bk: lambda e: e.matmul(
                            pb[bk][:, :NT], win[:, kc, oc * 128:(oc + 1) * 128], h[:, kc, :],
                            start=(kc == 0), stop=(kc == 7)))(oc, kc, bk), reads=[r_win, r_h[kc]], writes=[r_pb[bk]])
                    S.add("act", (lambda g, bk: lambda e: e.copy(uext[b][:, g, 16:16 + NT], pb[bk][:, :NT]))(g, bk),
                          reads=[r_pb[bk]], writes=[r_ue[b][g]])

            def pool(n):
                b = n % 2
                t0 = n * NT
                W = 16 + NT
                for g in range(4):
                    w = 2 ** (g + 1)
                    src = uext[b][:, g, :]
                    r_src = r_ue[b][g]
                    bufs = [(ta, r_ta), (tb, r_tb)]
                    sh = 1
                    k = 0
                    cur, r_cur = src, r_src
                    while sh < w:
                        dst, r_dst = bufs[k % 2]
                        S.add("pool", (lambda dst, cur, sh: lambda e: e.tensor_tensor(
                            dst[:, sh:W], cur[:, sh:W], cur[:, 0:W - sh], ALU.add))(dst, cur, sh),
                            reads=[r_cur], writes=[r_dst])
                        cur, r_cur = dst[:], r_dst
                        sh *= 2
                        k += 1
                    S.add("dve", (lambda g, cur, w: lambda e: e.scalar_tensor_tensor(
                        rr[:, g, :], cur[:, 16:W], 1.0 / w, uext[b][:, g, 16:W], ALU.mult, ALU.subtract))(g, cur, w),
                        reads=[r_cur, r_ue[b][g]], writes=[r_rr[g]])
                    if n == 0:
                        S.add("dve", (lambda g, cur: lambda e: e.tensor_tensor(
                            ta[:, 0:16], cur[:, 16:32], cst[:, C_INVC + g * 16:C_INVC + (g + 1) * 16], ALU.mult))(g, cur),
                            reads=[r_cur, r_c], writes=[r_ta])
                        S.add("dve", (lambda g: lambda e: e.tensor_tensor(
                            rr[:, g, 0:16], ta[:, 0:16], uext[b][:, g, 16:32], ALU.subtract))(g),
                            reads=[r_ta, r_ue[b][g]], writes=[r_rr[g]])
                    if n + 1 < ntile:
                        S.add("pool", (lambda g: lambda e: e.tensor_copy(uext[1 - b][:, g, 0:16], uext[b][:, g, NT:NT + 16]))(g),
                              reads=[r_ue[b][g]], writes=[r_ue[1 - b][g]])
                    bk = 5 + g % 2
                    S.add("pe", (lambda g, bk: lambda e: e.matmul(pb[bk][:, :NT], pw[:, g, :], rr[:, g, :], start=True, stop=True))(g, bk),
                          reads=[r_pw, r_rr[g]], writes=[r_pb[bk]])
                    S.add("act", (lambda g, bk: lambda e: e.activation(
                        mo[b][:, g, :], pb[bk][:, :NT], AF.Copy, scale=vecs[:, V_PSCALE + 4 * e_ + g:V_PSCALE + 4 * e_ + g + 1]))(g, bk),
                        reads=[r_pb[bk], r_c], writes=[r_mo[b]])
                S.dma("pool", mTv[:, :, t0:t0 + NT], mo[b][:], reads=[r_mo[b]], writes=[r_sc])

            load(0)
            if ntile > 1:
                load(1)
            norm(0)
            for n in range(ntile):
                proj(n)
                if n + 1 < ntile:
                    norm(n + 1)
                pool(n)
                if n + 2 < ntile:
                    load(n + 2)
            S.flush()

    def a2_sweep(self, S, li, xin, r_xin, xout, r_xout):
        T = self.T
        NT = 512
        e_ = li // 2
        lam_init = 0.8 - 0.6 * math.exp(-0.3 * li)
        KG = 16
        with ExitStack() as st:
            cst, vecs, r_c = self.load_consts(S, st)
            ones = cst[:, C_ONES:C_ONES + 128]
            wout = self.sb(st, [128, 8, D], BF16)
            r_wout = S.res()
            onesb = self.sb(st, [128, 128], BF16)
            lamt = self.sb(st, [128, 256], F32)
            lprod = self.sb(st, [128, 128], F32)
            lsum = self.sb(st, [128, 2], F32)
            neglam = self.sb(st, [128, 1], F32)
            gsub = self.sb(st, [128, 1], F32)
            r_l = S.res()
            ediag = [[self.sb(st, [128, NT], BF16) for _ in range(2)] for _ in range(4)]
            r_ed = [[S.res() for _ in range(2)] for _ in range(4)]
            with ExitStack() as st2:
                wv = self.w["ab_w_out"][e_].rearrange("(kc p) n -> p kc n", p=128)
                self.load_weight(S, st2, wout, [wv[:, kc, :] for kc in range(8)], D, [r_c], r_wout)
                S.add("dve", lambda e: e.tensor_copy(onesb[:], ones), reads=[r_c], writes=[r_l])
                S.dma("sp", lamt[:], self.w["ab_lambda"][e_:e_ + 1, :].partition_broadcast(128), writes=[r_l])
                S.add("dve", lambda e: e.tensor_tensor(lprod[:, 0:64], lamt[:, 0:64], lamt[:, 64:128], ALU.mult), reads=[r_l], writes=[r_l])
                S.add("dve", lambda e: e.tensor_tensor(lprod[:, 64:128], lamt[:, 128:192], lamt[:, 192:256], ALU.mult), reads=[r_l], writes=[r_l])
                S.add("dve", lambda e: e.tensor_reduce(lsum[:], lprod[:].rearrange("p (a d) -> p a d", a=2), AX.X, ALU.add), reads=[r_l], writes=[r_l])
                S.add("act", lambda e: e.activation(lsum[:], lsum[:], AF.Exp), reads=[r_l], writes=[r_l])
                S.add("dve", lambda e: e.tensor_tensor(neglam[:], lsum[:, 1:2], lsum[:, 0:1], ALU.subtract), reads=[r_l], writes=[r_l])
                S.add("dve", lambda e: e.tensor_scalar(neglam[:], neglam[:], -float(lam_init), None, ALU.add), reads=[r_l], writes=[r_l])
                S.add("dve", lambda e: e.tensor_scalar(gsub[:], vecs[:, V_SUBLN + e_:V_SUBLN + e_ + 1], float(1.0 - lam_init), None, ALU.mult),
                      reads=[r_c], writes=[r_l])
                for j in range(4):
                    for a in range(2):
                        S.add("pool", (lambda j, a: lambda e: e.memset(ediag[j][a][:], 0.0))(j, a), writes=[r_ed[j][a]])
                S.flush()
            xt = [self.sb(st, [128, 8, NT], F32) for _ in range(2)]
            xo = self.sb(st, [128, 8, NT], F32)
            qt = [self.sb(st, [128, 4, NT], BF16) for _ in range(2)]
            mo = [self.sb(st, [128, 4, NT], BF16) for _ in range(2)]
            ao = self.sb(st, [128, 4, NT], BF16)
            kb = [self.sb(st, [128, KG * 128], BF16) for _ in range(3)]
            vb = [self.sb(st, [128, KG, 128], BF16) for _ in range(3)]
            eb = [[self.sb(st, [128, NT], BF16) for _ in range(2)] for _ in range(3)]
            rl1 = self.sb(st, [128, NT], F32); rl2 = self.sb(st, [128, NT], F32)
            t1 = self.sb(st, [128, NT], F32); t2 = self.sb(st, [128, NT], F32)
            A = self.sb(st, [128, NT], F32); asq = self.sb(st, [128, NT], F32)
            sd = self.sb(st, [128, NT], F32); rs = self.sb(st, [128, NT], F32)
            pb = self.psum(st)
            R = S.res
            r_xt = [R(), R()]; r_xo = R(); r_qt = [R(), R()]; r_mo = [R(), R()]; r_ao = [R() for _ in range(4)]
            r_kb = [R() for _ in range(3)]; r_vb = [R() for _ in range(3)]
            r_eb = [[R(), R()] for _ in range(3)]
            r_ep = R()
            r_pb = [R() for _ in range(8)]
            r_sc = self.r_scr
            xv = xin.rearrange("(c p) t -> p c t", p=128)
            yv = xout.rearrange("(c p) t -> p c t", p=128)
            qTv = self.qT.rearrange("h p t -> p h t")
            mTv = self.mT.rearrange("g p t -> p g t")
            nblk = T // NT
            PS_S = [(0, 1), (2, 3)]
            PO1, PO2, PL1, PL2 = 4, 5, 6, 7
            grp_ctr = [0]

            def load(n):
                t0 = n * NT
                b = n % 2
                S.dma("sp", xt[b][:], xv[:, :, t0:t0 + NT], reads=[r_xin], writes=[r_xt[b]])
                S.dma("sp", qt[b][:], qTv[:, :, t0:t0 + NT], reads=[r_sc], writes=[r_qt[b]])
                S.dma("sp", mo[b][:], mTv[:, :, t0:t0 + NT], reads=[r_sc], writes=[r_mo[b]])

            def load_kv(h, g0, ng):
                i = grp_ctr[0] % 3
                grp_ctr[0] += 1
                S.dma("sp", kb[i][:, :ng * 128], self.kT[h, :, g0 * 128:(g0 + ng) * 128], reads=[r_sc], writes=[r_kb[i]])
                S.dma("sp", vb[i][:, :ng, :], self.Vs[h, :, g0:g0 + ng, :], reads=[r_sc], writes=[r_vb[i]])
                return i

            def attn_head(qb, h):
                b = qb % 2
                nk = 4 * (qb + 1)
                groups = []
                for g0 in range(0, nk, KG):
                    groups.append((g0, min(KG, nk - g0)))
                gbuf = {}
                for gi in range(min(2, len(groups))):
                    gbuf[gi] = load_kv(h, *groups[gi])
                ectr = [0]
                pend = []

                def qk(kt):
                    gi, lk = kt // KG, kt % KG
                    if gi not in gbuf:
                        gbuf[gi] = load_kv(h, *groups[gi])
                    i = gbuf[gi]
                    j = kt - 4 * qb
                    c0 = 128 * j if j >= 0 else 0
                    sa, sb_ = PS_S[kt % 2]
                    S.add("pe", lambda e: e.matmul(pb[sa][:, c0:NT], kb[i][0:64, lk * 128:(lk + 1) * 128], qt[b][0:64, h, c0:NT],
                                                   start=True, stop=True), reads=[r_kb[i], r_qt[b]], writes=[r_pb[sa]])
                    S.add("pe", lambda e: e.matmul(pb[sb_][:, c0:NT], kb[i][64:128, lk * 128:(lk + 1) * 128], qt[b][64:128, h, c0:NT],
                                                   start=True, stop=True), reads=[r_kb[i], r_qt[b]], writes=[r_pb[sb_]])
                    if j < 0:
                        k3 = ectr[0] % 3
                        ectr[0] += 1
                        e1, e2 = eb[k3][0], eb[k3][1]
                        re1, re2 = r_eb[k3][0], r_eb[k3][1]
                        S.add("act", lambda e: e.activation(e1[:, :], pb[sa][:, :NT], AF.Exp, scale=0.125), reads=[r_pb[sa]], writes=[re1])
                        S.add("act", lambda e: e.activation(e2[:, :], pb[sb_][:, :NT], AF.Exp, scale=0.125), reads=[r_pb[sb_]], writes=[re2])
                    else:
                        e1, e2 = ediag[j][0], ediag[j][1]
                        re1, re2 = r_ed[j][0], r_ed[j][1]
                        for (et, re_, pbn) in ((e1, re1, sa), (e2, re2, sb_)):
                            S.add("act", (lambda et, pbn: lambda e: e.activation(et[0:64, c0:NT], pb[pbn][0:64, c0:NT], AF.Exp, scale=0.125))(et, pbn),
                                  reads=[r_pb[pbn]], writes=[re_])
                            S.add("act", (lambda et, pbn: lambda e: e.activation(et[64:128, c0 + 64:NT], pb[pbn][64:128, c0 + 64:NT], AF.Exp, scale=0.125))(et, pbn),
                                  reads=[r_pb[pbn]], writes=[re_])
                    return (kt, c0, e1, e2, re1, re2, i, lk)

                def pv(item):
                    kt, c0, e1, e2, re1, re2, i, lk = item
                    first, last = (kt == 0), (kt == nk - 1)
                    S.add("pe", lambda e: e.matmul(pb[PO1][:, c0:NT], vb[i][:, lk, :], e1[:, c0:NT], start=first, stop=last, skip_group_check=True),
                          reads=[r_vb[i], re1], writes=[r_pb[PO1]])
                    S.add("pe", lambda e: e.matmul(pb[PO2][:, c0:NT], vb[i][:, lk, :], e2[:, c0:NT], start=first, stop=last, skip_group_check=True),
                          reads=[r_vb[i], re2], writes=[r_pb[PO2]])
                    S.add("pe", lambda e: e.matmul(pb[PL1][:, c0:NT], onesb[:], e1[:, c0:NT], start=first, stop=last, skip_group_check=True),
                          reads=[r_l, re1], writes=[r_pb[PL1]])
                    S.add("pe", lambda e: e.matmul(pb[PL2][:, c0:NT], onesb[:], e2[:, c0:NT], start=first, stop=last, skip_group_check=True),
                          reads=[r_l, re2], writes=[r_pb[PL2]])

                prev = qk(0)
                for kt in range(1, nk):
                    cur = qk(kt)
                    pv(prev)
                    prev = cur
                pv(prev)
                S.add("dve", lambda e: e.reciprocal(rl1[:], pb[PL1][:, :NT]), reads=[r_pb[PL1]], writes=[r_ep])
                S.add("dve", lambda e: e.reciprocal(rl2[:], pb[PL2][:, :NT]), reads=[r_pb[PL2]], writes=[r_ep])
                S.add("dve", lambda e: e.tensor_tensor(t1[:], pb[PO1][:, :NT], rl1[:], ALU.mult), reads=[r_pb[PO1], r_ep], writes=[r_ep])
                S.add("dve", lambda e: e.tensor_tensor(t2[:], pb[PO2][:, :NT], rl2[:], ALU.mult), reads=[r_pb[PO2], r_ep], writes=[r_ep])
                S.add("dve", lambda e: e.scalar_tensor_tensor(A[:], t2[:], neglam[:, 0:1], t1[:], ALU.mult, ALU.add), reads=[r_ep, r_l], writes=[r_ep])
                S.add("pool", lambda e: e.tensor_tensor(asq[:], A[:], A[:], ALU.mult), reads=[r_ep], writes=[r_ep])
                sa = PS_S[nk % 2][0]
                S.add("pe", lambda e: e.matmul(pb[sa][:, :NT], ones, asq[:], start=True, stop=True), reads=[r_c, r_ep], writes=[r_pb[sa]])
                S.add("act", lambda e: e.activation(sd[:], pb[sa][:, :NT], AF.Sqrt, bias=float(EPS), scale=1.0 / 128), reads=[r_pb[sa]], writes=[r_ep])
                S.add("dve", lambda e: e.reciprocal(rs[:], sd[:]), reads=[r_ep], writes=[r_ep])
                S.add("dve", lambda e: e.scalar_tensor_tensor(ao[:, h, :], A[:], gsub[:, 0:1], rs[:], ALU.mult, ALU.mult),
                      reads=[r_ep, r_l], writes=[r_ao[h]])

            def outproj(qb):
                b = qb % 2
                t0 = qb * NT
                for oc in range(8):
                    bk = PS_S[oc % 2][1]
                    for c in range(8):
                        if c < 4:
                            S.add("pe", (lambda oc, c, bk: lambda e: e.matmul(pb[bk][:, :NT], wout[:, c, oc * 128:(oc + 1) * 128], ao[:, c, :],
                                                                               start=(c == 0), stop=False))(oc, c, bk),
                                  reads=[r_wout, r_ao[c]], writes=[r_pb[bk]])
                        else:
                            S.add("pe", (lambda oc, c, bk: lambda e: e.matmul(pb[bk][:, :NT], wout[:, c, oc * 128:(oc + 1) * 128], mo[b][:, c - 4, :],
                                                                               start=False, stop=(c == 7)))(oc, c, bk),
                                  reads=[r_wout, r_mo[b]], writes=[r_pb[bk]])
                    S.add("dve", (lambda oc, bk: lambda e: e.tensor_tensor(xo[:, oc, :], pb[bk][:, :NT], xt[b][:, oc, :], ALU.add))(oc, bk),
                          reads=[r_pb[bk], r_xt[b]], writes=[r_xo])
                S.dma("pool", yv[:, :, t0:t0 + NT], xo[:], reads=[r_xo], writes=[r_xout])

            load(0)
            for qb in range(nblk):
                if qb + 1 < nblk:
                    load(qb + 1)
                for h in range(4):
                    attn_head(qb, h)
                outproj(qb)
            S.flush()

    def gla_sweep(self, S, li, xin, r_xin, xout, r_xout):
        T = self.T
        NT = 512
        o_ = li // 2
        scale = 128.0 ** -0.5
        with ExitStack() as st:
            cst, vecs, r_c = self.load_consts(S, st)
            ones = cst[:, C_ONES:C_ONES + 128]
            tri = cst[:, C_TRI:C_TRI + 128]
            su = cst[:, C_SU:C_SU + 128]
            tri4 = cst[:, C_TRI4:C_TRI4 + 512]
            win = self.sb(st, [128, 8, 3104], BF16)
            wout = self.sb(st, [128, 8, D], BF16)
            wgk = self.sb(st, [64, 1, 512], BF16)
            r_win, r_wout, r_wgk = S.res(), S.res(), S.res()
            with ExitStack() as st2:
                wv = self.w["gla_w_in"][o_].rearrange("(kc p) n -> p kc n", p=128)
                self.load_weight(S, st2, win, [wv[:, kc, :] for kc in range(8)], 3088, [r_c], r_win,
                                 scale_cols=[vecs[:, V_NMIX + 8 * li + kc:V_NMIX + 8 * li + kc + 1] for kc in range(8)], cw=1024)
                wo = self.w["gla_w_out"][o_].rearrange("(kc p) n -> p kc n", p=128)
                self.load_weight(S, st2, wout, [wo[:, kc, :] for kc in range(8)], D, [r_c], r_wout)
                stg = self.sb(st2, [64, 512], F32)
                r_s = S.res()
                S.add("pool", lambda e: e.memset(stg[:], 0.0), writes=[r_s])
                S.add("pool", lambda e: e.memset(win[:, :, 3088:3104], 0.0), writes=[r_win])
                S.dma("sp", stg[0:16, :], self.w["gla_w_gk_up"][o_], reads=[r_s], writes=[r_s])
                S.dma("sp", stg[32:33, :], self.w["gla_b_gk"][o_], reads=[r_s], writes=[r_s])
                S.add("dve", lambda e: e.tensor_copy(wgk[:, 0, :], stg[:]), reads=[r_s], writes=[r_wgk])
                S.flush()
            xt = [self.sb(st, [128, 8, NT], F32) for _ in range(2)]
            sq = self.sb(st, [128, 8, NT], F32)
            xo = sq
            ssum = self.sb(st, [128, NT], F32)
            rstd = self.sb(st, [128, NT], F32)
            h = self.sb(st, [128, 8, NT], BF16)
            qf = self.sb(st, [128, 4, NT], F32)
            kf = self.sb(st, [128, 4, NT], F32)
            gate = self.sb(st, [128, 8, NT], F32)
            gl = self.sb(st, [64, NT], BF16)
            ktok = self.sb(st, [128, 512], F32)
            vtok = self.sb(st, [128, 1024], BF16)
            ez = self.sb(st, [128, 512], F32)
            gtok = self.sb(st, [128, 512], F32)
            ebp = self.sb(st, [128, 512], F32)
            enb = self.sb(st, [128, 512], F32)
            er = ez
            qd = self.sb(st, [128, 4, 128], BF16)
            kd = self.sb(st, [128, 4, 128], BF16)
            kl = self.sb(st, [128, 512], BF16)
            am = self.sb(st, [128, 512], BF16)
            Sf = self.sb(st, [128, 4, 256], F32)
            Sb = [self.sb(st, [128, 4, 256], BF16) for _ in range(2)]
            osq = self.sb(st, [128, 8, 128], F32)
            sdn = self.sb(st, [128, 512], F32)
            rsn = self.sb(st, [128, 512], F32)
            tg = self.sb(st, [128, 8, 128], F32)
            og = self.sb(st, [128, 8, NT], BF16)
            pb = self.psum(st)
            R = S.res
            r_xt = [R(), R()]; r_sq = R(); r_xo = r_sq; r_ssum = R(); r_rstd = R()
            r_h = [R() for _ in range(8)]; r_pb = [R() for _ in range(8)]
            r_qf, r_kf, r_gate, r_gl = R(), R(), R(), R()
            r_ktok, r_vtok, r_ez, r_gtok, r_ebp, r_enb = R(), R(), R(), R(), R(), R(); r_er = r_ez
            r_qd, r_kd, r_kl, r_am = R(), R(), R(), R()
            r_Sf = [R() for _ in range(4)]; r_Sb = [[R() for _ in range(4)] for _ in range(2)]
            r_osq, r_sdn, r_rsn, r_tg, r_og = R(), R(), R(), R(), R()
            xv = xin.rearrange("(c p) t -> p c t", p=128)
            yv = xout.rearrange("(c p) t -> p c t", p=128)
            ntile = T // NT
            S.add("pool", lambda e: e.memset(Sf[:], 0.0), writes=r_Sf)
            S.add("pool", lambda e: e.memset(Sb[0][:], 0.0), writes=r_Sb[0])
            S.add("pool", lambda e: e.memset(Sb[1][:], 0.0), writes=r_Sb[1])
            S.add("pool", lambda e: e.memset(gl[:], 0.0), writes=[r_gl])
            S.add("pool", lambda e: e.memset(gl[32:64, :], 1.0), writes=[r_gl])
            chunk_ctr = [0]

            def load(n):
                S.dma("sp", xt[n % 2][:], xv[:, :, n * NT:(n + 1) * NT], reads=[r_xin], writes=[r_xt[n % 2]])

            def norm(n):
                b = n % 2
                self.rmsnorm_tile(S, xt[b], r_xt[b], NT, sq, r_sq, ssum, r_ssum, rstd, r_rstd, pb[1], r_pb[1],
                                  ones, r_c, h, r_h)

            def fm(oc0, ncols, dst_fn, bk):
                for kc in range(8):
                    S.add("pe", (lambda kc: lambda e: e.matmul(pb[bk][:ncols, :NT], win[:, kc, oc0:oc0 + ncols], h[:, kc, :],
                                                               start=(kc == 0), stop=(kc == 7)))(kc),
                          reads=[r_win, r_h[kc]], writes=[r_pb[bk]])
                dst_fn(bk)

            def proj_fm(n):
                k = 0
                for c in range(4):
                    bk = k % 2; k += 1
                    fm(c * 128, 128, (lambda c: lambda bk: S.add("act", lambda e: e.copy(qf[:, c, :], pb[bk][:, :NT]),
                                                                  reads=[r_pb[bk]], writes=[r_qf]))(c), bk)
                for c in range(4):
                    bk = k % 2; k += 1
                    fm(512 + c * 128, 128, (lambda c: lambda bk: S.add("dve", lambda e: e.tensor_copy(kf[:, c, :], pb[bk][:, :NT]),
                                                                        reads=[r_pb[bk]], writes=[r_kf]))(c), bk)
                for c in range(8):
                    bk = k % 2; k += 1
                    fm(2048 + c * 128, 128, (lambda c: lambda bk: S.add("act", lambda e: e.activation(gate[:, c, :], pb[bk][:, :NT], AF.Silu),
                                                                         reads=[r_pb[bk]], writes=[r_gate]))(c), bk)
                bk = k % 2; k += 1
                fm(3072, 32, lambda bk: S.add("dve", lambda e: e.tensor_copy(gl[0:16, :], pb[bk][0:16, :NT]),
                                              reads=[r_pb[bk]], writes=[r_gl]), bk)

            def chunk(n, s):
                b = n % 2
                ssl = slice(s * 128, (s + 1) * 128)
                cc = chunk_ctr[0]
                chunk_ctr[0] += 1
                sbr, sbw = Sb[cc % 2], Sb[1 - cc % 2]
                r_sbr, r_sbw = r_Sb[cc % 2], r_Sb[1 - cc % 2]
                for kc in range(8):
                    S.add("pe", (lambda kc: lambda e: e.matmul(pb[2][:, :512], h[:, kc, ssl], win[:, kc, 512:1024],
                                                               start=(kc == 0), stop=(kc == 7)))(kc), reads=[r_win, r_h[kc]], writes=[r_pb[2]])
                S.add("act", lambda e: e.copy(ktok[:], pb[2][:, :512]), reads=[r_pb[2]], writes=[r_ktok])
                for hf in range(2):
                    bk = 3 + hf
                    for kc in range(8):
                        S.add("pe", (lambda kc, hf, bk: lambda e: e.matmul(pb[bk][:, :512], h[:, kc, ssl], win[:, kc, 1024 + hf * 512:1536 + hf * 512],
                                                                           start=(kc == 0), stop=(kc == 7)))(kc, hf, bk),
                              reads=[r_win, r_h[kc]], writes=[r_pb[bk]])
                    S.add("dve", (lambda hf, bk: lambda e: e.tensor_copy(vtok[:, hf * 512:(hf + 1) * 512], pb[bk][:, :512]))(hf, bk),
                          reads=[r_pb[bk]], writes=[r_vtok])
                S.add("pe", lambda e: e.matmul(pb[5][:, :512], gl[:, ssl], wgk[:, 0, :], start=True, stop=True),
                      reads=[r_gl, r_wgk], writes=[r_pb[5]])
                S.add("act", lambda e: e.activation(ez[:], pb[5][:, :512], AF.Exp, scale=-1.0), reads=[r_pb[5]], writes=[r_ez])
                S.add("act", lambda e: e.activation(ez[:], ez[:], AF.Ln, bias=1.0), reads=[r_ez], writes=[r_ez])
                S.add("dve", lambda e: e.tensor_scalar(gtok[:], ez[:], -1.0 / 16.0, None, ALU.mult), reads=[r_ez], writes=[r_gtok])
                for hh in range(4):
                    S.add("pe", (lambda hh: lambda e: e.matmul(pb[6][:, hh * 128:(hh + 1) * 128], gtok[:, hh * 128:(hh + 1) * 128], tri,
                                                               start=True, stop=True))(hh), reads=[r_gtok, r_c], writes=[r_pb[6]])
                S.add("pe", lambda e: e.matmul(pb[7][:, :512], su, gtok[:], start=True, stop=True), reads=[r_gtok, r_c], writes=[r_pb[7]])
                S.add("act", lambda e: e.activation(ebp[:], pb[6][:, :512], AF.Exp), reads=[r_pb[6]], writes=[r_ebp])
                S.add("act", lambda e: e.activation(enb[:], pb[6][:, :512], AF.Exp, scale=-1.0), reads=[r_pb[6]], writes=[r_enb])
                S.add("act", lambda e: e.activation(er[:], pb[7][:, :512], AF.Exp), reads=[r_pb[7]], writes=[r_er])
                S.add("dve", lambda e: e.scalar_tensor_tensor(qd[:], qf[:, :, ssl], float(scale), ebp[:].rearrange("p (h t) -> p h t", h=4),
                                                              ALU.mult, ALU.mult), reads=[r_qf, r_ebp], writes=[r_qd])
                S.add("dve", lambda e: e.tensor_tensor(kd[:], kf[:, :, ssl], enb[:].rearrange("p (h t) -> p h t", h=4), ALU.mult),
                      reads=[r_kf, r_enb], writes=[r_kd])
                S.add("pool", lambda e: e.tensor_tensor(kl[:], ktok[:], er[:], ALU.mult), reads=[r_ktok, r_er], writes=[r_kl])
                for hh in range(4):
                    S.add("pe", (lambda hh: lambda e: e.matmul(pb[2][:, hh * 128:(hh + 1) * 128], kd[:, hh, :], qd[:, hh, :],
                                                               start=True, stop=True))(hh), reads=[r_kd, r_qd], writes=[r_pb[2]])
                S.add("dve", lambda e: e.tensor_tensor(am[:], pb[2][:, :512], tri4, ALU.mult), reads=[r_pb[2], r_c], writes=[r_am])
                for hh in range(4):
                    for vc in range(2):
                        bk = 3 + hh // 2
                        col = ((hh % 2) * 2 + vc) * 128
                        S.add("pe", (lambda hh, vc, bk, col: lambda e: e.matmul(
                            pb[bk][:, col:col + 128], vtok[:, hh * 256 + vc * 128:hh * 256 + (vc + 1) * 128], am[:, hh * 128:(hh + 1) * 128],
                            start=True, stop=False))(hh, vc, bk, col), reads=[r_vtok, r_am], writes=[r_pb[bk]])
                        S.add("pe", (lambda hh, vc, bk, col: lambda e: e.matmul(
                            pb[bk][:, col:col + 128], sbr[:, hh, vc * 128:(vc + 1) * 128], qd[:, hh, :],
                            start=False, stop=True))(hh, vc, bk, col), reads=[r_sbr[hh], r_qd], writes=[r_pb[bk]])
                for hh in range(4):
                    bk = 5 + hh // 2
                    col = (hh % 2) * 256
                    S.add("pe", (lambda hh, bk, col: lambda e: e.matmul(pb[bk][:, col:col + 256], kl[:, hh * 128:(hh + 1) * 128],
                                                                        vtok[:, hh * 256:(hh + 1) * 256], start=True, stop=True))(hh, bk, col),
                          reads=[r_kl, r_vtok], writes=[r_pb[bk]])
                    S.add("dve", (lambda hh, bk, col: lambda e: e.scalar_tensor_tensor(
                        Sf[:, hh, :], Sf[:, hh, :], ebp[:, hh * 128 + 127:hh * 128 + 128], pb[bk][:, col:col + 256], ALU.mult, ALU.add))(hh, bk, col),
                        reads=[r_pb[bk], r_ebp], writes=[r_Sf[hh]])
                    S.add("act", (lambda hh: lambda e: e.copy(sbw[:, hh, :], Sf[:, hh, :]))(hh), reads=[r_Sf[hh]], writes=[r_sbw[hh]])
                for q2 in range(2):
                    S.add("act", (lambda q2: lambda e: e.activation(osq[:, q2 * 4:(q2 + 1) * 4, :],
                                                                    pb[3 + q2][:, :512].rearrange("p (c t) -> p c t", c=4), AF.Square))(q2),
                          reads=[r_pb[3 + q2]], writes=[r_osq])
                for hh in range(4):
                    for vc in range(2):
                        S.add("pe", (lambda hh, vc: lambda e: e.matmul(pb[7][:, hh * 128:(hh + 1) * 128], ones, osq[:, hh * 2 + vc, :],
                                                                       start=(vc == 0), stop=(vc == 1)))(hh, vc), reads=[r_osq, r_c], writes=[r_pb[7]])
                S.add("act", lambda e: e.activation(sdn[:], pb[7][:, :512], AF.Sqrt, bias=float(EPS), scale=1.0 / 256), reads=[r_pb[7]], writes=[r_sdn])
                S.add("dve", lambda e: e.reciprocal(rsn[:], sdn[:]), reads=[r_sdn], writes=[r_rsn])
                for c in range(8):
                    hh = c // 2
                    S.add("pool", (lambda c, hh: lambda e: e.tensor_tensor(tg[:, c, :], gate[:, c, ssl], rsn[:, hh * 128:(hh + 1) * 128], ALU.mult))(c, hh),
                          reads=[r_gate, r_rsn], writes=[r_tg])
                for c in range(8):
                    bk = 3 + c // 4
                    col = (c % 4) * 128
                    S.add("dve", (lambda c, bk, col: lambda e: e.scalar_tensor_tensor(
                        og[:, c, ssl], pb[bk][:, col:col + 128], vecs[:, V_GNORM + 8 * o_ + c:V_GNORM + 8 * o_ + c + 1], tg[:, c, :],
                        ALU.mult, ALU.mult))(c, bk, col), reads=[r_pb[bk], r_tg, r_c], writes=[r_og])

            def outproj(n):
                b = n % 2
                t0 = n * NT
                for oc in range(8):
                    bk = oc % 2
                    for c in range(8):
                        S.add("pe", (lambda oc, c, bk: lambda e: e.matmul(pb[bk][:, :NT], wout[:, c, oc * 128:(oc + 1) * 128], og[:, c, :],
                                                                           start=(c == 0), stop=(c == 7)))(oc, c, bk),
                              reads=[r_wout, r_og], writes=[r_pb[bk]])
                    S.add("dve", (lambda oc, bk: lambda e: e.tensor_tensor(xo[:, oc, :], pb[bk][:, :NT], xt[b][:, oc, :], ALU.add))(oc, bk),
                          reads=[r_pb[bk], r_xt[b]], writes=[r_xo])
                S.dma("pool", yv[:, :, t0:t0 + NT], xo[:], reads=[r_xo], writes=[r_xout])

            load(0)
            for n in range(ntile):
                if n + 1 < ntile:
                    load(n + 1)
                norm(n)
                proj_fm(n)
                for s in range(4):
                    chunk(n, s)
                outproj(n)
            S.flush()

    def build(self):
        nc = self.nc
        with ExitStack() as st:
            S = Sched(nc, st)
            self.r_scr = S.res()
            r_in = S.res()
            r_a, r_b = S.res(), S.res()
            cur, r_cur = self.x_in, r_in
            nl = len(self.layers)
            for idx, li in enumerate(self.layers):
                last = (idx == nl - 1)
                if li % 2 == 0:
                    self.a1_sweep(S, li, cur, r_cur)
                    self.a2_sweep(S, li, cur, r_cur, self.xa, r_a)
                else:
                    self.gla_sweep(S, li, cur, r_cur, self.xa, r_a)
                if last:
                    self.ffn_sweep(S, li, self.xa, r_a, self.y_out, S.res(), final=self.final)
                else:
                    self.ffn_sweep(S, li, self.xa, r_a, self.xb, r_b, final=False)
                    cur, r_cur = self.xb, r_b
            self.ninst = S.ninst
        return nc


def host_inputs(inp, T_sl=None):
    cst = make_cst()
    vecs = make_vecs(inp)
    common = {
        "cst": cst, "vecs": vecs,
        "ab_w_in": np.ascontiguousarray(inp["ab_w_in"], np.float32),
        "ab_lambda": np.ascontiguousarray(np.asarray(inp["ab_lambda"], np.float32).reshape(2, 256)),
        "pool_w": np.ascontiguousarray(inp["pool_w"], np.float32),
        "ab_w_out": np.ascontiguousarray(inp["ab_w_out"], np.float32),
        "gla_w_in": np.ascontiguousarray(inp["gla_w_in"], np.float32),
        "gla_w_gk_up": np.ascontiguousarray(inp["gla_w_gk_up"], np.float32),
        "gla_b_gk": np.ascontiguousarray(np.asarray(inp["gla_b_gk"], np.float32).reshape(2, 1, 512)),
        "gla_w_out": np.ascontiguousarray(inp["gla_w_out"], np.float32),
        "ffn_w1": np.ascontiguousarray(inp["ffn_w1"], np.float32),
        "ffn_w2": np.ascontiguousarray(inp["ffn_w2"], np.float32),
    }
    return common


_CACHE = {}


def kernel(**inputs):
    x = np.asarray(inputs["x"], np.float32)
    B, T, _ = x.shape
    key = (T,)
    if key not in _CACHE:
        _CACHE[key] = Builder(T).build()
    nc = _CACHE[key]
    common = host_inputs(inputs)
    in_maps = []
    for c in range(NCORES):
        m = dict(common)
        m["xT"] = np.ascontiguousarray(x[c % B].T)
        in_maps.append(m)
    res = run_bass_kernel_spmd(nc, in_maps, core_ids=list(range(NCORES)))
    out = np.empty((B, T, D), np.float32)
    for b in range(B):
        out[b] = res.results[b]["yT"].T
    return out
```

```python
import math
from contextlib import ExitStack

import numpy as np
import concourse.bass as bass
import concourse.mybir as mybir
from concourse.bass_utils import run_bass_kernel_spmd

F32 = mybir.dt.float32
BF16 = mybir.dt.bfloat16
ALU = mybir.AluOpType
AF = mybir.ActivationFunctionType
AX = mybir.AxisListType

D = 1024
DFF = 4096
EPS = 1e-6
DEPTH = 4
NCORES = 8
ACTIVE = (0, 1, 4, 5)

ENGS = ("pe", "act", "dve", "pool", "sp")
NDMA_SEMS = 8


class Res:
    __slots__ = ("writer", "readers")

    def __init__(self):
        self.writer = None
        self.readers = []


class Op:
    __slots__ = ("eng", "fn", "deps", "is_dma", "sig", "count", "dsem", "dval", "waits", "snap", "done")

    def __init__(self, eng, fn, is_dma):
        self.eng = eng
        self.fn = fn
        self.is_dma = is_dma
        self.deps = []
        self.sig = False
        self.count = 0
        self.dsem = None
        self.dval = 0
        self.waits = []
        self.snap = None
        self.done = False


class Sched:
    def __init__(self, nc, stack):
        self.nc = nc
        self.ops = {e: [] for e in ENGS}
        self.cnt = {e: 0 for e in ENGS}
        self.ndma = {e: 0 for e in ENGS}
        self.dma_hist = {e: [] for e in ENGS}
        self.known = {e: [0] * len(ENGS) for e in ENGS}
        self.kdma = {e: {} for e in ENGS}
        self.esem = {e: stack.enter_context(nc.semaphore(f"s_{e}")) for e in ENGS if e != "sp"}
        self.dsem = {}
        for e in ("sp", "pool", "act"):
            for k in range(NDMA_SEMS):
                self.dsem[(e, k)] = stack.enter_context(nc.semaphore(f"d_{e}{k}"))
        self.ninst = 0

    def res(self):
        return Res()

    def add(self, eng, fn, reads=(), writes=(), is_dma=False):
        op = Op(eng, fn, is_dma)
        deps = []
        for r in reads:
            if r.writer is not None:
                deps.append(r.writer)
            r.readers.append(op)
        for w in writes:
            if w.writer is not None:
                deps.append(w.writer)
            deps.extend(w.readers)
            w.writer = op
            w.readers = []
        seen = set()
        for d in deps:
            if d is op or d.done or id(d) in seen:
                continue
            seen.add(id(d))
            op.deps.append(d)
        self.ops[eng].append(op)
        return op

    def dma(self, q, out, in_, reads=(), writes=()):
        return self.add(q, lambda e: e.dma_start(out=out, in_=in_), reads, writes, is_dma=True)

    def barrier(self):
        last = []
        for e in ENGS:
            for o in reversed(self.ops[e]):
                if o.fn is not None and not o.is_dma:
                    last.append(o)
                    break
        rec = []
        for e in ENGS:
            dm = [o for o in self.ops[e] if o.is_dma][-NDMA_SEMS:]
            rec.extend(dm)
        for e in ENGS:
            op = Op(e, None, False)
            op.deps = list(last) + list(rec)
            self.ops[e].append(op)

    def flush(self):
        self.barrier()
        for e in ENGS:
            for op in self.ops[e]:
                for d in op.deps:
                    d.sig = True
        for e in ENGS:
            for op in self.ops[e]:
                if op.is_dma:
                    n = self.ndma[e]
                    op.dsem = (e, n % NDMA_SEMS)
                    op.dval = 16 * (n // NDMA_SEMS + 1)
                    self.dma_hist[e].append(op)
                    self.ndma[e] += 1
                elif op.sig:
                    self.cnt[e] += 1
                    op.count = self.cnt[e]
        eidx = {e: i for i, e in enumerate(ENGS)}
        for e in ENGS:
            known = self.known[e]
            kdma = self.kdma[e]
            hist = self.dma_hist[e]
            nd = len(hist) - sum(1 for o in self.ops[e] if o.is_dma)
            for op in self.ops[e]:
                deps = list(op.deps)
                if op.is_dma:
                    if nd >= NDMA_SEMS:
                        deps.append(hist[nd - NDMA_SEMS])
                    nd += 1
                best = {}
                for d in deps:
                    if d.is_dma:
                        if kdma.get(d.dsem, 0) >= d.dval:
                            continue
                        kdma[d.dsem] = d.dval
                        best[("dma", d.dsem)] = d.dval
                    else:
                        if d.eng == "pe" and e == "pe":
                            continue
                        j = eidx[d.eng]
                        if known[j] >= d.count:
                            continue
                        known[j] = d.count
                        best[("eng", d.eng)] = max(best.get(("eng", d.eng), 0), d.count)
                        if d.snap is not None:
                            for k in range(len(ENGS)):
                                if d.snap[k] > known[k]:
                                    known[k] = d.snap[k]
                op.waits = [(k[0], k[1], v) for k, v in best.items()]
                op.snap = tuple(known)
        nc = self.nc
        ops = self.ops
        esem, dsem = self.esem, self.dsem

        def run(eng_name):
            lst = ops[eng_name]

            def body(eng):
                for op in lst:
                    for kind, key, val in op.waits:
                        eng.wait_ge(dsem[key] if kind == "dma" else esem[key], val)
                    if op.fn is None:
                        continue
                    ins = op.fn(eng)
                    if op.is_dma:
                        ins.then_inc(dsem[op.dsem], 16)
                    elif op.sig:
                        ins.then_inc(esem[eng_name], 1)
            return body

        with nc.Block() as block:
            block.tensor(run("pe"))
            block.scalar(run("act"))
            block.vector(run("dve"))
            block.gpsimd(run("pool"))
            block.sync(run("sp"))
        for e in ENGS:
            self.ninst += len(self.ops[e])
            for op in self.ops[e]:
                op.done = True
                op.fn = None
                op.deps = []
            self.ops[e] = []


C_ONES = 0
C_TRI = 128
C_SU = 256
C_TRI4 = 384
C_INVC = 896
NCST = 960


def make_cst():
    c = np.zeros((128, NCST), np.float32)
    c[:, C_ONES:C_ONES + 128] = 1.0
    s = np.arange(128)
    tri = (s[:, None] <= s[None, :]).astype(np.float32)
    c[:, C_TRI:C_TRI + 128] = tri
    c[:, C_SU:C_SU + 128] = (s[:, None] > s[None, :]).astype(np.float32)
    for h in range(4):
        c[:, C_TRI4 + h * 128:C_TRI4 + (h + 1) * 128] = tri
    for g, w in enumerate((2, 4, 8, 16)):
        t = np.arange(16)
        c[:, C_INVC + g * 16:C_INVC + (g + 1) * 16] = 1.0 / np.minimum(t + 1, w)
    return c


V_NMIX = 0
V_NFFN = 32
V_NFIN = 64
V_PSCALE = 72
V_SUBLN = 80
V_GNORM = 82
NVEC = 98


def make_vecs(inp):
    v = np.zeros((128, NVEC), np.float32)

    def put(col, arr):
        a = np.asarray(arr, np.float32).reshape(-1, 128).T
        v[:, col:col + a.shape[1]] = a

    for i in range(DEPTH):
        put(V_NMIX + 8 * i, inp["norm_mix"][i])
        put(V_NFFN + 8 * i, inp["norm_ffn"][i])
    put(V_NFIN, inp["norm_final"])
    for e in range(2):
        put(V_PSCALE + 4 * e, inp["pool_scale"][e])
        put(V_SUBLN + e, inp["ab_subln"][e])
        put(V_GNORM + 8 * e, inp["gla_norm"][e].reshape(-1))
    return v


class Builder:
    def __init__(self, T, layers=(0, 1, 2, 3), final=True, dbg=False):
        self.T = T
        self.layers = tuple(layers)
        self.final = final
        nc = self.nc = bass.Bass("TRN2", target_bir_lowering=False)
        dt = nc.dram_tensor
        self.x_in = dt("xT", [D, T], F32, kind="ExternalInput").ap()
        self.y_out = dt("yT", [D, T], F32, kind="ExternalOutput").ap()
        self.cst_d = dt("cst", [128, NCST], F32, kind="ExternalInput").ap()
        self.vecs_d = dt("vecs", [128, NVEC], F32, kind="ExternalInput").ap()
        self.w = {}
        for name, shape in (("ab_w_in", [2, D, 2048]), ("ab_lambda", [2, 256]), ("pool_w", [2, 4, 128, 128]),
                            ("ab_w_out", [2, D, D]), ("gla_w_in", [2, D, 3088]), ("gla_w_gk_up", [2, 16, 512]),
                            ("gla_b_gk", [2, 1, 512]), ("gla_w_out", [2, D, D]), ("ffn_w1", [4, D, DFF]),
                            ("ffn_w2", [4, DFF, D])):
            self.w[name] = dt(name, shape, F32, kind="ExternalInput").ap()
        kw = {"kind": "ExternalOutput"} if dbg else {}
        self.xa = dt("xa", [D, T], F32, **kw).ap()
        self.xb = dt("xb", [D, T], F32, **kw).ap()
        self.qT = dt("qTs", [4, 128, T], BF16, **kw).ap()
        self.kT = dt("kTs", [4, 128, T], BF16, **kw).ap()
        self.Vs = dt("Vs", [4, 128, T // 128, 128], BF16, **kw).ap()
        self.mT = dt("mTs", [4, 128, T], BF16, **kw).ap()
        self.r_x = {}
        self.nsb = 0

    def sb(self, st, shape, dtp):
        self.nsb += 1
        return st.enter_context(self.nc.sbuf_tensor(f"sb{self.nsb}", shape, dtp))

    def psum(self, st):
        self.nsb += 1
        return [st.enter_context(self.nc.psum_tensor(f"ps{self.nsb}_{i}", [128, 512], F32)) for i in range(8)]

    def load_consts(self, S, st):
        cst = self.sb(st, [128, NCST], F32)
        vecs = self.sb(st, [128, NVEC], F32)
        r = S.res()
        S.dma("sp", cst[:], self.cst_d, writes=[r])
        S.dma("sp", vecs[:], self.vecs_d, writes=[r])
        return cst, vecs, r

    def load_weight(self, S, st_tmp, dst, src_rows, ncols, rres, wres, scale_cols=None, cw=1024):
        stg = [self.sb(st_tmp, [128, cw], F32) for _ in range(3)]
        r_stg = [S.res() for _ in range(3)]
        engs = ["pool", "dve", "act"]
        ci = 0
        for r, src in enumerate(src_rows):
            for c0 in range(0, ncols, cw):
                c1 = min(ncols, c0 + cw)
                b = ci % 3
                S.dma("sp", stg[b][:, :c1 - c0], src[:, c0:c1], writes=[r_stg[b]])
                out = dst[:, r, c0:c1]
                in_ = stg[b][:, :c1 - c0]
                eng = engs[ci % 3]
                sc = None if scale_cols is None else scale_cols[r]
                if sc is None:
                    if eng == "act":
                        S.add("act", (lambda o, i: lambda e: e.copy(o, i))(out, in_), reads=[r_stg[b]] + rres, writes=[wres])
                    else:
                        S.add(eng, (lambda o, i: lambda e: e.tensor_copy(o, i))(out, in_), reads=[r_stg[b]] + rres, writes=[wres])
                else:
                    if eng == "act":
                        S.add("act", (lambda o, i, s: lambda e: e.activation(o, i, AF.Copy, scale=s))(out, in_, sc),
                              reads=[r_stg[b]] + rres, writes=[wres])
                    else:
                        S.add(eng, (lambda o, i, s: lambda e: e.tensor_scalar(o, i, s, None, ALU.mult))(out, in_, sc),
                              reads=[r_stg[b]] + rres, writes=[wres])
                ci += 1

    def rmsnorm_tile(self, S, xt_c, r_xt, NT, sq, r_sq, ssum, r_ssum, rstd, r_rstd, psb, r_psb, ones, r_c, h, r_h, nfeat_inv=1.0 / D):
        S.add("act", lambda e: e.activation(sq[:], xt_c[:], AF.Square), reads=[r_xt], writes=[r_sq])
        S.add("dve", lambda e: e.tensor_reduce(ssum[:], sq[:].rearrange("p c t -> p t c"), AX.X, ALU.add),
              reads=[r_sq], writes=[r_ssum])
        S.add("pe", lambda e: e.matmul(psb[:, :NT], ones, ssum[:], start=True, stop=True),
              reads=[r_c, r_ssum], writes=[r_psb])
        S.add("act", lambda e: e.activation(ssum[:], psb[:, :NT], AF.Sqrt, bias=float(EPS), scale=nfeat_inv),
              reads=[r_psb], writes=[r_ssum])
        S.add("dve", lambda e: e.reciprocal(rstd[:], ssum[:]), reads=[r_ssum], writes=[r_rstd])
        for c in range(8):
            eng = "dve" if c % 2 == 0 else "pool"
            S.add(eng, (lambda c: lambda e: e.tensor_tensor(h[:, c, :], xt_c[:, c, :], rstd[:], ALU.mult))(c),
                  reads=[r_xt, r_rstd], writes=[r_h[c]])

    def ffn_sweep(self, S, li, xin, r_xin, xout, r_xout, final):
        T = self.T
        FNT = 256
        nc = self.nc
        with ExitStack() as st:
            cst, vecs, r_c = self.load_consts(S, st)
            ones = cst[:, C_ONES:C_ONES + 128]
            w1sb = self.sb(st, [128, 8, DFF], BF16)
            w2sb = self.sb(st, [128, 32, D], BF16)
            r_w1, r_w2 = S.res(), S.res()
            with ExitStack() as st2:
                w1v = self.w["ffn_w1"][li].rearrange("(kc p) n -> p kc n", p=128)
                w2v = self.w["ffn_w2"][li].rearrange("(hc p) n -> p hc n", p=128)
                self.load_weight(S, st2, w1sb, [w1v[:, kc, :] for kc in range(8)], DFF, [r_c], r_w1,
                                 scale_cols=[vecs[:, V_NFFN + 8 * li + kc:V_NFFN + 8 * li + kc + 1] for kc in range(8)])
                self.load_weight(S, st2, w2sb, [w2v[:, hc, :] for hc in range(32)], D, [r_c], r_w2)
                S.flush()
            xt = [self.sb(st, [128, 8, FNT], F32) for _ in range(2)]
            xo = self.sb(st, [128, 8, FNT], F32)
            sq = self.sb(st, [128, 8, FNT], F32)
            ssum = self.sb(st, [128, FNT], F32)
            rstd = self.sb(st, [128, FNT], F32)
            h = self.sb(st, [128, 8, FNT], BF16)
            hid = self.sb(st, [128, 32, FNT], BF16)
            rl = [self.sb(st, [128, FNT], F32) for _ in range(4)]
            pb = self.psum(st)
            R = S.res
            r_xt = [R(), R()]; r_xo = R(); r_sq = R(); r_ssum = R(); r_rstd = R()
            r_h = [R() for _ in range(8)]; r_hid = [R() for _ in range(32)]
            r_pb = [R() for _ in range(8)]; r_rl = [R() for _ in range(4)]
            xv = xin.rearrange("(c p) t -> p c t", p=128)
            yv = xout.rearrange("(c p) t -> p c t", p=128)
            ntile = T // FNT

            def load(n):
                S.dma("sp", xt[n % 2][:], xv[:, :, n * FNT:(n + 1) * FNT], reads=[r_xin], writes=[r_xt[n % 2]])

            def norm(n):
                b = n % 2
                self.rmsnorm_tile(S, xt[b], r_xt[b], FNT, sq, r_sq, ssum, r_ssum, rstd, r_rstd, pb[7], r_pb[7],
                                  ones, r_c, h, r_h)

            def up(n):
                for j in range(32):
                    bk = j % 4
                    for kc in range(8):
                        S.add("pe", (lambda j, kc, bk: lambda e: e.matmul(
                            pb[bk][:, :FNT], w1sb[:, kc, j * 128:(j + 1) * 128], h[:, kc, :],
                            start=(kc == 0), stop=(kc == 7)))(j, kc, bk), reads=[r_w1, r_h[kc]], writes=[r_pb[bk]])
                    S.add("act", (lambda j, bk: lambda e: e.activation(rl[j % 4][:], pb[bk][:, :FNT], AF.Relu))(j, bk),
                          reads=[r_pb[bk]], writes=[r_rl[j % 4]])
                    S.add("pool", (lambda j: lambda e: e.tensor_tensor(hid[:, j, :], rl[j % 4][:], rl[j % 4][:], ALU.mult))(j),
                          reads=[r_rl[j % 4]], writes=[r_hid[j]])

            def down(n):
                b = n % 2
                for oc in range(8):
                    bk = 4 + oc % 2
                    for hc in range(32):
                        S.add("pe", (lambda oc, hc, bk: lambda e: e.matmul(
                            pb[bk][:, :FNT], w2sb[:, hc, oc * 128:(oc + 1) * 128], hid[:, hc, :],
                            start=(hc == 0), stop=(hc == 31)))(oc, hc, bk), reads=[r_w2, r_hid[hc]], writes=[r_pb[bk]])
                    S.add("dve", (lambda oc, bk: lambda e: e.tensor_tensor(
                        xo[:, oc, :], pb[bk][:, :FNT], xt[b][:, oc, :], ALU.add))(oc, bk),
                        reads=[r_pb[bk], r_xt[b]], writes=[r_xo])
                if final:
                    S.add("act", lambda e: e.activation(sq[:], xo[:], AF.Square), reads=[r_xo], writes=[r_sq])
                    S.add("dve", lambda e: e.tensor_reduce(ssum[:], sq[:].rearrange("p c t -> p t c"), AX.X, ALU.add),
                          reads=[r_sq], writes=[r_ssum])
                    S.add("pe", lambda e: e.matmul(pb[6][:, :FNT], ones, ssum[:], start=True, stop=True),
                          reads=[r_c, r_ssum], writes=[r_pb[6]])
                    S.add("act", lambda e: e.activation(ssum[:], pb[6][:, :FNT], AF.Sqrt, bias=float(EPS), scale=1.0 / D),
                          reads=[r_pb[6]], writes=[r_ssum])
                    S.add("dve", lambda e: e.reciprocal(rstd[:], ssum[:]), reads=[r_ssum], writes=[r_rstd])
                    for c in range(8):
                        S.add("dve", (lambda c: lambda e: e.scalar_tensor_tensor(
                            sq[:, c, :], xo[:, c, :], vecs[:, V_NFIN + c:V_NFIN + c + 1], rstd[:], ALU.mult, ALU.mult))(c),
                            reads=[r_xo, r_rstd, r_c], writes=[r_sq])
                    S.dma("pool", yv[:, :, n * FNT:(n + 1) * FNT], sq[:], reads=[r_sq], writes=[r_xout])
                else:
                    S.dma("pool", yv[:, :, n * FNT:(n + 1) * FNT], xo[:], reads=[r_xo], writes=[r_xout])

            load(0)
            if ntile > 1:
                load(1)
            norm(0)
            for n in range(ntile):
                up(n)
                if n + 1 < ntile and not final:
                    norm(n + 1)
                down(n)
                if n + 1 < ntile and final:
                    norm(n + 1)
                if n + 2 < ntile:
                    load(n + 2)
            S.flush()

    def a1_sweep(self, S, li, xin, r_xin):
        T = self.T
        NT = 512
        e_ = li // 2
        with ExitStack() as st:
            cst, vecs, r_c = self.load_consts(S, st)
            ones = cst[:, C_ONES:C_ONES + 128]
            win = self.sb(st, [128, 8, 2048], BF16)
            pw = self.sb(st, [128, 4, 128], BF16)
            r_win, r_pw = S.res(), S.res()
            with ExitStack() as st2:
                wv = self.w["ab_w_in"][e_].rearrange("(kc p) n -> p kc n", p=128)
                self.load_weight(S, st2, win, [wv[:, kc, :] for kc in range(8)], 2048, [r_c], r_win,
                                 scale_cols=[vecs[:, V_NMIX + 8 * li + kc:V_NMIX + 8 * li + kc + 1] for kc in range(8)])
                pwv = self.w["pool_w"][e_].rearrange("g c d -> c g d")
                self.load_weight(S, st2, pw, [pwv[:, g, :] for g in range(4)], 128, [r_c], r_pw, cw=128)
                S.flush()
            xt = [self.sb(st, [128, 8, NT], F32) for _ in range(2)]
            sq = self.sb(st, [128, 8, NT], F32)
            ssum = self.sb(st, [128, NT], F32)
            rstd = self.sb(st, [128, NT], F32)
            h = self.sb(st, [128, 8, NT], BF16)
            qk = [self.sb(st, [128, 8, NT], BF16) for _ in range(2)]
            vt = [self.sb(st, [128, 4, 512], BF16) for _ in range(2)]
            uext = [self.sb(st, [128, 4, 16 + NT], F32) for _ in range(2)]
            ta = self.sb(st, [128, 16 + NT], F32)
            tb = self.sb(st, [128, 16 + NT], F32)
            rr = self.sb(st, [128, 4, NT], BF16)
            mo = [self.sb(st, [128, 4, NT], BF16) for _ in range(2)]
            pb = self.psum(st)
            R = S.res
            r_xt = [R(), R()]; r_sq = R(); r_ssum = R(); r_rstd = R()
            r_h = [R() for _ in range(8)]; r_pb = [R() for _ in range(8)]
            r_qk = [R(), R()]; r_vt = [R(), R()]; r_ue = [[R() for _ in range(4)] for _ in range(2)]
            r_ta, r_tb = R(), R(); r_rr = [R() for _ in range(4)]; r_mo = [R(), R()]
            r_sc = self.r_scr
            xv = xin.rearrange("(c p) t -> p c t", p=128)
            ntile = T // NT
            qTv = self.qT.rearrange("h p t -> p h t")
            kTv = self.kT.rearrange("h p t -> p h t")
            mTv = self.mT.rearrange("g p t -> p g t")
            Vv = self.Vs.rearrange("h p k v -> p h k v")

            def load(n):
                S.dma("sp", xt[n % 2][:], xv[:, :, n * NT:(n + 1) * NT], reads=[r_xin], writes=[r_xt[n % 2]])

            def norm(n):
                b = n % 2
                self.rmsnorm_tile(S, xt[b], r_xt[b], NT, sq, r_sq, ssum, r_ssum, rstd, r_rstd, pb[7], r_pb[7],
                                  ones, r_c, h, r_h)

            for g in range(4):
                S.add("pool", (lambda g: lambda e: e.memset(uext[0][:, g, 0:16], 0.0))(g), writes=[r_ue[0][g]])

            def proj(n):
                b = n % 2
                t0 = n * NT
                for oc in range(8):
                    bk = oc % 3
                    for kc in range(8):
                        S.add("pe", (lambda oc, kc, bk: lambda e: e.matmul(
                            pb[bk][:, :NT], win[:, kc, oc * 128:(oc + 1) * 128], h[:, kc, :],
                            start=(kc == 0), stop=(kc == 7)))(oc, kc, bk), reads=[r_win, r_h[kc]], writes=[r_pb[bk]])
                    S.add("act", (lambda oc, bk: lambda e: e.copy(qk[b][:, oc, :], pb[bk][:, :NT]))(oc, bk),
                          reads=[r_pb[bk]], writes=[r_qk[b]])
                S.dma("pool", qTv[:, :, t0:t0 + NT], qk[b][:, 0:4, :], reads=[r_qk[b]], writes=[r_sc])
                S.dma("pool", kTv[:, :, t0:t0 + NT], qk[b][:, 4:8, :], reads=[r_qk[b]], writes=[r_sc])
                for s in range(4):
                    bk = 3 + s % 2
                    for kc in range(8):
                        S.add("pe", (lambda s, kc, bk: lambda e: e.matmul(
                            pb[bk][:, :512], h[:, kc, s * 128:(s + 1) * 128], win[:, kc, 1024:1536],
                            start=(kc == 0), stop=(kc == 7)))(s, kc, bk), reads=[r_win, r_h[kc]], writes=[r_pb[bk]])
                    S.add("dve", (lambda s, bk: lambda e: e.tensor_copy(vt[b][:, s, :], pb[bk][:, :512]))(s, bk),
                          reads=[r_pb[bk]], writes=[r_vt[b]])
                for hh in range(4):
                    S.dma("pool", self.Vs[hh, :, 4 * n:4 * n + 4, :], vt[b][:, :, hh * 128:(hh + 1) * 128],
                          reads=[r_vt[b]], writes=[r_sc])
                for g in range(4):
                    bk = 5 + g % 2
                    oc = 12 + g
                    for kc in range(8):
                        S.add("pe", (lambda oc, kc, bk: lambda e: e.matmul(
                            pb[bk][:, :NT], win[:, kc, oc * 128:(oc + 1) * 128], h[:, kc, :],
                            start=(kc == 0), stop=(kc == 7)))(oc, kc, bk), reads=[r_win, r_h[kc]], writes=[r_pb[bk]])
                    S.add("act", (lambda g, bk: lambda e: e.copy(uext[b][:, g, 16:16 + NT], pb[bk][:, :NT]))(g, bk),
                          reads=[r_pb[bk]], writes=[r_ue[b][g]])

            def pool(n):
                b = n % 2
                t0 = n * NT
                W = 16 + NT
                for g in range(4):
                    w = 2 ** (g + 1)
                    src = uext[b][:, g, :]
                    r_src = r_ue[b][g]
                    bufs = [(ta, r_ta), (tb, r_tb)]
                    sh = 1
                    k = 0
                    cur, r_cur = src, r_src
                    while sh < w:
                        dst, r_dst = bufs[k % 2]
                        S.add("pool", (lambda dst, cur, sh: lambda e: e.tensor_tensor(
                            dst[:, sh:W], cur[:, sh:W], cur[:, 0:W - sh], ALU.add))(dst, cur, sh),
                            reads=[r_cur], writes=[r_dst])
                        cur, r_cur = dst[:], r_dst
                        sh *= 2
                        k += 1
                    S.add("dve", (lambda g, cur, w: lambda e: e.scalar_tensor_tensor(
                        rr[:, g, :], cur[:, 16:W], 1.0 / w, uext[b][:, g, 16:W], ALU.mult, ALU.subtract))(g, cur, w),
                        reads=[r_cur, r_ue[b][g]], writes=[r_rr[g]])
                    if n == 0:
                        S.add("dve", (lambda g, cur: lambda e: e.tensor_tensor(
                            ta[:, 0:16], cur[:, 16:32], cst[:, C_INVC + g * 16:C_INVC + (g + 1) * 16], ALU.mult))(g, cur),
                            reads=[r_cur, r_c], writes=[r_ta])
                        S.add("dve", (lambda g: lambda e: e.tensor_tensor(
                            rr[:, g, 0:16], ta[:, 0:16], uext[b][:, g, 16:32], ALU.subtract))(g),
                            reads=[r_ta, r_ue[b][g]], writes=[r_rr[g]])
                    if n + 1 < ntile:
                        S.add("pool", (lambda g: lambda e: e.tensor_copy(uext[1 - b][:, g, 0:16], uext[b][:, g, NT:NT + 16]))(g),
                              reads=[r_ue[b][g]], writes=[r_ue[1 - b][g]])
                    bk = 5 + g % 2
                    S.add("pe", (lambda g, bk: lambda e: e.matmul(pb[bk][:, :NT], pw[:, g, :], rr[:, g, :], start=True, stop=True))(g, bk),
                          reads=[r_pw, r_rr[g]], writes=[r_pb[bk]])
                    S.add("act", (lambda g, bk: lambda e: e.activation(
                        mo[b][:, g, :], pb[bk][:, :NT], AF.Copy, scale=vecs[:, V_PSCALE + 4 * e_ + g:V_PSCALE + 4 * e_ + g + 1]))(g, bk),
                        reads=[r_pb[bk], r_c], writes=[r_mo[b]])
                S.dma("pool", mTv[:, :, t0:t0 + NT], mo[b][:], reads=[r_mo[b]], writes=[r_sc])

            load(0)
            if ntile > 1:
                load(1)
            norm(0)
            for n in range(ntile):
                proj(n)
                if n + 1 < ntile:
                    norm(n + 1)
                pool(n)
                if n + 2 < ntile:
                    load(n + 2)
            S.flush()

    def a2_sweep(self, S, li, xin, r_xin, xout, r_xout):
        T = self.T
        NT = 512
        e_ = li // 2
        lam_init = 0.8 - 0.6 * math.exp(-0.3 * li)
        KG = 16
        with ExitStack() as st:
            cst, vecs, r_c = self.load_consts(S, st)
            ones = cst[:, C_ONES:C_ONES + 128]
            wout = self.sb(st, [128, 8, D], BF16)
            r_wout = S.res()
            onesb = self.sb(st, [128, 128], BF16)
            lamt = self.sb(st, [128, 256], F32)
            lprod = self.sb(st, [128, 128], F32)
            lsum = self.sb(st, [128, 2], F32)
            neglam = self.sb(st, [128, 1], F32)
            gsub = self.sb(st, [128, 1], F32)
            r_l = S.res()
            ediag = [[self.sb(st, [128, NT], BF16) for _ in range(2)] for _ in range(4)]
            r_ed = [[S.res() for _ in range(2)] for _ in range(4)]
            with ExitStack() as st2:
                wv = self.w["ab_w_out"][e_].rearrange("(kc p) n -> p kc n", p=128)
                self.load_weight(S, st2, wout, [wv[:, kc, :] for kc in range(8)], D, [r_c], r_wout)
                S.add("dve", lambda e: e.tensor_copy(onesb[:], ones), reads=[r_c], writes=[r_l])
                S.dma("sp", lamt[:], self.w["ab_lambda"][e_:e_ + 1, :].partition_broadcast(128), writes=[r_l])
                S.add("dve", lambda e: e.tensor_tensor(lprod[:, 0:64], lamt[:, 0:64], lamt[:, 64:128], ALU.mult), reads=[r_l], writes=[r_l])
                S.add("dve", lambda e: e.tensor_tensor(lprod[:, 64:128], lamt[:, 128:192], lamt[:, 192:256], ALU.mult), reads=[r_l], writes=[r_l])
                S.add("dve", lambda e: e.tensor_reduce(lsum[:], lprod[:].rearrange("p (a d) -> p a d", a=2), AX.X, ALU.add), reads=[r_l], writes=[r_l])
                S.add("act", lambda e: e.activation(lsum[:], lsum[:], AF.Exp), reads=[r_l], writes=[r_l])
                S.add("dve", lambda e: e.tensor_tensor(neglam[:], lsum[:, 1:2], lsum[:, 0:1], ALU.subtract), reads=[r_l], writes=[r_l])
                S.add("dve", lambda e: e.tensor_scalar(neglam[:], neglam[:], -float(lam_init), None, ALU.add), reads=[r_l], writes=[r_l])
                S.add("dve", lambda e: e.tensor_scalar(gsub[:], vecs[:, V_SUBLN + e_:V_SUBLN + e_ + 1], float(1.0 - lam_init), None, ALU.mult),
                      reads=[r_c], writes=[r_l])
                for j in range(4):
                    for a in range(2):
                        S.add("pool", (lambda j, a: lambda e: e.memset(ediag[j][a][:], 0.0))(j, a), writes=[r_ed[j][a]])
                S.flush()
            xt = [self.sb(st, [128, 8, NT], F32) for _ in range(2)]
            xo = self.sb(st, [128, 8, NT], F32)
            qt = [self.sb(st, [128, 4, NT], BF16) for _ in range(2)]
            mo = [self.sb(st, [128, 4, NT], BF16) for _ in range(2)]
            ao = self.sb(st, [128, 4, NT], BF16)
            kb = [self.sb(st, [128, KG * 128], BF16) for _ in range(3)]
            vb = [self.sb(st, [128, KG, 128], BF16) for _ in range(3)]
            eb = [[self.sb(st, [128, NT], BF16) for _ in range(2)] for _ in range(3)]
            rl1 = self.sb(st, [128, NT], F32); rl2 = self.sb(st, [128, NT], F32)
            t1 = self.sb(st, [128, NT], F32); t2 = self.sb(st, [128, NT], F32)
            A = self.sb(st, [128, NT], F32); asq = self.sb(st, [128, NT], F32)
            sd = self.sb(st, [128, NT], F32); rs = self.sb(st, [128, NT], F32)
            pb = self.psum(st)
            R = S.res
            r_xt = [R(), R()]; r_xo = R(); r_qt = [R(), R()]; r_mo = [R(), R()]; r_ao = [R() for _ in range(4)]
            r_kb = [R() for _ in range(3)]; r_vb = [R() for _ in range(3)]
            r_eb = [[R(), R()] for _ in range(3)]
            r_ep = R()
            r_pb = [R() for _ in range(8)]
            r_sc = self.r_scr
            xv = xin.rearrange("(c p) t -> p c t", p=128)
            yv = xout.rearrange("(c p) t -> p c t", p=128)
            qTv = self.qT.rearrange("h p t -> p h t")
            mTv = self.mT.rearrange("g p t -> p g t")
            nblk = T // NT
            PS_S = [(0, 1), (2, 3)]
            PO1, PO2, PL1, PL2 = 4, 5, 6, 7
            grp_ctr = [0]

            def load(n):
                t0 = n * NT
                b = n % 2
                S.dma("sp", xt[b][:], xv[:, :, t0:t0 + NT], reads=[r_xin], writes=[r_xt[b]])
                S.dma("sp", qt[b][:], qTv[:, :, t0:t0 + NT], reads=[r_sc], writes=[r_qt[b]])
                S.dma("sp", mo[b][:], mTv[:, :, t0:t0 + NT], reads=[r_sc], writes=[r_mo[b]])

            def load_kv(h, g0, ng):
                i = grp_ctr[0] % 3
                grp_ctr[0] += 1
                S.dma("sp", kb[i][:, :ng * 128], self.kT[h, :, g0 * 128:(g0 + ng) * 128], reads=[r_sc], writes=[r_kb[i]])
                S.dma("sp", vb[i][:, :ng, :], self.Vs[h, :, g0:g0 + ng, :], reads=[r_sc], writes=[r_vb[i]])
                return i

            def attn_head(qb, h):
                b = qb % 2
                nk = 4 * (qb + 1)
                groups = []
                for g0 in range(0, nk, KG):
                    groups.append((g0, min(KG, nk - g0)))
                gbuf = {}
                for gi in range(min(2, len(groups))):
                    gbuf[gi] = load_kv(h, *groups[gi])
                ectr = [0]
                pend = []

                def qk(kt):
                    gi, lk = kt // KG, kt % KG
                    if gi not in gbuf:
                        gbuf[gi] = load_kv(h, *groups[gi])
                    i = gbuf[gi]
                    j = kt - 4 * qb
                    c0 = 128 * j if j >= 0 else 0
                    sa, sb_ = PS_S[kt % 2]
                    S.add("pe", lambda e: e.matmul(pb[sa][:, c0:NT], kb[i][0:64, lk * 128:(lk + 1) * 128], qt[b][0:64, h, c0:NT],
                                                   start=True, stop=True), reads=[r_kb[i], r_qt[b]], writes=[r_pb[sa]])
                    S.add("pe", lambda e: e.matmul(pb[sb_][:, c0:NT], kb[i][64:128, lk * 128:(lk + 1) * 128], qt[b][64:128, h, c0:NT],
                                                   start=True, stop=True), reads=[r_kb[i], r_qt[b]], writes=[r_pb[sb_]])
                    if j < 0:
                        k3 = ectr[0] % 3
                        ectr[0] += 1
                        e1, e2 = eb[k3][0], eb[k3][1]
                        re1, re2 = r_eb[k3][0], r_eb[k3][1]
                        S.add("act", lambda e: e.activation(e1[:, :], pb[sa][:, :NT], AF.Exp, scale=0.125), reads=[r_pb[sa]], writes=[re1])
                        S.add("act", lambda e: e.activation(e2[:, :], pb[sb_][:, :NT], AF.Exp, scale=0.125), reads=[r_pb[sb_]], writes=[re2])
                    else:
                        e1, e2 = ediag[j][0], ediag[j][1]
                        re1, re2 = r_ed[j][0], r_ed[j][1]
                        for (et, re_, pbn) in ((e1, re1, sa), (e2, re2, sb_)):
                            S.add("act", (lambda et, pbn: lambda e: e.activation(et[0:64, c0:NT], pb[pbn][0:64, c0:NT], AF.Exp, scale=0.125))(et, pbn),
                                  reads=[r_pb[pbn]], writes=[re_])
                            S.add("act", (lambda et, pbn: lambda e: e.activation(et[64:128, c0 + 64:NT], pb[pbn][64:128, c0 + 64:NT], AF.Exp, scale=0.125))(et, pbn),
                                  reads=[r_pb[pbn]], writes=[re_])
                    return (kt, c0, e1, e2, re1, re2, i, lk)

                def pv(item):
                    kt, c0, e1, e2, re1, re2, i, lk = item
                    first, last = (kt == 0), (kt == nk - 1)
                    S.add("pe", lambda e: e.matmul(pb[PO1][:, c0:NT], vb[i][:, lk, :], e1[:, c0:NT], start=first, stop=last, skip_group_check=True),
                          reads=[r_vb[i], re1], writes=[r_pb[PO1]])
                    S.add("pe", lambda e: e.matmul(pb[PO2][:, c0:NT], vb[i][:, lk, :], e2[:, c0:NT], start=first, stop=last, skip_group_check=True),
                          reads=[r_vb[i], re2], writes=[r_pb[PO2]])
                    S.add("pe", lambda e: e.matmul(pb[PL1][:, c0:NT], onesb[:], e1[:, c0:NT], start=first, stop=last, skip_group_check=True),
                          reads=[r_l, re1], writes=[r_pb[PL1]])
                    S.add("pe", lambda e: e.matmul(pb[PL2][:, c0:NT], onesb[:], e2[:, c0:NT], start=first, stop=last, skip_group_check=True),
                          reads=[r_l, re2], writes=[r_pb[PL2]])

                prev = qk(0)
                for kt in range(1, nk):
                    cur = qk(kt)
                    pv(prev)
                    prev = cur
                pv(prev)
                S.add("dve", lambda e: e.reciprocal(rl1[:], pb[PL1][:, :NT]), reads=[r_pb[PL1]], writes=[r_ep])
                S.add("dve", lambda e: e.reciprocal(rl2[:], pb[PL2][:, :NT]), reads=[r_pb[PL2]], writes=[r_ep])
                S.add("dve", lambda e: e.tensor_tensor(t1[:], pb[PO1][:, :NT], rl1[:], ALU.mult), reads=[r_pb[PO1], r_ep], writes=[r_ep])
                S.add("dve", lambda e: e.tensor_tensor(t2[:], pb[PO2][:, :NT], rl2[:], ALU.mult), reads=[r_pb[PO2], r_ep], writes=[r_ep])
                S.add("dve", lambda e: e.scalar_tensor_tensor(A[:], t2[:], neglam[:, 0:1], t1[:], ALU.mult, ALU.add), reads=[r_ep, r_l], writes=[r_ep])
                S.add("pool", lambda e: e.tensor_tensor(asq[:], A[:], A[:], ALU.mult), reads=[r_ep], writes=[r_ep])
                sa = PS_S[nk % 2][0]
                S.add("pe", lambda e: e.matmul(pb[sa][:, :NT], ones, asq[:], start=True, stop=True), reads=[r_c, r_ep], writes=[r_pb[sa]])
                S.add("act", lambda e: e.activation(sd[:], pb[sa][:, :NT], AF.Sqrt, bias=float(EPS), scale=1.0 / 128), reads=[r_pb[sa]], writes=[r_ep])
                S.add("dve", lambda e: e.reciprocal(rs[:], sd[:]), reads=[r_ep], writes=[r_ep])
                S.add("dve", lambda e: e.scalar_tensor_tensor(ao[:, h, :], A[:], gsub[:, 0:1], rs[:], ALU.mult, ALU.mult),
                      reads=[r_ep, r_l], writes=[r_ao[h]])

            def outproj(qb):
                b = qb % 2
                t0 = qb * NT
                for oc in range(8):
                    bk = PS_S[oc % 2][1]
                    for c in range(8):
                        if c < 4:
                            S.add("pe", (lambda oc, c, bk: lambda e: e.matmul(pb[bk][:, :NT], wout[:, c, oc * 128:(oc + 1) * 128], ao[:, c, :],
                                                                               start=(c == 0), stop=False))(oc, c, bk),
                                  reads=[r_wout, r_ao[c]], writes=[r_pb[bk]])
                        else:
                            S.add("pe", (lambda oc, c, bk: lambda e: e.matmul(pb[bk][:, :NT], wout[:, c, oc * 128:(oc + 1) * 128], mo[b][:, c - 4, :],
                                                                               start=False, stop=(c == 7)))(oc, c, bk),
                                  reads=[r_wout, r_mo[b]], writes=[r_pb[bk]])
                    S.add("dve", (lambda oc, bk: lambda e: e.tensor_tensor(xo[:, oc, :], pb[bk][:, :NT], xt[b][:, oc, :], ALU.add))(oc, bk),
                          reads=[r_pb[bk], r_xt[b]], writes=[r_xo])
                S.dma("pool", yv[:, :, t0:t0 + NT], xo[:], reads=[r_xo], writes=[r_xout])

            load(0)
            for qb in range(nblk):
                if qb + 1 < nblk:
                    load(qb + 1)
                for h in range(4):
                    attn_head(qb, h)
                outproj(qb)
            S.flush()

    def gla_sweep(self, S, li, xin, r_xin, xout, r_xout):
        T = self.T
        NT = 512
        o_ = li // 2
        scale = 128.0 ** -0.5
        with ExitStack() as st:
            cst, vecs, r_c = self.load_consts(S, st)
            ones = cst[:, C_ONES:C_ONES + 128]
            tri = cst[:, C_TRI:C_TRI + 128]
            su = cst[:, C_SU:C_SU + 128]
            tri4 = cst[:, C_TRI4:C_TRI4 + 512]
            win = self.sb(st, [128, 8, 3104], BF16)
            wout = self.sb(st, [128, 8, D], BF16)
            wgk = self.sb(st, [64, 1, 512], BF16)
            r_win, r_wout, r_wgk = S.res(), S.res(), S.res()
            with ExitStack() as st2:
                wv = self.w["gla_w_in"][o_].rearrange("(kc p) n -> p kc n", p=128)
                self.load_weight(S, st2, win, [wv[:, kc, :] for kc in range(8)], 3088, [r_c], r_win,
                                 scale_cols=[vecs[:, V_NMIX + 8 * li + kc:V_NMIX + 8 * li + kc + 1] for kc in range(8)], cw=1024)
                wo = self.w["gla_w_out"][o_].rearrange("(kc p) n -> p kc n", p=128)
                self.load_weight(S, st2, wout, [wo[:, kc, :] for kc in range(8)], D, [r_c], r_wout)
                stg = self.sb(st2, [64, 512], F32)
                r_s = S.res()
                S.add("pool", lambda e: e.memset(stg[:], 0.0), writes=[r_s])
                S.add("pool", lambda e: e.memset(win[:, :, 3088:3104], 0.0), writes=[r_win])
                S.dma("sp", stg[0:16, :], self.w["gla_w_gk_up"][o_], reads=[r_s], writes=[r_s])
                S.dma("sp", stg[32:33, :], self.w["gla_b_gk"][o_], reads=[r_s], writes=[r_s])
                S.add("dve", lambda e: e.tensor_copy(wgk[:, 0, :], stg[:]), reads=[r_s], writes=[r_wgk])
                S.flush()
            xt = [self.sb(st, [128, 8, NT], F32) for _ in range(2)]
            sq = self.sb(st, [128, 8, NT], F32)
            xo = sq
            ssum = self.sb(st, [128, NT], F32)
            rstd = self.sb(st, [128, NT], F32)
            h = self.sb(st, [128, 8, NT], BF16)
            qf = self.sb(st, [128, 4, NT], F32)
            kf = self.sb(st, [128, 4, NT], F32)
            gate = self.sb(st, [128, 8, NT], F32)
            gl = self.sb(st, [64, NT], BF16)
            ktok = self.sb(st, [128, 512], F32)
            vtok = self.sb(st, [128, 1024], BF16)
            ez = self.sb(st, [128, 512], F32)
            gtok = self.sb(st, [128, 512], F32)
            ebp = self.sb(st, [128, 512], F32)
            enb = self.sb(st, [128, 512], F32)
            er = ez
            qd = self.sb(st, [128, 4, 128], BF16)
            kd = self.sb(st, [128, 4, 128], BF16)
            kl = self.sb(st, [128, 512], BF16)
            am = self.sb(st, [128, 512], BF16)
            Sf = self.sb(st, [128, 4, 256], F32)
            Sb = [self.sb(st, [128, 4, 256], BF16) for _ in range(2)]
            osq = self.sb(st, [128, 8, 128], F32)
            sdn = self.sb(st, [128, 512], F32)
            rsn = self.sb(st, [128, 512], F32)
            tg = self.sb(st, [128, 8, 128], F32)
            og = self.sb(st, [128, 8, NT], BF16)
            pb = self.psum(st)
            R = S.res
            r_xt = [R(), R()]; r_sq = R(); r_xo = r_sq; r_ssum = R(); r_rstd = R()
            r_h = [R() for _ in range(8)]; r_pb = [R() for _ in range(8)]
            r_qf, r_kf, r_gate, r_gl = R(), R(), R(), R()
            r_ktok, r_vtok, r_ez, r_gtok, r_ebp, r_enb = R(), R(), R(), R(), R(), R(); r_er = r_ez
            r_qd, r_kd, r_kl, r_am = R(), R(), R(), R()
            r_Sf = [R() for _ in range(4)]; r_Sb = [[R() for _ in range(4)] for _ in range(2)]
            r_osq, r_sdn, r_rsn, r_tg, r_og = R(), R(), R(), R(), R()
            xv = xin.rearrange("(c p) t -> p c t", p=128)
            yv = xout.rearrange("(c p) t -> p c t", p=128)
            ntile = T // NT
            S.add("pool", lambda e: e.memset(Sf[:], 0.0), writes=r_Sf)
            S.add("pool", lambda e: e.memset(Sb[0][:], 0.0), writes=r_Sb[0])
            S.add("pool", lambda e: e.memset(Sb[1][:], 0.0), writes=r_Sb[1])
            S.add("pool", lambda e: e.memset(gl[:], 0.0), writes=[r_gl])
            S.add("pool", lambda e: e.memset(gl[32:64, :], 1.0), writes=[r_gl])
            chunk_ctr = [0]

            def load(n):
                S.dma("sp", xt[n % 2][:], xv[:, :, n * NT:(n + 1) * NT], reads=[r_xin], writes=[r_xt[n % 2]])

            def norm(n):
                b = n % 2
                self.rmsnorm_tile(S, xt[b], r_xt[b], NT, sq, r_sq, ssum, r_ssum, rstd, r_rstd, pb[1], r_pb[1],
                                  ones, r_c, h, r_h)

            def fm(oc0, ncols, dst_fn, bk):
                for kc in range(8):
                    S.add("pe", (lambda kc: lambda e: e.matmul(pb[bk][:ncols, :NT], win[:, kc, oc0:oc0 + ncols], h[:, kc, :],
                                                               start=(kc == 0), stop=(kc == 7)))(kc),
                          reads=[r_win, r_h[kc]], writes=[r_pb[bk]])
                dst_fn(bk)

            def proj_fm(n):
                k = 0
                for c in range(4):
                    bk = k % 2; k += 1
                    fm(c * 128, 128, (lambda c: lambda bk: S.add("act", lambda e: e.copy(qf[:, c, :], pb[bk][:, :NT]),
                                                                  reads=[r_pb[bk]], writes=[r_qf]))(c), bk)
                for c in range(4):
                    bk = k % 2; k += 1
                    fm(512 + c * 128, 128, (lambda c: lambda bk: S.add("dve", lambda e: e.tensor_copy(kf[:, c, :], pb[bk][:, :NT]),
                                                                        reads=[r_pb[bk]], writes=[r_kf]))(c), bk)
                for c in range(8):
                    bk = k % 2; k += 1
                    fm(2048 + c * 128, 128, (lambda c: lambda bk: S.add("act", lambda e: e.activation(gate[:, c, :], pb[bk][:, :NT], AF.Silu),
                                                                         reads=[r_pb[bk]], writes=[r_gate]))(c), bk)
                bk = k % 2; k += 1
                fm(3072, 32, lambda bk: S.add("dve", lambda e: e.tensor_copy(gl[0:16, :], pb[bk][0:16, :NT]),
                                              reads=[r_pb[bk]], writes=[r_gl]), bk)

            def chunk(n, s):
                b = n % 2
                ssl = slice(s * 128, (s + 1) * 128)
                cc = chunk_ctr[0]
                chunk_ctr[0] += 1
                sbr, sbw = Sb[cc % 2], Sb[1 - cc % 2]
                r_sbr, r_sbw = r_Sb[cc % 2], r_Sb[1 - cc % 2]
                for kc in range(8):
                    S.add("pe", (lambda kc: lambda e: e.matmul(pb[2][:, :512], h[:, kc, ssl], win[:, kc, 512:1024],
                                                               start=(kc == 0), stop=(kc == 7)))(kc), reads=[r_win, r_h[kc]], writes=[r_pb[2]])
                S.add("act", lambda e: e.copy(ktok[:], pb[2][:, :512]), reads=[r_pb[2]], writes=[r_ktok])
                for hf in range(2):
                    bk = 3 + hf
                    for kc in range(8):
                        S.add("pe", (lambda kc, hf, bk: lambda e: e.matmul(pb[bk][:, :512], h[:, kc, ssl], win[:, kc, 1024 + hf * 512:1536 + hf * 512],
                                                                           start=(kc == 0), stop=(kc == 7)))(kc, hf, bk),
                              reads=[r_win, r_h[kc]], writes=[r_pb[bk]])
                    S.add("dve", (lambda hf, bk: lambda e: e.tensor_copy(vtok[:, hf * 512:(hf + 1) * 512], pb[bk][:, :512]))(hf, bk),
                          reads=[r_pb[bk]], writes=[r_vtok])
                S.add("pe", lambda e: e.matmul(pb[5][:, :512], gl[:, ssl], wgk[:, 0, :], start=True, stop=True),
                      reads=[r_gl, r_wgk], writes=[r_pb[5]])
                S.add("act", lambda e: e.activation(ez[:], pb[5][:, :512], AF.Exp, scale=-1.0), reads=[r_pb[5]], writes=[r_ez])
                S.add("act", lambda e: e.activation(ez[:], ez[:], AF.Ln, bias=1.0), reads=[r_ez], writes=[r_ez])
                S.add("dve", lambda e: e.tensor_scalar(gtok[:], ez[:], -1.0 / 16.0, None, ALU.mult), reads=[r_ez], writes=[r_gtok])
                for hh in range(4):
                    S.add("pe", (lambda hh: lambda e: e.matmul(pb[6][:, hh * 128:(hh + 1) * 128], gtok[:, hh * 128:(hh + 1) * 128], tri,
                                                               start=True, stop=True))(hh), reads=[r_gtok, r_c], writes=[r_pb[6]])
                S.add("pe", lambda e: e.matmul(pb[7][:, :512], su, gtok[:], start=True, stop=True), reads=[r_gtok, r_c], writes=[r_pb[7]])
                S.add("act", lambda e: e.activation(ebp[:], pb[6][:, :512], AF.Exp), reads=[r_pb[6]], writes=[r_ebp])
                S.add("act", lambda e: e.activation(enb[:], pb[6][:, :512], AF.Exp, scale=-1.0), reads=[r_pb[6]], writes=[r_enb])
                S.add("act", lambda e: e.activation(er[:], pb[7][:, :512], AF.Exp), reads=[r_pb[7]], writes=[r_er])
                S.add("dve", lambda e: e.scalar_tensor_tensor(qd[:], qf[:, :, ssl], float(scale), ebp[:].rearrange("p (h t) -> p h t", h=4),
                                                              ALU.mult, ALU.mult), reads=[r_qf, r_ebp], writes=[r_qd])
                S.add("dve", lambda e: e.tensor_tensor(kd[:], kf[:, :, ssl], enb[:].rearrange("p (h t) -> p h t", h=4), ALU.mult),
                      reads=[r_kf, r_enb], writes=[r_kd])
                S.add("pool", lambda e: e.tensor_tensor(kl[:], ktok[:], er[:], ALU.mult), reads=[r_ktok, r_er], writes=[r_kl])
                for hh in range(4):
                    S.add("pe", (lambda hh: lambda e: e.matmul(pb[2][:, hh * 128:(hh + 1) * 128], kd[:, hh, :], qd[:, hh, :],
                                                               start=True, stop=True))(hh), reads=[r_kd, r_qd], writes=[r_pb[2]])
                S.add("dve", lambda e: e.tensor_tensor(am[:], pb[2][:, :512], tri4, ALU.mult), reads=[r_pb[2], r_c], writes=[r_am])
                for hh in range(4):
                    for vc in range(2):
                        bk = 3 + hh // 2
                        col = ((hh % 2) * 2 + vc) * 128
                        S.add("pe", (lambda hh, vc, bk, col: lambda e: e.matmul(
                            pb[bk][:, col:col + 128], vtok[:, hh * 256 + vc * 128:hh * 256 + (vc + 1) * 128], am[:, hh * 128:(hh + 1) * 128],
                            start=True, stop=False))(hh, vc, bk, col), reads=[r_vtok, r_am], writes=[r_pb[bk]])
                        S.add("pe", (lambda hh, vc, bk, col: lambda e: e.matmul(
                            pb[bk][:, col:col + 128], sbr[:, hh, vc * 128:(vc + 1) * 128], qd[:, hh, :],
                            start=False, stop=True))(hh, vc, bk, col), reads=[r_sbr[hh], r_qd], writes=[r_pb[bk]])
                for hh in range(4):
                    bk = 5 + hh // 2
                    col = (hh % 2) * 256
                    S.add("pe", (lambda hh, bk, col: lambda e: e.matmul(pb[bk][:, col:col + 256], kl[:, hh * 128:(hh + 1) * 128],
                                                                        vtok[:, hh * 256:(hh + 1) * 256], start=True, stop=True))(hh, bk, col),
                          reads=[r_kl, r_vtok], writes=[r_pb[bk]])
                    S.add("dve", (lambda hh, bk, col: lambda e: e.scalar_tensor_tensor(
                        Sf[:, hh, :], Sf[:, hh, :], ebp[:, hh * 128 + 127:hh * 128 + 128], pb[bk][:, col:col + 256], ALU.mult, ALU.add))(hh, bk, col),
                        reads=[r_pb[bk], r_ebp], writes=[r_Sf[hh]])
                    S.add("act", (lambda hh: lambda e: e.copy(sbw[:, hh, :], Sf[:, hh, :]))(hh), reads=[r_Sf[hh]], writes=[r_sbw[hh]])
                for q2 in range(2):
                    S.add("act", (lambda q2: lambda e: e.activation(osq[:, q2 * 4:(q2 + 1) * 4, :],
                                                                    pb[3 + q2][:, :512].rearrange("p (c t) -> p c t", c=4), AF.Square))(q2),
                          reads=[r_pb[3 + q2]], writes=[r_osq])
                for hh in range(4):
                    for vc in range(2):
                        S.add("pe", (lambda hh, vc: lambda e: e.matmul(pb[7][:, hh * 128:(hh + 1) * 128], ones, osq[:, hh * 2 + vc, :],
                                                                       start=(vc == 0), stop=(vc == 1)))(hh, vc), reads=[r_osq, r_c], writes=[r_pb[7]])
                S.add("act", lambda e: e.activation(sdn[:], pb[7][:, :512], AF.Sqrt, bias=float(EPS), scale=1.0 / 256), reads=[r_pb[7]], writes=[r_sdn])
                S.add("dve", lambda e: e.reciprocal(rsn[:], sdn[:]), reads=[r_sdn], writes=[r_rsn])
                for c in range(8):
                    hh = c // 2
                    S.add("pool", (lambda c, hh: lambda e: e.tensor_tensor(tg[:, c, :], gate[:, c, ssl], rsn[:, hh * 128:(hh + 1) * 128], ALU.mult))(c, hh),
                          reads=[r_gate, r_rsn], writes=[r_tg])
                for c in range(8):
                    bk = 3 + c // 4
                    col = (c % 4) * 128
                    S.add("dve", (lambda c, bk, col: lambda e: e.scalar_tensor_tensor(
                        og[:, c, ssl], pb[bk][:, col:col + 128], vecs[:, V_GNORM + 8 * o_ + c:V_GNORM + 8 * o_ + c + 1], tg[:, c, :],
                        ALU.mult, ALU.mult))(c, bk, col), reads=[r_pb[bk], r_tg, r_c], writes=[r_og])

            def outproj(n):
                b = n % 2
                t0 = n * NT
                for oc in range(8):
                    bk = oc % 2
                    for c in range(8):
                        S.add("pe", (lambda oc, c, bk: lambda e: e.matmul(pb[bk][:, :NT], wout[:, c, oc * 128:(oc + 1) * 128], og[:, c, :],
                                                                           start=(c == 0), stop=(c == 7)))(oc, c, bk),
                              reads=[r_wout, r_og], writes=[r_pb[bk]])
                    S.add("dve", (lambda oc, bk: lambda e: e.tensor_tensor(xo[:, oc, :], pb[bk][:, :NT], xt[b][:, oc, :], ALU.add))(oc, bk),
                          reads=[r_pb[bk], r_xt[b]], writes=[r_xo])
                S.dma("pool", yv[:, :, t0:t0 + NT], xo[:], reads=[r_xo], writes=[r_xout])

            load(0)
            for n in range(ntile):
                if n + 1 < ntile:
                    load(n + 1)
                norm(n)
                proj_fm(n)
                for s in range(4):
                    chunk(n, s)
                outproj(n)
            S.flush()

    def build(self):
        nc = self.nc
        with ExitStack() as st:
            S = Sched(nc, st)
            self.r_scr = S.res()
            r_in = S.res()
            r_a, r_b = S.res(), S.res()
            cur, r_cur = self.x_in, r_in
            nl = len(self.layers)
            for idx, li in enumerate(self.layers):
                last = (idx == nl - 1)
                if li % 2 == 0:
                    self.a1_sweep(S, li, cur, r_cur)
                    self.a2_sweep(S, li, cur, r_cur, self.xa, r_a)
                else:
                    self.gla_sweep(S, li, cur, r_cur, self.xa, r_a)
                if last:
                    self.ffn_sweep(S, li, self.xa, r_a, self.y_out, S.res(), final=self.final)
                else:
                    self.ffn_sweep(S, li, self.xa, r_a, self.xb, r_b, final=False)
                    cur, r_cur = self.xb, r_b
            self.ninst = S.ninst
        return nc


def host_inputs(inp, T_sl=None):
    cst = make_cst()
    vecs = make_vecs(inp)
    common = {
        "cst": cst, "vecs": vecs,
        "ab_w_in": np.ascontiguousarray(inp["ab_w_in"], np.float32),
        "ab_lambda": np.ascontiguousarray(np.asarray(inp["ab_lambda"], np.float32).reshape(2, 256)),
        "pool_w": np.ascontiguousarray(inp["pool_w"], np.float32),
        "ab_w_out": np.ascontiguousarray(inp["ab_w_out"], np.float32),
        "gla_w_in": np.ascontiguousarray(inp["gla_w_in"], np.float32),
        "gla_w_gk_up": np.ascontiguousarray(inp["gla_w_gk_up"], np.float32),
        "gla_b_gk": np.ascontiguousarray(np.asarray(inp["gla_b_gk"], np.float32).reshape(2, 1, 512)),
        "gla_w_out": np.ascontiguousarray(inp["gla_w_out"], np.float32),
        "ffn_w1": np.ascontiguousarray(inp["ffn_w1"], np.float32),
        "ffn_w2": np.ascontiguousarray(inp["ffn_w2"], np.float32),
    }
    return common


_CACHE = {}


def kernel(**inputs):
    x = np.asarray(inputs["x"], np.float32)
    B, T, _ = x.shape
    key = (T,)
    if key not in _CACHE:
        _CACHE[key] = Builder(T).build()
    nc = _CACHE[key]
    common = host_inputs(inputs)
    zeros = {k: np.zeros_like(v) for k, v in common.items()}
    zeros["xT"] = np.zeros((D, T), np.float32)
    in_maps = []
    for c in range(NCORES):
        if c in ACTIVE:
            m = dict(common)
            m["xT"] = np.ascontiguousarray(x[ACTIVE.index(c)].T)
        else:
            m = zeros
        in_maps.append(m)
    res = run_bass_kernel_spmd(nc, in_maps, core_ids=list(range(NCORES)))
    out = np.empty((B, T, D), np.float32)
    for b in range(B):
        out[b] = res.results[ACTIVE[b]]["yT"].T
    return out
```

```python
import math
from contextlib import ExitStack

import numpy as np
import concourse.bass as bass
import concourse.mybir as mybir
from concourse.bass_utils import run_bass_kernel_spmd

F32 = mybir.dt.float32
BF16 = mybir.dt.bfloat16
ALU = mybir.AluOpType
AF = mybir.ActivationFunctionType
AX = mybir.AxisListType

D = 1024
DFF = 4096
EPS = 1e-6
DEPTH = 4
NCORES = 8
ACTIVE = (0, 1, 4, 5)

ENGS = ("pe", "act", "dve", "pool", "sp")
NDMA_SEMS = 8


class Res:
    __slots__ = ("writer", "readers")

    def __init__(self):
        self.writer = None
        self.readers = []


class Op:
    __slots__ = ("eng", "fn", "deps", "is_dma", "sig", "count", "dsem", "dval", "waits", "snap", "done")

    def __init__(self, eng, fn, is_dma):
        self.eng = eng
        self.fn = fn
        self.is_dma = is_dma
        self.deps = []
        self.sig = False
        self.count = 0
        self.dsem = None
        self.dval = 0
        self.waits = []
        self.snap = None
        self.done = False


class Sched:
    def __init__(self, nc, stack):
        self.nc = nc
        self.ops = {e: [] for e in ENGS}
        self.cnt = {e: 0 for e in ENGS}
        self.ndma = {e: 0 for e in ENGS}
        self.dma_hist = {e: [] for e in ENGS}
        self.known = {e: [0] * len(ENGS) for e in ENGS}
        self.kdma = {e: {} for e in ENGS}
        self.esem = {e: stack.enter_context(nc.semaphore(f"s_{e}")) for e in ENGS if e != "sp"}
        self.dsem = {}
        for e in ("sp", "pool", "act"):
            for k in range(NDMA_SEMS):
                self.dsem[(e, k)] = stack.enter_context(nc.semaphore(f"d_{e}{k}"))
        self.ninst = 0

    def res(self):
        return Res()

    def add(self, eng, fn, reads=(), writes=(), is_dma=False):
        op = Op(eng, fn, is_dma)
        deps = []
        for r in reads:
            if r.writer is not None:
                deps.append(r.writer)
            r.readers.append(op)
        for w in writes:
            if w.writer is not None:
                deps.append(w.writer)
            deps.extend(w.readers)
            w.writer = op
            w.readers = []
        seen = set()
        for d in deps:
            if d is op or d.done or id(d) in seen:
                continue
            seen.add(id(d))
            op.deps.append(d)
        self.ops[eng].append(op)
        return op

    def dma(self, q, out, in_, reads=(), writes=()):
        return self.add(q, lambda e: e.dma_start(out=out, in_=in_), reads, writes, is_dma=True)

    def barrier(self):
        last = []
        for e in ENGS:
            for o in reversed(self.ops[e]):
                if o.fn is not None and not o.is_dma:
                    last.append(o)
                    break
        rec = []
        for e in ENGS:
            dm = [o for o in self.ops[e] if o.is_dma][-NDMA_SEMS:]
            rec.extend(dm)
        for e in ENGS:
            op = Op(e, None, False)
            op.deps = list(last) + list(rec)
            self.ops[e].append(op)

    def flush(self):
        self.barrier()
        for e in ENGS:
            for op in self.ops[e]:
                for d in op.deps:
                    d.sig = True
        for e in ENGS:
            for op in self.ops[e]:
                if op.is_dma:
                    n = self.ndma[e]
                    op.dsem = (e, n % NDMA_SEMS)
                    op.dval = 16 * (n // NDMA_SEMS + 1)
                    self.dma_hist[e].append(op)
                    self.ndma[e] += 1
                elif op.sig:
                    self.cnt[e] += 1
                    op.count = self.cnt[e]
        eidx = {e: i for i, e in enumerate(ENGS)}
        for e in ENGS:
            known = self.known[e]
            kdma = self.kdma[e]
            hist = self.dma_hist[e]
            nd = len(hist) - sum(1 for o in self.ops[e] if o.is_dma)
            for op in self.ops[e]:
                deps = list(op.deps)
                if op.is_dma:
                    if nd >= NDMA_SEMS:
                        deps.append(hist[nd - NDMA_SEMS])
                    nd += 1
                best = {}
                for d in deps:
                    if d.is_dma:
                        if kdma.get(d.dsem, 0) >= d.dval:
                            continue
                        kdma[d.dsem] = d.dval
                        best[("dma", d.dsem)] = d.dval
                    else:
                        if d.eng == "pe" and e == "pe":
                            continue
                        j = eidx[d.eng]
                        if known[j] >= d.count:
                            continue
                        known[j] = d.count
                        best[("eng", d.eng)] = max(best.get(("eng", d.eng), 0), d.count)
                        if d.snap is not None:
                            for k in range(len(ENGS)):
                                if d.snap[k] > known[k]:
                                    known[k] = d.snap[k]
                op.waits = [(k[0], k[1], v) for k, v in best.items()]
                op.snap = tuple(known)
        nc = self.nc
        ops = self.ops
        esem, dsem = self.esem, self.dsem

        def run(eng_name):
            lst = ops[eng_name]

            def body(eng):
                for op in lst:
                    for kind, key, val in op.waits:
                        eng.wait_ge(dsem[key] if kind == "dma" else esem[key], val)
                    if op.fn is None:
                        continue
                    ins = op.fn(eng)
                    if op.is_dma:
                        ins.then_inc(dsem[op.dsem], 16)
                    elif op.sig:
                        ins.then_inc(esem[eng_name], 1)
            return body

        with nc.Block() as block:
            block.tensor(run("pe"))
            block.scalar(run("act"))
            block.vector(run("dve"))
            block.gpsimd(run("pool"))
            block.sync(run("sp"))
        for e in ENGS:
            self.ninst += len(self.ops[e])
            for op in self.ops[e]:
                op.done = True
                op.fn = None
                op.deps = []
            self.ops[e] = []


C_ONES = 0
C_TRI = 128
C_SU = 256
C_TRI4 = 384
C_INVC = 896
NCST = 960


def make_cst():
    c = np.zeros((128, NCST), np.float32)
    c[:, C_ONES:C_ONES + 128] = 1.0
    s = np.arange(128)
    tri = (s[:, None] <= s[None, :]).astype(np.float32)
    c[:, C_TRI:C_TRI + 128] = tri
    c[:, C_SU:C_SU + 128] = (s[:, None] > s[None, :]).astype(np.float32)
    for h in range(4):
        c[:, C_TRI4 + h * 128:C_TRI4 + (h + 1) * 128] = tri
    for g, w in enumerate((2, 4, 8, 16)):
        t = np.arange(16)
        c[:, C_INVC + g * 16:C_INVC + (g + 1) * 16] = 1.0 / np.minimum(t + 1, w)
    return c


V_NMIX = 0
V_NFFN = 32
V_NFIN = 64
V_PSCALE = 72
V_SUBLN = 80
V_GNORM = 82
NVEC = 98


def make_vecs(inp):
    v = np.zeros((128, NVEC), np.float32)

    def put(col, arr):
        a = np.asarray(arr, np.float32).reshape(-1, 128).T
        v[:, col:col + a.shape[1]] = a

    for i in range(DEPTH):
        put(V_NMIX + 8 * i, inp["norm_mix"][i])
        put(V_NFFN + 8 * i, inp["norm_ffn"][i])
    put(V_NFIN, inp["norm_final"])
    for e in range(2):
        put(V_PSCALE + 4 * e, inp["pool_scale"][e])
        put(V_SUBLN + e, inp["ab_subln"][e])
        put(V_GNORM + 8 * e, inp["gla_norm"][e].reshape(-1))
    return v


class Builder:
    def __init__(self, T, layers=(0, 1, 2, 3), final=True, dbg=False):
        self.T = T
        self.layers = tuple(layers)
        self.final = final
        nc = self.nc = bass.Bass("TRN2", target_bir_lowering=False)
        dt = nc.dram_tensor
        self.x_in = dt("xT", [D, T], F32, kind="ExternalInput").ap()
        self.y_out = dt("yT", [D, T], F32, kind="ExternalOutput").ap()
        self.cst_d = dt("cst", [128, NCST], F32, kind="ExternalInput").ap()
        self.vecs_d = dt("vecs", [128, NVEC], F32, kind="ExternalInput").ap()
        self.w = {}
        for name, shape in (("ab_w_in", [2, D, 2048]), ("ab_lambda", [2, 256]), ("pool_w", [2, 4, 128, 128]),
                            ("ab_w_out", [2, D, D]), ("gla_w_in", [2, D, 3088]), ("gla_w_gk_up", [2, 16, 512]),
                            ("gla_b_gk", [2, 1, 512]), ("gla_w_out", [2, D, D]), ("ffn_w1", [4, D, DFF]),
                            ("ffn_w2", [4, DFF, D])):
            self.w[name] = dt(name, shape, F32, kind="ExternalInput").ap()
        kw = {"kind": "ExternalOutput"} if dbg else {}
        self.xa = dt("xa", [D, T], F32, **kw).ap()
        self.xb = dt("xb", [D, T], F32, **kw).ap()
        self.qT = dt("qTs", [4, 128, T], BF16, **kw).ap()
        self.kT = dt("kTs", [4, 128, T], BF16, **kw).ap()
        self.Vs = dt("Vs", [4, 128, T // 128, 128], BF16, **kw).ap()
        self.mT = dt("mTs", [4, 128, T], BF16, **kw).ap()
        self.r_x = {}
        self.nsb = 0

    def sb(self, st, shape, dtp):
        self.nsb += 1
        return st.enter_context(self.nc.sbuf_tensor(f"sb{self.nsb}", shape, dtp))

    def psum(self, st):
        self.nsb += 1
        return [st.enter_context(self.nc.psum_tensor(f"ps{self.nsb}_{i}", [128, 512], F32)) for i in range(8)]

    def load_consts(self, S, st):
        cst = self.sb(st, [128, NCST], F32)
        vecs = self.sb(st, [128, NVEC], F32)
        r = S.res()
        S.dma("sp", cst[:], self.cst_d, writes=[r])
        S.dma("sp", vecs[:], self.vecs_d, writes=[r])
        return cst, vecs, r

    def load_weight(self, S, st_tmp, dst, src_rows, ncols, rres, wres, scale_cols=None, cw=1024):
        stg = [self.sb(st_tmp, [128, cw], F32) for _ in range(3)]
        r_stg = [S.res() for _ in range(3)]
        engs = ["pool", "dve", "act"]
        ci = 0
        for r, src in enumerate(src_rows):
            for c0 in range(0, ncols, cw):
                c1 = min(ncols, c0 + cw)
                b = ci % 3
                S.dma("sp", stg[b][:, :c1 - c0], src[:, c0:c1], writes=[r_stg[b]])
                out = dst[:, r, c0:c1]
                in_ = stg[b][:, :c1 - c0]
                eng = engs[ci % 3]
                sc = None if scale_cols is None else scale_cols[r]
                if sc is None:
                    if eng == "act":
                        S.add("act", (lambda o, i: lambda e: e.copy(o, i))(out, in_), reads=[r_stg[b]] + rres, writes=[wres])
                    else:
                        S.add(eng, (lambda o, i: lambda e: e.tensor_copy(o, i))(out, in_), reads=[r_stg[b]] + rres, writes=[wres])
                else:
                    if eng == "act":
                        S.add("act", (lambda o, i, s: lambda e: e.activation(o, i, AF.Copy, scale=s))(out, in_, sc),
                              reads=[r_stg[b]] + rres, writes=[wres])
                    else:
                        S.add(eng, (lambda o, i, s: lambda e: e.tensor_scalar(o, i, s, None, ALU.mult))(out, in_, sc),
                              reads=[r_stg[b]] + rres, writes=[wres])
                ci += 1

    def rmsnorm_tile(self, S, xt_c, r_xt, NT, sq, r_sq, ssum, r_ssum, rstd, r_rstd, psb, r_psb, ones, r_c, h, r_h, nfeat_inv=1.0 / D):
        S.add("act", lambda e: e.activation(sq[:], xt_c[:], AF.Square), reads=[r_xt], writes=[r_sq])
        S.add("dve", lambda e: e.tensor_reduce(ssum[:], sq[:].rearrange("p c t -> p t c"), AX.X, ALU.add),
              reads=[r_sq], writes=[r_ssum])
        S.add("pe", lambda e: e.matmul(psb[:, :NT], ones, ssum[:], start=True, stop=True),
              reads=[r_c, r_ssum], writes=[r_psb])
        S.add("act", lambda e: e.activation(ssum[:], psb[:, :NT], AF.Ln, bias=float(EPS), scale=nfeat_inv),
              reads=[r_psb], writes=[r_ssum])
        S.add("act", lambda e: e.activation(rstd[:], ssum[:], AF.Exp, scale=-0.5), reads=[r_ssum], writes=[r_rstd])
        for c in range(8):
            eng = "dve" if c % 2 == 0 else "pool"
            S.add(eng, (lambda c: lambda e: e.tensor_tensor(h[:, c, :], xt_c[:, c, :], rstd[:], ALU.mult))(c),
                  reads=[r_xt, r_rstd], writes=[r_h[c]])

    def ffn_sweep(self, S, li, xin, r_xin, xout, r_xout, final):
        T = self.T
        FNT = 256
        nc = self.nc
        with ExitStack() as st:
            cst, vecs, r_c = self.load_consts(S, st)
            ones = cst[:, C_ONES:C_ONES + 128]
            w1sb = self.sb(st, [128, 8, DFF], BF16)
            w2sb = self.sb(st, [128, 32, D], BF16)
            r_w1, r_w2 = S.res(), S.res()
            with ExitStack() as st2:
                w1v = self.w["ffn_w1"][li].rearrange("(kc p) n -> p kc n", p=128)
                w2v = self.w["ffn_w2"][li].rearrange("(hc p) n -> p hc n", p=128)
                self.load_weight(S, st2, w1sb, [w1v[:, kc, :] for kc in range(8)], DFF, [r_c], r_w1,
                                 scale_cols=[vecs[:, V_NFFN + 8 * li + kc:V_NFFN + 8 * li + kc + 1] for kc in range(8)])
                self.load_weight(S, st2, w2sb, [w2v[:, hc, :] for hc in range(32)], D, [r_c], r_w2)
                S.flush()
            xt = [self.sb(st, [128, 8, FNT], F32) for _ in range(2)]
            xo = self.sb(st, [128, 8, FNT], F32)
            sq = self.sb(st, [128, 8, FNT], F32)
            ssum = self.sb(st, [128, FNT], F32)
            rstd = self.sb(st, [128, FNT], F32)
            h = self.sb(st, [128, 8, FNT], BF16)
            hid = self.sb(st, [128, 32, FNT], BF16)
            rl = [self.sb(st, [128, FNT], F32) for _ in range(4)]
            pb = self.psum(st)
            R = S.res
            r_xt = [R(), R()]; r_xo = R(); r_sq = R(); r_ssum = R(); r_rstd = R()
            r_h = [R() for _ in range(8)]; r_hid = [R() for _ in range(32)]
            r_pb = [R() for _ in range(8)]; r_rl = [R() for _ in range(4)]
            xv = xin.rearrange("(c p) t -> p c t", p=128)
            yv = xout.rearrange("(c p) t -> p c t", p=128)
            ntile = T // FNT

            def load(n):
                S.dma("sp", xt[n % 2][:], xv[:, :, n * FNT:(n + 1) * FNT], reads=[r_xin], writes=[r_xt[n % 2]])

            def norm(n):
                b = n % 2
                self.rmsnorm_tile(S, xt[b], r_xt[b], FNT, sq, r_sq, ssum, r_ssum, rstd, r_rstd, pb[7], r_pb[7],
                                  ones, r_c, h, r_h)

            def up(n):
                for j in range(32):
                    bk = j % 4
                    for kc in range(8):
                        S.add("pe", (lambda j, kc, bk: lambda e: e.matmul(
                            pb[bk][:, :FNT], w1sb[:, kc, j * 128:(j + 1) * 128], h[:, kc, :],
                            start=(kc == 0), stop=(kc == 7)))(j, kc, bk), reads=[r_w1, r_h[kc]], writes=[r_pb[bk]])
                    S.add("act", (lambda j, bk: lambda e: e.activation(rl[j % 4][:], pb[bk][:, :FNT], AF.Relu))(j, bk),
                          reads=[r_pb[bk]], writes=[r_rl[j % 4]])
                    S.add("pool", (lambda j: lambda e: e.tensor_tensor(hid[:, j, :], rl[j % 4][:], rl[j % 4][:], ALU.mult))(j),
                          reads=[r_rl[j % 4]], writes=[r_hid[j]])

            def down(n):
                b = n % 2
                for oc in range(8):
                    bk = 4 + oc % 2
                    for hc in range(32):
                        S.add("pe", (lambda oc, hc, bk: lambda e: e.matmul(
                            pb[bk][:, :FNT], w2sb[:, hc, oc * 128:(oc + 1) * 128], hid[:, hc, :],
                            start=(hc == 0), stop=(hc == 31)))(oc, hc, bk), reads=[r_w2, r_hid[hc]], writes=[r_pb[bk]])
                    S.add("dve", (lambda oc, bk: lambda e: e.tensor_tensor(
                        xo[:, oc, :], pb[bk][:, :FNT], xt[b][:, oc, :], ALU.add))(oc, bk),
                        reads=[r_pb[bk], r_xt[b]], writes=[r_xo])
                if final:
                    S.add("act", lambda e: e.activation(sq[:], xo[:], AF.Square), reads=[r_xo], writes=[r_sq])
                    S.add("dve", lambda e: e.tensor_reduce(ssum[:], sq[:].rearrange("p c t -> p t c"), AX.X, ALU.add),
                          reads=[r_sq], writes=[r_ssum])
                    S.add("pe", lambda e: e.matmul(pb[6][:, :FNT], ones, ssum[:], start=True, stop=True),
                          reads=[r_c, r_ssum], writes=[r_pb[6]])
                    S.add("act", lambda e: e.activation(ssum[:], pb[6][:, :FNT], AF.Ln, bias=float(EPS), scale=1.0 / D),
                          reads=[r_pb[6]], writes=[r_ssum])
                    S.add("act", lambda e: e.activation(rstd[:], ssum[:], AF.Exp, scale=-0.5), reads=[r_ssum], writes=[r_rstd])
                    for c in range(8):
                        S.add("dve", (lambda c: lambda e: e.scalar_tensor_tensor(
                            sq[:, c, :], xo[:, c, :], vecs[:, V_NFIN + c:V_NFIN + c + 1], rstd[:], ALU.mult, ALU.mult))(c),
                            reads=[r_xo, r_rstd, r_c], writes=[r_sq])
                    S.dma("pool", yv[:, :, n * FNT:(n + 1) * FNT], sq[:], reads=[r_sq], writes=[r_xout])
                else:
                    S.dma("pool", yv[:, :, n * FNT:(n + 1) * FNT], xo[:], reads=[r_xo], writes=[r_xout])

            load(0)
            if ntile > 1:
                load(1)
            norm(0)
            for n in range(ntile):
                up(n)
                if n + 1 < ntile and not final:
                    norm(n + 1)
                down(n)
                if n + 1 < ntile and final:
                    norm(n + 1)
                if n + 2 < ntile:
                    load(n + 2)
            S.flush()

    def a1_sweep(self, S, li, xin, r_xin):
        T = self.T
        NT = 512
        e_ = li // 2
        with ExitStack() as st:
            cst, vecs, r_c = self.load_consts(S, st)
            ones = cst[:, C_ONES:C_ONES + 128]
            win = self.sb(st, [128, 8, 2048], BF16)
            pw = self.sb(st, [128, 4, 128], BF16)
            r_win, r_pw = S.res(), S.res()
            with ExitStack() as st2:
                wv = self.w["ab_w_in"][e_].rearrange("(kc p) n -> p kc n", p=128)
                self.load_weight(S, st2, win, [wv[:, kc, :] for kc in range(8)], 2048, [r_c], r_win,
                                 scale_cols=[vecs[:, V_NMIX + 8 * li + kc:V_NMIX + 8 * li + kc + 1] for kc in range(8)])
                pwv = self.w["pool_w"][e_].rearrange("g c d -> c g d")
                self.load_weight(S, st2, pw, [pwv[:, g, :] for g in range(4)], 128, [r_c], r_pw, cw=128)
                S.flush()
            xt = [self.sb(st, [128, 8, NT], F32) for _ in range(2)]
            sq = self.sb(st, [128, 8, NT], F32)
            ssum = self.sb(st, [128, NT], F32)
            rstd = self.sb(st, [128, NT], F32)
            h = self.sb(st, [128, 8, NT], BF16)
            qk = [self.sb(st, [128, 8, NT], BF16) for _ in range(2)]
            vt = [self.sb(st, [128, 4, 512], BF16) for _ in range(2)]
            uext = [self.sb(st, [128, 4, 16 + NT], F32) for _ in range(2)]
            ta = self.sb(st, [128, 16 + NT], F32)
            tb = self.sb(st, [128, 16 + NT], F32)
            rr = self.sb(st, [128, 4, NT], BF16)
            mo = [self.sb(st, [128, 4, NT], BF16) for _ in range(2)]
            pb = self.psum(st)
            R = S.res
            r_xt = [R(), R()]; r_sq = R(); r_ssum = R(); r_rstd = R()
            r_h = [R() for _ in range(8)]; r_pb = [R() for _ in range(8)]
            r_qk = [R(), R()]; r_vt = [R(), R()]; r_ue = [[R() for _ in range(4)] for _ in range(2)]
            r_ta, r_tb = R(), R(); r_rr = [R() for _ in range(4)]; r_mo = [R(), R()]
            r_sc = self.r_scr
            xv = xin.rearrange("(c p) t -> p c t", p=128)
            ntile = T // NT
            qTv = self.qT.rearrange("h p t -> p h t")
            kTv = self.kT.rearrange("h p t -> p h t")
            mTv = self.mT.rearrange("g p t -> p g t")
            Vv = self.Vs.rearrange("h p k v -> p h k v")

            def load(n):
                S.dma("sp", xt[n % 2][:], xv[:, :, n * NT:(n + 1) * NT], reads=[r_xin], writes=[r_xt[n % 2]])

            def norm(n):
                b = n % 2
                self.rmsnorm_tile(S, xt[b], r_xt[b], NT, sq, r_sq, ssum, r_ssum, rstd, r_rstd, pb[7], r_pb[7],
                                  ones, r_c, h, r_h)

            for g in range(4):
                S.add("pool", (lambda g: lambda e: e.memset(uext[0][:, g, 0:16], 0.0))(g), writes=[r_ue[0][g]])
            S.add("pool", lambda e: e.memset(ta[:], 0.0), writes=[r_ta])
            S.add("pool", lambda e: e.memset(tb[:], 0.0), writes=[r_tb])

            def proj(n):
                b = n % 2
                t0 = n * NT
                for oc in range(8):
                    bk = oc % 3
                    for kc in range(8):
                        S.add("pe", (lambda oc, kc, bk: lambda e: e.matmul(
                            pb[bk][:, :NT], win[:, kc, oc * 128:(oc + 1) * 128], h[:, kc, :],
                            start=(kc == 0), stop=(kc == 7)))(oc, kc, bk), reads=[r_win, r_h[kc]], writes=[r_pb[bk]])
                    S.add("act", (lambda oc, bk: lambda e: e.copy(qk[b][:, oc, :], pb[bk][:, :NT]))(oc, bk),
                          reads=[r_pb[bk]], writes=[r_qk[b]])
                S.dma("pool", qTv[:, :, t0:t0 + NT], qk[b][:, 0:4, :], reads=[r_qk[b]], writes=[r_sc])
                S.dma("pool", kTv[:, :, t0:t0 + NT], qk[b][:, 4:8, :], reads=[r_qk[b]], writes=[r_sc])
                for s in range(4):
                    bk = 3 + s % 2
                    for kc in range(8):
                        S.add("pe", (lambda s, kc, bk: lambda e: e.matmul(
                            pb[bk][:, :512], h[:, kc, s * 128:(s + 1) * 128], win[:, kc, 1024:1536],
                            start=(kc == 0), stop=(kc == 7)))(s, kc, bk), reads=[r_win, r_h[kc]], writes=[r_pb[bk]])
                    S.add("dve", (lambda s, bk: lambda e: e.tensor_copy(vt[b][:, s, :], pb[bk][:, :512]))(s, bk),
                          reads=[r_pb[bk]], writes=[r_vt[b]])
                for hh in range(4):
                    S.dma("pool", self.Vs[hh, :, 4 * n:4 * n + 4, :], vt[b][:, :, hh * 128:(hh + 1) * 128],
                          reads=[r_vt[b]], writes=[r_sc])
                for g in range(4):
                    bk = 5 + g % 2
                    oc = 12 + g
                    for kc in range(8):
                        S.add("pe", (lambda oc, kc, bk: lambda e: e.matmul(
                            pb[bk][:, :NT], win[:, kc, oc * 128:(oc + 1) * 128], h[:, kc, :],
                            start=(kc == 0), stop=(kc == 7)))(oc, kc, bk), reads=[r_win, r_h[kc]], writes=[r_pb[bk]])
                    S.add("act", (lambda g, bk: lambda e: e.copy(uext[b][:, g, 16:16 + NT], pb[bk][:, :NT]))(g, bk),
                          reads=[r_pb[bk]], writes=[r_ue[b][g]])

            def pool(n):
                b = n % 2
                t0 = n * NT
                W = 16 + NT
                for g in range(4):
                    w = 2 ** (g + 1)
                    src = uext[b][:, g, :]
                    r_src = r_ue[b][g]
                    bufs = [(ta, r_ta), (tb, r_tb)]
                    sh = 1
                    k = 0
                    cur, r_cur = src, r_src
                    while sh < w:
                        dst, r_dst = bufs[k % 2]
                        S.add("pool", (lambda dst, cur, sh: lambda e: e.tensor_tensor(
                            dst[:, sh:W], cur[:, sh:W], cur[:, 0:W - sh], ALU.add))(dst, cur, sh),
                            reads=[r_cur], writes=[r_dst])
                        cur, r_cur = dst[:], r_dst
                        sh *= 2
                        k += 1
                    S.add("dve", (lambda g, cur, w: lambda e: e.scalar_tensor_tensor(
                        rr[:, g, :], cur[:, 16:W], 1.0 / w, uext[b][:, g, 16:W], ALU.mult, ALU.subtract))(g, cur, w),
                        reads=[r_cur, r_ue[b][g]], writes=[r_rr[g]])
                    if n == 0:
                        S.add("dve", (lambda g, cur: lambda e: e.tensor_tensor(
                            ta[:, 0:16], cur[:, 16:32], cst[:, C_INVC + g * 16:C_INVC + (g + 1) * 16], ALU.mult))(g, cur),
                            reads=[r_cur, r_c], writes=[r_ta])
                        S.add("dve", (lambda g: lambda e: e.tensor_tensor(
                            rr[:, g, 0:16], ta[:, 0:16], uext[b][:, g, 16:32], ALU.subtract))(g),
                            reads=[r_ta, r_ue[b][g]], writes=[r_rr[g]])
                    if n + 1 < ntile:
                        S.add("pool", (lambda g: lambda e: e.tensor_copy(uext[1 - b][:, g, 0:16], uext[b][:, g, NT:NT + 16]))(g),
                              reads=[r_ue[b][g]], writes=[r_ue[1 - b][g]])

            def pool_mm(n):
                b = n % 2
                t0 = n * NT
                for g in range(4):
                    bk = 5 + g % 2
                    S.add("pe", (lambda g, bk: lambda e: e.matmul(pb[bk][:, :NT], pw[:, g, :], rr[:, g, :], start=True, stop=True))(g, bk),
                          reads=[r_pw, r_rr[g]], writes=[r_pb[bk]])
                    S.add("act", (lambda g, bk: lambda e: e.activation(
                        mo[b][:, g, :], pb[bk][:, :NT], AF.Copy, scale=vecs[:, V_PSCALE + 4 * e_ + g:V_PSCALE + 4 * e_ + g + 1]))(g, bk),
                        reads=[r_pb[bk], r_c], writes=[r_mo[b]])
                S.dma("pool", mTv[:, :, t0:t0 + NT], mo[b][:], reads=[r_mo[b]], writes=[r_sc])

            load(0)
            if ntile > 1:
                load(1)
            norm(0)
            for n in range(ntile):
                proj(n)
                if n > 0:
                    pool_mm(n - 1)
                if n + 1 < ntile:
                    norm(n + 1)
                pool(n)
                if n + 2 < ntile:
                    load(n + 2)
            pool_mm(ntile - 1)
            S.flush()

    def a2_sweep(self, S, li, xin, r_xin, xout, r_xout):
        T = self.T
        NT = 512
        e_ = li // 2
        lam_init = 0.8 - 0.6 * math.exp(-0.3 * li)
        KG = 16
        with ExitStack() as st:
            cst, vecs, r_c = self.load_consts(S, st)
            ones = cst[:, C_ONES:C_ONES + 128]
            wout = self.sb(st, [128, 8, D], BF16)
            r_wout = S.res()
            onesb = self.sb(st, [128, 128], BF16)
            lamt = self.sb(st, [128, 256], F32)
            lprod = self.sb(st, [128, 128], F32)
            lsum = self.sb(st, [128, 2], F32)
            neglam = self.sb(st, [128, 1], F32)
            gsub = self.sb(st, [128, 1], F32)
            r_l = S.res()
            ediag = [[self.sb(st, [128, NT], BF16) for _ in range(2)] for _ in range(4)]
            r_ed = [[S.res() for _ in range(2)] for _ in range(4)]
            with ExitStack() as st2:
                wv = self.w["ab_w_out"][e_].rearrange("(kc p) n -> p kc n", p=128)
                self.load_weight(S, st2, wout, [wv[:, kc, :] for kc in range(8)], D, [r_c], r_wout)
                S.add("dve", lambda e: e.tensor_copy(onesb[:], ones), reads=[r_c], writes=[r_l])
                S.dma("sp", lamt[:], self.w["ab_lambda"][e_:e_ + 1, :].partition_broadcast(128), writes=[r_l])
                S.add("dve", lambda e: e.tensor_tensor(lprod[:, 0:64], lamt[:, 0:64], lamt[:, 64:128], ALU.mult), reads=[r_l], writes=[r_l])
                S.add("dve", lambda e: e.tensor_tensor(lprod[:, 64:128], lamt[:, 128:192], lamt[:, 192:256], ALU.mult), reads=[r_l], writes=[r_l])
                S.add("dve", lambda e: e.tensor_reduce(lsum[:], lprod[:].rearrange("p (a d) -> p a d", a=2), AX.X, ALU.add), reads=[r_l], writes=[r_l])
                S.add("act", lambda e: e.activation(lsum[:], lsum[:], AF.Exp), reads=[r_l], writes=[r_l])
                S.add("dve", lambda e: e.tensor_tensor(neglam[:], lsum[:, 1:2], lsum[:, 0:1], ALU.subtract), reads=[r_l], writes=[r_l])
                S.add("dve", lambda e: e.tensor_scalar(neglam[:], neglam[:], -float(lam_init), None, ALU.add), reads=[r_l], writes=[r_l])
                S.add("dve", lambda e: e.tensor_scalar(gsub[:], vecs[:, V_SUBLN + e_:V_SUBLN + e_ + 1], float(1.0 - lam_init), None, ALU.mult),
                      reads=[r_c], writes=[r_l])
                for j in range(4):
                    for a in range(2):
                        S.add("pool", (lambda j, a: lambda e: e.memset(ediag[j][a][:], 0.0))(j, a), writes=[r_ed[j][a]])
                S.flush()
            xt = [self.sb(st, [128, 8, NT], F32) for _ in range(2)]
            xo = self.sb(st, [128, 8, NT], F32)
            qt = [self.sb(st, [128, 4, NT], BF16) for _ in range(2)]
            mo = [self.sb(st, [128, 4, NT], BF16) for _ in range(2)]
            ao = self.sb(st, [128, 4, NT], BF16)
            kb = [self.sb(st, [128, KG * 128], BF16) for _ in range(3)]
            vb = [self.sb(st, [128, KG, 128], BF16) for _ in range(3)]
            eb = [[self.sb(st, [128, NT], BF16) for _ in range(2)] for _ in range(3)]
            rl1 = self.sb(st, [128, NT], F32); rl2 = self.sb(st, [128, NT], F32)
            t1 = self.sb(st, [128, NT], F32); t2 = self.sb(st, [128, NT], F32)
            A = self.sb(st, [128, NT], F32); asq = self.sb(st, [128, NT], F32)
            sd = self.sb(st, [128, NT], F32); rs = self.sb(st, [128, NT], F32)
            acc2 = self.sb(st, [128, NT], F32)
            r_acc2 = S.res()
            pb = self.psum(st)
            R = S.res
            r_xt = [R(), R()]; r_xo = R(); r_qt = [R(), R()]; r_mo = [R(), R()]; r_ao = [R() for _ in range(4)]
            r_kb = [R() for _ in range(3)]; r_vb = [R() for _ in range(3)]
            r_eb = [[R(), R()] for _ in range(3)]
            r_ep = R()
            r_pb = [R() for _ in range(8)]
            r_sc = self.r_scr
            xv = xin.rearrange("(c p) t -> p c t", p=128)
            yv = xout.rearrange("(c p) t -> p c t", p=128)
            qTv = self.qT.rearrange("h p t -> p h t")
            mTv = self.mT.rearrange("g p t -> p g t")
            nblk = T // NT
            PS_S = [(0, 1), (2, 3)]
            PO1, PO2, PL1, PL2 = 4, 5, 6, 7
            grp_ctr = [0]

            def load(n):
                t0 = n * NT
                b = n % 2
                S.dma("sp", xt[b][:], xv[:, :, t0:t0 + NT], reads=[r_xin], writes=[r_xt[b]])
                S.dma("sp", qt[b][:], qTv[:, :, t0:t0 + NT], reads=[r_sc], writes=[r_qt[b]])
                S.dma("sp", mo[b][:], mTv[:, :, t0:t0 + NT], reads=[r_sc], writes=[r_mo[b]])

            def load_kv(h, g0, ng):
                i = grp_ctr[0] % 3
                grp_ctr[0] += 1
                S.dma("sp", kb[i][:, :ng * 128], self.kT[h, :, g0 * 128:(g0 + ng) * 128], reads=[r_sc], writes=[r_kb[i]])
                S.dma("sp", vb[i][:, :ng, :], self.Vs[h, :, g0:g0 + ng, :], reads=[r_sc], writes=[r_vb[i]])
                return i

            def attn_head(qb, h):
                b = qb % 2
                nk = 4 * (qb + 1)
                groups = []
                for g0 in range(0, nk, KG):
                    groups.append((g0, min(KG, nk - g0)))
                gbuf = {}
                for gi in range(min(2, len(groups))):
                    gbuf[gi] = load_kv(h, *groups[gi])
                ectr = [0]
                pend = []

                def qk(kt):
                    gi, lk = kt // KG, kt % KG
                    if gi not in gbuf:
                        gbuf[gi] = load_kv(h, *groups[gi])
                    i = gbuf[gi]
                    j = kt - 4 * qb
                    c0 = 128 * j if j >= 0 else 0
                    sa, sb_ = PS_S[kt % 2]
                    S.add("pe", lambda e: e.matmul(pb[sa][:, c0:NT], kb[i][0:64, lk * 128:(lk + 1) * 128], qt[b][0:64, h, c0:NT],
                                                   start=True, stop=True), reads=[r_kb[i], r_qt[b]], writes=[r_pb[sa]])
                    S.add("pe", lambda e: e.matmul(pb[sb_][:, c0:NT], kb[i][64:128, lk * 128:(lk + 1) * 128], qt[b][64:128, h, c0:NT],
                                                   start=True, stop=True), reads=[r_kb[i], r_qt[b]], writes=[r_pb[sb_]])
                    if j < 0:
                        k3 = ectr[0] % 3
                        ectr[0] += 1
                        e1, e2 = eb[k3][0], eb[k3][1]
                        re1, re2 = r_eb[k3][0], r_eb[k3][1]
                        S.add("act", lambda e: e.activation(e1[:, :], pb[sa][:, :NT], AF.Exp, scale=0.125), reads=[r_pb[sa]], writes=[re1])
                        S.add("act", lambda e: e.activation(e2[:, :], pb[sb_][:, :NT], AF.Exp, scale=0.125), reads=[r_pb[sb_]], writes=[re2])
                    else:
                        e1, e2 = ediag[j][0], ediag[j][1]
                        re1, re2 = r_ed[j][0], r_ed[j][1]
                        for (et, re_, pbn) in ((e1, re1, sa), (e2, re2, sb_)):
                            S.add("act", (lambda et, pbn: lambda e: e.activation(et[0:64, c0:NT], pb[pbn][0:64, c0:NT], AF.Exp, scale=0.125))(et, pbn),
                                  reads=[r_pb[pbn]], writes=[re_])
                            S.add("act", (lambda et, pbn: lambda e: e.activation(et[64:128, c0 + 64:NT], pb[pbn][64:128, c0 + 64:NT], AF.Exp, scale=0.125))(et, pbn),
                                  reads=[r_pb[pbn]], writes=[re_])
                    return (kt, c0, e1, e2, re1, re2, i, lk)

                def pv(item):
                    kt, c0, e1, e2, re1, re2, i, lk = item
                    first, last = (kt == 0), (kt == nk - 1)
                    S.add("pe", lambda e: e.matmul(pb[PO1][:, c0:NT], vb[i][:, lk, :], e1[:, c0:NT], start=first, stop=last, skip_group_check=True),
                          reads=[r_vb[i], re1], writes=[r_pb[PO1]])
                    S.add("pe", lambda e: e.matmul(pb[PO2][:, c0:NT], vb[i][:, lk, :], e2[:, c0:NT], start=first, stop=last, skip_group_check=True),
                          reads=[r_vb[i], re2], writes=[r_pb[PO2]])
                    S.add("pe", lambda e: e.matmul(pb[PL1][:, c0:NT], onesb[:], e1[:, c0:NT], start=first, stop=last, skip_group_check=True),
                          reads=[r_l, re1], writes=[r_pb[PL1]])
                    if first:
                        S.add("dve", lambda e: e.tensor_copy(acc2[:, c0:NT], e2[:, c0:NT]), reads=[re2], writes=[r_acc2])
                    else:
                        S.add("dve", lambda e: e.tensor_tensor(acc2[:, c0:NT], acc2[:, c0:NT], e2[:, c0:NT], ALU.add),
                              reads=[re2, r_acc2], writes=[r_acc2])
                    if last:
                        S.add("pe", lambda e: e.matmul(pb[PL2][:, :NT], ones, acc2[:], start=True, stop=True),
                              reads=[r_c, r_acc2], writes=[r_pb[PL2]])

                prev = qk(0)
                for kt in range(1, nk):
                    cur = qk(kt)
                    pv(prev)
                    prev = cur
                pv(prev)
                S.add("act", lambda e: e.activation(rl1[:], pb[PL1][:, :NT], AF.Ln), reads=[r_pb[PL1]], writes=[r_ep])
                S.add("act", lambda e: e.activation(rl1[:], rl1[:], AF.Exp, scale=-1.0), reads=[r_ep], writes=[r_ep])
                S.add("act", lambda e: e.activation(rl2[:], pb[PL2][:, :NT], AF.Ln), reads=[r_pb[PL2]], writes=[r_ep])
                S.add("act", lambda e: e.activation(rl2[:], rl2[:], AF.Exp, scale=-1.0), reads=[r_ep], writes=[r_ep])
                S.add("dve", lambda e: e.tensor_tensor(t1[:], pb[PO1][:, :NT], rl1[:], ALU.mult), reads=[r_pb[PO1], r_ep], writes=[r_ep])
                S.add("dve", lambda e: e.tensor_tensor(t2[:], pb[PO2][:, :NT], rl2[:], ALU.mult), reads=[r_pb[PO2], r_ep], writes=[r_ep])
                S.add("dve", lambda e: e.scalar_tensor_tensor(A[:], t2[:], neglam[:, 0:1], t1[:], ALU.mult, ALU.add), reads=[r_ep, r_l], writes=[r_ep])
                S.add("pool", lambda e: e.tensor_tensor(asq[:], A[:], A[:], ALU.mult), reads=[r_ep], writes=[r_ep])
                sa = PS_S[nk % 2][0]
                S.add("pe", lambda e: e.matmul(pb[sa][:, :NT], ones, asq[:], start=True, stop=True), reads=[r_c, r_ep], writes=[r_pb[sa]])
                S.add("act", lambda e: e.activation(sd[:], pb[sa][:, :NT], AF.Ln, bias=float(EPS), scale=1.0 / 128), reads=[r_pb[sa]], writes=[r_ep])
                S.add("act", lambda e: e.activation(rs[:], sd[:], AF.Exp, scale=-0.5), reads=[r_ep], writes=[r_ep])
                S.add("dve", lambda e: e.scalar_tensor_tensor(ao[:, h, :], A[:], gsub[:, 0:1], rs[:], ALU.mult, ALU.mult),
                      reads=[r_ep, r_l], writes=[r_ao[h]])

            def outproj(qb):
                b = qb % 2
                t0 = qb * NT
                for oc in range(8):
                    bk = PS_S[oc % 2][1]
                    for c in range(8):
                        if c < 4:
                            S.add("pe", (lambda oc, c, bk: lambda e: e.matmul(pb[bk][:, :NT], wout[:, c, oc * 128:(oc + 1) * 128], ao[:, c, :],
                                                                               start=(c == 0), stop=False))(oc, c, bk),
                                  reads=[r_wout, r_ao[c]], writes=[r_pb[bk]])
                        else:
                            S.add("pe", (lambda oc, c, bk: lambda e: e.matmul(pb[bk][:, :NT], wout[:, c, oc * 128:(oc + 1) * 128], mo[b][:, c - 4, :],
                                                                               start=False, stop=(c == 7)))(oc, c, bk),
                                  reads=[r_wout, r_mo[b]], writes=[r_pb[bk]])
                    S.add("dve", (lambda oc, bk: lambda e: e.tensor_tensor(xo[:, oc, :], pb[bk][:, :NT], xt[b][:, oc, :], ALU.add))(oc, bk),
                          reads=[r_pb[bk], r_xt[b]], writes=[r_xo])
                S.dma("pool", yv[:, :, t0:t0 + NT], xo[:], reads=[r_xo], writes=[r_xout])

            load(0)
            for qb in range(nblk):
                if qb + 1 < nblk:
                    load(qb + 1)
                for h in range(4):
                    attn_head(qb, h)
                outproj(qb)
            S.flush()

    def gla_sweep(self, S, li, xin, r_xin, xout, r_xout):
        T = self.T
        NT = 512
        o_ = li // 2
        scale = 128.0 ** -0.5
        with ExitStack() as st:
            cst, vecs, r_c = self.load_consts(S, st)
            ones = cst[:, C_ONES:C_ONES + 128]
            tri = cst[:, C_TRI:C_TRI + 128]
            su = cst[:, C_SU:C_SU + 128]
            tri4 = cst[:, C_TRI4:C_TRI4 + 512]
            win = self.sb(st, [128, 8, 3104], BF16)
            wout = self.sb(st, [128, 8, D], BF16)
            wgk = self.sb(st, [64, 1, 512], BF16)
            r_win, r_wout, r_wgk = S.res(), S.res(), S.res()
            with ExitStack() as st2:
                wv = self.w["gla_w_in"][o_].rearrange("(kc p) n -> p kc n", p=128)
                self.load_weight(S, st2, win, [wv[:, kc, :] for kc in range(8)], 3088, [r_c], r_win,
                                 scale_cols=[vecs[:, V_NMIX + 8 * li + kc:V_NMIX + 8 * li + kc + 1] for kc in range(8)], cw=1024)
                wo = self.w["gla_w_out"][o_].rearrange("(kc p) n -> p kc n", p=128)
                self.load_weight(S, st2, wout, [wo[:, kc, :] for kc in range(8)], D, [r_c], r_wout)
                stg = self.sb(st2, [64, 512], F32)
                r_s = S.res()
                S.add("pool", lambda e: e.memset(stg[:], 0.0), writes=[r_s])
                S.add("pool", lambda e: e.memset(win[:, :, 3088:3104], 0.0), writes=[r_win])
                S.dma("sp", stg[0:16, :], self.w["gla_w_gk_up"][o_], reads=[r_s], writes=[r_s])
                S.dma("sp", stg[32:33, :], self.w["gla_b_gk"][o_], reads=[r_s], writes=[r_s])
                S.add("dve", lambda e: e.tensor_copy(wgk[:, 0, :], stg[:]), reads=[r_s], writes=[r_wgk])
                S.flush()
            xt = [self.sb(st, [128, 8, NT], F32) for _ in range(2)]
            sq = self.sb(st, [128, 8, NT], F32)
            xo = sq
            ssum = self.sb(st, [128, NT], F32)
            rstd = self.sb(st, [128, NT], F32)
            h = self.sb(st, [128, 8, NT], BF16)
            qf = self.sb(st, [128, 4, NT], F32)
            kf = self.sb(st, [128, 4, NT], F32)
            gate = self.sb(st, [128, 8, NT], F32)
            gl = self.sb(st, [64, NT], BF16)
            ktok = self.sb(st, [128, 512], F32)
            vtok = self.sb(st, [128, 1024], BF16)
            ez = self.sb(st, [128, 512], F32)
            gtok = self.sb(st, [128, 512], F32)
            ebp = self.sb(st, [128, 512], F32)
            enb = self.sb(st, [128, 512], F32)
            er = ez
            qd = self.sb(st, [128, 4, 128], BF16)
            kd = self.sb(st, [128, 4, 128], BF16)
            kl = self.sb(st, [128, 512], BF16)
            am = self.sb(st, [128, 512], BF16)
            Sf = self.sb(st, [128, 4, 256], F32)
            Sb = [self.sb(st, [128, 4, 256], BF16) for _ in range(2)]
            osq = self.sb(st, [128, 8, 128], F32)
            sdn = self.sb(st, [128, 512], F32)
            rsn = self.sb(st, [128, 512], F32)
            tg = self.sb(st, [128, 8, 128], F32)
            og = self.sb(st, [128, 8, NT], BF16)
            pb = self.psum(st)
            R = S.res
            r_xt = [R(), R()]; r_sq = R(); r_xo = r_sq; r_ssum = R(); r_rstd = R()
            r_h = [R() for _ in range(8)]; r_pb = [R() for _ in range(8)]
            r_qf, r_kf, r_gate, r_gl = R(), R(), R(), R()
            r_ktok, r_vtok, r_ez, r_gtok, r_ebp, r_enb = R(), R(), R(), R(), R(), R(); r_er = r_ez
            r_qd, r_kd, r_kl, r_am = R(), R(), R(), R()
            r_Sf = [R() for _ in range(4)]; r_Sb = [[R() for _ in range(4)] for _ in range(2)]
            r_osq, r_sdn, r_rsn, r_tg, r_og = R(), R(), R(), R(), R()
            xv = xin.rearrange("(c p) t -> p c t", p=128)
            yv = xout.rearrange("(c p) t -> p c t", p=128)
            ntile = T // NT
            S.add("pool", lambda e: e.memset(Sf[:], 0.0), writes=r_Sf)
            S.add("pool", lambda e: e.memset(Sb[0][:], 0.0), writes=r_Sb[0])
            S.add("pool", lambda e: e.memset(Sb[1][:], 0.0), writes=r_Sb[1])
            S.add("pool", lambda e: e.memset(gl[:], 0.0), writes=[r_gl])
            S.add("pool", lambda e: e.memset(gl[32:64, :], 1.0), writes=[r_gl])
            chunk_ctr = [0]

            def load(n):
                S.dma("sp", xt[n % 2][:], xv[:, :, n * NT:(n + 1) * NT], reads=[r_xin], writes=[r_xt[n % 2]])

            def norm(n):
                b = n % 2
                self.rmsnorm_tile(S, xt[b], r_xt[b], NT, sq, r_sq, ssum, r_ssum, rstd, r_rstd, pb[1], r_pb[1],
                                  ones, r_c, h, r_h)

            def fm(oc0, ncols, dst_fn, bk):
                for kc in range(8):
                    S.add("pe", (lambda kc: lambda e: e.matmul(pb[bk][:ncols, :NT], win[:, kc, oc0:oc0 + ncols], h[:, kc, :],
                                                               start=(kc == 0), stop=(kc == 7)))(kc),
                          reads=[r_win, r_h[kc]], writes=[r_pb[bk]])
                dst_fn(bk)

            def proj_fm(n):
                k = 0
                for c in range(4):
                    bk = k % 2; k += 1
                    fm(c * 128, 128, (lambda c: lambda bk: S.add("act", lambda e: e.copy(qf[:, c, :], pb[bk][:, :NT]),
                                                                  reads=[r_pb[bk]], writes=[r_qf]))(c), bk)
                for c in range(4):
                    bk = k % 2; k += 1
                    fm(512 + c * 128, 128, (lambda c: lambda bk: S.add("dve", lambda e: e.tensor_copy(kf[:, c, :], pb[bk][:, :NT]),
                                                                        reads=[r_pb[bk]], writes=[r_kf]))(c), bk)
                for c in range(8):
                    bk = k % 2; k += 1
                    fm(2048 + c * 128, 128, (lambda c: lambda bk: S.add("act", lambda e: e.activation(gate[:, c, :], pb[bk][:, :NT], AF.Silu),
                                                                         reads=[r_pb[bk]], writes=[r_gate]))(c), bk)
                bk = k % 2; k += 1
                fm(3072, 32, lambda bk: S.add("dve", lambda e: e.tensor_copy(gl[0:16, :], pb[bk][0:16, :NT]),
                                              reads=[r_pb[bk]], writes=[r_gl]), bk)

            def chunk(n, s):
                b = n % 2
                ssl = slice(s * 128, (s + 1) * 128)
                cc = chunk_ctr[0]
                chunk_ctr[0] += 1
                sbr, sbw = Sb[cc % 2], Sb[1 - cc % 2]
                r_sbr, r_sbw = r_Sb[cc % 2], r_Sb[1 - cc % 2]
                for kc in range(8):
                    S.add("pe", (lambda kc: lambda e: e.matmul(pb[2][:, :512], h[:, kc, ssl], win[:, kc, 512:1024],
                                                               start=(kc == 0), stop=(kc == 7)))(kc), reads=[r_win, r_h[kc]], writes=[r_pb[2]])
                S.add("act", lambda e: e.copy(ktok[:], pb[2][:, :512]), reads=[r_pb[2]], writes=[r_ktok])
                for hf in range(2):
                    bk = 3 + hf
                    for kc in range(8):
                        S.add("pe", (lambda kc, hf, bk: lambda e: e.matmul(pb[bk][:, :512], h[:, kc, ssl], win[:, kc, 1024 + hf * 512:1536 + hf * 512],
                                                                           start=(kc == 0), stop=(kc == 7)))(kc, hf, bk),
                              reads=[r_win, r_h[kc]], writes=[r_pb[bk]])
                    S.add("dve", (lambda hf, bk: lambda e: e.tensor_copy(vtok[:, hf * 512:(hf + 1) * 512], pb[bk][:, :512]))(hf, bk),
                          reads=[r_pb[bk]], writes=[r_vtok])
                S.add("pe", lambda e: e.matmul(pb[5][:, :512], gl[:, ssl], wgk[:, 0, :], start=True, stop=True),
                      reads=[r_gl, r_wgk], writes=[r_pb[5]])
                S.add("act", lambda e: e.activation(ez[:], pb[5][:, :512], AF.Exp, scale=-1.0), reads=[r_pb[5]], writes=[r_ez])
                S.add("act", lambda e: e.activation(ez[:], ez[:], AF.Ln, bias=1.0), reads=[r_ez], writes=[r_ez])
                S.add("dve", lambda e: e.tensor_scalar(gtok[:], ez[:], -1.0 / 16.0, None, ALU.mult), reads=[r_ez], writes=[r_gtok])
                for hh in range(4):
                    S.add("pe", (lambda hh: lambda e: e.matmul(pb[6][:, hh * 128:(hh + 1) * 128], gtok[:, hh * 128:(hh + 1) * 128], tri,
                                                               start=True, stop=True))(hh), reads=[r_gtok, r_c], writes=[r_pb[6]])
                S.add("pe", lambda e: e.matmul(pb[7][:, :512], su, gtok[:], start=True, stop=True), reads=[r_gtok, r_c], writes=[r_pb[7]])
                S.add("act", lambda e: e.activation(ebp[:], pb[6][:, :512], AF.Exp), reads=[r_pb[6]], writes=[r_ebp])
                S.add("act", lambda e: e.activation(enb[:], pb[6][:, :512], AF.Exp, scale=-1.0), reads=[r_pb[6]], writes=[r_enb])
                S.add("act", lambda e: e.activation(er[:], pb[7][:, :512], AF.Exp), reads=[r_pb[7]], writes=[r_er])
                S.add("dve", lambda e: e.scalar_tensor_tensor(qd[:], qf[:, :, ssl], float(scale), ebp[:].rearrange("p (h t) -> p h t", h=4),
                                                              ALU.mult, ALU.mult), reads=[r_qf, r_ebp], writes=[r_qd])
                S.add("dve", lambda e: e.tensor_tensor(kd[:], kf[:, :, ssl], enb[:].rearrange("p (h t) -> p h t", h=4), ALU.mult),
                      reads=[r_kf, r_enb], writes=[r_kd])
                S.add("pool", lambda e: e.tensor_tensor(kl[:], ktok[:], er[:], ALU.mult), reads=[r_ktok, r_er], writes=[r_kl])
                for hh in range(4):
                    S.add("pe", (lambda hh: lambda e: e.matmul(pb[2][:, hh * 128:(hh + 1) * 128], kd[:, hh, :], qd[:, hh, :],
                                                               start=True, stop=True))(hh), reads=[r_kd, r_qd], writes=[r_pb[2]])
                S.add("dve", lambda e: e.tensor_tensor(am[:], pb[2][:, :512], tri4, ALU.mult), reads=[r_pb[2], r_c], writes=[r_am])
                for hh in range(4):
                    for vc in range(2):
                        bk = 3 + hh // 2
                        col = ((hh % 2) * 2 + vc) * 128
                        S.add("pe", (lambda hh, vc, bk, col: lambda e: e.matmul(
                            pb[bk][:, col:col + 128], vtok[:, hh * 256 + vc * 128:hh * 256 + (vc + 1) * 128], am[:, hh * 128:(hh + 1) * 128],
                            start=True, stop=False))(hh, vc, bk, col), reads=[r_vtok, r_am], writes=[r_pb[bk]])
                        S.add("pe", (lambda hh, vc, bk, col: lambda e: e.matmul(
                            pb[bk][:, col:col + 128], sbr[:, hh, vc * 128:(vc + 1) * 128], qd[:, hh, :],
                            start=False, stop=True))(hh, vc, bk, col), reads=[r_sbr[hh], r_qd], writes=[r_pb[bk]])
                for hh in range(4):
                    bk = 5 + hh // 2
                    col = (hh % 2) * 256
                    S.add("pe", (lambda hh, bk, col: lambda e: e.matmul(pb[bk][:, col:col + 256], kl[:, hh * 128:(hh + 1) * 128],
                                                                        vtok[:, hh * 256:(hh + 1) * 256], start=True, stop=True))(hh, bk, col),
                          reads=[r_kl, r_vtok], writes=[r_pb[bk]])
                    S.add("dve", (lambda hh, bk, col: lambda e: e.scalar_tensor_tensor(
                        Sf[:, hh, :], Sf[:, hh, :], ebp[:, hh * 128 + 127:hh * 128 + 128], pb[bk][:, col:col + 256], ALU.mult, ALU.add))(hh, bk, col),
                        reads=[r_pb[bk], r_ebp], writes=[r_Sf[hh]])
                    S.add("act", (lambda hh: lambda e: e.copy(sbw[:, hh, :], Sf[:, hh, :]))(hh), reads=[r_Sf[hh]], writes=[r_sbw[hh]])
                for q2 in range(2):
                    S.add("act", (lambda q2: lambda e: e.activation(osq[:, q2 * 4:(q2 + 1) * 4, :],
                                                                    pb[3 + q2][:, :512].rearrange("p (c t) -> p c t", c=4), AF.Square))(q2),
                          reads=[r_pb[3 + q2]], writes=[r_osq])
                for hh in range(4):
                    for vc in range(2):
                        S.add("pe", (lambda hh, vc: lambda e: e.matmul(pb[7][:, hh * 128:(hh + 1) * 128], ones, osq[:, hh * 2 + vc, :],
                                                                       start=(vc == 0), stop=(vc == 1)))(hh, vc), reads=[r_osq, r_c], writes=[r_pb[7]])
                S.add("act", lambda e: e.activation(sdn[:], pb[7][:, :512], AF.Ln, bias=float(EPS), scale=1.0 / 256), reads=[r_pb[7]], writes=[r_sdn])
                S.add("act", lambda e: e.activation(rsn[:], sdn[:], AF.Exp, scale=-0.5), reads=[r_sdn], writes=[r_rsn])
                for c in range(8):
                    hh = c // 2
                    S.add("pool", (lambda c, hh: lambda e: e.tensor_tensor(tg[:, c, :], gate[:, c, ssl], rsn[:, hh * 128:(hh + 1) * 128], ALU.mult))(c, hh),
                          reads=[r_gate, r_rsn], writes=[r_tg])
                for c in range(8):
                    bk = 3 + c // 4
                    col = (c % 4) * 128
                    S.add("dve", (lambda c, bk, col: lambda e: e.scalar_tensor_tensor(
                        og[:, c, ssl], pb[bk][:, col:col + 128], vecs[:, V_GNORM + 8 * o_ + c:V_GNORM + 8 * o_ + c + 1], tg[:, c, :],
                        ALU.mult, ALU.mult))(c, bk, col), reads=[r_pb[bk], r_tg, r_c], writes=[r_og])

            def outproj(n):
                b = n % 2
                t0 = n * NT
                for oc in range(8):
                    bk = oc % 2
                    for c in range(8):
                        S.add("pe", (lambda oc, c, bk: lambda e: e.matmul(pb[bk][:, :NT], wout[:, c, oc * 128:(oc + 1) * 128], og[:, c, :],
                                                                           start=(c == 0), stop=(c == 7)))(oc, c, bk),
                              reads=[r_wout, r_og], writes=[r_pb[bk]])
                    S.add("dve", (lambda oc, bk: lambda e: e.tensor_tensor(xo[:, oc, :], pb[bk][:, :NT], xt[b][:, oc, :], ALU.add))(oc, bk),
                          reads=[r_pb[bk], r_xt[b]], writes=[r_xo])
                S.dma("pool", yv[:, :, t0:t0 + NT], xo[:], reads=[r_xo], writes=[r_xout])

            load(0)
            for n in range(ntile):
                if n + 1 < ntile:
                    load(n + 1)
                norm(n)
                proj_fm(n)
                for s in range(4):
                    chunk(n, s)
                outproj(n)
            S.flush()

    def build(self):
        nc = self.nc
        with ExitStack() as st:
            S = Sched(nc, st)
            self.r_scr = S.res()
            r_in = S.res()
            r_a, r_b = S.res(), S.res()
            cur, r_cur = self.x_in, r_in
            nl = len(self.layers)
            for idx, li in enumerate(self.layers):
                last = (idx == nl - 1)
                if li % 2 == 0:
                    self.a1_sweep(S, li, cur, r_cur)
                    self.a2_sweep(S, li, cur, r_cur, self.xa, r_a)
                else:
                    self.gla_sweep(S, li, cur, r_cur, self.xa, r_a)
                if last:
                    self.ffn_sweep(S, li, self.xa, r_a, self.y_out, S.res(), final=self.final)
                else:
                    self.ffn_sweep(S, li, self.xa, r_a, self.xb, r_b, final=False)
                    cur, r_cur = self.xb, r_b
            self.ninst = S.ninst
        return nc


def host_inputs(inp, T_sl=None):
    cst = make_cst()
    vecs = make_vecs(inp)
    common = {
        "cst": cst, "vecs": vecs,
        "ab_w_in": np.ascontiguousarray(inp["ab_w_in"], np.float32),
        "ab_lambda": np.ascontiguousarray(np.asarray(inp["ab_lambda"], np.float32).reshape(2, 256)),
        "pool_w": np.ascontiguousarray(inp["pool_w"], np.float32),
        "ab_w_out": np.ascontiguousarray(inp["ab_w_out"], np.float32),
        "gla_w_in": np.ascontiguousarray(inp["gla_w_in"], np.float32),
        "gla_w_gk_up": np.ascontiguousarray(inp["gla_w_gk_up"], np.float32),
        "gla_b_gk": np.ascontiguousarray(np.asarray(inp["gla_b_gk"], np.float32).reshape(2, 1, 512)),
        "gla_w_out": np.ascontiguousarray(inp["gla_w_out"], np.float32),
        "ffn_w1": np.ascontiguousarray(inp["ffn_w1"], np.float32),
        "ffn_w2": np.ascontiguousarray(inp["ffn_w2"], np.float32),
    }
    return common


_CACHE = {}


def kernel(**inputs):
    x = np.asarray(inputs["x"], np.float32)
    B, T, _ = x.shape
    key = (T,)
    if key not in _CACHE:
        _CACHE[key] = Builder(T).build()
    nc = _CACHE[key]
    common = host_inputs(inputs)
    zeros = {k: np.zeros_like(v) for k, v in common.items()}
    zeros["xT"] = np.zeros((D, T), np.float32)
    in_maps = []
    for c in range(NCORES):
        if c in ACTIVE:
            m = dict(common)
            m["xT"] = np.ascontiguousarray(x[ACTIVE.index(c)].T)
        else:
            m = zeros
        in_maps.append(m)
    res = run_bass_kernel_spmd(nc, in_maps, core_ids=list(range(NCORES)))
    out = np.empty((B, T, D), np.float32)
    for b in range(B):
        out[b] = res.results[ACTIVE[b]]["yT"].T
    return out
```

```python
import math
from contextlib import ExitStack

import numpy as np
import concourse.bass as bass
import concourse.mybir as mybir
from concourse.bass_utils import run_bass_kernel_spmd

F32 = mybir.dt.float32
BF16 = mybir.dt.bfloat16
ALU = mybir.AluOpType
AF = mybir.ActivationFunctionType
AX = mybir.AxisListType

D = 1024
DFF = 4096
EPS = 1e-6
DEPTH = 4
NCORES = 8
ACTIVE = (0, 1, 4, 5)

ENGS = ("pe", "act", "dve", "pool", "sp")
NDMA_SEMS = 8


class Res:
    __slots__ = ("writer", "readers")

    def __init__(self):
        self.writer = None
        self.readers = []


class Op:
    __slots__ = ("eng", "fn", "deps", "is_dma", "sig", "count", "dsem", "dval", "waits", "snap", "done")

    def __init__(self, eng, fn, is_dma):
        self.eng = eng
        self.fn = fn
        self.is_dma = is_dma
        self.deps = []
        self.sig = False
        self.count = 0
        self.dsem = None
        self.dval = 0
        self.waits = []
        self.snap = None
        self.done = False


class Sched:
    def __init__(self, nc, stack):
        self.nc = nc
        self.ops = {e: [] for e in ENGS}
        self.cnt = {e: 0 for e in ENGS}
        self.ndma = {e: 0 for e in ENGS}
        self.dma_hist = {e: [] for e in ENGS}
        self.known = {e: [0] * len(ENGS) for e in ENGS}
        self.kdma = {e: {} for e in ENGS}
        self.esem = {e: stack.enter_context(nc.semaphore(f"s_{e}")) for e in ENGS if e != "sp"}
        self.dsem = {}
        for e in ("sp", "pool", "act"):
            for k in range(NDMA_SEMS):
                self.dsem[(e, k)] = stack.enter_context(nc.semaphore(f"d_{e}{k}"))
        self.ninst = 0

    def res(self):
        return Res()

    def add(self, eng, fn, reads=(), writes=(), is_dma=False):
        op = Op(eng, fn, is_dma)
        deps = []
        for r in reads:
            if r.writer is not None:
                deps.append(r.writer)
            r.readers.append(op)
        for w in writes:
            if w.writer is not None:
                deps.append(w.writer)
            deps.extend(w.readers)
            w.writer = op
            w.readers = []
        seen = set()
        for d in deps:
            if d is op or d.done or id(d) in seen:
                continue
            seen.add(id(d))
            op.deps.append(d)
        self.ops[eng].append(op)
        return op

    def dma(self, q, out, in_, reads=(), writes=()):
        return self.add(q, lambda e: e.dma_start(out=out, in_=in_), reads, writes, is_dma=True)

    def barrier(self):
        last = []
        for e in ENGS:
            for o in reversed(self.ops[e]):
                if o.fn is not None and not o.is_dma:
                    last.append(o)
                    break
        rec = []
        for e in ENGS:
            dm = [o for o in self.ops[e] if o.is_dma][-NDMA_SEMS:]
            rec.extend(dm)
        for e in ENGS:
            op = Op(e, None, False)
            op.deps = list(last) + list(rec)
            self.ops[e].append(op)

    def flush(self):
        self.barrier()
        for e in ENGS:
            for op in self.ops[e]:
                for d in op.deps:
                    d.sig = True
        for e in ENGS:
            for op in self.ops[e]:
                if op.is_dma:
                    n = self.ndma[e]
                    op.dsem = (e, n % NDMA_SEMS)
                    op.dval = 16 * (n // NDMA_SEMS + 1)
                    self.dma_hist[e].append(op)
                    self.ndma[e] += 1
                elif op.sig:
                    self.cnt[e] += 1
                    op.count = self.cnt[e]
        eidx = {e: i for i, e in enumerate(ENGS)}
        for e in ENGS:
            known = self.known[e]
            kdma = self.kdma[e]
            hist = self.dma_hist[e]
            nd = len(hist) - sum(1 for o in self.ops[e] if o.is_dma)
            for op in self.ops[e]:
                deps = list(op.deps)
                if op.is_dma:
                    if nd >= NDMA_SEMS:
                        deps.append(hist[nd - NDMA_SEMS])
                    nd += 1
                best = {}
                for d in deps:
                    if d.is_dma:
                        if kdma.get(d.dsem, 0) >= d.dval:
                            continue
                        kdma[d.dsem] = d.dval
                        best[("dma", d.dsem)] = d.dval
                    else:
                        if d.eng == "pe" and e == "pe":
                            continue
                        j = eidx[d.eng]
                        if known[j] >= d.count:
                            continue
                        known[j] = d.count
                        best[("eng", d.eng)] = max(best.get(("eng", d.eng), 0), d.count)
                        if d.snap is not None:
                            for k in range(len(ENGS)):
                                if d.snap[k] > known[k]:
                                    known[k] = d.snap[k]
                op.waits = [(k[0], k[1], v) for k, v in best.items()]
                op.snap = tuple(known)
        nc = self.nc
        ops = self.ops
        esem, dsem = self.esem, self.dsem

        def run(eng_name):
            lst = ops[eng_name]

            def body(eng):
                for op in lst:
                    for kind, key, val in op.waits:
                        eng.wait_ge(dsem[key] if kind == "dma" else esem[key], val)
                    if op.fn is None:
                        continue
                    ins = op.fn(eng)
                    if op.is_dma:
                        ins.then_inc(dsem[op.dsem], 16)
                    elif op.sig:
                        ins.then_inc(esem[eng_name], 1)
            return body

        with nc.Block() as block:
            block.tensor(run("pe"))
            block.scalar(run("act"))
            block.vector(run("dve"))
            block.gpsimd(run("pool"))
            block.sync(run("sp"))
        for e in ENGS:
            self.ninst += len(self.ops[e])
            for op in self.ops[e]:
                op.done = True
                op.fn = None
                op.deps = []
            self.ops[e] = []


C_ONES = 0
C_TRI = 128
C_SU = 256
C_TRI4 = 384
C_INVC = 896
NCST = 960


def make_cst():
    c = np.zeros((128, NCST), np.float32)
    c[:, C_ONES:C_ONES + 128] = 1.0
    s = np.arange(128)
    tri = (s[:, None] <= s[None, :]).astype(np.float32)
    c[:, C_TRI:C_TRI + 128] = tri
    c[:, C_SU:C_SU + 128] = (s[:, None] > s[None, :]).astype(np.float32)
    for h in range(4):
        c[:, C_TRI4 + h * 128:C_TRI4 + (h + 1) * 128] = tri
    for g, w in enumerate((2, 4, 8, 16)):
        t = np.arange(16)
        c[:, C_INVC + g * 16:C_INVC + (g + 1) * 16] = 1.0 / np.minimum(t + 1, w)
    return c


V_NMIX = 0
V_NFFN = 32
V_NFIN = 64
V_PSCALE = 72
V_SUBLN = 80
V_GNORM = 82
NVEC = 98


def make_vecs(inp):
    v = np.zeros((128, NVEC), np.float32)

    def put(col, arr):
        a = np.asarray(arr, np.float32).reshape(-1, 128).T
        v[:, col:col + a.shape[1]] = a

    for i in range(DEPTH):
        put(V_NMIX + 8 * i, inp["norm_mix"][i])
        put(V_NFFN + 8 * i, inp["norm_ffn"][i])
    put(V_NFIN, inp["norm_final"])
    for e in range(2):
        put(V_PSCALE + 4 * e, inp["pool_scale"][e])
        put(V_SUBLN + e, inp["ab_subln"][e])
        put(V_GNORM + 8 * e, inp["gla_norm"][e].reshape(-1))
    return v


class Builder:
    def __init__(self, T, layers=(0, 1, 2, 3), final=True, dbg=False):
        self.T = T
        self.layers = tuple(layers)
        self.final = final
        nc = self.nc = bass.Bass("TRN2", target_bir_lowering=False)
        dt = nc.dram_tensor
        self.x_in = dt("xT", [D, T], F32, kind="ExternalInput").ap()
        self.y_out = dt("yT", [D, T], F32, kind="ExternalOutput").ap()
        self.cst_d = dt("cst", [128, NCST], F32, kind="ExternalInput").ap()
        self.vecs_d = dt("vecs", [128, NVEC], F32, kind="ExternalInput").ap()
        self.w = {}
        for name, shape in (("ab_w_in", [2, D, 2048]), ("ab_lambda", [2, 256]), ("pool_w", [2, 4, 128, 128]),
                            ("ab_w_out", [2, D, D]), ("gla_w_in", [2, D, 3088]), ("gla_w_gk_up", [2, 16, 512]),
                            ("gla_b_gk", [2, 1, 512]), ("gla_w_out", [2, D, D]), ("ffn_w1", [4, D, DFF]),
                            ("ffn_w2", [4, DFF, D])):
            self.w[name] = dt(name, shape, F32, kind="ExternalInput").ap()
        kw = {"kind": "ExternalOutput"} if dbg else {}
        self.xa = dt("xa", [D, T], F32, **kw).ap()
        self.xb = dt("xb", [D, T], F32, **kw).ap()
        self.qT = dt("qTs", [4, 128, T], BF16, **kw).ap()
        self.kT = dt("kTs", [4, 128, T], BF16, **kw).ap()
        self.Vs = dt("Vs", [4, 128, T // 128, 128], BF16, **kw).ap()
        self.mT = dt("mTs", [4, 128, T], BF16, **kw).ap()
        self.r_x = {}
        self.nsb = 0

    def sb(self, st, shape, dtp):
        self.nsb += 1
        return st.enter_context(self.nc.sbuf_tensor(f"sb{self.nsb}", shape, dtp))

    def psum(self, st):
        self.nsb += 1
        return [st.enter_context(self.nc.psum_tensor(f"ps{self.nsb}_{i}", [128, 512], F32)) for i in range(8)]

    def load_consts(self, S, st):
        cst = self.sb(st, [128, NCST], F32)
        vecs = self.sb(st, [128, NVEC], F32)
        r = S.res()
        S.dma("sp", cst[:], self.cst_d, writes=[r])
        S.dma("sp", vecs[:], self.vecs_d, writes=[r])
        return cst, vecs, r

    def load_weight(self, S, st_tmp, dst, src_rows, ncols, rres, wres, scale_cols=None, cw=1024):
        stg = [self.sb(st_tmp, [128, cw], F32) for _ in range(3)]
        r_stg = [S.res() for _ in range(3)]
        engs = ["pool", "dve", "act"]
        ci = 0
        for r, src in enumerate(src_rows):
            for c0 in range(0, ncols, cw):
                c1 = min(ncols, c0 + cw)
                b = ci % 3
                S.dma("sp", stg[b][:, :c1 - c0], src[:, c0:c1], writes=[r_stg[b]])
                out = dst[:, r, c0:c1]
                in_ = stg[b][:, :c1 - c0]
                eng = engs[ci % 3]
                sc = None if scale_cols is None else scale_cols[r]
                if sc is None:
                    if eng == "act":
                        S.add("act", (lambda o, i: lambda e: e.copy(o, i))(out, in_), reads=[r_stg[b]] + rres, writes=[wres])
                    else:
                        S.add(eng, (lambda o, i: lambda e: e.tensor_copy(o, i))(out, in_), reads=[r_stg[b]] + rres, writes=[wres])
                else:
                    if eng == "act":
                        S.add("act", (lambda o, i, s: lambda e: e.activation(o, i, AF.Copy, scale=s))(out, in_, sc),
                              reads=[r_stg[b]] + rres, writes=[wres])
                    else:
                        S.add(eng, (lambda o, i, s: lambda e: e.tensor_scalar(o, i, s, None, ALU.mult))(out, in_, sc),
                              reads=[r_stg[b]] + rres, writes=[wres])
                ci += 1

    def rmsnorm_tile(self, S, xt_c, r_xt, NT, sq, r_sq, ssum, r_ssum, rstd, r_rstd, psb, r_psb, ones, r_c, h, r_h, nfeat_inv=1.0 / D):
        S.add("act", lambda e: e.activation(sq[:], xt_c[:], AF.Square), reads=[r_xt], writes=[r_sq])
        S.add("dve", lambda e: e.tensor_reduce(ssum[:], sq[:].rearrange("p c t -> p t c"), AX.X, ALU.add),
              reads=[r_sq], writes=[r_ssum])
        S.add("pe", lambda e: e.matmul(psb[:, :NT], ones, ssum[:], start=True, stop=True),
              reads=[r_c, r_ssum], writes=[r_psb])
        S.add("act", lambda e: e.activation(ssum[:], psb[:, :NT], AF.Ln, bias=float(EPS), scale=nfeat_inv),
              reads=[r_psb], writes=[r_ssum])
        S.add("act", lambda e: e.activation(rstd[:], ssum[:], AF.Exp, scale=-0.5), reads=[r_ssum], writes=[r_rstd])
        for c in range(8):
            eng = "dve" if c % 2 == 0 else "pool"
            S.add(eng, (lambda c: lambda e: e.tensor_tensor(h[:, c, :], xt_c[:, c, :], rstd[:], ALU.mult))(c),
                  reads=[r_xt, r_rstd], writes=[r_h[c]])

    def ffn_sweep(self, S, li, xin, r_xin, xout, r_xout, final):
        T = self.T
        FNT = 256
        nc = self.nc
        with ExitStack() as st:
            cst, vecs, r_c = self.load_consts(S, st)
            ones = cst[:, C_ONES:C_ONES + 128]
            w1sb = self.sb(st, [128, 8, DFF], BF16)
            w2sb = self.sb(st, [128, 32, D], BF16)
            r_w1, r_w2 = S.res(), S.res()
            with ExitStack() as st2:
                w1v = self.w["ffn_w1"][li].rearrange("(kc p) n -> p kc n", p=128)
                w2v = self.w["ffn_w2"][li].rearrange("(hc p) n -> p hc n", p=128)
                self.load_weight(S, st2, w1sb, [w1v[:, kc, :] for kc in range(8)], DFF, [r_c], r_w1,
                                 scale_cols=[vecs[:, V_NFFN + 8 * li + kc:V_NFFN + 8 * li + kc + 1] for kc in range(8)])
                self.load_weight(S, st2, w2sb, [w2v[:, hc, :] for hc in range(32)], D, [r_c], r_w2)
                S.flush()
            xt = [self.sb(st, [128, 8, FNT], F32) for _ in range(2)]
            xo = self.sb(st, [128, 8, FNT], F32)
            sq = self.sb(st, [128, 8, FNT], F32)
            ssum = self.sb(st, [128, FNT], F32)
            rstd = self.sb(st, [128, FNT], F32)
            h = self.sb(st, [128, 8, FNT], BF16)
            hid = self.sb(st, [128, 32, FNT], BF16)
            rl = [self.sb(st, [128, FNT], F32) for _ in range(4)]
            pb = self.psum(st)
            R = S.res
            r_xt = [R(), R()]; r_xo = R(); r_sq = R(); r_ssum = R(); r_rstd = R()
            r_h = [R() for _ in range(8)]; r_hid = [R() for _ in range(32)]
            r_pb = [R() for _ in range(8)]; r_rl = [R() for _ in range(4)]
            xv = xin.rearrange("(c p) t -> p c t", p=128)
            yv = xout.rearrange("(c p) t -> p c t", p=128)
            ntile = T // FNT

            def load(n):
                S.dma("sp", xt[n % 2][:], xv[:, :, n * FNT:(n + 1) * FNT], reads=[r_xin], writes=[r_xt[n % 2]])

            def norm(n):
                b = n % 2
                self.rmsnorm_tile(S, xt[b], r_xt[b], FNT, sq, r_sq, ssum, r_ssum, rstd, r_rstd, pb[7], r_pb[7],
                                  ones, r_c, h, r_h)

            def up(n):
                for j in range(32):
                    bk = j % 4
                    for kc in range(8):
                        S.add("pe", (lambda j, kc, bk: lambda e: e.matmul(
                            pb[bk][:, :FNT], w1sb[:, kc, j * 128:(j + 1) * 128], h[:, kc, :],
                            start=(kc == 0), stop=(kc == 7)))(j, kc, bk), reads=[r_w1, r_h[kc]], writes=[r_pb[bk]])
                    S.add("act", (lambda j, bk: lambda e: e.activation(rl[j % 4][:], pb[bk][:, :FNT], AF.Relu))(j, bk),
                          reads=[r_pb[bk]], writes=[r_rl[j % 4]])
                    S.add("pool", (lambda j: lambda e: e.tensor_tensor(hid[:, j, :], rl[j % 4][:], rl[j % 4][:], ALU.mult))(j),
                          reads=[r_rl[j % 4]], writes=[r_hid[j]])

            def down(n):
                b = n % 2
                for oc in range(8):
                    bk = 4 + oc % 2
                    for hc in range(32):
                        S.add("pe", (lambda oc, hc, bk: lambda e: e.matmul(
                            pb[bk][:, :FNT], w2sb[:, hc, oc * 128:(oc + 1) * 128], hid[:, hc, :],
                            start=(hc == 0), stop=(hc == 31)))(oc, hc, bk), reads=[r_w2, r_hid[hc]], writes=[r_pb[bk]])
                    S.add("dve", (lambda oc, bk: lambda e: e.tensor_tensor(
                        xo[:, oc, :], pb[bk][:, :FNT], xt[b][:, oc, :], ALU.add))(oc, bk),
                        reads=[r_pb[bk], r_xt[b]], writes=[r_xo])
                if final:
                    S.add("act", lambda e: e.activation(sq[:], xo[:], AF.Square), reads=[r_xo], writes=[r_sq])
                    S.add("dve", lambda e: e.tensor_reduce(ssum[:], sq[:].rearrange("p c t -> p t c"), AX.X, ALU.add),
                          reads=[r_sq], writes=[r_ssum])
                    S.add("pe", lambda e: e.matmul(pb[6][:, :FNT], ones, ssum[:], start=True, stop=True),
                          reads=[r_c, r_ssum], writes=[r_pb[6]])
                    S.add("act", lambda e: e.activation(ssum[:], pb[6][:, :FNT], AF.Ln, bias=float(EPS), scale=1.0 / D),
                          reads=[r_pb[6]], writes=[r_ssum])
                    S.add("act", lambda e: e.activation(rstd[:], ssum[:], AF.Exp, scale=-0.5), reads=[r_ssum], writes=[r_rstd])
                    for c in range(8):
                        S.add("dve", (lambda c: lambda e: e.scalar_tensor_tensor(
                            sq[:, c, :], xo[:, c, :], vecs[:, V_NFIN + c:V_NFIN + c + 1], rstd[:], ALU.mult, ALU.mult))(c),
                            reads=[r_xo, r_rstd, r_c], writes=[r_sq])
                    S.dma("pool", yv[:, :, n * FNT:(n + 1) * FNT], sq[:], reads=[r_sq], writes=[r_xout])
                else:
                    S.dma("pool", yv[:, :, n * FNT:(n + 1) * FNT], xo[:], reads=[r_xo], writes=[r_xout])

            load(0)
            if ntile > 1:
                load(1)
            norm(0)
            for n in range(ntile):
                up(n)
                if n + 1 < ntile and not final:
                    norm(n + 1)
                down(n)
                if n + 1 < ntile and final:
                    norm(n + 1)
                if n + 2 < ntile:
                    load(n + 2)
            S.flush()

    def a1_sweep(self, S, li, xin, r_xin):
        T = self.T
        NT = 512
        e_ = li // 2
        with ExitStack() as st:
            cst, vecs, r_c = self.load_consts(S, st)
            ones = cst[:, C_ONES:C_ONES + 128]
            win = self.sb(st, [128, 8, 2048], BF16)
            pw = self.sb(st, [128, 4, 128], BF16)
            r_win, r_pw = S.res(), S.res()
            with ExitStack() as st2:
                wv = self.w["ab_w_in"][e_].rearrange("(kc p) n -> p kc n", p=128)
                self.load_weight(S, st2, win, [wv[:, kc, :] for kc in range(8)], 2048, [r_c], r_win,
                                 scale_cols=[vecs[:, V_NMIX + 8 * li + kc:V_NMIX + 8 * li + kc + 1] for kc in range(8)])
                pwv = self.w["pool_w"][e_].rearrange("g c d -> c g d")
                self.load_weight(S, st2, pw, [pwv[:, g, :] for g in range(4)], 128, [r_c], r_pw, cw=128)
                S.flush()
            xt = [self.sb(st, [128, 8, NT], F32) for _ in range(2)]
            sq = self.sb(st, [128, 8, NT], F32)
            ssum = self.sb(st, [128, NT], F32)
            rstd = self.sb(st, [128, NT], F32)
            h = self.sb(st, [128, 8, NT], BF16)
            qk = [self.sb(st, [128, 8, NT], BF16) for _ in range(2)]
            vt = [self.sb(st, [128, 4, 512], BF16) for _ in range(2)]
            uext = [self.sb(st, [128, 4, 16 + NT], F32) for _ in range(2)]
            ta = self.sb(st, [128, 16 + NT], F32)
            tb = self.sb(st, [128, 16 + NT], F32)
            rr = self.sb(st, [128, 4, NT], BF16)
            mo = [self.sb(st, [128, 4, NT], BF16) for _ in range(2)]
            pb = self.psum(st)
            R = S.res
            r_xt = [R(), R()]; r_sq = R(); r_ssum = R(); r_rstd = R()
            r_h = [R() for _ in range(8)]; r_pb = [R() for _ in range(8)]
            r_qk = [R(), R()]; r_vt = [R(), R()]; r_ue = [[R() for _ in range(4)] for _ in range(2)]
            r_ta, r_tb = R(), R(); r_rr = [R() for _ in range(4)]; r_mo = [R(), R()]
            r_sc = self.r_scr
            xv = xin.rearrange("(c p) t -> p c t", p=128)
            ntile = T // NT
            qTv = self.qT.rearrange("h p t -> p h t")
            kTv = self.kT.rearrange("h p t -> p h t")
            mTv = self.mT.rearrange("g p t -> p g t")
            Vv = self.Vs.rearrange("h p k v -> p h k v")

            def load(n):
                S.dma("sp", xt[n % 2][:], xv[:, :, n * NT:(n + 1) * NT], reads=[r_xin], writes=[r_xt[n % 2]])

            def norm(n):
                b = n % 2
                self.rmsnorm_tile(S, xt[b], r_xt[b], NT, sq, r_sq, ssum, r_ssum, rstd, r_rstd, pb[7], r_pb[7],
                                  ones, r_c, h, r_h)

            for g in range(4):
                S.add("pool", (lambda g: lambda e: e.memset(uext[0][:, g, 0:16], 0.0))(g), writes=[r_ue[0][g]])
            S.add("pool", lambda e: e.memset(ta[:], 0.0), writes=[r_ta])
            S.add("pool", lambda e: e.memset(tb[:], 0.0), writes=[r_tb])

            def proj(n):
                b = n % 2
                t0 = n * NT
                for oc in range(8):
                    bk = oc % 3
                    for kc in range(8):
                        S.add("pe", (lambda oc, kc, bk: lambda e: e.matmul(
                            pb[bk][:, :NT], win[:, kc, oc * 128:(oc + 1) * 128], h[:, kc, :],
                            start=(kc == 0), stop=(kc == 7)))(oc, kc, bk), reads=[r_win, r_h[kc]], writes=[r_pb[bk]])
                    S.add("act", (lambda oc, bk: lambda e: e.copy(qk[b][:, oc, :], pb[bk][:, :NT]))(oc, bk),
                          reads=[r_pb[bk]], writes=[r_qk[b]])
                S.dma("pool", qTv[:, :, t0:t0 + NT], qk[b][:, 0:4, :], reads=[r_qk[b]], writes=[r_sc])
                S.dma("pool", kTv[:, :, t0:t0 + NT], qk[b][:, 4:8, :], reads=[r_qk[b]], writes=[r_sc])
                for s in range(4):
                    bk = 3 + s % 2
                    for kc in range(8):
                        S.add("pe", (lambda s, kc, bk: lambda e: e.matmul(
                            pb[bk][:, :512], h[:, kc, s * 128:(s + 1) * 128], win[:, kc, 1024:1536],
                            start=(kc == 0), stop=(kc == 7)))(s, kc, bk), reads=[r_win, r_h[kc]], writes=[r_pb[bk]])
                    S.add("dve", (lambda s, bk: lambda e: e.tensor_copy(vt[b][:, s, :], pb[bk][:, :512]))(s, bk),
                          reads=[r_pb[bk]], writes=[r_vt[b]])
                for hh in range(4):
                    S.dma("pool", self.Vs[hh, :, 4 * n:4 * n + 4, :], vt[b][:, :, hh * 128:(hh + 1) * 128],
                          reads=[r_vt[b]], writes=[r_sc])
                for g in range(4):
                    bk = 5 + g % 2
                    oc = 12 + g
                    for kc in range(8):
                        S.add("pe", (lambda oc, kc, bk: lambda e: e.matmul(
                            pb[bk][:, :NT], win[:, kc, oc * 128:(oc + 1) * 128], h[:, kc, :],
                            start=(kc == 0), stop=(kc == 7)))(oc, kc, bk), reads=[r_win, r_h[kc]], writes=[r_pb[bk]])
                    S.add("act", (lambda g, bk: lambda e: e.copy(uext[b][:, g, 16:16 + NT], pb[bk][:, :NT]))(g, bk),
                          reads=[r_pb[bk]], writes=[r_ue[b][g]])

            def pool(n):
                b = n % 2
                t0 = n * NT
                W = 16 + NT
                for g in range(4):
                    w = 2 ** (g + 1)
                    src = uext[b][:, g, :]
                    r_src = r_ue[b][g]
                    bufs = [(ta, r_ta), (tb, r_tb)]
                    sh = 1
                    k = 0
                    cur, r_cur = src, r_src
                    while sh < w:
                        dst, r_dst = bufs[k % 2]
                        S.add("pool", (lambda dst, cur, sh: lambda e: e.tensor_tensor(
                            dst[:, sh:W], cur[:, sh:W], cur[:, 0:W - sh], ALU.add))(dst, cur, sh),
                            reads=[r_cur], writes=[r_dst])
                        cur, r_cur = dst[:], r_dst
                        sh *= 2
                        k += 1
                    S.add("dve", (lambda g, cur, w: lambda e: e.scalar_tensor_tensor(
                        rr[:, g, :], cur[:, 16:W], 1.0 / w, uext[b][:, g, 16:W], ALU.mult, ALU.subtract))(g, cur, w),
                        reads=[r_cur, r_ue[b][g]], writes=[r_rr[g]])
                    if n == 0:
                        S.add("dve", (lambda g, cur: lambda e: e.tensor_tensor(
                            ta[:, 0:16], cur[:, 16:32], cst[:, C_INVC + g * 16:C_INVC + (g + 1) * 16], ALU.mult))(g, cur),
                            reads=[r_cur, r_c], writes=[r_ta])
                        S.add("dve", (lambda g: lambda e: e.tensor_tensor(
                            rr[:, g, 0:16], ta[:, 0:16], uext[b][:, g, 16:32], ALU.subtract))(g),
                            reads=[r_ta, r_ue[b][g]], writes=[r_rr[g]])
                    if n + 1 < ntile:
                        S.add("pool", (lambda g: lambda e: e.tensor_copy(uext[1 - b][:, g, 0:16], uext[b][:, g, NT:NT + 16]))(g),
                              reads=[r_ue[b][g]], writes=[r_ue[1 - b][g]])

            def pool_mm(n):
                b = n % 2
                t0 = n * NT
                for g in range(4):
                    bk = 5 + g % 2
                    S.add("pe", (lambda g, bk: lambda e: e.matmul(pb[bk][:, :NT], pw[:, g, :], rr[:, g, :], start=True, stop=True))(g, bk),
                          reads=[r_pw, r_rr[g]], writes=[r_pb[bk]])
                    S.add("act", (lambda g, bk: lambda e: e.activation(
                        mo[b][:, g, :], pb[bk][:, :NT], AF.Copy, scale=vecs[:, V_PSCALE + 4 * e_ + g:V_PSCALE + 4 * e_ + g + 1]))(g, bk),
                        reads=[r_pb[bk], r_c], writes=[r_mo[b]])
                S.dma("pool", mTv[:, :, t0:t0 + NT], mo[b][:], reads=[r_mo[b]], writes=[r_sc])

            load(0)
            if ntile > 1:
                load(1)
            norm(0)
            for n in range(ntile):
                proj(n)
                if n > 0:
                    pool_mm(n - 1)
                if n + 1 < ntile:
                    norm(n + 1)
                pool(n)
                if n + 2 < ntile:
                    load(n + 2)
            pool_mm(ntile - 1)
            S.flush()

    def a2_sweep(self, S, li, xin, r_xin, xout, r_xout):
        T = self.T
        NT = 512
        e_ = li // 2
        lam_init = 0.8 - 0.6 * math.exp(-0.3 * li)
        KG = 16
        with ExitStack() as st:
            cst, vecs, r_c = self.load_consts(S, st)
            ones = cst[:, C_ONES:C_ONES + 128]
            wout = self.sb(st, [128, 8, D], BF16)
            r_wout = S.res()
            onesb = self.sb(st, [128, 128], BF16)
            lamt = self.sb(st, [128, 256], F32)
            lprod = self.sb(st, [128, 128], F32)
            lsum = self.sb(st, [128, 2], F32)
            neglam = self.sb(st, [128, 1], F32)
            gsub = self.sb(st, [128, 1], F32)
            r_l = S.res()
            ediag = [[self.sb(st, [128, NT], BF16) for _ in range(2)] for _ in range(4)]
            r_ed = [[S.res() for _ in range(2)] for _ in range(4)]
            with ExitStack() as st2:
                wv = self.w["ab_w_out"][e_].rearrange("(kc p) n -> p kc n", p=128)
                self.load_weight(S, st2, wout, [wv[:, kc, :] for kc in range(8)], D, [r_c], r_wout)
                S.add("dve", lambda e: e.tensor_copy(onesb[:], ones), reads=[r_c], writes=[r_l])
                S.dma("sp", lamt[:], self.w["ab_lambda"][e_:e_ + 1, :].partition_broadcast(128), writes=[r_l])
                S.add("dve", lambda e: e.tensor_tensor(lprod[:, 0:64], lamt[:, 0:64], lamt[:, 64:128], ALU.mult), reads=[r_l], writes=[r_l])
                S.add("dve", lambda e: e.tensor_tensor(lprod[:, 64:128], lamt[:, 128:192], lamt[:, 192:256], ALU.mult), reads=[r_l], writes=[r_l])
                S.add("dve", lambda e: e.tensor_reduce(lsum[:], lprod[:].rearrange("p (a d) -> p a d", a=2), AX.X, ALU.add), reads=[r_l], writes=[r_l])
                S.add("act", lambda e: e.activation(lsum[:], lsum[:], AF.Exp), reads=[r_l], writes=[r_l])
                S.add("dve", lambda e: e.tensor_tensor(neglam[:], lsum[:, 1:2], lsum[:, 0:1], ALU.subtract), reads=[r_l], writes=[r_l])
                S.add("dve", lambda e: e.tensor_scalar(neglam[:], neglam[:], -float(lam_init), None, ALU.add), reads=[r_l], writes=[r_l])
                S.add("dve", lambda e: e.tensor_scalar(gsub[:], vecs[:, V_SUBLN + e_:V_SUBLN + e_ + 1], float(1.0 - lam_init), None, ALU.mult),
                      reads=[r_c], writes=[r_l])
                for j in range(4):
                    for a in range(2):
                        S.add("pool", (lambda j, a: lambda e: e.memset(ediag[j][a][:], 0.0))(j, a), writes=[r_ed[j][a]])
                S.flush()
            xt = [self.sb(st, [128, 8, NT], F32) for _ in range(2)]
            xo = self.sb(st, [128, 8, NT], F32)
            qt = [self.sb(st, [128, 4, NT], BF16) for _ in range(2)]
            mo = [self.sb(st, [128, 4, NT], BF16) for _ in range(2)]
            ao = self.sb(st, [128, 4, NT], BF16)
            kb = [self.sb(st, [128, KG * 128], BF16) for _ in range(3)]
            vb = [self.sb(st, [128, KG, 128], BF16) for _ in range(3)]
            eb = [[self.sb(st, [128, NT], BF16) for _ in range(2)] for _ in range(3)]
            rl1 = self.sb(st, [128, NT], F32); rl2 = self.sb(st, [128, NT], F32)
            t1 = self.sb(st, [128, NT], F32); t2 = self.sb(st, [128, NT], F32)
            A = self.sb(st, [128, NT], F32); asq = self.sb(st, [128, NT], F32)
            sd = self.sb(st, [128, NT], F32); rs = self.sb(st, [128, NT], F32)
            acc2 = self.sb(st, [128, NT], F32)
            r_acc2 = S.res()
            pb = self.psum(st)
            R = S.res
            r_xt = [R(), R()]; r_xo = R(); r_qt = [R(), R()]; r_mo = [R(), R()]; r_ao = [R() for _ in range(4)]
            r_kb = [R() for _ in range(3)]; r_vb = [R() for _ in range(3)]
            r_eb = [[R(), R()] for _ in range(3)]
            r_ep = R()
            r_pb = [R() for _ in range(8)]
            r_sc = self.r_scr
            xv = xin.rearrange("(c p) t -> p c t", p=128)
            yv = xout.rearrange("(c p) t -> p c t", p=128)
            qTv = self.qT.rearrange("h p t -> p h t")
            mTv = self.mT.rearrange("g p t -> p g t")
            nblk = T // NT
            PS_S = [(0, 1), (2, 3)]
            PO1, PO2, PL1, PL2 = 4, 5, 6, 7
            grp_ctr = [0]

            def load(n):
                t0 = n * NT
                b = n % 2
                S.dma("sp", xt[b][:], xv[:, :, t0:t0 + NT], reads=[r_xin], writes=[r_xt[b]])
                S.dma("sp", qt[b][:], qTv[:, :, t0:t0 + NT], reads=[r_sc], writes=[r_qt[b]])
                S.dma("sp", mo[b][:], mTv[:, :, t0:t0 + NT], reads=[r_sc], writes=[r_mo[b]])

            def load_kv(h, g0, ng):
                i = grp_ctr[0] % 3
                grp_ctr[0] += 1
                S.dma("sp", kb[i][:, :ng * 128], self.kT[h, :, g0 * 128:(g0 + ng) * 128], reads=[r_sc], writes=[r_kb[i]])
                S.dma("sp", vb[i][:, :ng, :], self.Vs[h, :, g0:g0 + ng, :], reads=[r_sc], writes=[r_vb[i]])
                return i

            def attn_head(qb, h, pending):
                b = qb % 2
                nk = 4 * (qb + 1)
                groups = []
                for g0 in range(0, nk, KG):
                    groups.append((g0, min(KG, nk - g0)))
                gbuf = {}
                for gi in range(min(2, len(groups))):
                    gbuf[gi] = load_kv(h, *groups[gi])
                ectr = [0]
                pend = []

                def qk(kt):
                    gi, lk = kt // KG, kt % KG
                    if gi not in gbuf:
                        gbuf[gi] = load_kv(h, *groups[gi])
                    i = gbuf[gi]
                    j = kt - 4 * qb
                    c0 = 128 * j if j >= 0 else 0
                    sa, sb_ = PS_S[kt % 2]
                    S.add("pe", lambda e: e.matmul(pb[sa][:, c0:NT], kb[i][0:64, lk * 128:(lk + 1) * 128], qt[b][0:64, h, c0:NT],
                                                   start=True, stop=True), reads=[r_kb[i], r_qt[b]], writes=[r_pb[sa]])
                    S.add("pe", lambda e: e.matmul(pb[sb_][:, c0:NT], kb[i][64:128, lk * 128:(lk + 1) * 128], qt[b][64:128, h, c0:NT],
                                                   start=True, stop=True), reads=[r_kb[i], r_qt[b]], writes=[r_pb[sb_]])
                    if j < 0:
                        k3 = ectr[0] % 3
                        ectr[0] += 1
                        e1, e2 = eb[k3][0], eb[k3][1]
                        re1, re2 = r_eb[k3][0], r_eb[k3][1]
                        S.add("act", lambda e: e.activation(e1[:, :], pb[sa][:, :NT], AF.Exp, scale=0.125), reads=[r_pb[sa]], writes=[re1])
                        S.add("act", lambda e: e.activation(e2[:, :], pb[sb_][:, :NT], AF.Exp, scale=0.125), reads=[r_pb[sb_]], writes=[re2])
                    else:
                        e1, e2 = ediag[j][0], ediag[j][1]
                        re1, re2 = r_ed[j][0], r_ed[j][1]
                        for (et, re_, pbn) in ((e1, re1, sa), (e2, re2, sb_)):
                            S.add("act", (lambda et, pbn: lambda e: e.activation(et[0:64, c0:NT], pb[pbn][0:64, c0:NT], AF.Exp, scale=0.125))(et, pbn),
                                  reads=[r_pb[pbn]], writes=[re_])
                            S.add("act", (lambda et, pbn: lambda e: e.activation(et[64:128, c0 + 64:NT], pb[pbn][64:128, c0 + 64:NT], AF.Exp, scale=0.125))(et, pbn),
                                  reads=[r_pb[pbn]], writes=[re_])
                    return (kt, c0, e1, e2, re1, re2, i, lk)

                def pv(item):
                    kt, c0, e1, e2, re1, re2, i, lk = item
                    first, last = (kt == 0), (kt == nk - 1)
                    S.add("pe", lambda e: e.matmul(pb[PO1][:, c0:NT], vb[i][:, lk, :], e1[:, c0:NT], start=first, stop=last, skip_group_check=True),
                          reads=[r_vb[i], re1], writes=[r_pb[PO1]])
                    S.add("pe", lambda e: e.matmul(pb[PO2][:, c0:NT], vb[i][:, lk, :], e2[:, c0:NT], start=first, stop=last, skip_group_check=True),
                          reads=[r_vb[i], re2], writes=[r_pb[PO2]])
                    S.add("pe", lambda e: e.matmul(pb[PL1][:, c0:NT], onesb[:], e1[:, c0:NT], start=first, stop=last, skip_group_check=True),
                          reads=[r_l, re1], writes=[r_pb[PL1]])
                    if first:
                        S.add("dve", lambda e: e.tensor_copy(acc2[:, c0:NT], e2[:, c0:NT]), reads=[re2], writes=[r_acc2])
                    else:
                        S.add("dve", lambda e: e.tensor_tensor(acc2[:, c0:NT], acc2[:, c0:NT], e2[:, c0:NT], ALU.add),
                              reads=[re2, r_acc2], writes=[r_acc2])

                prev = qk(0)
                for kt in range(1, nk):
                    cur = qk(kt)
                    if kt == 1 and pending is not None:
                        pending[0]()
                    pv(prev)
                    if kt == 3 and pending is not None:
                        pending[1]()
                    prev = cur
                pv(prev)
                return (lambda: ep_a(qb, h, nk), lambda: ep_b(qb, h, nk))

            def ep_a(qb, h, nk):
                S.add("pe", lambda e: e.matmul(pb[PL2][:, :NT], ones, acc2[:], start=True, stop=True),
                      reads=[r_c, r_acc2], writes=[r_pb[PL2]])
                S.add("act", lambda e: e.activation(rl1[:], pb[PL1][:, :NT], AF.Ln), reads=[r_pb[PL1]], writes=[r_ep])
                S.add("act", lambda e: e.activation(rl1[:], rl1[:], AF.Exp, scale=-1.0), reads=[r_ep], writes=[r_ep])
                S.add("act", lambda e: e.activation(rl2[:], pb[PL2][:, :NT], AF.Ln), reads=[r_pb[PL2]], writes=[r_ep])
                S.add("act", lambda e: e.activation(rl2[:], rl2[:], AF.Exp, scale=-1.0), reads=[r_ep], writes=[r_ep])
                S.add("dve", lambda e: e.tensor_tensor(t1[:], pb[PO1][:, :NT], rl1[:], ALU.mult), reads=[r_pb[PO1], r_ep], writes=[r_ep])
                S.add("dve", lambda e: e.tensor_tensor(t2[:], pb[PO2][:, :NT], rl2[:], ALU.mult), reads=[r_pb[PO2], r_ep], writes=[r_ep])
                S.add("dve", lambda e: e.scalar_tensor_tensor(A[:], t2[:], neglam[:, 0:1], t1[:], ALU.mult, ALU.add), reads=[r_ep, r_l], writes=[r_ep])
                S.add("pool", lambda e: e.tensor_tensor(asq[:], A[:], A[:], ALU.mult), reads=[r_ep], writes=[r_ep])

            def ep_b(qb, h, nk):
                sa = PS_S[0][0]
                S.add("pe", lambda e: e.matmul(pb[sa][:, :NT], ones, asq[:], start=True, stop=True), reads=[r_c, r_ep], writes=[r_pb[sa]])
                S.add("act", lambda e: e.activation(sd[:], pb[sa][:, :NT], AF.Ln, bias=float(EPS), scale=1.0 / 128), reads=[r_pb[sa]], writes=[r_ep])
                S.add("act", lambda e: e.activation(rs[:], sd[:], AF.Exp, scale=-0.5), reads=[r_ep], writes=[r_ep])
                S.add("dve", lambda e: e.scalar_tensor_tensor(ao[:, h, :], A[:], gsub[:, 0:1], rs[:], ALU.mult, ALU.mult),
                      reads=[r_ep, r_l], writes=[r_ao[h]])
                if h == 3:
                    outproj(qb)

            def outproj(qb):
                b = qb % 2
                t0 = qb * NT
                for oc in range(8):
                    bk = PS_S[oc % 2][1]
                    for c in range(8):
                        if c < 4:
                            S.add("pe", (lambda oc, c, bk: lambda e: e.matmul(pb[bk][:, :NT], wout[:, c, oc * 128:(oc + 1) * 128], ao[:, c, :],
                                                                               start=(c == 0), stop=False))(oc, c, bk),
                                  reads=[r_wout, r_ao[c]], writes=[r_pb[bk]])
                        else:
                            S.add("pe", (lambda oc, c, bk: lambda e: e.matmul(pb[bk][:, :NT], wout[:, c, oc * 128:(oc + 1) * 128], mo[b][:, c - 4, :],
                                                                               start=False, stop=(c == 7)))(oc, c, bk),
                                  reads=[r_wout, r_mo[b]], writes=[r_pb[bk]])
                    S.add("dve", (lambda oc, bk: lambda e: e.tensor_tensor(xo[:, oc, :], pb[bk][:, :NT], xt[b][:, oc, :], ALU.add))(oc, bk),
                          reads=[r_pb[bk], r_xt[b]], writes=[r_xo])
                S.dma("pool", yv[:, :, t0:t0 + NT], xo[:], reads=[r_xo], writes=[r_xout])

            load(0)
            pending = None
            for qb in range(nblk):
                for h in range(4):
                    if h == 1 and qb + 1 < nblk:
                        load(qb + 1)
                    pending = attn_head(qb, h, pending)
            pending[0]()
            pending[1]()
            S.flush()

    def gla_sweep(self, S, li, xin, r_xin, xout, r_xout):
        T = self.T
        NT = 512
        o_ = li // 2
        scale = 128.0 ** -0.5
        with ExitStack() as st:
            cst, vecs, r_c = self.load_consts(S, st)
            ones = cst[:, C_ONES:C_ONES + 128]
            tri = cst[:, C_TRI:C_TRI + 128]
            su = cst[:, C_SU:C_SU + 128]
            tri4 = cst[:, C_TRI4:C_TRI4 + 512]
            win = self.sb(st, [128, 8, 3104], BF16)
            wout = self.sb(st, [128, 8, D], BF16)
            wgk = self.sb(st, [64, 1, 512], BF16)
            r_win, r_wout, r_wgk = S.res(), S.res(), S.res()
            with ExitStack() as st2:
                wv = self.w["gla_w_in"][o_].rearrange("(kc p) n -> p kc n", p=128)
                self.load_weight(S, st2, win, [wv[:, kc, :] for kc in range(8)], 3088, [r_c], r_win,
                                 scale_cols=[vecs[:, V_NMIX + 8 * li + kc:V_NMIX + 8 * li + kc + 1] for kc in range(8)], cw=1024)
                wo = self.w["gla_w_out"][o_].rearrange("(kc p) n -> p kc n", p=128)
                self.load_weight(S, st2, wout, [wo[:, kc, :] for kc in range(8)], D, [r_c], r_wout)
                stg = self.sb(st2, [64, 512], F32)
                r_s = S.res()
                S.add("pool", lambda e: e.memset(stg[:], 0.0), writes=[r_s])
                S.add("pool", lambda e: e.memset(win[:, :, 3088:3104], 0.0), writes=[r_win])
                S.dma("sp", stg[0:16, :], self.w["gla_w_gk_up"][o_], reads=[r_s], writes=[r_s])
                S.dma("sp", stg[32:33, :], self.w["gla_b_gk"][o_], reads=[r_s], writes=[r_s])
                S.add("dve", lambda e: e.tensor_copy(wgk[:, 0, :], stg[:]), reads=[r_s], writes=[r_wgk])
                S.flush()
            xt = [self.sb(st, [128, 8, NT], F32) for _ in range(2)]
            sq = self.sb(st, [128, 8, NT], F32)
            xo = sq
            ssum = self.sb(st, [128, NT], F32)
            rstd = self.sb(st, [128, NT], F32)
            h = self.sb(st, [128, 8, NT], BF16)
            qf = self.sb(st, [128, 4, NT], F32)
            kf = self.sb(st, [128, 4, NT], F32)
            gate = self.sb(st, [128, 8, NT], F32)
            gl = self.sb(st, [64, NT], BF16)
            ktok = self.sb(st, [128, 512], F32)
            vtok = [self.sb(st, [128, 1024], BF16) for _ in range(2)]
            ez = self.sb(st, [128, 512], F32)
            gtok = self.sb(st, [128, 512], F32)
            ebp = [self.sb(st, [128, 512], F32) for _ in range(2)]
            enb = gtok
            er = ez
            qd = [self.sb(st, [128, 4, 128], BF16) for _ in range(2)]
            kd = [self.sb(st, [128, 4, 128], BF16) for _ in range(2)]
            kl = [self.sb(st, [128, 512], BF16) for _ in range(2)]
            am = self.sb(st, [128, 512], BF16)
            Sf = self.sb(st, [128, 4, 256], F32)
            Sb = [self.sb(st, [128, 4, 256], BF16) for _ in range(2)]
            osq = self.sb(st, [128, 8, 128], F32)
            sdn = ssum
            rsn = rstd
            tg = self.sb(st, [128, 8, 128], F32)
            og = self.sb(st, [128, 8, NT], BF16)
            pb = self.psum(st)
            R = S.res
            r_xt = [R(), R()]; r_sq = R(); r_xo = r_sq; r_ssum = R(); r_rstd = R()
            r_h = [R() for _ in range(8)]; r_pb = [R() for _ in range(8)]
            r_qf, r_kf, r_gate, r_gl = R(), R(), R(), R()
            r_ktok, r_ez, r_gtok = R(), R(), R(); r_enb = r_gtok; r_er = r_ez
            r_vtok = [R(), R()]; r_ebp = [R(), R()]
            r_qd = [R(), R()]; r_kd = [R(), R()]; r_kl = [R(), R()]; r_am = R()
            r_Sf = [R() for _ in range(4)]; r_Sb = [[R() for _ in range(4)] for _ in range(2)]
            r_osq, r_tg, r_og = R(), R(), R(); r_sdn = r_ssum; r_rsn = r_rstd
            xv = xin.rearrange("(c p) t -> p c t", p=128)
            yv = xout.rearrange("(c p) t -> p c t", p=128)
            ntile = T // NT
            S.add("pool", lambda e: e.memset(Sf[:], 0.0), writes=r_Sf)
            S.add("pool", lambda e: e.memset(Sb[0][:], 0.0), writes=r_Sb[0])
            S.add("pool", lambda e: e.memset(Sb[1][:], 0.0), writes=r_Sb[1])
            S.add("pool", lambda e: e.memset(gl[:], 0.0), writes=[r_gl])
            S.add("pool", lambda e: e.memset(gl[32:64, :], 1.0), writes=[r_gl])

            def load(n):
                S.dma("sp", xt[n % 2][:], xv[:, :, n * NT:(n + 1) * NT], reads=[r_xin], writes=[r_xt[n % 2]])

            def norm(n):
                b = n % 2
                self.rmsnorm_tile(S, xt[b], r_xt[b], NT, sq, r_sq, ssum, r_ssum, rstd, r_rstd, pb[1], r_pb[1],
                                  ones, r_c, h, r_h)

            def fm(oc0, ncols, dst_fn, bk):
                for kc in range(8):
                    S.add("pe", (lambda kc: lambda e: e.matmul(pb[bk][:ncols, :NT], win[:, kc, oc0:oc0 + ncols], h[:, kc, :],
                                                               start=(kc == 0), stop=(kc == 7)))(kc),
                          reads=[r_win, r_h[kc]], writes=[r_pb[bk]])
                dst_fn(bk)

            def proj_fm(n):
                k = 0
                for c in range(4):
                    bk = k % 2; k += 1
                    fm(c * 128, 128, (lambda c: lambda bk: S.add("act", lambda e: e.copy(qf[:, c, :], pb[bk][:, :NT]),
                                                                  reads=[r_pb[bk]], writes=[r_qf]))(c), bk)
                for c in range(4):
                    bk = k % 2; k += 1
                    fm(512 + c * 128, 128, (lambda c: lambda bk: S.add("dve", lambda e: e.tensor_copy(kf[:, c, :], pb[bk][:, :NT]),
                                                                        reads=[r_pb[bk]], writes=[r_kf]))(c), bk)
                for c in range(8):
                    bk = k % 2; k += 1
                    fm(2048 + c * 128, 128, (lambda c: lambda bk: S.add("act", lambda e: e.activation(gate[:, c, :], pb[bk][:, :NT], AF.Silu),
                                                                         reads=[r_pb[bk]], writes=[r_gate]))(c), bk)
                bk = k % 2; k += 1
                fm(3072, 32, lambda bk: S.add("dve", lambda e: e.tensor_copy(gl[0:16, :], pb[bk][0:16, :NT]),
                                              reads=[r_pb[bk]], writes=[r_gl]), bk)

            def prep(n, s, cc):
                p2 = cc % 2
                ssl = slice(s * 128, (s + 1) * 128)
                vt_, r_vt_ = vtok[p2], r_vtok[p2]
                for kc in range(8):
                    S.add("pe", (lambda kc: lambda e: e.matmul(pb[2][:, :512], h[:, kc, ssl], win[:, kc, 512:1024],
                                                               start=(kc == 0), stop=(kc == 7)))(kc), reads=[r_win, r_h[kc]], writes=[r_pb[2]])
                S.add("act", lambda e: e.copy(ktok[:], pb[2][:, :512]), reads=[r_pb[2]], writes=[r_ktok])
                for hf in range(2):
                    bk = 3 + hf
                    for kc in range(8):
                        S.add("pe", (lambda kc, hf, bk: lambda e: e.matmul(pb[bk][:, :512], h[:, kc, ssl], win[:, kc, 1024 + hf * 512:1536 + hf * 512],
                                                                           start=(kc == 0), stop=(kc == 7)))(kc, hf, bk),
                              reads=[r_win, r_h[kc]], writes=[r_pb[bk]])
                    S.add("dve", (lambda hf, bk: lambda e: e.tensor_copy(vt_[:, hf * 512:(hf + 1) * 512], pb[bk][:, :512]))(hf, bk),
                          reads=[r_pb[bk]], writes=[r_vt_])
                S.add("pe", lambda e: e.matmul(pb[2][:, :512], gl[:, ssl], wgk[:, 0, :], start=True, stop=True),
                      reads=[r_gl, r_wgk], writes=[r_pb[2]])
                S.add("act", lambda e: e.activation(ez[:], pb[2][:, :512], AF.Exp, scale=-1.0), reads=[r_pb[2]], writes=[r_ez])
                S.add("act", lambda e: e.activation(ez[:], ez[:], AF.Ln, bias=1.0), reads=[r_ez], writes=[r_ez])
                S.add("dve", lambda e: e.tensor_scalar(gtok[:], ez[:], -1.0 / 16.0, None, ALU.mult), reads=[r_ez], writes=[r_gtok])
                for hh in range(4):
                    S.add("pe", (lambda hh: lambda e: e.matmul(pb[3][:, hh * 128:(hh + 1) * 128], gtok[:, hh * 128:(hh + 1) * 128], tri,
                                                               start=True, stop=True))(hh), reads=[r_gtok, r_c], writes=[r_pb[3]])
                S.add("pe", lambda e: e.matmul(pb[4][:, :512], su, gtok[:], start=True, stop=True), reads=[r_gtok, r_c], writes=[r_pb[4]])
                S.add("act", lambda e: e.activation(ebp[p2][:], pb[3][:, :512], AF.Exp), reads=[r_pb[3]], writes=[r_ebp[p2]])
                S.add("act", lambda e: e.activation(enb[:], pb[3][:, :512], AF.Exp, scale=-1.0), reads=[r_pb[3]], writes=[r_enb])
                S.add("act", lambda e: e.activation(er[:], pb[4][:, :512], AF.Exp), reads=[r_pb[4]], writes=[r_er])
                S.add("dve", lambda e: e.scalar_tensor_tensor(qd[p2][:], qf[:, :, ssl], float(scale), ebp[p2][:].rearrange("p (h t) -> p h t", h=4),
                                                              ALU.mult, ALU.mult), reads=[r_qf, r_ebp[p2]], writes=[r_qd[p2]])
                S.add("dve", lambda e: e.tensor_tensor(kd[p2][:], kf[:, :, ssl], enb[:].rearrange("p (h t) -> p h t", h=4), ALU.mult),
                      reads=[r_kf, r_enb], writes=[r_kd[p2]])
                S.add("pool", lambda e: e.tensor_tensor(kl[p2][:], ktok[:], er[:], ALU.mult), reads=[r_ktok, r_er], writes=[r_kl[p2]])

            def scan(n, s, cc):
                p2 = cc % 2
                ssl = slice(s * 128, (s + 1) * 128)
                vt_, r_vt_ = vtok[p2], r_vtok[p2]
                qd_, kd_, kl_, eb_ = qd[p2], kd[p2], kl[p2], ebp[p2]
                r_qd_, r_kd_, r_kl_, r_eb_ = r_qd[p2], r_kd[p2], r_kl[p2], r_ebp[p2]
                sbr, sbw = Sb[cc % 2], Sb[1 - cc % 2]
                r_sbr, r_sbw = r_Sb[cc % 2], r_Sb[1 - cc % 2]
                for hh in range(4):
                    S.add("pe", (lambda hh: lambda e: e.matmul(pb[5][:, hh * 128:(hh + 1) * 128], kd_[:, hh, :], qd_[:, hh, :],
                                                               start=True, stop=True))(hh), reads=[r_kd_, r_qd_], writes=[r_pb[5]])
                S.add("dve", lambda e: e.tensor_tensor(am[:], pb[5][:, :512], tri4, ALU.mult), reads=[r_pb[5], r_c], writes=[r_am])
                for hh in range(4):
                    for vc in range(2):
                        bk = 6 + hh // 2
                        col = ((hh % 2) * 2 + vc) * 128
                        S.add("pe", (lambda hh, vc, bk, col: lambda e: e.matmul(
                            pb[bk][:, col:col + 128], vt_[:, hh * 256 + vc * 128:hh * 256 + (vc + 1) * 128], am[:, hh * 128:(hh + 1) * 128],
                            start=True, stop=False))(hh, vc, bk, col), reads=[r_vt_, r_am], writes=[r_pb[bk]])
                        S.add("pe", (lambda hh, vc, bk, col: lambda e: e.matmul(
                            pb[bk][:, col:col + 128], sbr[:, hh, vc * 128:(vc + 1) * 128], qd_[:, hh, :],
                            start=False, stop=True))(hh, vc, bk, col), reads=[r_sbr[hh], r_qd_], writes=[r_pb[bk]])
                for hh in range(4):
                    bk = hh // 2
                    col = (hh % 2) * 256
                    S.add("pe", (lambda hh, bk, col: lambda e: e.matmul(pb[bk][:, col:col + 256], kl_[:, hh * 128:(hh + 1) * 128],
                                                                        vt_[:, hh * 256:(hh + 1) * 256], start=True, stop=True))(hh, bk, col),
                          reads=[r_kl_, r_vt_], writes=[r_pb[bk]])
                    S.add("dve", (lambda hh, bk, col: lambda e: e.scalar_tensor_tensor(
                        Sf[:, hh, :], Sf[:, hh, :], eb_[:, hh * 128 + 127:hh * 128 + 128], pb[bk][:, col:col + 256], ALU.mult, ALU.add))(hh, bk, col),
                        reads=[r_pb[bk], r_eb_], writes=[r_Sf[hh]])
                    S.add("act", (lambda hh: lambda e: e.copy(sbw[:, hh, :], Sf[:, hh, :]))(hh), reads=[r_Sf[hh]], writes=[r_sbw[hh]])
                for q2 in range(2):
                    S.add("act", (lambda q2: lambda e: e.activation(osq[:, q2 * 4:(q2 + 1) * 4, :],
                                                                    pb[6 + q2][:, :512].rearrange("p (c t) -> p c t", c=4), AF.Square))(q2),
                          reads=[r_pb[6 + q2]], writes=[r_osq])
                for hh in range(4):
                    for vc in range(2):
                        S.add("pe", (lambda hh, vc: lambda e: e.matmul(pb[5][:, hh * 128:(hh + 1) * 128], ones, osq[:, hh * 2 + vc, :],
                                                                       start=(vc == 0), stop=(vc == 1)))(hh, vc), reads=[r_osq, r_c], writes=[r_pb[5]])
                S.add("act", lambda e: e.activation(sdn[:], pb[5][:, :512], AF.Ln, bias=float(EPS), scale=1.0 / 256), reads=[r_pb[5]], writes=[r_sdn])
                S.add("act", lambda e: e.activation(rsn[:], sdn[:], AF.Exp, scale=-0.5), reads=[r_sdn], writes=[r_rsn])
                for c in range(8):
                    hh = c // 2
                    S.add("pool", (lambda c, hh: lambda e: e.tensor_tensor(tg[:, c, :], gate[:, c, ssl], rsn[:, hh * 128:(hh + 1) * 128], ALU.mult))(c, hh),
                          reads=[r_gate, r_rsn], writes=[r_tg])
                for c in range(8):
                    bk = 6 + c // 4
                    col = (c % 4) * 128
                    S.add("dve", (lambda c, bk, col: lambda e: e.scalar_tensor_tensor(
                        og[:, c, ssl], pb[bk][:, col:col + 128], vecs[:, V_GNORM + 8 * o_ + c:V_GNORM + 8 * o_ + c + 1], tg[:, c, :],
                        ALU.mult, ALU.mult))(c, bk, col), reads=[r_pb[bk], r_tg, r_c], writes=[r_og])

            def outproj(n):
                b = n % 2
                t0 = n * NT
                for oc in range(8):
                    bk = oc % 2
                    for c in range(8):
                        S.add("pe", (lambda oc, c, bk: lambda e: e.matmul(pb[bk][:, :NT], wout[:, c, oc * 128:(oc + 1) * 128], og[:, c, :],
                                                                           start=(c == 0), stop=(c == 7)))(oc, c, bk),
                              reads=[r_wout, r_og], writes=[r_pb[bk]])
                    S.add("dve", (lambda oc, bk: lambda e: e.tensor_tensor(xo[:, oc, :], pb[bk][:, :NT], xt[b][:, oc, :], ALU.add))(oc, bk),
                          reads=[r_pb[bk], r_xt[b]], writes=[r_xo])
                S.dma("pool", yv[:, :, t0:t0 + NT], xo[:], reads=[r_xo], writes=[r_xout])

            load(0)
            for n in range(ntile):
                if n + 1 < ntile:
                    load(n + 1)
                norm(n)
                proj_fm(n)
                base = 4 * n
                prep(n, 0, base)
                for s_ in range(4):
                    if s_ + 1 < 4:
                        prep(n, s_ + 1, base + s_ + 1)
                    scan(n, s_, base + s_)
                outproj(n)
            S.flush()

    def build(self):
        nc = self.nc
        with ExitStack() as st:
            S = Sched(nc, st)
            self.r_scr = S.res()
            r_in = S.res()
            r_a, r_b = S.res(), S.res()
            cur, r_cur = self.x_in, r_in
            nl = len(self.layers)
            for idx, li in enumerate(self.layers):
                last = (idx == nl - 1)
                if li % 2 == 0:
                    self.a1_sweep(S, li, cur, r_cur)
                    self.a2_sweep(S, li, cur, r_cur, self.xa, r_a)
                else:
                    self.gla_sweep(S, li, cur, r_cur, self.xa, r_a)
                if last:
                    self.ffn_sweep(S, li, self.xa, r_a, self.y_out, S.res(), final=self.final)
                else:
                    self.ffn_sweep(S, li, self.xa, r_a, self.xb, r_b, final=False)
                    cur, r_cur = self.xb, r_b
            self.ninst = S.ninst
        return nc


def host_inputs(inp, T_sl=None):
    cst = make_cst()
    vecs = make_vecs(inp)
    common = {
        "cst": cst, "vecs": vecs,
        "ab_w_in": np.ascontiguousarray(inp["ab_w_in"], np.float32),
        "ab_lambda": np.ascontiguousarray(np.asarray(inp["ab_lambda"], np.float32).reshape(2, 256)),
        "pool_w": np.ascontiguousarray(inp["pool_w"], np.float32),
        "ab_w_out": np.ascontiguousarray(inp["ab_w_out"], np.float32),
        "gla_w_in": np.ascontiguousarray(inp["gla_w_in"], np.float32),
        "gla_w_gk_up": np.ascontiguousarray(inp["gla_w_gk_up"], np.float32),
        "gla_b_gk": np.ascontiguousarray(np.asarray(inp["gla_b_gk"], np.float32).reshape(2, 1, 512)),
        "gla_w_out": np.ascontiguousarray(inp["gla_w_out"], np.float32),
        "ffn_w1": np.ascontiguousarray(inp["ffn_w1"], np.float32),
        "ffn_w2": np.ascontiguousarray(inp["ffn_w2"], np.float32),
    }
    return common


_CACHE = {}


def kernel(**inputs):
    x = np.asarray(inputs["x"], np.float32)
    B, T, _ = x.shape
    key = (T,)
    if key not in _CACHE:
        _CACHE[key] = Builder(T).build()
    nc = _CACHE[key]
    common = host_inputs(inputs)
    zeros = {k: np.zeros_like(v) for k, v in common.items()}
    zeros["xT"] = np.zeros((D, T), np.float32)
    in_maps = []
    for c in range(NCORES):
        if c in ACTIVE:
            m = dict(common)
            m["xT"] = np.ascontiguousarray(x[ACTIVE.index(c)].T)
        else:
            m = zeros
        in_maps.append(m)
    res = run_bass_kernel_spmd(nc, in_maps, core_ids=list(range(NCORES)))
    out = np.empty((B, T, D), np.float32)
    for b in range(B):
        out[b] = res.results[ACTIVE[b]]["yT"].T
    return out
```

```python
import math
from contextlib import ExitStack

import numpy as np
import concourse.bass as bass
import concourse.mybir as mybir
from concourse.bass_utils import run_bass_kernel_spmd

F32 = mybir.dt.float32
BF16 = mybir.dt.bfloat16
ALU = mybir.AluOpType
AF = mybir.ActivationFunctionType
AX = mybir.AxisListType

D = 1024
DFF = 4096
EPS = 1e-6
DEPTH = 4
NCORES = 8
ACTIVE = (0, 1, 4, 5)

ENGS = ("pe", "act", "dve", "pool", "sp")
NDMA_SEMS = 8


class Res:
    __slots__ = ("writer", "readers")

    def __init__(self):
        self.writer = None
        self.readers = []


class Op:
    __slots__ = ("eng", "fn", "deps", "is_dma", "sig", "count", "dsem", "dval", "waits", "snap", "done")

    def __init__(self, eng, fn, is_dma):
        self.eng = eng
        self.fn = fn
        self.is_dma = is_dma
        self.deps = []
        self.sig = False
        self.count = 0
        self.dsem = None
        self.dval = 0
        self.waits = []
        self.snap = None
        self.done = False


class Sched:
    def __init__(self, nc, stack):
        self.nc = nc
        self.ops = {e: [] for e in ENGS}
        self.cnt = {e: 0 for e in ENGS}
        self.ndma = {e: 0 for e in ENGS}
        self.dma_hist = {e: [] for e in ENGS}
        self.known = {e: [0] * len(ENGS) for e in ENGS}
        self.kdma = {e: {} for e in ENGS}
        self.esem = {e: stack.enter_context(nc.semaphore(f"s_{e}")) for e in ENGS if e != "sp"}
        self.dsem = {}
        for e in ("sp", "pool", "act"):
            for k in range(NDMA_SEMS):
                self.dsem[(e, k)] = stack.enter_context(nc.semaphore(f"d_{e}{k}"))
        self.ninst = 0

    def res(self):
        return Res()

    def add(self, eng, fn, reads=(), writes=(), is_dma=False):
        op = Op(eng, fn, is_dma)
        deps = []
        for r in reads:
            if r.writer is not None:
                deps.append(r.writer)
            r.readers.append(op)
        for w in writes:
            if w.writer is not None:
                deps.append(w.writer)
            deps.extend(w.readers)
            w.writer = op
            w.readers = []
        seen = set()
        for d in deps:
            if d is op or d.done or id(d) in seen:
                continue
            seen.add(id(d))
            op.deps.append(d)
        self.ops[eng].append(op)
        return op

    def dma(self, q, out, in_, reads=(), writes=()):
        return self.add(q, lambda e: e.dma_start(out=out, in_=in_), reads, writes, is_dma=True)

    def barrier(self):
        last = []
        for e in ENGS:
            for o in reversed(self.ops[e]):
                if o.fn is not None and not o.is_dma:
                    last.append(o)
                    break
        rec = []
        for e in ENGS:
            dm = [o for o in self.ops[e] if o.is_dma][-NDMA_SEMS:]
            rec.extend(dm)
        for e in ENGS:
            op = Op(e, None, False)
            op.deps = list(last) + list(rec)
            self.ops[e].append(op)

    def flush(self):
        self.barrier()
        for e in ENGS:
            for op in self.ops[e]:
                for d in op.deps:
                    d.sig = True
        for e in ENGS:
            for op in self.ops[e]:
                if op.is_dma:
                    n = self.ndma[e]
                    op.dsem = (e, n % NDMA_SEMS)
                    op.dval = 16 * (n // NDMA_SEMS + 1)
                    self.dma_hist[e].append(op)
                    self.ndma[e] += 1
                elif op.sig:
                    self.cnt[e] += 1
                    op.count = self.cnt[e]
        eidx = {e: i for i, e in enumerate(ENGS)}
        for e in ENGS:
            known = self.known[e]
            kdma = self.kdma[e]
            hist = self.dma_hist[e]
            nd = len(hist) - sum(1 for o in self.ops[e] if o.is_dma)
            for op in self.ops[e]:
                deps = list(op.deps)
                if op.is_dma:
                    if nd >= NDMA_SEMS:
                        deps.append(hist[nd - NDMA_SEMS])
                    nd += 1
                best = {}
                for d in deps:
                    if d.is_dma:
                        if kdma.get(d.dsem, 0) >= d.dval:
                            continue
                        kdma[d.dsem] = d.dval
                        best[("dma", d.dsem)] = d.dval
                    else:
                        if d.eng == "pe" and e == "pe":
                            continue
                        j = eidx[d.eng]
                        if known[j] >= d.count:
                            continue
                        known[j] = d.count
                        best[("eng", d.eng)] = max(best.get(("eng", d.eng), 0), d.count)
                        if d.snap is not None:
                            for k in range(len(ENGS)):
                                if d.snap[k] > known[k]:
                                    known[k] = d.snap[k]
                op.waits = [(k[0], k[1], v) for k, v in best.items()]
                op.snap = tuple(known)
        nc = self.nc
        ops = self.ops
        esem, dsem = self.esem, self.dsem

        def run(eng_name):
            lst = ops[eng_name]

            def body(eng):
                for op in lst:
                    for kind, key, val in op.waits:
                        eng.wait_ge(dsem[key] if kind == "dma" else esem[key], val)
                    if op.fn is None:
                        continue
                    ins = op.fn(eng)
                    if op.is_dma:
                        ins.then_inc(dsem[op.dsem], 16)
                    elif op.sig:
                        ins.then_inc(esem[eng_name], 1)
            return body

        with nc.Block() as block:
            block.tensor(run("pe"))
            block.scalar(run("act"))
            block.vector(run("dve"))
            block.gpsimd(run("pool"))
            block.sync(run("sp"))
        for e in ENGS:
            self.ninst += len(self.ops[e])
            for op in self.ops[e]:
                op.done = True
                op.fn = None
                op.deps = []
            self.ops[e] = []


C_ONES = 0
C_TRI = 128
C_SU = 256
C_TRI4 = 384
C_INVC = 896
NCST = 960


def make_cst():
    c = np.zeros((128, NCST), np.float32)
    c[:, C_ONES:C_ONES + 128] = 1.0
    s = np.arange(128)
    tri = (s[:, None] <= s[None, :]).astype(np.float32)
    c[:, C_TRI:C_TRI + 128] = tri
    c[:, C_SU:C_SU + 128] = (s[:, None] > s[None, :]).astype(np.float32)
    for h in range(4):
        c[:, C_TRI4 + h * 128:C_TRI4 + (h + 1) * 128] = tri
    for g, w in enumerate((2, 4, 8, 16)):
        t = np.arange(16)
        c[:, C_INVC + g * 16:C_INVC + (g + 1) * 16] = 1.0 / np.minimum(t + 1, w)
    return c


V_NMIX = 0
V_NFFN = 32
V_NFIN = 64
V_PSCALE = 72
V_SUBLN = 80
V_GNORM = 82
NVEC = 98


def make_vecs(inp):
    v = np.zeros((128, NVEC), np.float32)

    def put(col, arr):
        a = np.asarray(arr, np.float32).reshape(-1, 128).T
        v[:, col:col + a.shape[1]] = a

    for i in range(DEPTH):
        put(V_NMIX + 8 * i, inp["norm_mix"][i])
        put(V_NFFN + 8 * i, inp["norm_ffn"][i])
    put(V_NFIN, inp["norm_final"])
    for e in range(2):
        put(V_PSCALE + 4 * e, inp["pool_scale"][e])
        put(V_SUBLN + e, inp["ab_subln"][e])
        put(V_GNORM + 8 * e, inp["gla_norm"][e].reshape(-1))
    return v


class Builder:
    def __init__(self, T, layers=(0, 1, 2, 3), final=True, dbg=False):
        self.T = T
        self.layers = tuple(layers)
        self.final = final
        nc = self.nc = bass.Bass("TRN2", target_bir_lowering=False)
        dt = nc.dram_tensor
        self.x_in = dt("xT", [D, T], F32, kind="ExternalInput").ap()
        self.y_out = dt("yT", [D, T], F32, kind="ExternalOutput").ap()
        self.cst_d = dt("cst", [128, NCST], F32, kind="ExternalInput").ap()
        self.vecs_d = dt("vecs", [128, NVEC], F32, kind="ExternalInput").ap()
        self.w = {}
        for name, shape in (("ab_w_in", [2, D, 2048]), ("ab_lambda", [2, 256]), ("pool_w", [2, 4, 128, 128]),
                            ("ab_w_out", [2, D, D]), ("gla_w_in", [2, D, 3088]), ("gla_w_gk_up", [2, 16, 512]),
                            ("gla_b_gk", [2, 1, 512]), ("gla_w_out", [2, D, D]), ("ffn_w1", [4, D, DFF]),
                            ("ffn_w2", [4, DFF, D])):
            self.w[name] = dt(name, shape, F32, kind="ExternalInput").ap()
        kw = {"kind": "ExternalOutput"} if dbg else {}
        self.xa = dt("xa", [D, T], F32, **kw).ap()
        self.xb = dt("xb", [D, T], F32, **kw).ap()
        self.qT = dt("qTs", [4, 128, T], BF16, **kw).ap()
        self.kT = dt("kTs", [4, 128, T], BF16, **kw).ap()
        self.Vs = dt("Vs", [4, 128, T // 128, 128], BF16, **kw).ap()
        self.mT = dt("mTs", [4, 128, T], BF16, **kw).ap()
        self.r_x = {}
        self.nsb = 0

    def sb(self, st, shape, dtp):
        self.nsb += 1
        return st.enter_context(self.nc.sbuf_tensor(f"sb{self.nsb}", shape, dtp))

    def psum(self, st):
        self.nsb += 1
        return [st.enter_context(self.nc.psum_tensor(f"ps{self.nsb}_{i}", [128, 512], F32)) for i in range(8)]

    def load_consts(self, S, st):
        cst = self.sb(st, [128, NCST], F32)
        vecs = self.sb(st, [128, NVEC], F32)
        r = S.res()
        S.dma("sp", cst[:], self.cst_d, writes=[r])
        S.dma("sp", vecs[:], self.vecs_d, writes=[r])
        return cst, vecs, r

    def load_weight(self, S, st_tmp, dst, src_rows, ncols, rres, wres, scale_cols=None, cw=1024):
        stg = [self.sb(st_tmp, [128, cw], F32) for _ in range(3)]
        r_stg = [S.res() for _ in range(3)]
        engs = ["pool", "dve", "act"]
        ci = 0
        for r, src in enumerate(src_rows):
            for c0 in range(0, ncols, cw):
                c1 = min(ncols, c0 + cw)
                b = ci % 3
                S.dma("sp", stg[b][:, :c1 - c0], src[:, c0:c1], writes=[r_stg[b]])
                out = dst[:, r, c0:c1]
                in_ = stg[b][:, :c1 - c0]
                eng = engs[ci % 3]
                sc = None if scale_cols is None else scale_cols[r]
                if sc is None:
                    if eng == "act":
                        S.add("act", (lambda o, i: lambda e: e.copy(o, i))(out, in_), reads=[r_stg[b]] + rres, writes=[wres])
                    else:
                        S.add(eng, (lambda o, i: lambda e: e.tensor_copy(o, i))(out, in_), reads=[r_stg[b]] + rres, writes=[wres])
                else:
                    if eng == "act":
                        S.add("act", (lambda o, i, s: lambda e: e.activation(o, i, AF.Copy, scale=s))(out, in_, sc),
                              reads=[r_stg[b]] + rres, writes=[wres])
                    else:
                        S.add(eng, (lambda o, i, s: lambda e: e.tensor_scalar(o, i, s, None, ALU.mult))(out, in_, sc),
                              reads=[r_stg[b]] + rres, writes=[wres])
                ci += 1

    def rmsnorm_tile(self, S, xt_c, r_xt, NT, sq, r_sq, ssum, r_ssum, rstd, r_rstd, psb, r_psb, ones, r_c, h, r_h, nfeat_inv=1.0 / D):
        S.add("act", lambda e: e.activation(sq[:], xt_c[:], AF.Square), reads=[r_xt], writes=[r_sq])
        S.add("dve", lambda e: e.tensor_reduce(ssum[:], sq[:].rearrange("p c t -> p t c"), AX.X, ALU.add),
              reads=[r_sq], writes=[r_ssum])
        S.add("pe", lambda e: e.matmul(psb[:, :NT], ones, ssum[:], start=True, stop=True),
              reads=[r_c, r_ssum], writes=[r_psb])
        S.add("act", lambda e: e.activation(ssum[:], psb[:, :NT], AF.Ln, bias=float(EPS), scale=nfeat_inv),
              reads=[r_psb], writes=[r_ssum])
        S.add("act", lambda e: e.activation(rstd[:], ssum[:], AF.Exp, scale=-0.5), reads=[r_ssum], writes=[r_rstd])
        for c in range(8):
            eng = "dve" if c % 2 == 0 else "pool"
            S.add(eng, (lambda c: lambda e: e.tensor_tensor(h[:, c, :], xt_c[:, c, :], rstd[:], ALU.mult))(c),
                  reads=[r_xt, r_rstd], writes=[r_h[c]])

    def ffn_sweep(self, S, li, xin, r_xin, xout, r_xout, final):
        T = self.T
        FNT = 256
        nc = self.nc
        with ExitStack() as st:
            cst, vecs, r_c = self.load_consts(S, st)
            ones = cst[:, C_ONES:C_ONES + 128]
            w1sb = self.sb(st, [128, 8, DFF], BF16)
            w2sb = self.sb(st, [128, 32, D], BF16)
            r_w1, r_w2 = S.res(), S.res()
            with ExitStack() as st2:
                w1v = self.w["ffn_w1"][li].rearrange("(kc p) n -> p kc n", p=128)
                w2v = self.w["ffn_w2"][li].rearrange("(hc p) n -> p hc n", p=128)
                self.load_weight(S, st2, w1sb, [w1v[:, kc, :] for kc in range(8)], DFF, [r_c], r_w1,
                                 scale_cols=[vecs[:, V_NFFN + 8 * li + kc:V_NFFN + 8 * li + kc + 1] for kc in range(8)])
                self.load_weight(S, st2, w2sb, [w2v[:, hc, :] for hc in range(32)], D, [r_c], r_w2)
                S.flush()
            xt = [self.sb(st, [128, 8, FNT], F32) for _ in range(2)]
            xo = self.sb(st, [128, 8, FNT], F32)
            sq = self.sb(st, [128, 8, FNT], F32)
            ssum = self.sb(st, [128, FNT], F32)
            rstd = self.sb(st, [128, FNT], F32)
            h = self.sb(st, [128, 8, FNT], BF16)
            hid = self.sb(st, [128, 32, FNT], BF16)
            rl = [self.sb(st, [128, FNT], F32) for _ in range(4)]
            pb = self.psum(st)
            R = S.res
            r_xt = [R(), R()]; r_xo = R(); r_sq = R(); r_ssum = R(); r_rstd = R()
            r_h = [R() for _ in range(8)]; r_hid = [R() for _ in range(32)]
            r_pb = [R() for _ in range(8)]; r_rl = [R() for _ in range(4)]
            xv = xin.rearrange("(c p) t -> p c t", p=128)
            yv = xout.rearrange("(c p) t -> p c t", p=128)
            ntile = T // FNT

            def load(n):
                S.dma("sp", xt[n % 2][:], xv[:, :, n * FNT:(n + 1) * FNT], reads=[r_xin], writes=[r_xt[n % 2]])

            def norm(n):
                b = n % 2
                self.rmsnorm_tile(S, xt[b], r_xt[b], FNT, sq, r_sq, ssum, r_ssum, rstd, r_rstd, pb[7], r_pb[7],
                                  ones, r_c, h, r_h)

            def up(n):
                for j in range(32):
                    bk = j % 4
                    for kc in range(8):
                        S.add("pe", (lambda j, kc, bk: lambda e: e.matmul(
                            pb[bk][:, :FNT], w1sb[:, kc, j * 128:(j + 1) * 128], h[:, kc, :],
                            start=(kc == 0), stop=(kc == 7)))(j, kc, bk), reads=[r_w1, r_h[kc]], writes=[r_pb[bk]])
                    S.add("act", (lambda j, bk: lambda e: e.activation(rl[j % 4][:], pb[bk][:, :FNT], AF.Relu))(j, bk),
                          reads=[r_pb[bk]], writes=[r_rl[j % 4]])
                    S.add("pool", (lambda j: lambda e: e.tensor_tensor(hid[:, j, :], rl[j % 4][:], rl[j % 4][:], ALU.mult))(j),
                          reads=[r_rl[j % 4]], writes=[r_hid[j]])

            def down(n):
                b = n % 2
                for oc in range(8):
                    bk = 4 + oc % 2
                    for hc in range(32):
                        S.add("pe", (lambda oc, hc, bk: lambda e: e.matmul(
                            pb[bk][:, :FNT], w2sb[:, hc, oc * 128:(oc + 1) * 128], hid[:, hc, :],
                            start=(hc == 0), stop=(hc == 31)))(oc, hc, bk), reads=[r_w2, r_hid[hc]], writes=[r_pb[bk]])
                    S.add("dve", (lambda oc, bk: lambda e: e.tensor_tensor(
                        xo[:, oc, :], pb[bk][:, :FNT], xt[b][:, oc, :], ALU.add))(oc, bk),
                        reads=[r_pb[bk], r_xt[b]], writes=[r_xo])
                if final:
                    S.add("act", lambda e: e.activation(sq[:], xo[:], AF.Square), reads=[r_xo], writes=[r_sq])
                    S.add("dve", lambda e: e.tensor_reduce(ssum[:], sq[:].rearrange("p c t -> p t c"), AX.X, ALU.add),
                          reads=[r_sq], writes=[r_ssum])
                    S.add("pe", lambda e: e.matmul(pb[6][:, :FNT], ones, ssum[:], start=True, stop=True),
                          reads=[r_c, r_ssum], writes=[r_pb[6]])
                    S.add("act", lambda e: e.activation(ssum[:], pb[6][:, :FNT], AF.Ln, bias=float(EPS), scale=1.0 / D),
                          reads=[r_pb[6]], writes=[r_ssum])
                    S.add("act", lambda e: e.activation(rstd[:], ssum[:], AF.Exp, scale=-0.5), reads=[r_ssum], writes=[r_rstd])
                    for c in range(8):
                        S.add("dve", (lambda c: lambda e: e.scalar_tensor_tensor(
                            sq[:, c, :], xo[:, c, :], vecs[:, V_NFIN + c:V_NFIN + c + 1], rstd[:], ALU.mult, ALU.mult))(c),
                            reads=[r_xo, r_rstd, r_c], writes=[r_sq])
                    S.dma("pool", yv[:, :, n * FNT:(n + 1) * FNT], sq[:], reads=[r_sq], writes=[r_xout])
                else:
                    S.dma("pool", yv[:, :, n * FNT:(n + 1) * FNT], xo[:], reads=[r_xo], writes=[r_xout])

            load(0)
            if ntile > 1:
                load(1)
            norm(0)
            for n in range(ntile):
                up(n)
                if n + 1 < ntile and not final:
                    norm(n + 1)
                down(n)
                if n + 1 < ntile and final:
                    norm(n + 1)
                if n + 2 < ntile:
                    load(n + 2)
            S.flush()

    def a1_sweep(self, S, li, xin, r_xin):
        T = self.T
        NT = 512
        e_ = li // 2
        with ExitStack() as st:
            cst, vecs, r_c = self.load_consts(S, st)
            ones = cst[:, C_ONES:C_ONES + 128]
            win = self.sb(st, [128, 8, 2048], BF16)
            pw = self.sb(st, [128, 4, 128], BF16)
            r_win, r_pw = S.res(), S.res()
            with ExitStack() as st2:
                wv = self.w["ab_w_in"][e_].rearrange("(kc p) n -> p kc n", p=128)
                self.load_weight(S, st2, win, [wv[:, kc, :] for kc in range(8)], 2048, [r_c], r_win,
                                 scale_cols=[vecs[:, V_NMIX + 8 * li + kc:V_NMIX + 8 * li + kc + 1] for kc in range(8)])
                pwv = self.w["pool_w"][e_].rearrange("g c d -> c g d")
                self.load_weight(S, st2, pw, [pwv[:, g, :] for g in range(4)], 128, [r_c], r_pw, cw=128)
                S.flush()
            xt = [self.sb(st, [128, 8, NT], F32) for _ in range(2)]
            sq = self.sb(st, [128, 8, NT], F32)
            ssum = self.sb(st, [128, NT], F32)
            rstd = self.sb(st, [128, NT], F32)
            h = self.sb(st, [128, 8, NT], BF16)
            qk = [self.sb(st, [128, 8, NT], BF16) for _ in range(2)]
            vt = [self.sb(st, [128, 4, 512], BF16) for _ in range(2)]
            uext = [self.sb(st, [128, 4, 16 + NT], F32) for _ in range(2)]
            ta = self.sb(st, [128, 16 + NT], F32)
            tb = self.sb(st, [128, 16 + NT], F32)
            rr = self.sb(st, [128, 4, NT], BF16)
            mo = [self.sb(st, [128, 4, NT], BF16) for _ in range(2)]
            pb = self.psum(st)
            R = S.res
            r_xt = [R(), R()]; r_sq = R(); r_ssum = R(); r_rstd = R()
            r_h = [R() for _ in range(8)]; r_pb = [R() for _ in range(8)]
            r_qk = [R(), R()]; r_vt = [R(), R()]; r_ue = [[R() for _ in range(4)] for _ in range(2)]
            r_ta, r_tb = R(), R(); r_rr = [R() for _ in range(4)]; r_mo = [R(), R()]
            r_sc = self.r_scr
            xv = xin.rearrange("(c p) t -> p c t", p=128)
            ntile = T // NT
            qTv = self.qT.rearrange("h p t -> p h t")
            kTv = self.kT.rearrange("h p t -> p h t")
            mTv = self.mT.rearrange("g p t -> p g t")
            Vv = self.Vs.rearrange("h p k v -> p h k v")

            def load(n):
                S.dma("sp", xt[n % 2][:], xv[:, :, n * NT:(n + 1) * NT], reads=[r_xin], writes=[r_xt[n % 2]])

            def norm(n):
                b = n % 2
                self.rmsnorm_tile(S, xt[b], r_xt[b], NT, sq, r_sq, ssum, r_ssum, rstd, r_rstd, pb[7], r_pb[7],
                                  ones, r_c, h, r_h)

            for g in range(4):
                S.add("pool", (lambda g: lambda e: e.memset(uext[0][:, g, 0:16], 0.0))(g), writes=[r_ue[0][g]])
            S.add("pool", lambda e: e.memset(ta[:], 0.0), writes=[r_ta])
            S.add("pool", lambda e: e.memset(tb[:], 0.0), writes=[r_tb])

            def proj(n):
                b = n % 2
                t0 = n * NT
                for oc in range(8):
                    bk = oc % 3
                    for kc in range(8):
                        S.add("pe", (lambda oc, kc, bk: lambda e: e.matmul(
                            pb[bk][:, :NT], win[:, kc, oc * 128:(oc + 1) * 128], h[:, kc, :],
                            start=(kc == 0), stop=(kc == 7)))(oc, kc, bk), reads=[r_win, r_h[kc]], writes=[r_pb[bk]])
                    S.add("act", (lambda oc, bk: lambda e: e.copy(qk[b][:, oc, :], pb[bk][:, :NT]))(oc, bk),
                          reads=[r_pb[bk]], writes=[r_qk[b]])
                S.dma("pool", qTv[:, :, t0:t0 + NT], qk[b][:, 0:4, :], reads=[r_qk[b]], writes=[r_sc])
                S.dma("pool", kTv[:, :, t0:t0 + NT], qk[b][:, 4:8, :], reads=[r_qk[b]], writes=[r_sc])
                for s in range(4):
                    bk = 3 + s % 2
                    for kc in range(8):
                        S.add("pe", (lambda s, kc, bk: lambda e: e.matmul(
                            pb[bk][:, :512], h[:, kc, s * 128:(s + 1) * 128], win[:, kc, 1024:1536],
                            start=(kc == 0), stop=(kc == 7)))(s, kc, bk), reads=[r_win, r_h[kc]], writes=[r_pb[bk]])
                    S.add("dve", (lambda s, bk: lambda e: e.tensor_copy(vt[b][:, s, :], pb[bk][:, :512]))(s, bk),
                          reads=[r_pb[bk]], writes=[r_vt[b]])
                for hh in range(4):
                    S.dma("pool", self.Vs[hh, :, 4 * n:4 * n + 4, :], vt[b][:, :, hh * 128:(hh + 1) * 128],
                          reads=[r_vt[b]], writes=[r_sc])
                for g in range(4):
                    bk = 5 + g % 2
                    oc = 12 + g
                    for kc in range(8):
                        S.add("pe", (lambda oc, kc, bk: lambda e: e.matmul(
                            pb[bk][:, :NT], win[:, kc, oc * 128:(oc + 1) * 128], h[:, kc, :],
                            start=(kc == 0), stop=(kc == 7)))(oc, kc, bk), reads=[r_win, r_h[kc]], writes=[r_pb[bk]])
                    S.add("act", (lambda g, bk: lambda e: e.copy(uext[b][:, g, 16:16 + NT], pb[bk][:, :NT]))(g, bk),
                          reads=[r_pb[bk]], writes=[r_ue[b][g]])

            def pool(n):
                b = n % 2
                t0 = n * NT
                W = 16 + NT
                for g in range(4):
                    w = 2 ** (g + 1)
                    src = uext[b][:, g, :]
                    r_src = r_ue[b][g]
                    bufs = [(ta, r_ta), (tb, r_tb)]
                    sh = 1
                    k = 0
                    cur, r_cur = src, r_src
                    while sh < w:
                        dst, r_dst = bufs[k % 2]
                        S.add("pool", (lambda dst, cur, sh: lambda e: e.tensor_tensor(
                            dst[:, sh:W], cur[:, sh:W], cur[:, 0:W - sh], ALU.add))(dst, cur, sh),
                            reads=[r_cur], writes=[r_dst])
                        cur, r_cur = dst[:], r_dst
                        sh *= 2
                        k += 1
                    S.add("dve", (lambda g, cur, w: lambda e: e.scalar_tensor_tensor(
                        rr[:, g, :], cur[:, 16:W], 1.0 / w, uext[b][:, g, 16:W], ALU.mult, ALU.subtract))(g, cur, w),
                        reads=[r_cur, r_ue[b][g]], writes=[r_rr[g]])
                    if n == 0:
                        S.add("dve", (lambda g, cur: lambda e: e.tensor_tensor(
                            ta[:, 0:16], cur[:, 16:32], cst[:, C_INVC + g * 16:C_INVC + (g + 1) * 16], ALU.mult))(g, cur),
                            reads=[r_cur, r_c], writes=[r_ta])
                        S.add("dve", (lambda g: lambda e: e.tensor_tensor(
                            rr[:, g, 0:16], ta[:, 0:16], uext[b][:, g, 16:32], ALU.subtract))(g),
                            reads=[r_ta, r_ue[b][g]], writes=[r_rr[g]])
                    if n + 1 < ntile:
                        S.add("pool", (lambda g: lambda e: e.tensor_copy(uext[1 - b][:, g, 0:16], uext[b][:, g, NT:NT + 16]))(g),
                              reads=[r_ue[b][g]], writes=[r_ue[1 - b][g]])

            def pool_mm(n):
                b = n % 2
                t0 = n * NT
                for g in range(4):
                    bk = 5 + g % 2
                    S.add("pe", (lambda g, bk: lambda e: e.matmul(pb[bk][:, :NT], pw[:, g, :], rr[:, g, :], start=True, stop=True))(g, bk),
                          reads=[r_pw, r_rr[g]], writes=[r_pb[bk]])
                    S.add("act", (lambda g, bk: lambda e: e.activation(
                        mo[b][:, g, :], pb[bk][:, :NT], AF.Copy, scale=vecs[:, V_PSCALE + 4 * e_ + g:V_PSCALE + 4 * e_ + g + 1]))(g, bk),
                        reads=[r_pb[bk], r_c], writes=[r_mo[b]])
                S.dma("pool", mTv[:, :, t0:t0 + NT], mo[b][:], reads=[r_mo[b]], writes=[r_sc])

            load(0)
            if ntile > 1:
                load(1)
            norm(0)
            for n in range(ntile):
                proj(n)
                if n > 0:
                    pool_mm(n - 1)
                if n + 1 < ntile:
                    norm(n + 1)
                pool(n)
                if n + 2 < ntile:
                    load(n + 2)
            pool_mm(ntile - 1)
            S.flush()

    def a2_sweep(self, S, li, xin, r_xin, xout, r_xout):
        T = self.T
        NT = 512
        e_ = li // 2
        lam_init = 0.8 - 0.6 * math.exp(-0.3 * li)
        KG = 16
        with ExitStack() as st:
            cst, vecs, r_c = self.load_consts(S, st)
            ones = cst[:, C_ONES:C_ONES + 128]
            wout = self.sb(st, [128, 8, D], BF16)
            r_wout = S.res()
            onesb = self.sb(st, [128, 128], BF16)
            lamt = self.sb(st, [128, 256], F32)
            lprod = self.sb(st, [128, 128], F32)
            lsum = self.sb(st, [128, 2], F32)
            neglam = self.sb(st, [128, 1], F32)
            gsub = self.sb(st, [128, 1], F32)
            r_l = S.res()
            ediag = [[self.sb(st, [128, NT], BF16) for _ in range(2)] for _ in range(4)]
            r_ed = [[S.res() for _ in range(2)] for _ in range(4)]
            with ExitStack() as st2:
                wv = self.w["ab_w_out"][e_].rearrange("(kc p) n -> p kc n", p=128)
                self.load_weight(S, st2, wout, [wv[:, kc, :] for kc in range(8)], D, [r_c], r_wout)
                S.add("dve", lambda e: e.tensor_copy(onesb[:], ones), reads=[r_c], writes=[r_l])
                S.dma("sp", lamt[:], self.w["ab_lambda"][e_:e_ + 1, :].partition_broadcast(128), writes=[r_l])
                S.add("dve", lambda e: e.tensor_tensor(lprod[:, 0:64], lamt[:, 0:64], lamt[:, 64:128], ALU.mult), reads=[r_l], writes=[r_l])
                S.add("dve", lambda e: e.tensor_tensor(lprod[:, 64:128], lamt[:, 128:192], lamt[:, 192:256], ALU.mult), reads=[r_l], writes=[r_l])
                S.add("dve", lambda e: e.tensor_reduce(lsum[:], lprod[:].rearrange("p (a d) -> p a d", a=2), AX.X, ALU.add), reads=[r_l], writes=[r_l])
                S.add("act", lambda e: e.activation(lsum[:], lsum[:], AF.Exp), reads=[r_l], writes=[r_l])
                S.add("dve", lambda e: e.tensor_tensor(neglam[:], lsum[:, 1:2], lsum[:, 0:1], ALU.subtract), reads=[r_l], writes=[r_l])
                S.add("dve", lambda e: e.tensor_scalar(neglam[:], neglam[:], -float(lam_init), None, ALU.add), reads=[r_l], writes=[r_l])
                S.add("dve", lambda e: e.tensor_scalar(gsub[:], vecs[:, V_SUBLN + e_:V_SUBLN + e_ + 1], float(1.0 - lam_init), None, ALU.mult),
                      reads=[r_c], writes=[r_l])
                for j in range(4):
                    for a in range(2):
                        S.add("pool", (lambda j, a: lambda e: e.memset(ediag[j][a][:], 0.0))(j, a), writes=[r_ed[j][a]])
                S.flush()
            xt = [self.sb(st, [128, 8, NT], F32) for _ in range(2)]
            xo = self.sb(st, [128, 8, NT], F32)
            qt = [self.sb(st, [128, 4, NT], BF16) for _ in range(2)]
            mo = [self.sb(st, [128, 4, NT], BF16) for _ in range(2)]
            ao = self.sb(st, [128, 4, NT], BF16)
            kb = [self.sb(st, [128, KG * 128], BF16) for _ in range(3)]
            vb = [self.sb(st, [128, KG, 128], BF16) for _ in range(3)]
            eb = [[self.sb(st, [128, NT], BF16) for _ in range(2)] for _ in range(3)]
            rl1 = self.sb(st, [128, NT], F32); rl2 = self.sb(st, [128, NT], F32)
            t1 = self.sb(st, [128, NT], F32); t2 = self.sb(st, [128, NT], F32)
            A = self.sb(st, [128, NT], F32); asq = self.sb(st, [128, NT], F32)
            sd = self.sb(st, [128, NT], F32); rs = self.sb(st, [128, NT], F32)
            acc2 = self.sb(st, [128, NT], F32)
            r_acc2 = S.res()
            pb = self.psum(st)
            R = S.res
            r_xt = [R(), R()]; r_xo = R(); r_qt = [R(), R()]; r_mo = [R(), R()]; r_ao = [R() for _ in range(4)]
            r_kb = [R() for _ in range(3)]; r_vb = [R() for _ in range(3)]
            r_eb = [[R(), R()] for _ in range(3)]
            r_ep = R()
            r_pb = [R() for _ in range(8)]
            r_sc = self.r_scr
            xv = xin.rearrange("(c p) t -> p c t", p=128)
            yv = xout.rearrange("(c p) t -> p c t", p=128)
            qTv = self.qT.rearrange("h p t -> p h t")
            mTv = self.mT.rearrange("g p t -> p g t")
            nblk = T // NT
            PS_S = [(0, 1), (2, 3)]
            PO1, PO2, PL1, PL2 = 4, 5, 6, 7
            grp_ctr = [0]

            def load(n):
                t0 = n * NT
                b = n % 2
                S.dma("sp", xt[b][:], xv[:, :, t0:t0 + NT], reads=[r_xin], writes=[r_xt[b]])
                S.dma("sp", qt[b][:], qTv[:, :, t0:t0 + NT], reads=[r_sc], writes=[r_qt[b]])
                S.dma("sp", mo[b][:], mTv[:, :, t0:t0 + NT], reads=[r_sc], writes=[r_mo[b]])

            def load_kv(h, g0, ng):
                i = grp_ctr[0] % 3
                grp_ctr[0] += 1
                S.dma("sp", kb[i][:, :ng * 128], self.kT[h, :, g0 * 128:(g0 + ng) * 128], reads=[r_sc], writes=[r_kb[i]])
                S.dma("sp", vb[i][:, :ng, :], self.Vs[h, :, g0:g0 + ng, :], reads=[r_sc], writes=[r_vb[i]])
                return i

            def attn_head(qb, h, pending):
                b = qb % 2
                nk = 4 * (qb + 1)
                groups = []
                for g0 in range(0, nk, KG):
                    groups.append((g0, min(KG, nk - g0)))
                gbuf = {}
                for gi in range(min(2, len(groups))):
                    gbuf[gi] = load_kv(h, *groups[gi])
                ectr = [0]
                pend = []

                def qk(kt):
                    gi, lk = kt // KG, kt % KG
                    if gi not in gbuf:
                        gbuf[gi] = load_kv(h, *groups[gi])
                    i = gbuf[gi]
                    j = kt - 4 * qb
                    c0 = 128 * j if j >= 0 else 0
                    sa, sb_ = PS_S[kt % 2]
                    S.add("pe", lambda e: e.matmul(pb[sa][:, c0:NT], kb[i][0:64, lk * 128:(lk + 1) * 128], qt[b][0:64, h, c0:NT],
                                                   start=True, stop=True), reads=[r_kb[i], r_qt[b]], writes=[r_pb[sa]])
                    S.add("pe", lambda e: e.matmul(pb[sb_][:, c0:NT], kb[i][64:128, lk * 128:(lk + 1) * 128], qt[b][64:128, h, c0:NT],
                                                   start=True, stop=True), reads=[r_kb[i], r_qt[b]], writes=[r_pb[sb_]])
                    if j < 0:
                        k3 = ectr[0] % 3
                        ectr[0] += 1
                        e1, e2 = eb[k3][0], eb[k3][1]
                        re1, re2 = r_eb[k3][0], r_eb[k3][1]
                        S.add("act", lambda e: e.activation(e1[:, :], pb[sa][:, :NT], AF.Exp, scale=0.125), reads=[r_pb[sa]], writes=[re1])
                        S.add("act", lambda e: e.activation(e2[:, :], pb[sb_][:, :NT], AF.Exp, scale=0.125), reads=[r_pb[sb_]], writes=[re2])
                    else:
                        e1, e2 = ediag[j][0], ediag[j][1]
                        re1, re2 = r_ed[j][0], r_ed[j][1]
                        for (et, re_, pbn) in ((e1, re1, sa), (e2, re2, sb_)):
                            S.add("act", (lambda et, pbn: lambda e: e.activation(et[0:64, c0:NT], pb[pbn][0:64, c0:NT], AF.Exp, scale=0.125))(et, pbn),
                                  reads=[r_pb[pbn]], writes=[re_])
                            S.add("act", (lambda et, pbn: lambda e: e.activation(et[64:128, c0 + 64:NT], pb[pbn][64:128, c0 + 64:NT], AF.Exp, scale=0.125))(et, pbn),
                                  reads=[r_pb[pbn]], writes=[re_])
                    return (kt, c0, e1, e2, re1, re2, i, lk)

                def pv(item):
                    kt, c0, e1, e2, re1, re2, i, lk = item
                    first, last = (kt == 0), (kt == nk - 1)
                    S.add("pe", lambda e: e.matmul(pb[PO1][:, c0:NT], vb[i][:, lk, :], e1[:, c0:NT], start=first, stop=last, skip_group_check=True),
                          reads=[r_vb[i], re1], writes=[r_pb[PO1]])
                    S.add("pe", lambda e: e.matmul(pb[PO2][:, c0:NT], vb[i][:, lk, :], e2[:, c0:NT], start=first, stop=last, skip_group_check=True),
                          reads=[r_vb[i], re2], writes=[r_pb[PO2]])
                    S.add("pe", lambda e: e.matmul(pb[PL1][:, c0:NT], onesb[:], e1[:, c0:NT], start=first, stop=last, skip_group_check=True),
                          reads=[r_l, re1], writes=[r_pb[PL1]])
                    if first:
                        S.add("dve", lambda e: e.tensor_copy(acc2[:, c0:NT], e2[:, c0:NT]), reads=[re2], writes=[r_acc2])
                    else:
                        S.add("dve", lambda e: e.tensor_tensor(acc2[:, c0:NT], acc2[:, c0:NT], e2[:, c0:NT], ALU.add),
                              reads=[re2, r_acc2], writes=[r_acc2])

                prev = qk(0)
                for kt in range(1, nk):
                    cur = qk(kt)
                    if kt == 1 and pending is not None:
                        pending[0]()
                    pv(prev)
                    if kt == 3 and pending is not None:
                        pending[1]()
                    prev = cur
                pv(prev)
                return (lambda: ep_a(qb, h, nk), lambda: ep_b(qb, h, nk))

            def ep_a(qb, h, nk):
                S.add("pe", lambda e: e.matmul(pb[PL2][:, :NT], ones, acc2[:], start=True, stop=True),
                      reads=[r_c, r_acc2], writes=[r_pb[PL2]])
                S.add("act", lambda e: e.activation(rl1[:], pb[PL1][:, :NT], AF.Ln), reads=[r_pb[PL1]], writes=[r_ep])
                S.add("act", lambda e: e.activation(rl1[:], rl1[:], AF.Exp, scale=-1.0), reads=[r_ep], writes=[r_ep])
                S.add("act", lambda e: e.activation(rl2[:], pb[PL2][:, :NT], AF.Ln), reads=[r_pb[PL2]], writes=[r_ep])
                S.add("act", lambda e: e.activation(rl2[:], rl2[:], AF.Exp, scale=-1.0), reads=[r_ep], writes=[r_ep])
                S.add("dve", lambda e: e.tensor_tensor(t1[:], pb[PO1][:, :NT], rl1[:], ALU.mult), reads=[r_pb[PO1], r_ep], writes=[r_ep])
                S.add("dve", lambda e: e.tensor_tensor(t2[:], pb[PO2][:, :NT], rl2[:], ALU.mult), reads=[r_pb[PO2], r_ep], writes=[r_ep])
                S.add("dve", lambda e: e.scalar_tensor_tensor(A[:], t2[:], neglam[:, 0:1], t1[:], ALU.mult, ALU.add), reads=[r_ep, r_l], writes=[r_ep])
                S.add("pool", lambda e: e.tensor_tensor(asq[:], A[:], A[:], ALU.mult), reads=[r_ep], writes=[r_ep])

            def ep_b(qb, h, nk):
                sa = PS_S[0][0]
                S.add("pe", lambda e: e.matmul(pb[sa][:, :NT], ones, asq[:], start=True, stop=True), reads=[r_c, r_ep], writes=[r_pb[sa]])
                S.add("act", lambda e: e.activation(sd[:], pb[sa][:, :NT], AF.Ln, bias=float(EPS), scale=1.0 / 128), reads=[r_pb[sa]], writes=[r_ep])
                S.add("act", lambda e: e.activation(rs[:], sd[:], AF.Exp, scale=-0.5), reads=[r_ep], writes=[r_ep])
                S.add("dve", lambda e: e.scalar_tensor_tensor(ao[:, h, :], A[:], gsub[:, 0:1], rs[:], ALU.mult, ALU.mult),
                      reads=[r_ep, r_l], writes=[r_ao[h]])
                if h == 3:
                    outproj(qb)

            def outproj(qb):
                b = qb % 2
                t0 = qb * NT
                for oc in range(8):
                    bk = PS_S[oc % 2][1]
                    for c in range(8):
                        if c < 4:
                            S.add("pe", (lambda oc, c, bk: lambda e: e.matmul(pb[bk][:, :NT], wout[:, c, oc * 128:(oc + 1) * 128], ao[:, c, :],
                                                                               start=(c == 0), stop=False))(oc, c, bk),
                                  reads=[r_wout, r_ao[c]], writes=[r_pb[bk]])
                        else:
                            S.add("pe", (lambda oc, c, bk: lambda e: e.matmul(pb[bk][:, :NT], wout[:, c, oc * 128:(oc + 1) * 128], mo[b][:, c - 4, :],
                                                                               start=False, stop=(c == 7)))(oc, c, bk),
                                  reads=[r_wout, r_mo[b]], writes=[r_pb[bk]])
                    S.add("dve", (lambda oc, bk: lambda e: e.tensor_tensor(xo[:, oc, :], pb[bk][:, :NT], xt[b][:, oc, :], ALU.add))(oc, bk),
                          reads=[r_pb[bk], r_xt[b]], writes=[r_xo])
                S.dma("pool", yv[:, :, t0:t0 + NT], xo[:], reads=[r_xo], writes=[r_xout])

            load(0)
            pending = None
            for qb in range(nblk):
                for h in range(4):
                    if h == 1 and qb + 1 < nblk:
                        load(qb + 1)
                    pending = attn_head(qb, h, pending)
            pending[0]()
            pending[1]()
            S.flush()

    def gla_sweep(self, S, li, xin, r_xin, xout, r_xout):
        T = self.T
        NT = 512
        o_ = li // 2
        scale = 128.0 ** -0.5
        with ExitStack() as st:
            cst, vecs, r_c = self.load_consts(S, st)
            ones = cst[:, C_ONES:C_ONES + 128]
            tri = cst[:, C_TRI:C_TRI + 128]
            su = cst[:, C_SU:C_SU + 128]
            tri4 = cst[:, C_TRI4:C_TRI4 + 512]
            win = self.sb(st, [128, 8, 3104], BF16)
            wout = self.sb(st, [128, 8, D], BF16)
            wgk = self.sb(st, [64, 1, 512], BF16)
            r_win, r_wout, r_wgk = S.res(), S.res(), S.res()
            with ExitStack() as st2:
                wv = self.w["gla_w_in"][o_].rearrange("(kc p) n -> p kc n", p=128)
                self.load_weight(S, st2, win, [wv[:, kc, :] for kc in range(8)], 3088, [r_c], r_win,
                                 scale_cols=[vecs[:, V_NMIX + 8 * li + kc:V_NMIX + 8 * li + kc + 1] for kc in range(8)], cw=1024)
                wo = self.w["gla_w_out"][o_].rearrange("(kc p) n -> p kc n", p=128)
                self.load_weight(S, st2, wout, [wo[:, kc, :] for kc in range(8)], D, [r_c], r_wout,
                                 scale_cols=[vecs[:, V_GNORM + 8 * o_ + kc:V_GNORM + 8 * o_ + kc + 1] for kc in range(8)])
                stg = self.sb(st2, [64, 512], F32)
                r_s = S.res()
                S.add("pool", lambda e: e.memset(stg[:], 0.0), writes=[r_s])
                S.add("pool", lambda e: e.memset(win[:, :, 3088:3104], 0.0), writes=[r_win])
                S.dma("sp", stg[0:16, :], self.w["gla_w_gk_up"][o_], reads=[r_s], writes=[r_s])
                S.dma("sp", stg[32:33, :], self.w["gla_b_gk"][o_], reads=[r_s], writes=[r_s])
                S.add("dve", lambda e: e.tensor_copy(wgk[:, 0, :], stg[:]), reads=[r_s], writes=[r_wgk])
                S.flush()
            xt = [self.sb(st, [128, 8, NT], F32) for _ in range(2)]
            sq = self.sb(st, [128, 8, NT], BF16)
            ssum = self.sb(st, [128, NT], F32)
            rstd = self.sb(st, [128, NT], F32)
            h = self.sb(st, [128, 8, NT], BF16)
            qkf = self.sb(st, [128, 8, NT], F32)
            qf = qkf[:, 0:4, :]
            kf = qkf[:, 4:8, :]
            xo = qkf
            gate = self.sb(st, [128, 8, NT], BF16)
            gl = self.sb(st, [64, NT], BF16)
            ktok = self.sb(st, [128, 512], F32)
            vtok = [self.sb(st, [128, 1024], BF16) for _ in range(2)]
            ez = self.sb(st, [128, 512], F32)
            gtok = self.sb(st, [128, 512], F32)
            ebp = [self.sb(st, [128, 512], F32) for _ in range(2)]
            enb = gtok
            er = ez
            qd = [self.sb(st, [128, 4, 128], BF16) for _ in range(2)]
            kd = [self.sb(st, [128, 4, 128], BF16) for _ in range(2)]
            kl = [self.sb(st, [128, 512], BF16) for _ in range(2)]
            am = self.sb(st, [128, 512], BF16)
            Sf = self.sb(st, [128, 4, 256], F32)
            Sb = [self.sb(st, [128, 4, 256], BF16) for _ in range(2)]
            osq = self.sb(st, [128, 8, 128], F32)
            sdn = self.sb(st, [128, 512], F32)
            rsn = self.sb(st, [128, 8, 128], F32)
            tg = self.sb(st, [128, 8, 128], F32)
            og = self.sb(st, [128, 8, NT], BF16)
            pb = self.psum(st)
            R = S.res
            r_xt = [R(), R()]; r_sq = R(); r_ssum = R(); r_rstd = R()
            r_h = [R() for _ in range(8)]; r_pb = [R() for _ in range(8)]
            r_qf, r_kf, r_gate, r_gl = R(), R(), R(), R()
            r_ktok, r_ez, r_gtok = R(), R(), R(); r_enb = r_gtok; r_er = r_ez
            r_vtok = [R(), R()]; r_ebp = [R(), R()]
            r_qd = [R(), R()]; r_kd = [R(), R()]; r_kl = [R(), R()]; r_am = R()
            r_Sf = [R() for _ in range(4)]; r_Sb = [[R() for _ in range(4)] for _ in range(2)]
            r_osq, r_tg, r_og = R(), R(), R(); r_sdn = R(); r_rsn = R()
            xv = xin.rearrange("(c p) t -> p c t", p=128)
            yv = xout.rearrange("(c p) t -> p c t", p=128)
            ntile = T // NT
            S.add("pool", lambda e: e.memset(Sf[:], 0.0), writes=r_Sf)
            S.add("pool", lambda e: e.memset(Sb[0][:], 0.0), writes=r_Sb[0])
            S.add("pool", lambda e: e.memset(Sb[1][:], 0.0), writes=r_Sb[1])
            S.add("pool", lambda e: e.memset(gl[:], 0.0), writes=[r_gl])
            S.add("pool", lambda e: e.memset(gl[32:64, :], 1.0), writes=[r_gl])

            def load(n):
                S.dma("sp", xt[n % 2][:], xv[:, :, n * NT:(n + 1) * NT], reads=[r_xin], writes=[r_xt[n % 2]])

            def prenorm(n):
                b = n % 2
                S.add("act", lambda e: e.activation(sq[:], xt[b][:], AF.Square), reads=[r_xt[b]], writes=[r_sq])
                S.add("dve", lambda e: e.tensor_reduce(ssum[:], sq[:].rearrange("p c t -> p t c"), AX.X, ALU.add),
                      reads=[r_sq], writes=[r_ssum])
                S.add("pe", lambda e: e.matmul(pb[1][:, :NT], ones, ssum[:], start=True, stop=True),
                      reads=[r_c, r_ssum], writes=[r_pb[1]])
                S.add("act", lambda e: e.activation(ssum[:], pb[1][:, :NT], AF.Ln, bias=float(EPS), scale=1.0 / D),
                      reads=[r_pb[1]], writes=[r_ssum])
                S.add("act", lambda e: e.activation(rstd[:], ssum[:], AF.Exp, scale=-0.5), reads=[r_ssum], writes=[r_rstd])

            def hmul(n):
                b = n % 2
                for c in range(8):
                    eng = "dve" if c % 2 == 0 else "pool"
                    S.add(eng, (lambda c: lambda e: e.tensor_tensor(h[:, c, :], xt[b][:, c, :], rstd[:], ALU.mult))(c),
                          reads=[r_xt[b], r_rstd], writes=[r_h[c]])

            def fm(oc0, ncols, dst_fn, bk):
                for kc in range(8):
                    S.add("pe", (lambda kc: lambda e: e.matmul(pb[bk][:ncols, :NT], win[:, kc, oc0:oc0 + ncols], h[:, kc, :],
                                                               start=(kc == 0), stop=(kc == 7)))(kc),
                          reads=[r_win, r_h[kc]], writes=[r_pb[bk]])
                dst_fn(bk)

            def proj_fm(n):
                k = 0
                for c in range(4):
                    bk = k % 2; k += 1
                    fm(c * 128, 128, (lambda c: lambda bk: S.add("act", lambda e: e.copy(qf[:, c, :], pb[bk][:, :NT]),
                                                                  reads=[r_pb[bk]], writes=[r_qf]))(c), bk)
                for c in range(4):
                    bk = k % 2; k += 1
                    fm(512 + c * 128, 128, (lambda c: lambda bk: S.add("dve", lambda e: e.tensor_copy(kf[:, c, :], pb[bk][:, :NT]),
                                                                        reads=[r_pb[bk]], writes=[r_kf]))(c), bk)
                for c in range(8):
                    bk = k % 2; k += 1
                    fm(2048 + c * 128, 128, (lambda c: lambda bk: S.add("act", lambda e: e.activation(gate[:, c, :], pb[bk][:, :NT], AF.Silu),
                                                                         reads=[r_pb[bk]], writes=[r_gate]))(c), bk)
                bk = k % 2; k += 1
                fm(3072, 32, lambda bk: S.add("dve", lambda e: e.tensor_copy(gl[0:16, :], pb[bk][0:16, :NT]),
                                              reads=[r_pb[bk]], writes=[r_gl]), bk)

            def prep(n, s, cc):
                p2 = cc % 2
                ssl = slice(s * 128, (s + 1) * 128)
                vt_, r_vt_ = vtok[p2], r_vtok[p2]
                for kc in range(8):
                    S.add("pe", (lambda kc: lambda e: e.matmul(pb[2][:, :512], h[:, kc, ssl], win[:, kc, 512:1024],
                                                               start=(kc == 0), stop=(kc == 7)))(kc), reads=[r_win, r_h[kc]], writes=[r_pb[2]])
                S.add("act", lambda e: e.copy(ktok[:], pb[2][:, :512]), reads=[r_pb[2]], writes=[r_ktok])
                for hf in range(2):
                    bk = 3 + hf
                    for kc in range(8):
                        S.add("pe", (lambda kc, hf, bk: lambda e: e.matmul(pb[bk][:, :512], h[:, kc, ssl], win[:, kc, 1024 + hf * 512:1536 + hf * 512],
                                                                           start=(kc == 0), stop=(kc == 7)))(kc, hf, bk),
                              reads=[r_win, r_h[kc]], writes=[r_pb[bk]])
                    S.add("dve", (lambda hf, bk: lambda e: e.tensor_copy(vt_[:, hf * 512:(hf + 1) * 512], pb[bk][:, :512]))(hf, bk),
                          reads=[r_pb[bk]], writes=[r_vt_])
                S.add("pe", lambda e: e.matmul(pb[2][:, :512], gl[:, ssl], wgk[:, 0, :], start=True, stop=True),
                      reads=[r_gl, r_wgk], writes=[r_pb[2]])
                S.add("act", lambda e: e.activation(ez[:], pb[2][:, :512], AF.Exp, scale=-1.0), reads=[r_pb[2]], writes=[r_ez])
                S.add("act", lambda e: e.activation(ez[:], ez[:], AF.Ln, bias=1.0), reads=[r_ez], writes=[r_ez])
                S.add("act", lambda e: e.activation(gtok[:], ez[:], AF.Copy, scale=-1.0 / 16.0), reads=[r_ez], writes=[r_gtok])
                for hh in range(4):
                    S.add("pe", (lambda hh: lambda e: e.matmul(pb[3][:, hh * 128:(hh + 1) * 128], gtok[:, hh * 128:(hh + 1) * 128], tri,
                                                               start=True, stop=True))(hh), reads=[r_gtok, r_c], writes=[r_pb[3]])
                S.add("pe", lambda e: e.matmul(pb[4][:, :512], su, gtok[:], start=True, stop=True), reads=[r_gtok, r_c], writes=[r_pb[4]])
                S.add("act", lambda e: e.activation(ebp[p2][:], pb[3][:, :512], AF.Exp), reads=[r_pb[3]], writes=[r_ebp[p2]])
                S.add("act", lambda e: e.activation(enb[:], pb[3][:, :512], AF.Exp, scale=-1.0), reads=[r_pb[3]], writes=[r_enb])
                S.add("act", lambda e: e.activation(er[:], pb[4][:, :512], AF.Exp), reads=[r_pb[4]], writes=[r_er])
                S.add("dve", lambda e: e.scalar_tensor_tensor(qd[p2][:], qf[:, :, ssl], float(scale), ebp[p2][:].rearrange("p (h t) -> p h t", h=4),
                                                              ALU.mult, ALU.mult), reads=[r_qf, r_ebp[p2]], writes=[r_qd[p2]])
                S.add("dve", lambda e: e.tensor_tensor(kd[p2][:], kf[:, :, ssl], enb[:].rearrange("p (h t) -> p h t", h=4), ALU.mult),
                      reads=[r_kf, r_enb], writes=[r_kd[p2]])
                S.add("pool", lambda e: e.tensor_tensor(kl[p2][:], ktok[:], er[:], ALU.mult), reads=[r_ktok, r_er], writes=[r_kl[p2]])

            def scan(n, s, cc):
                p2 = cc % 2
                ssl = slice(s * 128, (s + 1) * 128)
                vt_, r_vt_ = vtok[p2], r_vtok[p2]
                qd_, kd_, kl_, eb_ = qd[p2], kd[p2], kl[p2], ebp[p2]
                r_qd_, r_kd_, r_kl_, r_eb_ = r_qd[p2], r_kd[p2], r_kl[p2], r_ebp[p2]
                sbr, sbw = Sb[cc % 2], Sb[1 - cc % 2]
                r_sbr, r_sbw = r_Sb[cc % 2], r_Sb[1 - cc % 2]
                for hh in range(4):
                    S.add("pe", (lambda hh: lambda e: e.matmul(pb[5][:, hh * 128:(hh + 1) * 128], kd_[:, hh, :], qd_[:, hh, :],
                                                               start=True, stop=True))(hh), reads=[r_kd_, r_qd_], writes=[r_pb[5]])
                S.add("dve", lambda e: e.tensor_tensor(am[:], pb[5][:, :512], tri4, ALU.mult), reads=[r_pb[5], r_c], writes=[r_am])
                for hh in range(4):
                    bk = hh // 2
                    col = (hh % 2) * 256
                    S.add("pe", (lambda hh, bk, col: lambda e: e.matmul(pb[bk][:, col:col + 256], kl_[:, hh * 128:(hh + 1) * 128],
                                                                        vt_[:, hh * 256:(hh + 1) * 256], start=True, stop=True))(hh, bk, col),
                          reads=[r_kl_, r_vt_], writes=[r_pb[bk]])
                for hh in range(4):
                    for vc in range(2):
                        bk = 6 + hh // 2
                        col = ((hh % 2) * 2 + vc) * 128
                        S.add("pe", (lambda hh, vc, bk, col: lambda e: e.matmul(
                            pb[bk][:, col:col + 128], vt_[:, hh * 256 + vc * 128:hh * 256 + (vc + 1) * 128], am[:, hh * 128:(hh + 1) * 128],
                            start=True, stop=False))(hh, vc, bk, col), reads=[r_vt_, r_am], writes=[r_pb[bk]])
                        S.add("pe", (lambda hh, vc, bk, col: lambda e: e.matmul(
                            pb[bk][:, col:col + 128], sbr[:, hh, vc * 128:(vc + 1) * 128], qd_[:, hh, :],
                            start=False, stop=True))(hh, vc, bk, col), reads=[r_sbr[hh], r_qd_], writes=[r_pb[bk]])
                for hh in range(4):
                    bk = hh // 2
                    col = (hh % 2) * 256
                    S.add("dve", (lambda hh, bk, col: lambda e: e.scalar_tensor_tensor(
                        Sf[:, hh, :], Sf[:, hh, :], eb_[:, hh * 128 + 127:hh * 128 + 128], pb[bk][:, col:col + 256], ALU.mult, ALU.add))(hh, bk, col),
                        reads=[r_pb[bk], r_eb_], writes=[r_Sf[hh]])
                    S.add("act", (lambda hh: lambda e: e.copy(sbw[:, hh, :], Sf[:, hh, :]))(hh), reads=[r_Sf[hh]], writes=[r_sbw[hh]])
                for q2 in range(2):
                    S.add("act", (lambda q2: lambda e: e.activation(osq[:, q2 * 4:(q2 + 1) * 4, :],
                                                                    pb[6 + q2][:, :512].rearrange("p (c t) -> p c t", c=4), AF.Square))(q2),
                          reads=[r_pb[6 + q2]], writes=[r_osq])
                for hh in range(4):
                    for vc in range(2):
                        S.add("pe", (lambda hh, vc: lambda e: e.matmul(pb[5][:, hh * 128:(hh + 1) * 128], ones, osq[:, hh * 2 + vc, :],
                                                                       start=(vc == 0), stop=(vc == 1)))(hh, vc), reads=[r_osq, r_c], writes=[r_pb[5]])
                S.add("act", lambda e: e.activation(sdn[:], pb[5][:, :512], AF.Ln, bias=float(EPS), scale=1.0 / 256), reads=[r_pb[5]], writes=[r_sdn])
                rsn_v = rsn[:].rearrange("p (h v) t -> p h v t", v=2)
                for vc in range(2):
                    S.add("act", (lambda vc: lambda e: e.activation(rsn_v[:, :, vc, :], sdn[:].rearrange("p (h t) -> p h t", h=4),
                                                                    AF.Exp, scale=-0.5))(vc), reads=[r_sdn], writes=[r_rsn])
                S.add("pool", lambda e: e.tensor_tensor(tg[:], gate[:, :, ssl], rsn[:], ALU.mult), reads=[r_gate, r_rsn], writes=[r_tg])
                for q2 in range(2):
                    S.add("dve", (lambda q2: lambda e: e.tensor_tensor(
                        og[:, q2 * 4:(q2 + 1) * 4, ssl], pb[6 + q2][:, :512].rearrange("p (c t) -> p c t", c=4), tg[:, q2 * 4:(q2 + 1) * 4, :],
                        ALU.mult))(q2), reads=[r_pb[6 + q2], r_tg], writes=[r_og])

            def outproj(n):
                b = n % 2
                t0 = n * NT
                for oc in range(8):
                    bk = oc % 2
                    for c in range(8):
                        S.add("pe", (lambda oc, c, bk: lambda e: e.matmul(pb[bk][:, :NT], wout[:, c, oc * 128:(oc + 1) * 128], og[:, c, :],
                                                                           start=(c == 0), stop=(c == 7)))(oc, c, bk),
                              reads=[r_wout, r_og], writes=[r_pb[bk]])
                    S.add("dve", (lambda oc, bk: lambda e: e.tensor_tensor(xo[:, oc, :], pb[bk][:, :NT], xt[b][:, oc, :], ALU.add))(oc, bk),
                          reads=[r_pb[bk], r_xt[b]], writes=[r_qf, r_kf])
                S.dma("pool", yv[:, :, t0:t0 + NT], xo[:], reads=[r_qf, r_kf], writes=[r_xout])

            load(0)
            for n in range(ntile):
                if n + 1 < ntile:
                    load(n + 1)
                if n == 0:
                    prenorm(0)
                    hmul(0)
                proj_fm(n)
                base = 4 * n
                prep(n, 0, base)
                prep(n, 1, base + 1)
                scan(n, 0, base)
                if n + 1 < ntile:
                    prenorm(n + 1)
                prep(n, 2, base + 2)
                scan(n, 1, base + 1)
                prep(n, 3, base + 3)
                scan(n, 2, base + 2)
                if n + 1 < ntile:
                    hmul(n + 1)
                scan(n, 3, base + 3)
                outproj(n)
            S.flush()

    def build(self):
        nc = self.nc
        with ExitStack() as st:
            S = Sched(nc, st)
            self.r_scr = S.res()
            r_in = S.res()
            r_a, r_b = S.res(), S.res()
            cur, r_cur = self.x_in, r_in
            nl = len(self.layers)
            for idx, li in enumerate(self.layers):
                last = (idx == nl - 1)
                if li % 2 == 0:
                    self.a1_sweep(S, li, cur, r_cur)
                    self.a2_sweep(S, li, cur, r_cur, self.xa, r_a)
                else:
                    self.gla_sweep(S, li, cur, r_cur, self.xa, r_a)
                if last:
                    self.ffn_sweep(S, li, self.xa, r_a, self.y_out, S.res(), final=self.final)
                else:
                    self.ffn_sweep(S, li, self.xa, r_a, self.xb, r_b, final=False)
                    cur, r_cur = self.xb, r_b
            self.ninst = S.ninst
        return nc


def host_inputs(inp, T_sl=None):
    cst = make_cst()
    vecs = make_vecs(inp)
    common = {
        "cst": cst, "vecs": vecs,
        "ab_w_in": np.ascontiguousarray(inp["ab_w_in"], np.float32),
        "ab_lambda": np.ascontiguousarray(np.asarray(inp["ab_lambda"], np.float32).reshape(2, 256)),
        "pool_w": np.ascontiguousarray(inp["pool_w"], np.float32),
        "ab_w_out": np.ascontiguousarray(inp["ab_w_out"], np.float32),
        "gla_w_in": np.ascontiguousarray(inp["gla_w_in"], np.float32),
        "gla_w_gk_up": np.ascontiguousarray(inp["gla_w_gk_up"], np.float32),
        "gla_b_gk": np.ascontiguousarray(np.asarray(inp["gla_b_gk"], np.float32).reshape(2, 1, 512)),
        "gla_w_out": np.ascontiguousarray(inp["gla_w_out"], np.float32),
        "ffn_w1": np.ascontiguousarray(inp["ffn_w1"], np.float32),
        "ffn_w2": np.ascontiguousarray(inp["ffn_w2"], np.float32),
    }
    return common


_CACHE = {}


def kernel(**inputs):
    x = np.asarray(inputs["x"], np.float32)
    B, T, _ = x.shape
    key = (T,)
    if key not in _CACHE:
        _CACHE[key] = Builder(T).build()
    nc = _CACHE[key]
    common = host_inputs(inputs)
    zeros = {k: np.zeros_like(v) for k, v in common.items()}
    zeros["xT"] = np.zeros((D, T), np.float32)
    in_maps = []
    for c in range(NCORES):
        if c in ACTIVE:
            m = dict(common)
            m["xT"] = np.ascontiguousarray(x[ACTIVE.index(c)].T)
        else:
            m = zeros
        in_maps.append(m)
    res = run_bass_kernel_spmd(nc, in_maps, core_ids=list(range(NCORES)))
    out = np.empty((B, T, D), np.float32)
    for b in range(B):
        out[b] = res.results[ACTIVE[b]]["yT"].T
    return out
```

```python
import math
from contextlib import ExitStack

import numpy as np
import concourse.bass as bass
import concourse.mybir as mybir
from concourse.bass_utils import run_bass_kernel_spmd

F32 = mybir.dt.float32
BF16 = mybir.dt.bfloat16
ALU = mybir.AluOpType
AF = mybir.ActivationFunctionType
AX = mybir.AxisListType

D = 1024
DFF = 4096
EPS = 1e-6
DEPTH = 4
NCORES = 8
ACTIVE = (0, 1, 4, 5)

ENGS = ("pe", "act", "dve", "pool", "sp")
NDMA_SEMS = 8


class Res:
    __slots__ = ("writer", "readers")

    def __init__(self):
        self.writer = None
        self.readers = []


class Op:
    __slots__ = ("eng", "fn", "deps", "is_dma", "sig", "count", "dsem", "dval", "waits", "snap", "done")

    def __init__(self, eng, fn, is_dma):
        self.eng = eng
        self.fn = fn
        self.is_dma = is_dma
        self.deps = []
        self.sig = False
        self.count = 0
        self.dsem = None
        self.dval = 0
        self.waits = []
        self.snap = None
        self.done = False


class Sched:
    def __init__(self, nc, stack):
        self.nc = nc
        self.ops = {e: [] for e in ENGS}
        self.cnt = {e: 0 for e in ENGS}
        self.ndma = {e: 0 for e in ENGS}
        self.dma_hist = {e: [] for e in ENGS}
        self.known = {e: [0] * len(ENGS) for e in ENGS}
        self.kdma = {e: {} for e in ENGS}
        self.esem = {e: stack.enter_context(nc.semaphore(f"s_{e}")) for e in ENGS if e != "sp"}
        self.dsem = {}
        for e in ("sp", "pool", "act"):
            for k in range(NDMA_SEMS):
                self.dsem[(e, k)] = stack.enter_context(nc.semaphore(f"d_{e}{k}"))
        self.ninst = 0

    def res(self):
        return Res()

    def add(self, eng, fn, reads=(), writes=(), is_dma=False):
        op = Op(eng, fn, is_dma)
        deps = []
        for r in reads:
            if r.writer is not None:
                deps.append(r.writer)
            r.readers.append(op)
        for w in writes:
            if w.writer is not None:
                deps.append(w.writer)
            deps.extend(w.readers)
            w.writer = op
            w.readers = []
        seen = set()
        for d in deps:
            if d is op or d.done or id(d) in seen:
                continue
            seen.add(id(d))
            op.deps.append(d)
        self.ops[eng].append(op)
        return op

    def dma(self, q, out, in_, reads=(), writes=()):
        return self.add(q, lambda e: e.dma_start(out=out, in_=in_), reads, writes, is_dma=True)

    def barrier(self):
        last = []
        for e in ENGS:
            for o in reversed(self.ops[e]):
                if o.fn is not None and not o.is_dma:
                    last.append(o)
                    break
        rec = []
        for e in ENGS:
            dm = [o for o in self.ops[e] if o.is_dma][-NDMA_SEMS:]
            rec.extend(dm)
        for e in ENGS:
            op = Op(e, None, False)
            op.deps = list(last) + list(rec)
            self.ops[e].append(op)

    def flush(self):
        self.barrier()
        for e in ENGS:
            for op in self.ops[e]:
                for d in op.deps:
                    d.sig = True
        for e in ENGS:
            for op in self.ops[e]:
                if op.is_dma:
                    n = self.ndma[e]
                    op.dsem = (e, n % NDMA_SEMS)
                    op.dval = 16 * (n // NDMA_SEMS + 1)
                    self.dma_hist[e].append(op)
                    self.ndma[e] += 1
                elif op.sig:
                    self.cnt[e] += 1
                    op.count = self.cnt[e]
        eidx = {e: i for i, e in enumerate(ENGS)}
        for e in ENGS:
            known = self.known[e]
            kdma = self.kdma[e]
            hist = self.dma_hist[e]
            nd = len(hist) - sum(1 for o in self.ops[e] if o.is_dma)
            for op in self.ops[e]:
                deps = list(op.deps)
                if op.is_dma:
                    if nd >= NDMA_SEMS:
                        deps.append(hist[nd - NDMA_SEMS])
                    nd += 1
                best = {}
                for d in deps:
                    if d.is_dma:
                        if kdma.get(d.dsem, 0) >= d.dval:
                            continue
                        kdma[d.dsem] = d.dval
                        best[("dma", d.dsem)] = d.dval
                    else:
                        if d.eng == "pe" and e == "pe":
                            continue
                        j = eidx[d.eng]
                        if known[j] >= d.count:
                            continue
                        known[j] = d.count
                        best[("eng", d.eng)] = max(best.get(("eng", d.eng), 0), d.count)
                        if d.snap is not None:
                            for k in range(len(ENGS)):
                                if d.snap[k] > known[k]:
                                    known[k] = d.snap[k]
                op.waits = [(k[0], k[1], v) for k, v in best.items()]
                op.snap = tuple(known)
        nc = self.nc
        ops = self.ops
        esem, dsem = self.esem, self.dsem

        def run(eng_name):
            lst = ops[eng_name]

            def body(eng):
                for op in lst:
                    for kind, key, val in op.waits:
                        eng.wait_ge(dsem[key] if kind == "dma" else esem[key], val)
                    if op.fn is None:
                        continue
                    ins = op.fn(eng)
                    if op.is_dma:
                        ins.then_inc(dsem[op.dsem], 16)
                    elif op.sig:
                        ins.then_inc(esem[eng_name], 1)
            return body

        with nc.Block() as block:
            block.tensor(run("pe"))
            block.scalar(run("act"))
            block.vector(run("dve"))
            block.gpsimd(run("pool"))
            block.sync(run("sp"))
        for e in ENGS:
            self.ninst += len(self.ops[e])
            for op in self.ops[e]:
                op.done = True
                op.fn = None
                op.deps = []
            self.ops[e] = []


C_ONES = 0
C_TRI = 128
C_SU = 256
C_TRI4 = 384
C_INVC = 896
NCST = 960


def make_cst():
    c = np.zeros((128, NCST), np.float32)
    c[:, C_ONES:C_ONES + 128] = 1.0
    s = np.arange(128)
    tri = (s[:, None] <= s[None, :]).astype(np.float32)
    c[:, C_TRI:C_TRI + 128] = tri
    c[:, C_SU:C_SU + 128] = (s[:, None] > s[None, :]).astype(np.float32)
    for h in range(4):
        c[:, C_TRI4 + h * 128:C_TRI4 + (h + 1) * 128] = tri
    for g, w in enumerate((2, 4, 8, 16)):
        t = np.arange(16)
        c[:, C_INVC + g * 16:C_INVC + (g + 1) * 16] = 1.0 / np.minimum(t + 1, w)
    return c


V_NMIX = 0
V_NFFN = 32
V_NFIN = 64
V_PSCALE = 72
V_SUBLN = 80
V_GNORM = 82
NVEC = 98


def make_vecs(inp):
    v = np.zeros((128, NVEC), np.float32)

    def put(col, arr):
        a = np.asarray(arr, np.float32).reshape(-1, 128).T
        v[:, col:col + a.shape[1]] = a

    for i in range(DEPTH):
        put(V_NMIX + 8 * i, inp["norm_mix"][i])
        put(V_NFFN + 8 * i, inp["norm_ffn"][i])
    put(V_NFIN, inp["norm_final"])
    for e in range(2):
        put(V_PSCALE + 4 * e, inp["pool_scale"][e])
        put(V_SUBLN + e, inp["ab_subln"][e])
        put(V_GNORM + 8 * e, inp["gla_norm"][e].reshape(-1))
    return v


class Builder:
    def __init__(self, T, layers=(0, 1, 2, 3), final=True, dbg=False):
        self.T = T
        self.layers = tuple(layers)
        self.final = final
        nc = self.nc = bass.Bass("TRN2", target_bir_lowering=False)
        dt = nc.dram_tensor
        self.x_in = dt("xT", [D, T], F32, kind="ExternalInput").ap()
        self.y_out = dt("yT", [D, T], F32, kind="ExternalOutput").ap()
        self.cst_d = dt("cst", [128, NCST], F32, kind="ExternalInput").ap()
        self.vecs_d = dt("vecs", [128, NVEC], F32, kind="ExternalInput").ap()
        self.w = {}
        for name, shape in (("ab_w_in", [2, D, 2048]), ("ab_lambda", [2, 256]), ("pool_w", [2, 4, 128, 128]),
                            ("ab_w_out", [2, D, D]), ("gla_w_in", [2, D, 3088]), ("gla_w_gk_up", [2, 16, 512]),
                            ("gla_b_gk", [2, 1, 512]), ("gla_w_out", [2, D, D]), ("ffn_w1", [4, D, DFF]),
                            ("ffn_w2", [4, DFF, D])):
            self.w[name] = dt(name, shape, F32, kind="ExternalInput").ap()
        kw = {"kind": "ExternalOutput"} if dbg else {}
        self.xa = dt("xa", [D, T], F32, **kw).ap()
        self.xb = dt("xb", [D, T], F32, **kw).ap()
        self.qT = dt("qTs", [4, 128, T], BF16, **kw).ap()
        self.kT = dt("kTs", [4, 128, T], BF16, **kw).ap()
        self.Vs = dt("Vs", [4, 128, T // 128, 128], BF16, **kw).ap()
        self.mT = dt("mTs", [4, 128, T], BF16, **kw).ap()
        self.r_x = {}
        self.nsb = 0

    def sb(self, st, shape, dtp):
        self.nsb += 1
        return st.enter_context(self.nc.sbuf_tensor(f"sb{self.nsb}", shape, dtp))

    def psum(self, st):
        self.nsb += 1
        return [st.enter_context(self.nc.psum_tensor(f"ps{self.nsb}_{i}", [128, 512], F32)) for i in range(8)]

    def load_consts(self, S, st):
        cst = self.sb(st, [128, NCST], F32)
        vecs = self.sb(st, [128, NVEC], F32)
        r = S.res()
        S.dma("sp", cst[:], self.cst_d, writes=[r])
        S.dma("sp", vecs[:], self.vecs_d, writes=[r])
        return cst, vecs, r

    def load_weight(self, S, st_tmp, dst, src_rows, ncols, rres, wres, scale_cols=None, cw=1024):
        stg = [self.sb(st_tmp, [128, cw], F32) for _ in range(3)]
        r_stg = [S.res() for _ in range(3)]
        engs = ["pool", "dve", "act"]
        ci = 0
        for r, src in enumerate(src_rows):
            for c0 in range(0, ncols, cw):
                c1 = min(ncols, c0 + cw)
                b = ci % 3
                S.dma("sp", stg[b][:, :c1 - c0], src[:, c0:c1], writes=[r_stg[b]])
                out = dst[:, r, c0:c1]
                in_ = stg[b][:, :c1 - c0]
                eng = engs[ci % 3]
                sc = None if scale_cols is None else scale_cols[r]
                if sc is None:
                    if eng == "act":
                        S.add("act", (lambda o, i: lambda e: e.copy(o, i))(out, in_), reads=[r_stg[b]] + rres, writes=[wres])
                    else:
                        S.add(eng, (lambda o, i: lambda e: e.tensor_copy(o, i))(out, in_), reads=[r_stg[b]] + rres, writes=[wres])
                else:
                    if eng == "act":
                        S.add("act", (lambda o, i, s: lambda e: e.activation(o, i, AF.Copy, scale=s))(out, in_, sc),
                              reads=[r_stg[b]] + rres, writes=[wres])
                    else:
                        S.add(eng, (lambda o, i, s: lambda e: e.tensor_scalar(o, i, s, None, ALU.mult))(out, in_, sc),
                              reads=[r_stg[b]] + rres, writes=[wres])
                ci += 1

    def rmsnorm_a(self, S, xt_c, r_xt, sq, r_sq, ssum, r_ssum):
        S.add("act", lambda e: e.activation(sq[:], xt_c[:], AF.Square), reads=[r_xt], writes=[r_sq])
        S.add("dve", lambda e: e.tensor_reduce(ssum[:], sq[:].rearrange("p c t -> p t c"), AX.X, ALU.add),
              reads=[r_sq], writes=[r_ssum])

    def rmsnorm_b(self, S, xt_c, r_xt, NT, ssum, r_ssum, rstd, r_rstd, psb, r_psb, ones, r_c, h, r_h, nfeat_inv=1.0 / D):
        S.add("pe", lambda e: e.matmul(psb[:, :NT], ones, ssum[:], start=True, stop=True),
              reads=[r_c, r_ssum], writes=[r_psb])
        S.add("act", lambda e: e.activation(ssum[:], psb[:, :NT], AF.Ln, bias=float(EPS), scale=nfeat_inv),
              reads=[r_psb], writes=[r_ssum])
        S.add("act", lambda e: e.activation(rstd[:], ssum[:], AF.Exp, scale=-0.5), reads=[r_ssum], writes=[r_rstd])
        for c in range(8):
            eng = "dve" if c % 2 == 0 else "pool"
            S.add(eng, (lambda c: lambda e: e.tensor_tensor(h[:, c, :], xt_c[:, c, :], rstd[:], ALU.mult))(c),
                  reads=[r_xt, r_rstd], writes=[r_h[c]])

    def ffn_sweep(self, S, li, xin, r_xin, xout, r_xout, final):
        T = self.T
        FNT = 256
        nc = self.nc
        with ExitStack() as st:
            cst, vecs, r_c = self.load_consts(S, st)
            ones = cst[:, C_ONES:C_ONES + 128]
            w1sb = self.sb(st, [128, 8, DFF], BF16)
            w2sb = self.sb(st, [128, 32, D], BF16)
            r_w1, r_w2 = S.res(), S.res()
            with ExitStack() as st2:
                w1v = self.w["ffn_w1"][li].rearrange("(kc p) n -> p kc n", p=128)
                w2v = self.w["ffn_w2"][li].rearrange("(hc p) n -> p hc n", p=128)
                self.load_weight(S, st2, w1sb, [w1v[:, kc, :] for kc in range(8)], DFF, [r_c], r_w1,
                                 scale_cols=[vecs[:, V_NFFN + 8 * li + kc:V_NFFN + 8 * li + kc + 1] for kc in range(8)])
                self.load_weight(S, st2, w2sb, [w2v[:, hc, :] for hc in range(32)], D, [r_c], r_w2)
                S.flush()
            xt = [self.sb(st, [128, 8, FNT], F32) for _ in range(2)]
            xo = self.sb(st, [128, 8, FNT], F32)
            sq = self.sb(st, [128, 8, FNT], F32)
            ssum = self.sb(st, [128, FNT], F32)
            rstd = self.sb(st, [128, FNT], F32)
            h = self.sb(st, [128, 8, FNT], BF16)
            hid = self.sb(st, [128, 32, FNT], BF16)
            rl = [self.sb(st, [128, FNT], F32) for _ in range(4)]
            if final:
                sq2 = self.sb(st, [128, 8, FNT], F32)
                ssum2 = self.sb(st, [128, FNT], F32)
                rstd2 = self.sb(st, [128, FNT], F32)
                r_sq2, r_ssum2, r_rstd2 = S.res(), S.res(), S.res()
            pb = self.psum(st)
            R = S.res
            r_xt = [R(), R()]; r_xo = R(); r_sq = R(); r_ssum = R(); r_rstd = R()
            r_h = [R() for _ in range(8)]; r_hid = [R() for _ in range(32)]
            r_pb = [R() for _ in range(8)]; r_rl = [R() for _ in range(4)]
            xv = xin.rearrange("(c p) t -> p c t", p=128)
            yv = xout.rearrange("(c p) t -> p c t", p=128)
            ntile = T // FNT

            def load(n):
                S.dma("sp", xt[n % 2][:], xv[:, :, n * FNT:(n + 1) * FNT], reads=[r_xin], writes=[r_xt[n % 2]])

            def norm_a(n):
                b = n % 2
                self.rmsnorm_a(S, xt[b], r_xt[b], sq, r_sq, ssum, r_ssum)

            def norm_b(n):
                b = n % 2
                self.rmsnorm_b(S, xt[b], r_xt[b], FNT, ssum, r_ssum, rstd, r_rstd, pb[7], r_pb[7], ones, r_c, h, r_h)

            def up(n):
                for j in range(32):
                    bk = j % 4
                    for kc in range(8):
                        S.add("pe", (lambda j, kc, bk: lambda e: e.matmul(
                            pb[bk][:, :FNT], w1sb[:, kc, j * 128:(j + 1) * 128], h[:, kc, :],
                            start=(kc == 0), stop=(kc == 7)))(j, kc, bk), reads=[r_w1, r_h[kc]], writes=[r_pb[bk]])
                    S.add("act", (lambda j, bk: lambda e: e.activation(rl[j % 4][:], pb[bk][:, :FNT], AF.Relu))(j, bk),
                          reads=[r_pb[bk]], writes=[r_rl[j % 4]])
                    S.add("pool", (lambda j: lambda e: e.tensor_tensor(hid[:, j, :], rl[j % 4][:], rl[j % 4][:], ALU.mult))(j),
                          reads=[r_rl[j % 4]], writes=[r_hid[j]])

            def down(n):
                b = n % 2
                for oc in range(8):
                    bk = 4 + oc % 2
                    for hc in range(32):
                        S.add("pe", (lambda oc, hc, bk: lambda e: e.matmul(
                            pb[bk][:, :FNT], w2sb[:, hc, oc * 128:(oc + 1) * 128], hid[:, hc, :],
                            start=(hc == 0), stop=(hc == 31)))(oc, hc, bk), reads=[r_w2, r_hid[hc]], writes=[r_pb[bk]])
                    S.add("dve", (lambda oc, bk: lambda e: e.tensor_tensor(
                        xo[:, oc, :], pb[bk][:, :FNT], xt[b][:, oc, :], ALU.add))(oc, bk),
                        reads=[r_pb[bk], r_xt[b]], writes=[r_xo])
                if final:
                    S.add("act", lambda e: e.activation(sq2[:], xo[:], AF.Square), reads=[r_xo], writes=[r_sq2])
                    S.add("dve", lambda e: e.tensor_reduce(ssum2[:], sq2[:].rearrange("p c t -> p t c"), AX.X, ALU.add),
                          reads=[r_sq2], writes=[r_ssum2])
                    S.add("pe", lambda e: e.matmul(pb[6][:, :FNT], ones, ssum2[:], start=True, stop=True),
                          reads=[r_c, r_ssum2], writes=[r_pb[6]])
                    S.add("act", lambda e: e.activation(ssum2[:], pb[6][:, :FNT], AF.Ln, bias=float(EPS), scale=1.0 / D),
                          reads=[r_pb[6]], writes=[r_ssum2])
                    S.add("act", lambda e: e.activation(rstd2[:], ssum2[:], AF.Exp, scale=-0.5), reads=[r_ssum2], writes=[r_rstd2])
                    for c in range(8):
                        S.add("dve", (lambda c: lambda e: e.scalar_tensor_tensor(
                            sq2[:, c, :], xo[:, c, :], vecs[:, V_NFIN + c:V_NFIN + c + 1], rstd2[:], ALU.mult, ALU.mult))(c),
                            reads=[r_xo, r_rstd2, r_c], writes=[r_sq2])
                    S.dma("pool", yv[:, :, n * FNT:(n + 1) * FNT], sq2[:], reads=[r_sq2], writes=[r_xout])
                else:
                    S.dma("pool", yv[:, :, n * FNT:(n + 1) * FNT], xo[:], reads=[r_xo], writes=[r_xout])

            load(0)
            if ntile > 1:
                load(1)
            norm_a(0)
            norm_b(0)
            for n in range(ntile):
                if n + 1 < ntile:
                    norm_a(n + 1)
                up(n)
                if n + 1 < ntile:
                    norm_b(n + 1)
                down(n)
                if n + 2 < ntile:
                    load(n + 2)
            S.flush()

    def a1_sweep(self, S, li, xin, r_xin):
        T = self.T
        NT = 512
        e_ = li // 2
        with ExitStack() as st:
            cst, vecs, r_c = self.load_consts(S, st)
            ones = cst[:, C_ONES:C_ONES + 128]
            win = self.sb(st, [128, 8, 2048], BF16)
            pw = self.sb(st, [128, 4, 128], BF16)
            r_win, r_pw = S.res(), S.res()
            with ExitStack() as st2:
                wv = self.w["ab_w_in"][e_].rearrange("(kc p) n -> p kc n", p=128)
                self.load_weight(S, st2, win, [wv[:, kc, :] for kc in range(8)], 2048, [r_c], r_win,
                                 scale_cols=[vecs[:, V_NMIX + 8 * li + kc:V_NMIX + 8 * li + kc + 1] for kc in range(8)])
                pwv = self.w["pool_w"][e_].rearrange("g c d -> c g d")
                self.load_weight(S, st2, pw, [pwv[:, g, :] for g in range(4)], 128, [r_c], r_pw, cw=128)
                S.flush()
            xt = [self.sb(st, [128, 8, NT], F32) for _ in range(2)]
            sq = self.sb(st, [128, 8, NT], F32)
            ssum = self.sb(st, [128, NT], F32)
            rstd = self.sb(st, [128, NT], F32)
            hh_ = [self.sb(st, [128, 8, NT], BF16) for _ in range(2)]
            qk = [self.sb(st, [128, 8, NT], BF16) for _ in range(2)]
            vt = [self.sb(st, [128, 4, 512], BF16) for _ in range(2)]
            uext = [self.sb(st, [128, 4, 16 + NT], F32) for _ in range(2)]
            ta = self.sb(st, [128, 16 + NT], F32)
            tb = self.sb(st, [128, 16 + NT], F32)
            rr = self.sb(st, [128, 4, NT], BF16)
            mo = [self.sb(st, [128, 4, NT], BF16) for _ in range(2)]
            pb = self.psum(st)
            R = S.res
            r_xt = [R(), R()]; r_sq = R(); r_ssum = R(); r_rstd = R()
            r_hh = [[R() for _ in range(8)] for _ in range(2)]; r_pb = [R() for _ in range(8)]
            r_qk = [R(), R()]; r_vt = [R(), R()]; r_ue = [[R() for _ in range(4)] for _ in range(2)]
            r_ta, r_tb = R(), R(); r_rr = [R() for _ in range(4)]; r_mo = [R(), R()]
            r_sc = self.r_scr
            xv = xin.rearrange("(c p) t -> p c t", p=128)
            ntile = T // NT
            qTv = self.qT.rearrange("h p t -> p h t")
            kTv = self.kT.rearrange("h p t -> p h t")
            mTv = self.mT.rearrange("g p t -> p g t")
            Vv = self.Vs.rearrange("h p k v -> p h k v")

            def load(n):
                S.dma("sp", xt[n % 2][:], xv[:, :, n * NT:(n + 1) * NT], reads=[r_xin], writes=[r_xt[n % 2]])

            def norm_a(n):
                b = n % 2
                self.rmsnorm_a(S, xt[b], r_xt[b], sq, r_sq, ssum, r_ssum)

            def norm_b(n):
                b = n % 2
                self.rmsnorm_b(S, xt[b], r_xt[b], NT, ssum, r_ssum, rstd, r_rstd, pb[7], r_pb[7], ones, r_c, hh_[b], r_hh[b])

            for g in range(4):
                S.add("pool", (lambda g: lambda e: e.memset(uext[0][:, g, 0:16], 0.0))(g), writes=[r_ue[0][g]])
            S.add("pool", lambda e: e.memset(ta[:], 0.0), writes=[r_ta])
            S.add("pool", lambda e: e.memset(tb[:], 0.0), writes=[r_tb])

            def proj(n, part):
                b = n % 2
                t0 = n * NT
                h = hh_[b]
                r_h = r_hh[b]
                if part == 1:
                    proj_b(n, b, t0, h, r_h)
                    return
                for oc in range(8):
                    bk = oc % 3
                    for kc in range(8):
                        S.add("pe", (lambda oc, kc, bk: lambda e: e.matmul(
                            pb[bk][:, :NT], win[:, kc, oc * 128:(oc + 1) * 128], h[:, kc, :],
                            start=(kc == 0), stop=(kc == 7)))(oc, kc, bk), reads=[r_win, r_h[kc]], writes=[r_pb[bk]])
                    S.add("act", (lambda oc, bk: lambda e: e.copy(qk[b][:, oc, :], pb[bk][:, :NT]))(oc, bk),
                          reads=[r_pb[bk]], writes=[r_qk[b]])
                S.dma("pool", qTv[:, :, t0:t0 + NT], qk[b][:, 0:4, :], reads=[r_qk[b]], writes=[r_sc])
                S.dma("pool", kTv[:, :, t0:t0 + NT], qk[b][:, 4:8, :], reads=[r_qk[b]], writes=[r_sc])

            def proj_b(n, b, t0, h, r_h):
                for s in range(4):
                    bk = 3 + s % 2
                    for kc in range(8):
                        S.add("pe", (lambda s, kc, bk: lambda e: e.matmul(
                            pb[bk][:, :512], h[:, kc, s * 128:(s + 1) * 128], win[:, kc, 1024:1536],
                            start=(kc == 0), stop=(kc == 7)))(s, kc, bk), reads=[r_win, r_h[kc]], writes=[r_pb[bk]])
                    S.add("dve", (lambda s, bk: lambda e: e.tensor_copy(vt[b][:, s, :], pb[bk][:, :512]))(s, bk),
                          reads=[r_pb[bk]], writes=[r_vt[b]])
                for hh in range(4):
                    S.dma("pool", self.Vs[hh, :, 4 * n:4 * n + 4, :], vt[b][:, :, hh * 128:(hh + 1) * 128],
                          reads=[r_vt[b]], writes=[r_sc])
                for g in range(4):
                    bk = 5 + g % 2
                    oc = 12 + g
                    for kc in range(8):
                        S.add("pe", (lambda oc, kc, bk: lambda e: e.matmul(
                            pb[bk][:, :NT], win[:, kc, oc * 128:(oc + 1) * 128], h[:, kc, :],
                            start=(kc == 0), stop=(kc == 7)))(oc, kc, bk), reads=[r_win, r_h[kc]], writes=[r_pb[bk]])
                    S.add("act", (lambda g, bk: lambda e: e.copy(uext[b][:, g, 16:16 + NT], pb[bk][:, :NT]))(g, bk),
                          reads=[r_pb[bk]], writes=[r_ue[b][g]])

            def pool(n):
                b = n % 2
                t0 = n * NT
                W = 16 + NT
                for g in range(4):
                    w = 2 ** (g + 1)
                    src = uext[b][:, g, :]
                    r_src = r_ue[b][g]
                    bufs = [(ta, r_ta), (tb, r_tb)]
                    sh = 1
                    k = 0
                    cur, r_cur = src, r_src
                    while sh < w:
                        dst, r_dst = bufs[k % 2]
                        S.add("pool", (lambda dst, cur, sh: lambda e: e.tensor_tensor(
                            dst[:, sh:W], cur[:, sh:W], cur[:, 0:W - sh], ALU.add))(dst, cur, sh),
                            reads=[r_cur], writes=[r_dst])
                        cur, r_cur = dst[:], r_dst
                        sh *= 2
                        k += 1
                    S.add("dve", (lambda g, cur, w: lambda e: e.scalar_tensor_tensor(
                        rr[:, g, :], cur[:, 16:W], 1.0 / w, uext[b][:, g, 16:W], ALU.mult, ALU.subtract))(g, cur, w),
                        reads=[r_cur, r_ue[b][g]], writes=[r_rr[g]])
                    if n == 0:
                        S.add("dve", (lambda g, cur: lambda e: e.tensor_tensor(
                            ta[:, 0:16], cur[:, 16:32], cst[:, C_INVC + g * 16:C_INVC + (g + 1) * 16], ALU.mult))(g, cur),
                            reads=[r_cur, r_c], writes=[r_ta])
                        S.add("dve", (lambda g: lambda e: e.tensor_tensor(
                            rr[:, g, 0:16], ta[:, 0:16], uext[b][:, g, 16:32], ALU.subtract))(g),
                            reads=[r_ta, r_ue[b][g]], writes=[r_rr[g]])
                    if n + 1 < ntile:
                        S.add("pool", (lambda g: lambda e: e.tensor_copy(uext[1 - b][:, g, 0:16], uext[b][:, g, NT:NT + 16]))(g),
                              reads=[r_ue[b][g]], writes=[r_ue[1 - b][g]])

            def pool_mm(n):
                b = n % 2
                t0 = n * NT
                for g in range(4):
                    bk = 5 + g % 2
                    S.add("pe", (lambda g, bk: lambda e: e.matmul(pb[bk][:, :NT], pw[:, g, :], rr[:, g, :], start=True, stop=True))(g, bk),
                          reads=[r_pw, r_rr[g]], writes=[r_pb[bk]])
                    S.add("act", (lambda g, bk: lambda e: e.activation(
                        mo[b][:, g, :], pb[bk][:, :NT], AF.Copy, scale=vecs[:, V_PSCALE + 4 * e_ + g:V_PSCALE + 4 * e_ + g + 1]))(g, bk),
                        reads=[r_pb[bk], r_c], writes=[r_mo[b]])
                S.dma("pool", mTv[:, :, t0:t0 + NT], mo[b][:], reads=[r_mo[b]], writes=[r_sc])

            load(0)
            if ntile > 1:
                load(1)
            norm_a(0)
            norm_b(0)
            for n in range(ntile):
                if n + 1 < ntile:
                    norm_a(n + 1)
                proj(n, 0)
                if n + 1 < ntile:
                    norm_b(n + 1)
                proj(n, 1)
                if n > 0:
                    pool_mm(n - 1)
                pool(n)
                if n + 2 < ntile:
                    load(n + 2)
            pool_mm(ntile - 1)
            S.flush()

    def a2_sweep(self, S, li, xin, r_xin, xout, r_xout):
        T = self.T
        NT = 512
        e_ = li // 2
        lam_init = 0.8 - 0.6 * math.exp(-0.3 * li)
        KG = 16
        with ExitStack() as st:
            cst, vecs, r_c = self.load_consts(S, st)
            ones = cst[:, C_ONES:C_ONES + 128]
            wout = self.sb(st, [128, 8, D], BF16)
            r_wout = S.res()
            onesb = self.sb(st, [128, 128], BF16)
            lamt = self.sb(st, [128, 256], F32)
            lprod = self.sb(st, [128, 128], F32)
            lsum = self.sb(st, [128, 2], F32)
            neglam = self.sb(st, [128, 1], F32)
            gsub = self.sb(st, [128, 1], F32)
            r_l = S.res()
            ediag = [[self.sb(st, [128, NT], BF16) for _ in range(2)] for _ in range(4)]
            r_ed = [[S.res() for _ in range(2)] for _ in range(4)]
            with ExitStack() as st2:
                wv = self.w["ab_w_out"][e_].rearrange("(kc p) n -> p kc n", p=128)
                self.load_weight(S, st2, wout, [wv[:, kc, :] for kc in range(8)], D, [r_c], r_wout)
                S.add("dve", lambda e: e.tensor_copy(onesb[:], ones), reads=[r_c], writes=[r_l])
                S.dma("sp", lamt[:], self.w["ab_lambda"][e_:e_ + 1, :].partition_broadcast(128), writes=[r_l])
                S.add("dve", lambda e: e.tensor_tensor(lprod[:, 0:64], lamt[:, 0:64], lamt[:, 64:128], ALU.mult), reads=[r_l], writes=[r_l])
                S.add("dve", lambda e: e.tensor_tensor(lprod[:, 64:128], lamt[:, 128:192], lamt[:, 192:256], ALU.mult), reads=[r_l], writes=[r_l])
                S.add("dve", lambda e: e.tensor_reduce(lsum[:], lprod[:].rearrange("p (a d) -> p a d", a=2), AX.X, ALU.add), reads=[r_l], writes=[r_l])
                S.add("act", lambda e: e.activation(lsum[:], lsum[:], AF.Exp), reads=[r_l], writes=[r_l])
                S.add("dve", lambda e: e.tensor_tensor(neglam[:], lsum[:, 1:2], lsum[:, 0:1], ALU.subtract), reads=[r_l], writes=[r_l])
                S.add("dve", lambda e: e.tensor_scalar(neglam[:], neglam[:], -float(lam_init), None, ALU.add), reads=[r_l], writes=[r_l])
                S.add("dve", lambda e: e.tensor_scalar(gsub[:], vecs[:, V_SUBLN + e_:V_SUBLN + e_ + 1], float(1.0 - lam_init), None, ALU.mult),
                      reads=[r_c], writes=[r_l])
                for j in range(4):
                    for a in range(2):
                        S.add("pool", (lambda j, a: lambda e: e.memset(ediag[j][a][:], 0.0))(j, a), writes=[r_ed[j][a]])
                S.flush()
            xt = [self.sb(st, [128, 8, NT], F32) for _ in range(2)]
            xo = self.sb(st, [128, 8, NT], F32)
            qt = [self.sb(st, [128, 4, NT], BF16) for _ in range(2)]
            mo = [self.sb(st, [128, 4, NT], BF16) for _ in range(2)]
            ao = self.sb(st, [128, 4, NT], BF16)
            kb = [self.sb(st, [128, KG * 128], BF16) for _ in range(3)]
            vb = [self.sb(st, [128, KG, 128], BF16) for _ in range(3)]
            eb = [[self.sb(st, [128, NT], BF16) for _ in range(2)] for _ in range(3)]
            rl1 = self.sb(st, [128, NT], F32); rl2 = self.sb(st, [128, NT], F32)
            t1 = self.sb(st, [128, NT], F32); t2 = self.sb(st, [128, NT], F32)
            A = self.sb(st, [128, NT], F32); asq = self.sb(st, [128, NT], F32)
            sd = self.sb(st, [128, NT], F32); rs = self.sb(st, [128, NT], F32)
            acc2 = self.sb(st, [128, NT], F32)
            r_acc2 = S.res()
            pb = self.psum(st)
            R = S.res
            r_xt = [R(), R()]; r_xo = R(); r_qt = [R(), R()]; r_mo = [R(), R()]; r_ao = [R() for _ in range(4)]
            r_kb = [R() for _ in range(3)]; r_vb = [R() for _ in range(3)]
            r_eb = [[R(), R()] for _ in range(3)]
            r_ep = R()
            r_pb = [R() for _ in range(8)]
            r_sc = self.r_scr
            xv = xin.rearrange("(c p) t -> p c t", p=128)
            yv = xout.rearrange("(c p) t -> p c t", p=128)
            qTv = self.qT.rearrange("h p t -> p h t")
            mTv = self.mT.rearrange("g p t -> p g t")
            nblk = T // NT
            PS_S = [(0, 1), (2, 3)]
            PO1, PO2, PL1, PL2 = 4, 5, 6, 7
            grp_ctr = [0]

            def load(n):
                t0 = n * NT
                b = n % 2
                S.dma("sp", xt[b][:], xv[:, :, t0:t0 + NT], reads=[r_xin], writes=[r_xt[b]])
                S.dma("sp", qt[b][:], qTv[:, :, t0:t0 + NT], reads=[r_sc], writes=[r_qt[b]])
                S.dma("sp", mo[b][:], mTv[:, :, t0:t0 + NT], reads=[r_sc], writes=[r_mo[b]])

            def load_kv(h, g0, ng):
                i = grp_ctr[0] % 3
                grp_ctr[0] += 1
                S.dma("sp", kb[i][:, :ng * 128], self.kT[h, :, g0 * 128:(g0 + ng) * 128], reads=[r_sc], writes=[r_kb[i]])
                S.dma("sp", vb[i][:, :ng, :], self.Vs[h, :, g0:g0 + ng, :], reads=[r_sc], writes=[r_vb[i]])
                return i

            def attn_head(qb, h, pending):
                b = qb % 2
                nk = 4 * (qb + 1)
                groups = []
                for g0 in range(0, nk, KG):
                    groups.append((g0, min(KG, nk - g0)))
                gbuf = {}
                for gi in range(min(2, len(groups))):
                    gbuf[gi] = load_kv(h, *groups[gi])
                ectr = [0]
                pend = []

                def qk(kt):
                    gi, lk = kt // KG, kt % KG
                    if gi not in gbuf:
                        gbuf[gi] = load_kv(h, *groups[gi])
                    i = gbuf[gi]
                    j = kt - 4 * qb
                    c0 = 128 * j if j >= 0 else 0
                    sa, sb_ = PS_S[kt % 2]
                    S.add("pe", lambda e: e.matmul(pb[sa][:, c0:NT], kb[i][0:64, lk * 128:(lk + 1) * 128], qt[b][0:64, h, c0:NT],
                                                   start=True, stop=True), reads=[r_kb[i], r_qt[b]], writes=[r_pb[sa]])
                    S.add("pe", lambda e: e.matmul(pb[sb_][:, c0:NT], kb[i][64:128, lk * 128:(lk + 1) * 128], qt[b][64:128, h, c0:NT],
                                                   start=True, stop=True), reads=[r_kb[i], r_qt[b]], writes=[r_pb[sb_]])
                    if j < 0:
                        k3 = ectr[0] % 3
                        ectr[0] += 1
                        e1, e2 = eb[k3][0], eb[k3][1]
                        re1, re2 = r_eb[k3][0], r_eb[k3][1]
                        S.add("act", lambda e: e.activation(e1[:, :], pb[sa][:, :NT], AF.Exp, scale=0.125), reads=[r_pb[sa]], writes=[re1])
                        S.add("act", lambda e: e.activation(e2[:, :], pb[sb_][:, :NT], AF.Exp, scale=0.125), reads=[r_pb[sb_]], writes=[re2])
                    else:
                        e1, e2 = ediag[j][0], ediag[j][1]
                        re1, re2 = r_ed[j][0], r_ed[j][1]
                        for (et, re_, pbn) in ((e1, re1, sa), (e2, re2, sb_)):
                            S.add("act", (lambda et, pbn: lambda e: e.activation(et[0:64, c0:NT], pb[pbn][0:64, c0:NT], AF.Exp, scale=0.125))(et, pbn),
                                  reads=[r_pb[pbn]], writes=[re_])
                            S.add("act", (lambda et, pbn: lambda e: e.activation(et[64:128, c0 + 64:NT], pb[pbn][64:128, c0 + 64:NT], AF.Exp, scale=0.125))(et, pbn),
                                  reads=[r_pb[pbn]], writes=[re_])
                    return (kt, c0, e1, e2, re1, re2, i, lk)

                def pv(item):
                    kt, c0, e1, e2, re1, re2, i, lk = item
                    first, last = (kt == 0), (kt == nk - 1)
                    S.add("pe", lambda e: e.matmul(pb[PO1][:, c0:NT], vb[i][:, lk, :], e1[:, c0:NT], start=first, stop=last, skip_group_check=True),
                          reads=[r_vb[i], re1], writes=[r_pb[PO1]])
                    S.add("pe", lambda e: e.matmul(pb[PO2][:, c0:NT], vb[i][:, lk, :], e2[:, c0:NT], start=first, stop=last, skip_group_check=True),
                          reads=[r_vb[i], re2], writes=[r_pb[PO2]])
                    S.add("pe", lambda e: e.matmul(pb[PL1][:, c0:NT], onesb[:], e1[:, c0:NT], start=first, stop=last, skip_group_check=True),
                          reads=[r_l, re1], writes=[r_pb[PL1]])
                    if first:
                        S.add("dve", lambda e: e.tensor_copy(acc2[:, c0:NT], e2[:, c0:NT]), reads=[re2], writes=[r_acc2])
                    else:
                        S.add("dve", lambda e: e.tensor_tensor(acc2[:, c0:NT], acc2[:, c0:NT], e2[:, c0:NT], ALU.add),
                              reads=[re2, r_acc2], writes=[r_acc2])

                prev = qk(0)
                for kt in range(1, nk):
                    cur = qk(kt)
                    if kt == 1 and pending is not None:
                        pending[0]()
                    pv(prev)
                    if kt == 3 and pending is not None:
                        pending[1]()
                    prev = cur
                pv(prev)
                return (lambda: ep_a(qb, h, nk), lambda: ep_b(qb, h, nk))

            def ep_a(qb, h, nk):
                S.add("pe", lambda e: e.matmul(pb[PL2][:, :NT], ones, acc2[:], start=True, stop=True),
                      reads=[r_c, r_acc2], writes=[r_pb[PL2]])
                S.add("act", lambda e: e.activation(rl1[:], pb[PL1][:, :NT], AF.Ln), reads=[r_pb[PL1]], writes=[r_ep])
                S.add("act", lambda e: e.activation(rl1[:], rl1[:], AF.Exp, scale=-1.0), reads=[r_ep], writes=[r_ep])
                S.add("act", lambda e: e.activation(rl2[:], pb[PL2][:, :NT], AF.Ln), reads=[r_pb[PL2]], writes=[r_ep])
                S.add("act", lambda e: e.activation(rl2[:], rl2[:], AF.Exp, scale=-1.0), reads=[r_ep], writes=[r_ep])
                S.add("dve", lambda e: e.tensor_tensor(t1[:], pb[PO1][:, :NT], rl1[:], ALU.mult), reads=[r_pb[PO1], r_ep], writes=[r_ep])
                S.add("dve", lambda e: e.tensor_tensor(t2[:], pb[PO2][:, :NT], rl2[:], ALU.mult), reads=[r_pb[PO2], r_ep], writes=[r_ep])
                S.add("dve", lambda e: e.scalar_tensor_tensor(A[:], t2[:], neglam[:, 0:1], t1[:], ALU.mult, ALU.add), reads=[r_ep, r_l], writes=[r_ep])
                S.add("pool", lambda e: e.tensor_tensor(asq[:], A[:], A[:], ALU.mult), reads=[r_ep], writes=[r_ep])

            def ep_b(qb, h, nk):
                sa = PS_S[0][0]
                S.add("pe", lambda e: e.matmul(pb[sa][:, :NT], ones, asq[:], start=True, stop=True), reads=[r_c, r_ep], writes=[r_pb[sa]])
                S.add("act", lambda e: e.activation(sd[:], pb[sa][:, :NT], AF.Ln, bias=float(EPS), scale=1.0 / 128), reads=[r_pb[sa]], writes=[r_ep])
                S.add("act", lambda e: e.activation(rs[:], sd[:], AF.Exp, scale=-0.5), reads=[r_ep], writes=[r_ep])
                S.add("dve", lambda e: e.scalar_tensor_tensor(ao[:, h, :], A[:], gsub[:, 0:1], rs[:], ALU.mult, ALU.mult),
                      reads=[r_ep, r_l], writes=[r_ao[h]])
                if h == 3:
                    outproj(qb)

            def outproj(qb):
                b = qb % 2
                t0 = qb * NT
                for oc in range(8):
                    bk = PS_S[oc % 2][1]
                    for c in range(8):
                        if c < 4:
                            S.add("pe", (lambda oc, c, bk: lambda e: e.matmul(pb[bk][:, :NT], wout[:, c, oc * 128:(oc + 1) * 128], ao[:, c, :],
                                                                               start=(c == 0), stop=False))(oc, c, bk),
                                  reads=[r_wout, r_ao[c]], writes=[r_pb[bk]])
                        else:
                            S.add("pe", (lambda oc, c, bk: lambda e: e.matmul(pb[bk][:, :NT], wout[:, c, oc * 128:(oc + 1) * 128], mo[b][:, c - 4, :],
                                                                               start=False, stop=(c == 7)))(oc, c, bk),
                                  reads=[r_wout, r_mo[b]], writes=[r_pb[bk]])
                    S.add("dve", (lambda oc, bk: lambda e: e.tensor_tensor(xo[:, oc, :], pb[bk][:, :NT], xt[b][:, oc, :], ALU.add))(oc, bk),
                          reads=[r_pb[bk], r_xt[b]], writes=[r_xo])
                S.dma("pool", yv[:, :, t0:t0 + NT], xo[:], reads=[r_xo], writes=[r_xout])

            load(0)
            pending = None
            for qb in range(nblk):
                for h in range(4):
                    if h == 1 and qb + 1 < nblk:
                        load(qb + 1)
                    pending = attn_head(qb, h, pending)
            pending[0]()
            pending[1]()
            S.flush()

    def gla_sweep(self, S, li, xin, r_xin, xout, r_xout):
        T = self.T
        NT = 512
        o_ = li // 2
        scale = 128.0 ** -0.5
        with ExitStack() as st:
            cst, vecs, r_c = self.load_consts(S, st)
            ones = cst[:, C_ONES:C_ONES + 128]
            tri = cst[:, C_TRI:C_TRI + 128]
            su = cst[:, C_SU:C_SU + 128]
            tri4 = cst[:, C_TRI4:C_TRI4 + 512]
            win = self.sb(st, [128, 8, 3104], BF16)
            wout = self.sb(st, [128, 8, D], BF16)
            wgk = self.sb(st, [64, 1, 512], BF16)
            r_win, r_wout, r_wgk = S.res(), S.res(), S.res()
            with ExitStack() as st2:
                wv = self.w["gla_w_in"][o_].rearrange("(kc p) n -> p kc n", p=128)
                self.load_weight(S, st2, win, [wv[:, kc, :] for kc in range(8)], 3088, [r_c], r_win,
                                 scale_cols=[vecs[:, V_NMIX + 8 * li + kc:V_NMIX + 8 * li + kc + 1] for kc in range(8)], cw=1024)
                wo = self.w["gla_w_out"][o_].rearrange("(kc p) n -> p kc n", p=128)
                self.load_weight(S, st2, wout, [wo[:, kc, :] for kc in range(8)], D, [r_c], r_wout,
                                 scale_cols=[vecs[:, V_GNORM + 8 * o_ + kc:V_GNORM + 8 * o_ + kc + 1] for kc in range(8)])
                stg = self.sb(st2, [64, 512], F32)
                r_s = S.res()
                S.add("pool", lambda e: e.memset(stg[:], 0.0), writes=[r_s])
                S.add("pool", lambda e: e.memset(win[:, :, 3088:3104], 0.0), writes=[r_win])
                S.dma("sp", stg[0:16, :], self.w["gla_w_gk_up"][o_], reads=[r_s], writes=[r_s])
                S.dma("sp", stg[32:33, :], self.w["gla_b_gk"][o_], reads=[r_s], writes=[r_s])
                S.add("dve", lambda e: e.tensor_copy(wgk[:, 0, :], stg[:]), reads=[r_s], writes=[r_wgk])
                S.flush()
            xt = [self.sb(st, [128, 8, NT], F32) for _ in range(2)]
            sq = self.sb(st, [128, 8, NT], BF16)
            ssum = self.sb(st, [128, NT], F32)
            rstd = self.sb(st, [128, NT], F32)
            h = self.sb(st, [128, 8, NT], BF16)
            qkf = self.sb(st, [128, 8, NT], F32)
            qf = qkf[:, 0:4, :]
            kf = qkf[:, 4:8, :]
            xo = qkf
            gate = self.sb(st, [128, 8, NT], BF16)
            gl = self.sb(st, [64, NT], BF16)
            ktok = self.sb(st, [128, 512], F32)
            vtok = [self.sb(st, [128, 1024], BF16) for _ in range(2)]
            ez = self.sb(st, [128, 512], F32)
            gtok = self.sb(st, [128, 512], F32)
            ebp = [self.sb(st, [128, 512], F32) for _ in range(2)]
            enb = gtok
            er = ez
            qd = [self.sb(st, [128, 4, 128], BF16) for _ in range(2)]
            kd = [self.sb(st, [128, 4, 128], BF16) for _ in range(2)]
            kl = [self.sb(st, [128, 512], BF16) for _ in range(2)]
            am = self.sb(st, [128, 512], BF16)
            Sf = self.sb(st, [128, 4, 256], F32)
            Sb = [self.sb(st, [128, 4, 256], BF16) for _ in range(2)]
            osq = self.sb(st, [128, 8, 128], F32)
            sdn = self.sb(st, [128, 512], F32)
            rsn = self.sb(st, [128, 8, 128], F32)
            tg = self.sb(st, [128, 8, 128], F32)
            og = self.sb(st, [128, 8, NT], BF16)
            pb = self.psum(st)
            R = S.res
            r_xt = [R(), R()]; r_sq = R(); r_ssum = R(); r_rstd = R()
            r_h = [R() for _ in range(8)]; r_pb = [R() for _ in range(8)]
            r_qf, r_kf, r_gate, r_gl = R(), R(), R(), R()
            r_ktok, r_ez, r_gtok = R(), R(), R(); r_enb = r_gtok; r_er = r_ez
            r_vtok = [R(), R()]; r_ebp = [R(), R()]
            r_qd = [R(), R()]; r_kd = [R(), R()]; r_kl = [R(), R()]; r_am = R()
            r_Sf = [R() for _ in range(4)]; r_Sb = [[R() for _ in range(4)] for _ in range(2)]
            r_osq, r_tg, r_og = R(), R(), R(); r_sdn = R(); r_rsn = R()
            xv = xin.rearrange("(c p) t -> p c t", p=128)
            yv = xout.rearrange("(c p) t -> p c t", p=128)
            ntile = T // NT
            S.add("pool", lambda e: e.memset(Sf[:], 0.0), writes=r_Sf)
            S.add("pool", lambda e: e.memset(Sb[0][:], 0.0), writes=r_Sb[0])
            S.add("pool", lambda e: e.memset(Sb[1][:], 0.0), writes=r_Sb[1])
            S.add("pool", lambda e: e.memset(gl[:], 0.0), writes=[r_gl])
            S.add("pool", lambda e: e.memset(gl[32:64, :], 1.0), writes=[r_gl])

            def load(n):
                S.dma("sp", xt[n % 2][:], xv[:, :, n * NT:(n + 1) * NT], reads=[r_xin], writes=[r_xt[n % 2]])

            def prenorm(n):
                b = n % 2
                S.add("act", lambda e: e.activation(sq[:], xt[b][:], AF.Square), reads=[r_xt[b]], writes=[r_sq])
                S.add("dve", lambda e: e.tensor_reduce(ssum[:], sq[:].rearrange("p c t -> p t c"), AX.X, ALU.add),
                      reads=[r_sq], writes=[r_ssum])
                S.add("pe", lambda e: e.matmul(pb[1][:, :NT], ones, ssum[:], start=True, stop=True),
                      reads=[r_c, r_ssum], writes=[r_pb[1]])
                S.add("act", lambda e: e.activation(ssum[:], pb[1][:, :NT], AF.Ln, bias=float(EPS), scale=1.0 / D),
                      reads=[r_pb[1]], writes=[r_ssum])
                S.add("act", lambda e: e.activation(rstd[:], ssum[:], AF.Exp, scale=-0.5), reads=[r_ssum], writes=[r_rstd])

            def hmul(n):
                b = n % 2
                for c in range(8):
                    eng = "dve" if c % 2 == 0 else "pool"
                    S.add(eng, (lambda c: lambda e: e.tensor_tensor(h[:, c, :], xt[b][:, c, :], rstd[:], ALU.mult))(c),
                          reads=[r_xt[b], r_rstd], writes=[r_h[c]])

            def fm(oc0, ncols, dst_fn, bk):
                for kc in range(8):
                    S.add("pe", (lambda kc: lambda e: e.matmul(pb[bk][:ncols, :NT], win[:, kc, oc0:oc0 + ncols], h[:, kc, :],
                                                               start=(kc == 0), stop=(kc == 7)))(kc),
                          reads=[r_win, r_h[kc]], writes=[r_pb[bk]])
                dst_fn(bk)

            def proj_fm(n):
                k = 0
                for c in range(4):
                    bk = k % 2; k += 1
                    fm(c * 128, 128, (lambda c: lambda bk: S.add("act", lambda e: e.copy(qf[:, c, :], pb[bk][:, :NT]),
                                                                  reads=[r_pb[bk]], writes=[r_qf]))(c), bk)
                for c in range(4):
                    bk = k % 2; k += 1
                    fm(512 + c * 128, 128, (lambda c: lambda bk: S.add("dve", lambda e: e.tensor_copy(kf[:, c, :], pb[bk][:, :NT]),
                                                                        reads=[r_pb[bk]], writes=[r_kf]))(c), bk)
                for c in range(8):
                    bk = k % 2; k += 1
                    fm(2048 + c * 128, 128, (lambda c: lambda bk: S.add("act", lambda e: e.activation(gate[:, c, :], pb[bk][:, :NT], AF.Silu),
                                                                         reads=[r_pb[bk]], writes=[r_gate]))(c), bk)
                bk = k % 2; k += 1
                fm(3072, 32, lambda bk: S.add("dve", lambda e: e.tensor_copy(gl[0:16, :], pb[bk][0:16, :NT]),
                                              reads=[r_pb[bk]], writes=[r_gl]), bk)

            def prep(n, s, cc):
                p2 = cc % 2
                ssl = slice(s * 128, (s + 1) * 128)
                vt_, r_vt_ = vtok[p2], r_vtok[p2]
                for kc in range(8):
                    S.add("pe", (lambda kc: lambda e: e.matmul(pb[2][:, :512], h[:, kc, ssl], win[:, kc, 512:1024],
                                                               start=(kc == 0), stop=(kc == 7)))(kc), reads=[r_win, r_h[kc]], writes=[r_pb[2]])
                S.add("act", lambda e: e.copy(ktok[:], pb[2][:, :512]), reads=[r_pb[2]], writes=[r_ktok])
                for hf in range(2):
                    bk = 3 + hf
                    for kc in range(8):
                        S.add("pe", (lambda kc, hf, bk: lambda e: e.matmul(pb[bk][:, :512], h[:, kc, ssl], win[:, kc, 1024 + hf * 512:1536 + hf * 512],
                                                                           start=(kc == 0), stop=(kc == 7)))(kc, hf, bk),
                              reads=[r_win, r_h[kc]], writes=[r_pb[bk]])
                    S.add("dve", (lambda hf, bk: lambda e: e.tensor_copy(vt_[:, hf * 512:(hf + 1) * 512], pb[bk][:, :512]))(hf, bk),
                          reads=[r_pb[bk]], writes=[r_vt_])
                S.add("pe", lambda e: e.matmul(pb[2][:, :512], gl[:, ssl], wgk[:, 0, :], start=True, stop=True),
                      reads=[r_gl, r_wgk], writes=[r_pb[2]])
                S.add("act", lambda e: e.activation(ez[:], pb[2][:, :512], AF.Exp, scale=-1.0), reads=[r_pb[2]], writes=[r_ez])
                S.add("act", lambda e: e.activation(ez[:], ez[:], AF.Ln, bias=1.0), reads=[r_ez], writes=[r_ez])
                S.add("act", lambda e: e.activation(gtok[:], ez[:], AF.Copy, scale=-1.0 / 16.0), reads=[r_ez], writes=[r_gtok])
                for hh in range(4):
                    S.add("pe", (lambda hh: lambda e: e.matmul(pb[3][:, hh * 128:(hh + 1) * 128], gtok[:, hh * 128:(hh + 1) * 128], tri,
                                                               start=True, stop=True))(hh), reads=[r_gtok, r_c], writes=[r_pb[3]])
                S.add("pe", lambda e: e.matmul(pb[4][:, :512], su, gtok[:], start=True, stop=True), reads=[r_gtok, r_c], writes=[r_pb[4]])
                S.add("act", lambda e: e.activation(ebp[p2][:], pb[3][:, :512], AF.Exp), reads=[r_pb[3]], writes=[r_ebp[p2]])
                S.add("act", lambda e: e.activation(enb[:], pb[3][:, :512], AF.Exp, scale=-1.0), reads=[r_pb[3]], writes=[r_enb])
                S.add("act", lambda e: e.activation(er[:], pb[4][:, :512], AF.Exp), reads=[r_pb[4]], writes=[r_er])
                S.add("dve", lambda e: e.scalar_tensor_tensor(qd[p2][:], qf[:, :, ssl], float(scale), ebp[p2][:].rearrange("p (h t) -> p h t", h=4),
                                                              ALU.mult, ALU.mult), reads=[r_qf, r_ebp[p2]], writes=[r_qd[p2]])
                S.add("dve", lambda e: e.tensor_tensor(kd[p2][:], kf[:, :, ssl], enb[:].rearrange("p (h t) -> p h t", h=4), ALU.mult),
                      reads=[r_kf, r_enb], writes=[r_kd[p2]])
                S.add("pool", lambda e: e.tensor_tensor(kl[p2][:], ktok[:], er[:], ALU.mult), reads=[r_ktok, r_er], writes=[r_kl[p2]])

            def scan(n, s, cc):
                p2 = cc % 2
                ssl = slice(s * 128, (s + 1) * 128)
                vt_, r_vt_ = vtok[p2], r_vtok[p2]
                qd_, kd_, kl_, eb_ = qd[p2], kd[p2], kl[p2], ebp[p2]
                r_qd_, r_kd_, r_kl_, r_eb_ = r_qd[p2], r_kd[p2], r_kl[p2], r_ebp[p2]
                sbr, sbw = Sb[cc % 2], Sb[1 - cc % 2]
                r_sbr, r_sbw = r_Sb[cc % 2], r_Sb[1 - cc % 2]
                for hh in range(4):
                    S.add("pe", (lambda hh: lambda e: e.matmul(pb[5][:, hh * 128:(hh + 1) * 128], kd_[:, hh, :], qd_[:, hh, :],
                                                               start=True, stop=True))(hh), reads=[r_kd_, r_qd_], writes=[r_pb[5]])
                S.add("dve", lambda e: e.tensor_tensor(am[:], pb[5][:, :512], tri4, ALU.mult), reads=[r_pb[5], r_c], writes=[r_am])
                for hh in range(4):
                    bk = hh // 2
                    col = (hh % 2) * 256
                    S.add("pe", (lambda hh, bk, col: lambda e: e.matmul(pb[bk][:, col:col + 256], kl_[:, hh * 128:(hh + 1) * 128],
                                                                        vt_[:, hh * 256:(hh + 1) * 256], start=True, stop=True))(hh, bk, col),
                          reads=[r_kl_, r_vt_], writes=[r_pb[bk]])
                for hh in range(4):
                    for vc in range(2):
                        bk = 6 + hh // 2
                        col = ((hh % 2) * 2 + vc) * 128
                        S.add("pe", (lambda hh, vc, bk, col: lambda e: e.matmul(
                            pb[bk][:, col:col + 128], vt_[:, hh * 256 + vc * 128:hh * 256 + (vc + 1) * 128], am[:, hh * 128:(hh + 1) * 128],
                            start=True, stop=False))(hh, vc, bk, col), reads=[r_vt_, r_am], writes=[r_pb[bk]])
                        S.add("pe", (lambda hh, vc, bk, col: lambda e: e.matmul(
                            pb[bk][:, col:col + 128], sbr[:, hh, vc * 128:(vc + 1) * 128], qd_[:, hh, :],
                            start=False, stop=True))(hh, vc, bk, col), reads=[r_sbr[hh], r_qd_], writes=[r_pb[bk]])
                for hh in range(4):
                    bk = hh // 2
                    col = (hh % 2) * 256
                    S.add("dve", (lambda hh, bk, col: lambda e: e.scalar_tensor_tensor(
                        Sf[:, hh, :], Sf[:, hh, :], eb_[:, hh * 128 + 127:hh * 128 + 128], pb[bk][:, col:col + 256], ALU.mult, ALU.add))(hh, bk, col),
                        reads=[r_pb[bk], r_eb_], writes=[r_Sf[hh]])
                    S.add("act", (lambda hh: lambda e: e.copy(sbw[:, hh, :], Sf[:, hh, :]))(hh), reads=[r_Sf[hh]], writes=[r_sbw[hh]])
                for q2 in range(2):
                    S.add("act", (lambda q2: lambda e: e.activation(osq[:, q2 * 4:(q2 + 1) * 4, :],
                                                                    pb[6 + q2][:, :512].rearrange("p (c t) -> p c t", c=4), AF.Square))(q2),
                          reads=[r_pb[6 + q2]], writes=[r_osq])
                for hh in range(4):
                    for vc in range(2):
                        S.add("pe", (lambda hh, vc: lambda e: e.matmul(pb[5][:, hh * 128:(hh + 1) * 128], ones, osq[:, hh * 2 + vc, :],
                                                                       start=(vc == 0), stop=(vc == 1)))(hh, vc), reads=[r_osq, r_c], writes=[r_pb[5]])
                S.add("act", lambda e: e.activation(sdn[:], pb[5][:, :512], AF.Ln, bias=float(EPS), scale=1.0 / 256), reads=[r_pb[5]], writes=[r_sdn])
                rsn_v = rsn[:].rearrange("p (h v) t -> p h v t", v=2)
                for vc in range(2):
                    S.add("act", (lambda vc: lambda e: e.activation(rsn_v[:, :, vc, :], sdn[:].rearrange("p (h t) -> p h t", h=4),
                                                                    AF.Exp, scale=-0.5))(vc), reads=[r_sdn], writes=[r_rsn])
                S.add("pool", lambda e: e.tensor_tensor(tg[:], gate[:, :, ssl], rsn[:], ALU.mult), reads=[r_gate, r_rsn], writes=[r_tg])
                for q2 in range(2):
                    S.add("dve", (lambda q2: lambda e: e.tensor_tensor(
                        og[:, q2 * 4:(q2 + 1) * 4, ssl], pb[6 + q2][:, :512].rearrange("p (c t) -> p c t", c=4), tg[:, q2 * 4:(q2 + 1) * 4, :],
                        ALU.mult))(q2), reads=[r_pb[6 + q2], r_tg], writes=[r_og])

            def outproj(n):
                b = n % 2
                t0 = n * NT
                for oc in range(8):
                    bk = oc % 2
                    for c in range(8):
                        S.add("pe", (lambda oc, c, bk: lambda e: e.matmul(pb[bk][:, :NT], wout[:, c, oc * 128:(oc + 1) * 128], og[:, c, :],
                                                                           start=(c == 0), stop=(c == 7)))(oc, c, bk),
                              reads=[r_wout, r_og], writes=[r_pb[bk]])
                    S.add("dve", (lambda oc, bk: lambda e: e.tensor_tensor(xo[:, oc, :], pb[bk][:, :NT], xt[b][:, oc, :], ALU.add))(oc, bk),
                          reads=[r_pb[bk], r_xt[b]], writes=[r_qf, r_kf])
                S.dma("pool", yv[:, :, t0:t0 + NT], xo[:], reads=[r_qf, r_kf], writes=[r_xout])

            load(0)
            for n in range(ntile):
                if n + 1 < ntile:
                    load(n + 1)
                if n == 0:
                    prenorm(0)
                    hmul(0)
                proj_fm(n)
                base = 4 * n
                prep(n, 0, base)
                prep(n, 1, base + 1)
                scan(n, 0, base)
                if n + 1 < ntile:
                    prenorm(n + 1)
                prep(n, 2, base + 2)
                scan(n, 1, base + 1)
                prep(n, 3, base + 3)
                scan(n, 2, base + 2)
                if n + 1 < ntile:
                    hmul(n + 1)
                scan(n, 3, base + 3)
                outproj(n)
            S.flush()

    def build(self):
        nc = self.nc
        with ExitStack() as st:
            S = Sched(nc, st)
            self.r_scr = S.res()
            r_in = S.res()
            r_a, r_b = S.res(), S.res()
            cur, r_cur = self.x_in, r_in
            nl = len(self.layers)
            for idx, li in enumerate(self.layers):
                last = (idx == nl - 1)
                if li % 2 == 0:
                    self.a1_sweep(S, li, cur, r_cur)
                    self.a2_sweep(S, li, cur, r_cur, self.xa, r_a)
                else:
                    self.gla_sweep(S, li, cur, r_cur, self.xa, r_a)
                if last:
                    self.ffn_sweep(S, li, self.xa, r_a, self.y_out, S.res(), final=self.final)
                else:
                    self.ffn_sweep(S, li, self.xa, r_a, self.xb, r_b, final=False)
                    cur, r_cur = self.xb, r_b
            self.ninst = S.ninst
        return nc


def host_inputs(inp, T_sl=None):
    cst = make_cst()
    vecs = make_vecs(inp)
    common = {
        "cst": cst, "vecs": vecs,
        "ab_w_in": np.ascontiguousarray(inp["ab_w_in"], np.float32),
        "ab_lambda": np.ascontiguousarray(np.asarray(inp["ab_lambda"], np.float32).reshape(2, 256)),
        "pool_w": np.ascontiguousarray(inp["pool_w"], np.float32),
        "ab_w_out": np.ascontiguousarray(inp["ab_w_out"], np.float32),
        "gla_w_in": np.ascontiguousarray(inp["gla_w_in"], np.float32),
        "gla_w_gk_up": np.ascontiguousarray(inp["gla_w_gk_up"], np.float32),
        "gla_b_gk": np.ascontiguousarray(np.asarray(inp["gla_b_gk"], np.float32).reshape(2, 1, 512)),
        "gla_w_out": np.ascontiguousarray(inp["gla_w_out"], np.float32),
        "ffn_w1": np.ascontiguousarray(inp["ffn_w1"], np.float32),
        "ffn_w2": np.ascontiguousarray(inp["ffn_w2"], np.float32),
    }
    return common


_CACHE = {}


def kernel(**inputs):
    x = np.asarray(inputs["x"], np.float32)
    B, T, _ = x.shape
    key = (T,)
    if key not in _CACHE:
        _CACHE[key] = Builder(T).build()
    nc = _CACHE[key]
    common = host_inputs(inputs)
    zeros = {k: np.zeros_like(v) for k, v in common.items()}
    zeros["xT"] = np.zeros((D, T), np.float32)
    in_maps = []
    for c in range(NCORES):
        if c in ACTIVE:
            m = dict(common)
            m["xT"] = np.ascontiguousarray(x[ACTIVE.index(c)].T)
        else:
            m = zeros
        in_maps.append(m)
    res = run_bass_kernel_spmd(nc, in_maps, core_ids=list(range(NCORES)))
    out = np.empty((B, T, D), np.float32)
    for b in range(B):
        out[b] = res.results[ACTIVE[b]]["yT"].T
    return out
```

```python
import math
from contextlib import ExitStack

import numpy as np
import concourse.bass as bass
import concourse.mybir as mybir
from concourse.bass_utils import run_bass_kernel_spmd

F32 = mybir.dt.float32
BF16 = mybir.dt.bfloat16
ALU = mybir.AluOpType
AF = mybir.ActivationFunctionType
AX = mybir.AxisListType

D = 1024
DFF = 4096
EPS = 1e-6
DEPTH = 4
NCORES = 8
ACTIVE = (0, 1, 4, 5)

ENGS = ("pe", "act", "dve", "pool", "sp")
NDMA_SEMS = 8


class Res:
    __slots__ = ("writer", "readers")

    def __init__(self):
        self.writer = None
        self.readers = []


class Op:
    __slots__ = ("eng", "fn", "deps", "is_dma", "sig", "count", "dsem", "dval", "waits", "snap", "done")

    def __init__(self, eng, fn, is_dma):
        self.eng = eng
        self.fn = fn
        self.is_dma = is_dma
        self.deps = []
        self.sig = False
        self.count = 0
        self.dsem = None
        self.dval = 0
        self.waits = []
        self.snap = None
        self.done = False


class Sched:
    def __init__(self, nc, stack):
        self.nc = nc
        self.ops = {e: [] for e in ENGS}
        self.cnt = {e: 0 for e in ENGS}
        self.ndma = {e: 0 for e in ENGS}
        self.dma_hist = {e: [] for e in ENGS}
        self.known = {e: [0] * len(ENGS) for e in ENGS}
        self.kdma = {e: {} for e in ENGS}
        self.esem = {e: stack.enter_context(nc.semaphore(f"s_{e}")) for e in ENGS if e != "sp"}
        self.dsem = {}
        for e in ("sp", "pool", "act"):
            for k in range(NDMA_SEMS):
                self.dsem[(e, k)] = stack.enter_context(nc.semaphore(f"d_{e}{k}"))
        self.ninst = 0

    def res(self):
        return Res()

    def add(self, eng, fn, reads=(), writes=(), is_dma=False):
        op = Op(eng, fn, is_dma)
        deps = []
        for r in reads:
            if r.writer is not None:
                deps.append(r.writer)
            r.readers.append(op)
        for w in writes:
            if w.writer is not None:
                deps.append(w.writer)
            deps.extend(w.readers)
            w.writer = op
            w.readers = []
        seen = set()
        for d in deps:
            if d is op or d.done or id(d) in seen:
                continue
            seen.add(id(d))
            op.deps.append(d)
        self.ops[eng].append(op)
        return op

    def dma(self, q, out, in_, reads=(), writes=()):
        return self.add(q, lambda e: e.dma_start(out=out, in_=in_), reads, writes, is_dma=True)

    def barrier(self):
        last = []
        for e in ENGS:
            for o in reversed(self.ops[e]):
                if o.fn is not None and not o.is_dma:
                    last.append(o)
                    break
        rec = []
        for e in ENGS:
            dm = [o for o in self.ops[e] if o.is_dma][-NDMA_SEMS:]
            rec.extend(dm)
        for e in ENGS:
            op = Op(e, None, False)
            op.deps = list(last) + list(rec)
            self.ops[e].append(op)

    def flush(self):
        self.barrier()
        for e in ENGS:
            for op in self.ops[e]:
                for d in op.deps:
                    d.sig = True
        for e in ENGS:
            for op in self.ops[e]:
                if op.is_dma:
                    n = self.ndma[e]
                    op.dsem = (e, n % NDMA_SEMS)
                    op.dval = 16 * (n // NDMA_SEMS + 1)
                    self.dma_hist[e].append(op)
                    self.ndma[e] += 1
                elif op.sig:
                    self.cnt[e] += 1
                    op.count = self.cnt[e]
        eidx = {e: i for i, e in enumerate(ENGS)}
        for e in ENGS:
            known = self.known[e]
            kdma = self.kdma[e]
            hist = self.dma_hist[e]
            nd = len(hist) - sum(1 for o in self.ops[e] if o.is_dma)
            for op in self.ops[e]:
                deps = list(op.deps)
                if op.is_dma:
                    if nd >= NDMA_SEMS:
                        deps.append(hist[nd - NDMA_SEMS])
                    nd += 1
                best = {}
                for d in deps:
                    if d.is_dma:
                        if kdma.get(d.dsem, 0) >= d.dval:
                            continue
                        kdma[d.dsem] = d.dval
                        best[("dma", d.dsem)] = d.dval
                    else:
                        if d.eng == "pe" and e == "pe":
                            continue
                        j = eidx[d.eng]
                        if known[j] >= d.count:
                            continue
                        known[j] = d.count
                        best[("eng", d.eng)] = max(best.get(("eng", d.eng), 0), d.count)
                        if d.snap is not None:
                            for k in range(len(ENGS)):
                                if d.snap[k] > known[k]:
                                    known[k] = d.snap[k]
                op.waits = [(k[0], k[1], v) for k, v in best.items()]
                op.snap = tuple(known)
        nc = self.nc
        ops = self.ops
        esem, dsem = self.esem, self.dsem

        def run(eng_name):
            lst = ops[eng_name]

            def body(eng):
                for op in lst:
                    for kind, key, val in op.waits:
                        eng.wait_ge(dsem[key] if kind == "dma" else esem[key], val)
                    if op.fn is None:
                        continue
                    ins = op.fn(eng)
                    if op.is_dma:
                        ins.then_inc(dsem[op.dsem], 16)
                    elif op.sig:
                        ins.then_inc(esem[eng_name], 1)
            return body

        with nc.Block() as block:
            block.tensor(run("pe"))
            block.scalar(run("act"))
            block.vector(run("dve"))
            block.gpsimd(run("pool"))
            block.sync(run("sp"))
        for e in ENGS:
            self.ninst += len(self.ops[e])
            for op in self.ops[e]:
                op.done = True
                op.fn = None
                op.deps = []
            self.ops[e] = []


C_ONES = 0
C_TRI = 128
C_SU = 256
C_TRI4 = 384
C_INVC = 896
NCST = 960


def make_cst():
    c = np.zeros((128, NCST), np.float32)
    c[:, C_ONES:C_ONES + 128] = 1.0
    s = np.arange(128)
    tri = (s[:, None] <= s[None, :]).astype(np.float32)
    c[:, C_TRI:C_TRI + 128] = tri
    c[:, C_SU:C_SU + 128] = (s[:, None] > s[None, :]).astype(np.float32)
    for h in range(4):
        c[:, C_TRI4 + h * 128:C_TRI4 + (h + 1) * 128] = tri
    for g, w in enumerate((2, 4, 8, 16)):
        t = np.arange(16)
        c[:, C_INVC + g * 16:C_INVC + (g + 1) * 16] = 1.0 / np.minimum(t + 1, w)
    return c


V_NMIX = 0
V_NFFN = 32
V_NFIN = 64
V_PSCALE = 72
V_SUBLN = 80
V_GNORM = 82
NVEC = 98


def make_vecs(inp):
    v = np.zeros((128, NVEC), np.float32)

    def put(col, arr):
        a = np.asarray(arr, np.float32).reshape(-1, 128).T
        v[:, col:col + a.shape[1]] = a

    for i in range(DEPTH):
        put(V_NMIX + 8 * i, inp["norm_mix"][i])
        put(V_NFFN + 8 * i, inp["norm_ffn"][i])
    put(V_NFIN, inp["norm_final"])
    for e in range(2):
        put(V_PSCALE + 4 * e, inp["pool_scale"][e])
        put(V_SUBLN + e, inp["ab_subln"][e])
        put(V_GNORM + 8 * e, inp["gla_norm"][e].reshape(-1))
    return v


class Builder:
    def __init__(self, T, layers=(0, 1, 2, 3), final=True, dbg=False):
        self.T = T
        self.layers = tuple(layers)
        self.final = final
        nc = self.nc = bass.Bass("TRN2", target_bir_lowering=False)
        dt = nc.dram_tensor
        self.x_in = dt("xT", [D, T], F32, kind="ExternalInput").ap()
        self.y_out = dt("yT", [D, T], F32, kind="ExternalOutput").ap()
        self.cst_d = dt("cst", [128, NCST], F32, kind="ExternalInput").ap()
        self.vecs_d = dt("vecs", [128, NVEC], F32, kind="ExternalInput").ap()
        self.w = {}
        for name, shape in (("ab_w_in", [2, D, 2048]), ("ab_lambda", [2, 256]), ("pool_w", [2, 4, 128, 128]),
                            ("ab_w_out", [2, D, D]), ("gla_w_in", [2, D, 3088]), ("gla_w_gk_up", [2, 16, 512]),
                            ("gla_b_gk", [2, 1, 512]), ("gla_w_out", [2, D, D]), ("ffn_w1", [4, D, DFF]),
                            ("ffn_w2", [4, DFF, D])):
            self.w[name] = dt(name, shape, F32, kind="ExternalInput").ap()
        kw = {"kind": "ExternalOutput"} if dbg else {}
        self.xa = dt("xa", [D, T], F32, **kw).ap()
        self.xb = dt("xb", [D, T], F32, **kw).ap()
        self.qT = dt("qTs", [4, 128, T], BF16, **kw).ap()
        self.kT = dt("kTs", [4, 128, T], BF16, **kw).ap()
        self.Vs = dt("Vs", [4, 128, T // 128, 128], BF16, **kw).ap()
        self.mT = dt("mTs", [4, 128, T], BF16, **kw).ap()
        self.r_x = {}
        self.nsb = 0

    def sb(self, st, shape, dtp):
        self.nsb += 1
        return st.enter_context(self.nc.sbuf_tensor(f"sb{self.nsb}", shape, dtp))

    def psum(self, st):
        self.nsb += 1
        return [st.enter_context(self.nc.psum_tensor(f"ps{self.nsb}_{i}", [128, 512], F32)) for i in range(8)]

    def load_consts(self, S, st):
        cst = self.sb(st, [128, NCST], F32)
        vecs = self.sb(st, [128, NVEC], F32)
        r = S.res()
        S.dma("sp", cst[:], self.cst_d, writes=[r])
        S.dma("sp", vecs[:], self.vecs_d, writes=[r])
        return cst, vecs, r

    def load_weight(self, S, st_tmp, dst, src_rows, ncols, rres, wres, scale_cols=None, cw=2048):
        NB = 6
        key = (id(st_tmp), cw)
        if getattr(self, "_stg_key", None) != key:
            self._stg_key = key
            self._stg = ([self.sb(st_tmp, [128, cw], F32) for _ in range(NB)], [S.res() for _ in range(NB)])
        stg, r_stg = self._stg
        engs = ["act", "dve", "act", "dve", "pool", "dve"]
        ci = 0
        for r, src in enumerate(src_rows):
            for c0 in range(0, ncols, cw):
                c1 = min(ncols, c0 + cw)
                b = ci % NB
                S.dma("sp" if ci % 2 == 0 else "act", stg[b][:, :c1 - c0], src[:, c0:c1], writes=[r_stg[b]])
                out = dst[:, r, c0:c1]
                in_ = stg[b][:, :c1 - c0]
                eng = engs[ci % len(engs)]
                sc = None if scale_cols is None else scale_cols[r]
                if sc is None:
                    if eng == "act":
                        S.add("act", (lambda o, i: lambda e: e.copy(o, i))(out, in_), reads=[r_stg[b]] + rres, writes=[wres])
                    else:
                        S.add(eng, (lambda o, i: lambda e: e.tensor_copy(o, i))(out, in_), reads=[r_stg[b]] + rres, writes=[wres])
                else:
                    if eng == "act":
                        S.add("act", (lambda o, i, s: lambda e: e.activation(o, i, AF.Copy, scale=s))(out, in_, sc),
                              reads=[r_stg[b]] + rres, writes=[wres])
                    else:
                        S.add(eng, (lambda o, i, s: lambda e: e.tensor_scalar(o, i, s, None, ALU.mult))(out, in_, sc),
                              reads=[r_stg[b]] + rres, writes=[wres])
                ci += 1

    def rmsnorm_a(self, S, xt_c, r_xt, sq, r_sq, ssum, r_ssum):
        S.add("act", lambda e: e.activation(sq[:], xt_c[:], AF.Square), reads=[r_xt], writes=[r_sq])
        S.add("dve", lambda e: e.tensor_reduce(ssum[:], sq[:].rearrange("p c t -> p t c"), AX.X, ALU.add),
              reads=[r_sq], writes=[r_ssum])

    def rmsnorm_b(self, S, xt_c, r_xt, NT, ssum, r_ssum, rstd, r_rstd, psb, r_psb, ones, r_c, h, r_h, nfeat_inv=1.0 / D):
        S.add("pe", lambda e: e.matmul(psb[:, :NT], ones, ssum[:], start=True, stop=True),
              reads=[r_c, r_ssum], writes=[r_psb])
        S.add("act", lambda e: e.activation(ssum[:], psb[:, :NT], AF.Ln, bias=float(EPS), scale=nfeat_inv),
              reads=[r_psb], writes=[r_ssum])
        S.add("act", lambda e: e.activation(rstd[:], ssum[:], AF.Exp, scale=-0.5), reads=[r_ssum], writes=[r_rstd])
        for c in range(8):
            eng = "dve" if c % 2 == 0 else "pool"
            S.add(eng, (lambda c: lambda e: e.tensor_tensor(h[:, c, :], xt_c[:, c, :], rstd[:], ALU.mult))(c),
                  reads=[r_xt, r_rstd], writes=[r_h[c]])

    def ffn_sweep(self, S, li, xin, r_xin, xout, r_xout, final):
        T = self.T
        FNT = 256
        nc = self.nc
        with ExitStack() as st:
            cst, vecs, r_c = self.load_consts(S, st)
            ones = cst[:, C_ONES:C_ONES + 128]
            w1sb = self.sb(st, [128, 8, DFF], BF16)
            w2sb = self.sb(st, [128, 32, D], BF16)
            r_w1, r_w2 = S.res(), S.res()
            with ExitStack() as st2:
                w1v = self.w["ffn_w1"][li].rearrange("(kc p) n -> p kc n", p=128)
                w2v = self.w["ffn_w2"][li].rearrange("(hc p) n -> p hc n", p=128)
                self.load_weight(S, st2, w1sb, [w1v[:, kc, :] for kc in range(8)], DFF, [r_c], r_w1,
                                 scale_cols=[vecs[:, V_NFFN + 8 * li + kc:V_NFFN + 8 * li + kc + 1] for kc in range(8)])
                self.load_weight(S, st2, w2sb, [w2v[:, hc, :] for hc in range(32)], D, [r_c], r_w2)
                S.flush()
            xt = [self.sb(st, [128, 8, FNT], F32) for _ in range(2)]
            xo = self.sb(st, [128, 8, FNT], F32)
            sq = self.sb(st, [128, 8, FNT], F32)
            ssum = self.sb(st, [128, FNT], F32)
            rstd = self.sb(st, [128, FNT], F32)
            h = self.sb(st, [128, 8, FNT], BF16)
            hid = self.sb(st, [128, 32, FNT], BF16)
            rl = [self.sb(st, [128, FNT], F32) for _ in range(4)]
            if final:
                sq2 = self.sb(st, [128, 8, FNT], F32)
                ssum2 = self.sb(st, [128, FNT], F32)
                rstd2 = self.sb(st, [128, FNT], F32)
                r_sq2, r_ssum2, r_rstd2 = S.res(), S.res(), S.res()
            pb = self.psum(st)
            R = S.res
            r_xt = [R(), R()]; r_xo = R(); r_sq = R(); r_ssum = R(); r_rstd = R()
            r_h = [R() for _ in range(8)]; r_hid = [R() for _ in range(32)]
            r_pb = [R() for _ in range(8)]; r_rl = [R() for _ in range(4)]
            xv = xin.rearrange("(c p) t -> p c t", p=128)
            yv = xout.rearrange("(c p) t -> p c t", p=128)
            ntile = T // FNT

            def load(n):
                S.dma("sp", xt[n % 2][:], xv[:, :, n * FNT:(n + 1) * FNT], reads=[r_xin], writes=[r_xt[n % 2]])

            def norm_a(n):
                b = n % 2
                self.rmsnorm_a(S, xt[b], r_xt[b], sq, r_sq, ssum, r_ssum)

            def norm_b(n):
                b = n % 2
                self.rmsnorm_b(S, xt[b], r_xt[b], FNT, ssum, r_ssum, rstd, r_rstd, pb[7], r_pb[7], ones, r_c, h, r_h)

            def up(n):
                for j in range(32):
                    bk = j % 4
                    for kc in range(8):
                        S.add("pe", (lambda j, kc, bk: lambda e: e.matmul(
                            pb[bk][:, :FNT], w1sb[:, kc, j * 128:(j + 1) * 128], h[:, kc, :],
                            start=(kc == 0), stop=(kc == 7)))(j, kc, bk), reads=[r_w1, r_h[kc]], writes=[r_pb[bk]])
                    S.add("act", (lambda j, bk: lambda e: e.activation(rl[j % 4][:], pb[bk][:, :FNT], AF.Relu))(j, bk),
                          reads=[r_pb[bk]], writes=[r_rl[j % 4]])
                    S.add("pool", (lambda j: lambda e: e.tensor_tensor(hid[:, j, :], rl[j % 4][:], rl[j % 4][:], ALU.mult))(j),
                          reads=[r_rl[j % 4]], writes=[r_hid[j]])

            def down(n):
                b = n % 2
                for oc in range(8):
                    bk = 4 + oc % 2
                    for hc in range(32):
                        S.add("pe", (lambda oc, hc, bk: lambda e: e.matmul(
                            pb[bk][:, :FNT], w2sb[:, hc, oc * 128:(oc + 1) * 128], hid[:, hc, :],
                            start=(hc == 0), stop=(hc == 31)))(oc, hc, bk), reads=[r_w2, r_hid[hc]], writes=[r_pb[bk]])
                    S.add("dve", (lambda oc, bk: lambda e: e.tensor_tensor(
                        xo[:, oc, :], pb[bk][:, :FNT], xt[b][:, oc, :], ALU.add))(oc, bk),
                        reads=[r_pb[bk], r_xt[b]], writes=[r_xo])
                if final:
                    S.add("act", lambda e: e.activation(sq2[:], xo[:], AF.Square), reads=[r_xo], writes=[r_sq2])
                    S.add("dve", lambda e: e.tensor_reduce(ssum2[:], sq2[:].rearrange("p c t -> p t c"), AX.X, ALU.add),
                          reads=[r_sq2], writes=[r_ssum2])
                    S.add("pe", lambda e: e.matmul(pb[6][:, :FNT], ones, ssum2[:], start=True, stop=True),
                          reads=[r_c, r_ssum2], writes=[r_pb[6]])
                    S.add("act", lambda e: e.activation(ssum2[:], pb[6][:, :FNT], AF.Ln, bias=float(EPS), scale=1.0 / D),
                          reads=[r_pb[6]], writes=[r_ssum2])
                    S.add("act", lambda e: e.activation(rstd2[:], ssum2[:], AF.Exp, scale=-0.5), reads=[r_ssum2], writes=[r_rstd2])
                    for c in range(8):
                        S.add("dve", (lambda c: lambda e: e.scalar_tensor_tensor(
                            sq2[:, c, :], xo[:, c, :], vecs[:, V_NFIN + c:V_NFIN + c + 1], rstd2[:], ALU.mult, ALU.mult))(c),
                            reads=[r_xo, r_rstd2, r_c], writes=[r_sq2])
                    S.dma("pool", yv[:, :, n * FNT:(n + 1) * FNT], sq2[:], reads=[r_sq2], writes=[r_xout])
                else:
                    S.dma("pool", yv[:, :, n * FNT:(n + 1) * FNT], xo[:], reads=[r_xo], writes=[r_xout])

            load(0)
            if ntile > 1:
                load(1)
            norm_a(0)
            norm_b(0)
            for n in range(ntile):
                if n + 1 < ntile:
                    norm_a(n + 1)
                up(n)
                if n + 1 < ntile:
                    norm_b(n + 1)
                down(n)
                if n + 2 < ntile:
                    load(n + 2)
            S.flush()

    def a1_sweep(self, S, li, xin, r_xin):
        T = self.T
        NT = 512
        e_ = li // 2
        with ExitStack() as st:
            cst, vecs, r_c = self.load_consts(S, st)
            ones = cst[:, C_ONES:C_ONES + 128]
            win = self.sb(st, [128, 8, 2048], BF16)
            pw = self.sb(st, [128, 4, 128], BF16)
            r_win, r_pw = S.res(), S.res()
            with ExitStack() as st2:
                wv = self.w["ab_w_in"][e_].rearrange("(kc p) n -> p kc n", p=128)
                self.load_weight(S, st2, win, [wv[:, kc, :] for kc in range(8)], 2048, [r_c], r_win,
                                 scale_cols=[vecs[:, V_NMIX + 8 * li + kc:V_NMIX + 8 * li + kc + 1] for kc in range(8)])
                pwv = self.w["pool_w"][e_].rearrange("g c d -> c g d")
                self.load_weight(S, st2, pw, [pwv[:, g, :] for g in range(4)], 128, [r_c], r_pw, cw=128)
                S.flush()
            xt = [self.sb(st, [128, 8, NT], F32) for _ in range(2)]
            sq = self.sb(st, [128, 8, NT], F32)
            ssum = self.sb(st, [128, NT], F32)
            rstd = self.sb(st, [128, NT], F32)
            hh_ = [self.sb(st, [128, 8, NT], BF16) for _ in range(2)]
            qk = [self.sb(st, [128, 8, NT], BF16) for _ in range(2)]
            vt = [self.sb(st, [128, 4, 512], BF16) for _ in range(2)]
            uext = [self.sb(st, [128, 4, 16 + NT], F32) for _ in range(2)]
            ta = self.sb(st, [128, 16 + NT], F32)
            tb = self.sb(st, [128, 16 + NT], F32)
            rr = self.sb(st, [128, 4, NT], BF16)
            mo = [self.sb(st, [128, 4, NT], BF16) for _ in range(2)]
            pb = self.psum(st)
            R = S.res
            r_xt = [R(), R()]; r_sq = R(); r_ssum = R(); r_rstd = R()
            r_hh = [[R() for _ in range(8)] for _ in range(2)]; r_pb = [R() for _ in range(8)]
            r_qk = [R(), R()]; r_vt = [R(), R()]; r_ue = [[R() for _ in range(4)] for _ in range(2)]
            r_ta, r_tb = R(), R(); r_rr = [R() for _ in range(4)]; r_mo = [R(), R()]
            r_sc = self.r_scr
            xv = xin.rearrange("(c p) t -> p c t", p=128)
            ntile = T // NT
            qTv = self.qT.rearrange("h p t -> p h t")
            kTv = self.kT.rearrange("h p t -> p h t")
            mTv = self.mT.rearrange("g p t -> p g t")
            Vv = self.Vs.rearrange("h p k v -> p h k v")

            def load(n):
                S.dma("sp", xt[n % 2][:], xv[:, :, n * NT:(n + 1) * NT], reads=[r_xin], writes=[r_xt[n % 2]])

            def norm_a(n):
                b = n % 2
                self.rmsnorm_a(S, xt[b], r_xt[b], sq, r_sq, ssum, r_ssum)

            def norm_b(n):
                b = n % 2
                self.rmsnorm_b(S, xt[b], r_xt[b], NT, ssum, r_ssum, rstd, r_rstd, pb[7], r_pb[7], ones, r_c, hh_[b], r_hh[b])

            for g in range(4):
                S.add("pool", (lambda g: lambda e: e.memset(uext[0][:, g, 0:16], 0.0))(g), writes=[r_ue[0][g]])
            S.add("pool", lambda e: e.memset(ta[:], 0.0), writes=[r_ta])
            S.add("pool", lambda e: e.memset(tb[:], 0.0), writes=[r_tb])

            def proj(n, part):
                b = n % 2
                t0 = n * NT
                h = hh_[b]
                r_h = r_hh[b]
                if part == 1:
                    proj_b(n, b, t0, h, r_h)
                    return
                for oc in range(8):
                    bk = oc % 3
                    for kc in range(8):
                        S.add("pe", (lambda oc, kc, bk: lambda e: e.matmul(
                            pb[bk][:, :NT], win[:, kc, oc * 128:(oc + 1) * 128], h[:, kc, :],
                            start=(kc == 0), stop=(kc == 7)))(oc, kc, bk), reads=[r_win, r_h[kc]], writes=[r_pb[bk]])
                    S.add("act", (lambda oc, bk: lambda e: e.copy(qk[b][:, oc, :], pb[bk][:, :NT]))(oc, bk),
                          reads=[r_pb[bk]], writes=[r_qk[b]])
                S.dma("pool", qTv[:, :, t0:t0 + NT], qk[b][:, 0:4, :], reads=[r_qk[b]], writes=[r_sc])
                S.dma("pool", kTv[:, :, t0:t0 + NT], qk[b][:, 4:8, :], reads=[r_qk[b]], writes=[r_sc])

            def proj_b(n, b, t0, h, r_h):
                for s in range(4):
                    bk = 3 + s % 2
                    for kc in range(8):
                        S.add("pe", (lambda s, kc, bk: lambda e: e.matmul(
                            pb[bk][:, :512], h[:, kc, s * 128:(s + 1) * 128], win[:, kc, 1024:1536],
                            start=(kc == 0), stop=(kc == 7)))(s, kc, bk), reads=[r_win, r_h[kc]], writes=[r_pb[bk]])
                    S.add("dve", (lambda s, bk: lambda e: e.tensor_copy(vt[b][:, s, :], pb[bk][:, :512]))(s, bk),
                          reads=[r_pb[bk]], writes=[r_vt[b]])
                for hh in range(4):
                    S.dma("pool", self.Vs[hh, :, 4 * n:4 * n + 4, :], vt[b][:, :, hh * 128:(hh + 1) * 128],
                          reads=[r_vt[b]], writes=[r_sc])
                for g in range(4):
                    bk = 5 + g % 2
                    oc = 12 + g
                    for kc in range(8):
                        S.add("pe", (lambda oc, kc, bk: lambda e: e.matmul(
                            pb[bk][:, :NT], win[:, kc, oc * 128:(oc + 1) * 128], h[:, kc, :],
                            start=(kc == 0), stop=(kc == 7)))(oc, kc, bk), reads=[r_win, r_h[kc]], writes=[r_pb[bk]])
                    S.add("act", (lambda g, bk: lambda e: e.copy(uext[b][:, g, 16:16 + NT], pb[bk][:, :NT]))(g, bk),
                          reads=[r_pb[bk]], writes=[r_ue[b][g]])

            def pool(n):
                b = n % 2
                t0 = n * NT
                W = 16 + NT
                for g in range(4):
                    w = 2 ** (g + 1)
                    src = uext[b][:, g, :]
                    r_src = r_ue[b][g]
                    bufs = [(ta, r_ta), (tb, r_tb)]
                    sh = 1
                    k = 0
                    cur, r_cur = src, r_src
                    while sh < w:
                        dst, r_dst = bufs[k % 2]
                        S.add("pool", (lambda dst, cur, sh: lambda e: e.tensor_tensor(
                            dst[:, sh:W], cur[:, sh:W], cur[:, 0:W - sh], ALU.add))(dst, cur, sh),
                            reads=[r_cur], writes=[r_dst])
                        cur, r_cur = dst[:], r_dst
                        sh *= 2
                        k += 1
                    S.add("dve", (lambda g, cur, w: lambda e: e.scalar_tensor_tensor(
                        rr[:, g, :], cur[:, 16:W], 1.0 / w, uext[b][:, g, 16:W], ALU.mult, ALU.subtract))(g, cur, w),
                        reads=[r_cur, r_ue[b][g]], writes=[r_rr[g]])
                    if n == 0:
                        S.add("dve", (lambda g, cur: lambda e: e.tensor_tensor(
                            ta[:, 0:16], cur[:, 16:32], cst[:, C_INVC + g * 16:C_INVC + (g + 1) * 16], ALU.mult))(g, cur),
                            reads=[r_cur, r_c], writes=[r_ta])
                        S.add("dve", (lambda g: lambda e: e.tensor_tensor(
                            rr[:, g, 0:16], ta[:, 0:16], uext[b][:, g, 16:32], ALU.subtract))(g),
                            reads=[r_ta, r_ue[b][g]], writes=[r_rr[g]])
                    if n + 1 < ntile:
                        S.add("pool", (lambda g: lambda e: e.tensor_copy(uext[1 - b][:, g, 0:16], uext[b][:, g, NT:NT + 16]))(g),
                              reads=[r_ue[b][g]], writes=[r_ue[1 - b][g]])

            def pool_mm(n):
                b = n % 2
                t0 = n * NT
                for g in range(4):
                    bk = 5 + g % 2
                    S.add("pe", (lambda g, bk: lambda e: e.matmul(pb[bk][:, :NT], pw[:, g, :], rr[:, g, :], start=True, stop=True))(g, bk),
                          reads=[r_pw, r_rr[g]], writes=[r_pb[bk]])
                    S.add("act", (lambda g, bk: lambda e: e.activation(
                        mo[b][:, g, :], pb[bk][:, :NT], AF.Copy, scale=vecs[:, V_PSCALE + 4 * e_ + g:V_PSCALE + 4 * e_ + g + 1]))(g, bk),
                        reads=[r_pb[bk], r_c], writes=[r_mo[b]])
                S.dma("pool", mTv[:, :, t0:t0 + NT], mo[b][:], reads=[r_mo[b]], writes=[r_sc])

            load(0)
            if ntile > 1:
                load(1)
            norm_a(0)
            norm_b(0)
            for n in range(ntile):
                if n + 1 < ntile:
                    norm_a(n + 1)
                proj(n, 0)
                if n + 1 < ntile:
                    norm_b(n + 1)
                proj(n, 1)
                if n > 0:
                    pool_mm(n - 1)
                pool(n)
                if n + 2 < ntile:
                    load(n + 2)
            pool_mm(ntile - 1)
            S.flush()

    def a2_sweep(self, S, li, xin, r_xin, xout, r_xout):
        T = self.T
        NT = 512
        e_ = li // 2
        lam_init = 0.8 - 0.6 * math.exp(-0.3 * li)
        KG = 16
        with ExitStack() as st:
            cst, vecs, r_c = self.load_consts(S, st)
            ones = cst[:, C_ONES:C_ONES + 128]
            wout = self.sb(st, [128, 8, D], BF16)
            r_wout = S.res()
            onesb = self.sb(st, [128, 128], BF16)
            lamt = self.sb(st, [128, 256], F32)
            lprod = self.sb(st, [128, 128], F32)
            lsum = self.sb(st, [128, 2], F32)
            neglam = self.sb(st, [128, 1], F32)
            gsub = self.sb(st, [128, 1], F32)
            r_l = S.res()
            ediag = [[self.sb(st, [128, NT], BF16) for _ in range(2)] for _ in range(4)]
            r_ed = [[S.res() for _ in range(2)] for _ in range(4)]
            with ExitStack() as st2:
                wv = self.w["ab_w_out"][e_].rearrange("(kc p) n -> p kc n", p=128)
                self.load_weight(S, st2, wout, [wv[:, kc, :] for kc in range(8)], D, [r_c], r_wout)
                S.add("dve", lambda e: e.tensor_copy(onesb[:], ones), reads=[r_c], writes=[r_l])
                S.dma("sp", lamt[:], self.w["ab_lambda"][e_:e_ + 1, :].partition_broadcast(128), writes=[r_l])
                S.add("dve", lambda e: e.tensor_tensor(lprod[:, 0:64], lamt[:, 0:64], lamt[:, 64:128], ALU.mult), reads=[r_l], writes=[r_l])
                S.add("dve", lambda e: e.tensor_tensor(lprod[:, 64:128], lamt[:, 128:192], lamt[:, 192:256], ALU.mult), reads=[r_l], writes=[r_l])
                S.add("dve", lambda e: e.tensor_reduce(lsum[:], lprod[:].rearrange("p (a d) -> p a d", a=2), AX.X, ALU.add), reads=[r_l], writes=[r_l])
                S.add("act", lambda e: e.activation(lsum[:], lsum[:], AF.Exp), reads=[r_l], writes=[r_l])
                S.add("dve", lambda e: e.tensor_tensor(neglam[:], lsum[:, 1:2], lsum[:, 0:1], ALU.subtract), reads=[r_l], writes=[r_l])
                S.add("dve", lambda e: e.tensor_scalar(neglam[:], neglam[:], -float(lam_init), None, ALU.add), reads=[r_l], writes=[r_l])
                S.add("dve", lambda e: e.tensor_scalar(gsub[:], vecs[:, V_SUBLN + e_:V_SUBLN + e_ + 1], float(1.0 - lam_init), None, ALU.mult),
                      reads=[r_c], writes=[r_l])
                for j in range(4):
                    for a in range(2):
                        S.add("pool", (lambda j, a: lambda e: e.memset(ediag[j][a][:], 0.0))(j, a), writes=[r_ed[j][a]])
                S.flush()
            xt = [self.sb(st, [128, 8, NT], F32) for _ in range(2)]
            xo = self.sb(st, [128, 8, NT], F32)
            qt = [self.sb(st, [128, 4, NT], BF16) for _ in range(2)]
            mo = [self.sb(st, [128, 4, NT], BF16) for _ in range(2)]
            ao = self.sb(st, [128, 4, NT], BF16)
            kb = [self.sb(st, [128, KG * 128], BF16) for _ in range(3)]
            vb = [self.sb(st, [128, KG, 128], BF16) for _ in range(3)]
            eb = [[self.sb(st, [128, NT], BF16) for _ in range(2)] for _ in range(3)]
            rl1 = self.sb(st, [128, NT], F32); rl2 = self.sb(st, [128, NT], F32)
            t1 = self.sb(st, [128, NT], F32); t2 = self.sb(st, [128, NT], F32)
            A = self.sb(st, [128, NT], F32); asq = self.sb(st, [128, NT], F32)
            sd = self.sb(st, [128, NT], F32); rs = self.sb(st, [128, NT], F32)
            acc2 = self.sb(st, [128, NT], F32)
            r_acc2 = S.res()
            pb = self.psum(st)
            R = S.res
            r_xt = [R(), R()]; r_xo = R(); r_qt = [R(), R()]; r_mo = [R(), R()]; r_ao = [R() for _ in range(4)]
            r_kb = [R() for _ in range(3)]; r_vb = [R() for _ in range(3)]
            r_eb = [[R(), R()] for _ in range(3)]
            r_ep = R()
            r_pb = [R() for _ in range(8)]
            r_sc = self.r_scr
            xv = xin.rearrange("(c p) t -> p c t", p=128)
            yv = xout.rearrange("(c p) t -> p c t", p=128)
            qTv = self.qT.rearrange("h p t -> p h t")
            mTv = self.mT.rearrange("g p t -> p g t")
            nblk = T // NT
            PS_S = [(0, 1), (2, 3)]
            PO1, PO2, PL1, PL2 = 4, 5, 6, 7
            grp_ctr = [0]

            def load(n):
                t0 = n * NT
                b = n % 2
                S.dma("sp", xt[b][:], xv[:, :, t0:t0 + NT], reads=[r_xin], writes=[r_xt[b]])
                S.dma("sp", qt[b][:], qTv[:, :, t0:t0 + NT], reads=[r_sc], writes=[r_qt[b]])
                S.dma("sp", mo[b][:], mTv[:, :, t0:t0 + NT], reads=[r_sc], writes=[r_mo[b]])

            def load_kv(h, g0, ng):
                i = grp_ctr[0] % 3
                grp_ctr[0] += 1
                S.dma("sp", kb[i][:, :ng * 128], self.kT[h, :, g0 * 128:(g0 + ng) * 128], reads=[r_sc], writes=[r_kb[i]])
                S.dma("sp", vb[i][:, :ng, :], self.Vs[h, :, g0:g0 + ng, :], reads=[r_sc], writes=[r_vb[i]])
                return i

            def attn_head(qb, h, pending):
                b = qb % 2
                nk = 4 * (qb + 1)
                groups = []
                for g0 in range(0, nk, KG):
                    groups.append((g0, min(KG, nk - g0)))
                gbuf = {}
                for gi in range(min(2, len(groups))):
                    gbuf[gi] = load_kv(h, *groups[gi])
                ectr = [0]
                pend = []

                def qk(kt):
                    gi, lk = kt // KG, kt % KG
                    if gi not in gbuf:
                        gbuf[gi] = load_kv(h, *groups[gi])
                    i = gbuf[gi]
                    j = kt - 4 * qb
                    c0 = 128 * j if j >= 0 else 0
                    sa, sb_ = PS_S[kt % 2]
                    S.add("pe", lambda e: e.matmul(pb[sa][:, c0:NT], kb[i][0:64, lk * 128:(lk + 1) * 128], qt[b][0:64, h, c0:NT],
                                                   start=True, stop=True), reads=[r_kb[i], r_qt[b]], writes=[r_pb[sa]])
                    S.add("pe", lambda e: e.matmul(pb[sb_][:, c0:NT], kb[i][64:128, lk * 128:(lk + 1) * 128], qt[b][64:128, h, c0:NT],
                                                   start=True, stop=True), reads=[r_kb[i], r_qt[b]], writes=[r_pb[sb_]])
                    if j < 0:
                        k3 = ectr[0] % 3
                        ectr[0] += 1
                        e1, e2 = eb[k3][0], eb[k3][1]
                        re1, re2 = r_eb[k3][0], r_eb[k3][1]
                        S.add("act", lambda e: e.activation(e1[:, :], pb[sa][:, :NT], AF.Exp, scale=0.125), reads=[r_pb[sa]], writes=[re1])
                        S.add("act", lambda e: e.activation(e2[:, :], pb[sb_][:, :NT], AF.Exp, scale=0.125), reads=[r_pb[sb_]], writes=[re2])
                    else:
                        e1, e2 = ediag[j][0], ediag[j][1]
                        re1, re2 = r_ed[j][0], r_ed[j][1]
                        for (et, re_, pbn) in ((e1, re1, sa), (e2, re2, sb_)):
                            S.add("act", (lambda et, pbn: lambda e: e.activation(et[0:64, c0:NT], pb[pbn][0:64, c0:NT], AF.Exp, scale=0.125))(et, pbn),
                                  reads=[r_pb[pbn]], writes=[re_])
                            S.add("act", (lambda et, pbn: lambda e: e.activation(et[64:128, c0 + 64:NT], pb[pbn][64:128, c0 + 64:NT], AF.Exp, scale=0.125))(et, pbn),
                                  reads=[r_pb[pbn]], writes=[re_])
                    return (kt, c0, e1, e2, re1, re2, i, lk)

                def pv(item):
                    kt, c0, e1, e2, re1, re2, i, lk = item
                    first, last = (kt == 0), (kt == nk - 1)
                    if first or c0 >= 256:
                        rngs = [(c0, NT)]
                    else:
                        rngs = [(c0, 256), (256, NT)]
                    for (a0, a1) in rngs:
                        S.add("pe", (lambda a0, a1: lambda e: e.matmul(pb[PO1][:, a0:a1], vb[i][:, lk, :], e1[:, a0:a1], start=first, stop=last, skip_group_check=True))(a0, a1),
                              reads=[r_vb[i], re1], writes=[r_pb[PO1]])
                    for (a0, a1) in rngs:
                        S.add("pe", (lambda a0, a1: lambda e: e.matmul(pb[PO2][:, a0:a1], vb[i][:, lk, :], e2[:, a0:a1], start=first, stop=last, skip_group_check=True))(a0, a1),
                              reads=[r_vb[i], re2], writes=[r_pb[PO2]])
                    for (a0, a1) in rngs:
                        S.add("pe", (lambda a0, a1: lambda e: e.matmul(pb[PL1][:, a0:a1], onesb[:], e1[:, a0:a1], start=first, stop=last, skip_group_check=True))(a0, a1),
                              reads=[r_l, re1], writes=[r_pb[PL1]])
                    if first:
                        S.add("dve", lambda e: e.tensor_copy(acc2[:, c0:NT], e2[:, c0:NT]), reads=[re2], writes=[r_acc2])
                    else:
                        S.add("dve", lambda e: e.tensor_tensor(acc2[:, c0:NT], acc2[:, c0:NT], e2[:, c0:NT], ALU.add),
                              reads=[re2, r_acc2], writes=[r_acc2])

                prev = qk(0)
                for kt in range(1, nk):
                    cur = qk(kt)
                    if kt == 1 and pending is not None:
                        pending[0]()
                    pv(prev)
                    if kt == 3 and pending is not None:
                        pending[1]()
                    prev = cur
                pv(prev)
                return (lambda: ep_a(qb, h, nk), lambda: ep_b(qb, h, nk))

            def ep_a(qb, h, nk):
                S.add("pe", lambda e: e.matmul(pb[PL2][:, :NT], ones, acc2[:], start=True, stop=True),
                      reads=[r_c, r_acc2], writes=[r_pb[PL2]])
                S.add("act", lambda e: e.activation(rl1[:], pb[PL1][:, :NT], AF.Ln), reads=[r_pb[PL1]], writes=[r_ep])
                S.add("act", lambda e: e.activation(rl1[:], rl1[:], AF.Exp, scale=-1.0), reads=[r_ep], writes=[r_ep])
                S.add("act", lambda e: e.activation(rl2[:], pb[PL2][:, :NT], AF.Ln), reads=[r_pb[PL2]], writes=[r_ep])
                S.add("act", lambda e: e.activation(rl2[:], rl2[:], AF.Exp, scale=-1.0), reads=[r_ep], writes=[r_ep])
                S.add("dve", lambda e: e.tensor_tensor(t1[:], pb[PO1][:, :NT], rl1[:], ALU.mult), reads=[r_pb[PO1], r_ep], writes=[r_ep])
                S.add("dve", lambda e: e.tensor_tensor(t2[:], pb[PO2][:, :NT], rl2[:], ALU.mult), reads=[r_pb[PO2], r_ep], writes=[r_ep])
                S.add("dve", lambda e: e.scalar_tensor_tensor(A[:], t2[:], neglam[:, 0:1], t1[:], ALU.mult, ALU.add), reads=[r_ep, r_l], writes=[r_ep])
                S.add("pool", lambda e: e.tensor_tensor(asq[:], A[:], A[:], ALU.mult), reads=[r_ep], writes=[r_ep])

            def ep_b(qb, h, nk):
                sa = PS_S[0][0]
                S.add("pe", lambda e: e.matmul(pb[sa][:, :NT], ones, asq[:], start=True, stop=True), reads=[r_c, r_ep], writes=[r_pb[sa]])
                S.add("act", lambda e: e.activation(sd[:], pb[sa][:, :NT], AF.Ln, bias=float(EPS), scale=1.0 / 128), reads=[r_pb[sa]], writes=[r_ep])
                S.add("act", lambda e: e.activation(rs[:], sd[:], AF.Exp, scale=-0.5), reads=[r_ep], writes=[r_ep])
                S.add("dve", lambda e: e.scalar_tensor_tensor(ao[:, h, :], A[:], gsub[:, 0:1], rs[:], ALU.mult, ALU.mult),
                      reads=[r_ep, r_l], writes=[r_ao[h]])
                if h == 3:
                    outproj(qb)

            def outproj(qb):
                b = qb % 2
                t0 = qb * NT
                for oc in range(8):
                    bk = PS_S[oc % 2][1]
                    for c in range(8):
                        if c < 4:
                            S.add("pe", (lambda oc, c, bk: lambda e: e.matmul(pb[bk][:, :NT], wout[:, c, oc * 128:(oc + 1) * 128], ao[:, c, :],
                                                                               start=(c == 0), stop=False))(oc, c, bk),
                                  reads=[r_wout, r_ao[c]], writes=[r_pb[bk]])
                        else:
                            S.add("pe", (lambda oc, c, bk: lambda e: e.matmul(pb[bk][:, :NT], wout[:, c, oc * 128:(oc + 1) * 128], mo[b][:, c - 4, :],
                                                                               start=False, stop=(c == 7)))(oc, c, bk),
                                  reads=[r_wout, r_mo[b]], writes=[r_pb[bk]])
                    S.add("dve", (lambda oc, bk: lambda e: e.tensor_tensor(xo[:, oc, :], pb[bk][:, :NT], xt[b][:, oc, :], ALU.add))(oc, bk),
                          reads=[r_pb[bk], r_xt[b]], writes=[r_xo])
                S.dma("pool", yv[:, :, t0:t0 + NT], xo[:], reads=[r_xo], writes=[r_xout])

            load(0)
            pending = None
            for qb in range(nblk):
                for h in range(4):
                    if h == 1 and qb + 1 < nblk:
                        load(qb + 1)
                    pending = attn_head(qb, h, pending)
            pending[0]()
            pending[1]()
            S.flush()

    def gla_sweep(self, S, li, xin, r_xin, xout, r_xout):
        T = self.T
        NT = 512
        o_ = li // 2
        scale = 128.0 ** -0.5
        with ExitStack() as st:
            cst, vecs, r_c = self.load_consts(S, st)
            ones = cst[:, C_ONES:C_ONES + 128]
            tri = cst[:, C_TRI:C_TRI + 128]
            su = cst[:, C_SU:C_SU + 128]
            tri4 = cst[:, C_TRI4:C_TRI4 + 512]
            win = self.sb(st, [128, 8, 3104], BF16)
            wout = self.sb(st, [128, 8, D], BF16)
            wgk = self.sb(st, [64, 1, 512], BF16)
            r_win, r_wout, r_wgk = S.res(), S.res(), S.res()
            with ExitStack() as st2:
                wv = self.w["gla_w_in"][o_].rearrange("(kc p) n -> p kc n", p=128)
                self.load_weight(S, st2, win, [wv[:, kc, :] for kc in range(8)], 3088, [r_c], r_win,
                                 scale_cols=[vecs[:, V_NMIX + 8 * li + kc:V_NMIX + 8 * li + kc + 1] for kc in range(8)])
                wo = self.w["gla_w_out"][o_].rearrange("(kc p) n -> p kc n", p=128)
                self.load_weight(S, st2, wout, [wo[:, kc, :] for kc in range(8)], D, [r_c], r_wout,
                                 scale_cols=[vecs[:, V_GNORM + 8 * o_ + kc:V_GNORM + 8 * o_ + kc + 1] for kc in range(8)])
                stg = self.sb(st2, [64, 512], F32)
                r_s = S.res()
                S.add("pool", lambda e: e.memset(stg[:], 0.0), writes=[r_s])
                S.add("pool", lambda e: e.memset(win[:, :, 3088:3104], 0.0), writes=[r_win])
                S.dma("sp", stg[0:16, :], self.w["gla_w_gk_up"][o_], reads=[r_s], writes=[r_s])
                S.dma("sp", stg[32:33, :], self.w["gla_b_gk"][o_], reads=[r_s], writes=[r_s])
                S.add("dve", lambda e: e.tensor_copy(wgk[:, 0, :], stg[:]), reads=[r_s], writes=[r_wgk])
                S.flush()
            xt = [self.sb(st, [128, 8, NT], F32) for _ in range(2)]
            sq = self.sb(st, [128, 8, NT], BF16)
            ssum = self.sb(st, [128, NT], F32)
            rstd = self.sb(st, [128, NT], F32)
            h = self.sb(st, [128, 8, NT], BF16)
            qkf = self.sb(st, [128, 8, NT], F32)
            qf = qkf[:, 0:4, :]
            kf = qkf[:, 4:8, :]
            xo = qkf
            gate = self.sb(st, [128, 8, NT], BF16)
            gl = self.sb(st, [64, NT], BF16)
            ktok = self.sb(st, [128, 512], F32)
            vtok = [self.sb(st, [128, 1024], BF16) for _ in range(2)]
            ez = self.sb(st, [128, 512], F32)
            gtok = self.sb(st, [128, 512], F32)
            ebp = [self.sb(st, [128, 512], F32) for _ in range(2)]
            enb = gtok
            er = ez
            qd = [self.sb(st, [128, 4, 128], BF16) for _ in range(2)]
            kd = [self.sb(st, [128, 4, 128], BF16) for _ in range(2)]
            kl = [self.sb(st, [128, 512], BF16) for _ in range(2)]
            am = self.sb(st, [128, 512], BF16)
            Sf = self.sb(st, [128, 4, 256], F32)
            Sb = [self.sb(st, [128, 4, 256], BF16) for _ in range(2)]
            osq = self.sb(st, [128, 8, 128], F32)
            sdn = self.sb(st, [128, 512], F32)
            rsn = self.sb(st, [128, 8, 128], F32)
            tg = self.sb(st, [128, 8, 128], F32)
            og = self.sb(st, [128, 8, NT], BF16)
            pb = self.psum(st)
            R = S.res
            r_xt = [R(), R()]; r_sq = R(); r_ssum = R(); r_rstd = R()
            r_h = [R() for _ in range(8)]; r_pb = [R() for _ in range(8)]
            r_qf, r_kf, r_gate, r_gl = R(), R(), R(), R()
            r_ktok, r_ez, r_gtok = R(), R(), R(); r_enb = r_gtok; r_er = r_ez
            r_vtok = [R(), R()]; r_ebp = [R(), R()]
            r_qd = [R(), R()]; r_kd = [R(), R()]; r_kl = [R(), R()]; r_am = R()
            r_Sf = [R() for _ in range(4)]; r_Sb = [[R() for _ in range(4)] for _ in range(2)]
            r_osq, r_tg, r_og = R(), R(), R(); r_sdn = R(); r_rsn = R()
            xv = xin.rearrange("(c p) t -> p c t", p=128)
            yv = xout.rearrange("(c p) t -> p c t", p=128)
            ntile = T // NT
            S.add("pool", lambda e: e.memset(Sf[:], 0.0), writes=r_Sf)
            S.add("pool", lambda e: e.memset(Sb[0][:], 0.0), writes=r_Sb[0])
            S.add("pool", lambda e: e.memset(Sb[1][:], 0.0), writes=r_Sb[1])
            S.add("pool", lambda e: e.memset(gl[:], 0.0), writes=[r_gl])
            S.add("pool", lambda e: e.memset(gl[32:64, :], 1.0), writes=[r_gl])

            def load(n):
                S.dma("sp", xt[n % 2][:], xv[:, :, n * NT:(n + 1) * NT], reads=[r_xin], writes=[r_xt[n % 2]])

            def prenorm(n):
                b = n % 2
                S.add("act", lambda e: e.activation(sq[:], xt[b][:], AF.Square), reads=[r_xt[b]], writes=[r_sq])
                S.add("dve", lambda e: e.tensor_reduce(ssum[:], sq[:].rearrange("p c t -> p t c"), AX.X, ALU.add),
                      reads=[r_sq], writes=[r_ssum])
                S.add("pe", lambda e: e.matmul(pb[1][:, :NT], ones, ssum[:], start=True, stop=True),
                      reads=[r_c, r_ssum], writes=[r_pb[1]])
                S.add("act", lambda e: e.activation(ssum[:], pb[1][:, :NT], AF.Ln, bias=float(EPS), scale=1.0 / D),
                      reads=[r_pb[1]], writes=[r_ssum])
                S.add("act", lambda e: e.activation(rstd[:], ssum[:], AF.Exp, scale=-0.5), reads=[r_ssum], writes=[r_rstd])

            def hmul(n):
                b = n % 2
                for c in range(8):
                    eng = "dve" if c % 2 == 0 else "pool"
                    S.add(eng, (lambda c: lambda e: e.tensor_tensor(h[:, c, :], xt[b][:, c, :], rstd[:], ALU.mult))(c),
                          reads=[r_xt[b], r_rstd], writes=[r_h[c]])

            def fm(oc0, ncols, dst_fn, bk):
                for kc in range(8):
                    S.add("pe", (lambda kc: lambda e: e.matmul(pb[bk][:ncols, :NT], win[:, kc, oc0:oc0 + ncols], h[:, kc, :],
                                                               start=(kc == 0), stop=(kc == 7)))(kc),
                          reads=[r_win, r_h[kc]], writes=[r_pb[bk]])
                dst_fn(bk)

            def proj_fm(n):
                k = 0
                for c in range(4):
                    bk = k % 2; k += 1
                    fm(c * 128, 128, (lambda c: lambda bk: S.add("act", lambda e: e.copy(qf[:, c, :], pb[bk][:, :NT]),
                                                                  reads=[r_pb[bk]], writes=[r_qf]))(c), bk)
                for c in range(4):
                    bk = k % 2; k += 1
                    fm(512 + c * 128, 128, (lambda c: lambda bk: S.add("dve", lambda e: e.tensor_copy(kf[:, c, :], pb[bk][:, :NT]),
                                                                        reads=[r_pb[bk]], writes=[r_kf]))(c), bk)
                for c in range(8):
                    bk = k % 2; k += 1
                    fm(2048 + c * 128, 128, (lambda c: lambda bk: S.add("act", lambda e: e.activation(gate[:, c, :], pb[bk][:, :NT], AF.Silu),
                                                                         reads=[r_pb[bk]], writes=[r_gate]))(c), bk)
                bk = k % 2; k += 1
                fm(3072, 32, lambda bk: S.add("dve", lambda e: e.tensor_copy(gl[0:16, :], pb[bk][0:16, :NT]),
                                              reads=[r_pb[bk]], writes=[r_gl]), bk)

            def prep(n, s, cc):
                p2 = cc % 2
                ssl = slice(s * 128, (s + 1) * 128)
                vt_, r_vt_ = vtok[p2], r_vtok[p2]
                for kc in range(8):
                    S.add("pe", (lambda kc: lambda e: e.matmul(pb[2][:, :512], h[:, kc, ssl], win[:, kc, 512:1024],
                                                               start=(kc == 0), stop=(kc == 7)))(kc), reads=[r_win, r_h[kc]], writes=[r_pb[2]])
                S.add("act", lambda e: e.copy(ktok[:], pb[2][:, :512]), reads=[r_pb[2]], writes=[r_ktok])
                for hf in range(2):
                    bk = 3 + hf
                    for kc in range(8):
                        S.add("pe", (lambda kc, hf, bk: lambda e: e.matmul(pb[bk][:, :512], h[:, kc, ssl], win[:, kc, 1024 + hf * 512:1536 + hf * 512],
                                                                           start=(kc == 0), stop=(kc == 7)))(kc, hf, bk),
                              reads=[r_win, r_h[kc]], writes=[r_pb[bk]])
                    S.add("dve", (lambda hf, bk: lambda e: e.tensor_copy(vt_[:, hf * 512:(hf + 1) * 512], pb[bk][:, :512]))(hf, bk),
                          reads=[r_pb[bk]], writes=[r_vt_])
                S.add("pe", lambda e: e.matmul(pb[2][:, :512], gl[:, ssl], wgk[:, 0, :], start=True, stop=True),
                      reads=[r_gl, r_wgk], writes=[r_pb[2]])
                S.add("act", lambda e: e.activation(ez[:], pb[2][:, :512], AF.Exp, scale=-1.0), reads=[r_pb[2]], writes=[r_ez])
                S.add("act", lambda e: e.activation(ez[:], ez[:], AF.Ln, bias=1.0), reads=[r_ez], writes=[r_ez])
                S.add("act", lambda e: e.activation(gtok[:], ez[:], AF.Copy, scale=-1.0 / 16.0), reads=[r_ez], writes=[r_gtok])
                for hh in range(4):
                    S.add("pe", (lambda hh: lambda e: e.matmul(pb[3][:, hh * 128:(hh + 1) * 128], gtok[:, hh * 128:(hh + 1) * 128], tri,
                                                               start=True, stop=True))(hh), reads=[r_gtok, r_c], writes=[r_pb[3]])
                S.add("pe", lambda e: e.matmul(pb[4][:, :512], su, gtok[:], start=True, stop=True), reads=[r_gtok, r_c], writes=[r_pb[4]])
                S.add("act", lambda e: e.activation(ebp[p2][:], pb[3][:, :512], AF.Exp), reads=[r_pb[3]], writes=[r_ebp[p2]])
                S.add("act", lambda e: e.activation(enb[:], pb[3][:, :512], AF.Exp, scale=-1.0), reads=[r_pb[3]], writes=[r_enb])
                S.add("act", lambda e: e.activation(er[:], pb[4][:, :512], AF.Exp), reads=[r_pb[4]], writes=[r_er])
                S.add("dve", lambda e: e.scalar_tensor_tensor(qd[p2][:], qf[:, :, ssl], float(scale), ebp[p2][:].rearrange("p (h t) -> p h t", h=4),
                                                              ALU.mult, ALU.mult), reads=[r_qf, r_ebp[p2]], writes=[r_qd[p2]])
                S.add("dve", lambda e: e.tensor_tensor(kd[p2][:], kf[:, :, ssl], enb[:].rearrange("p (h t) -> p h t", h=4), ALU.mult),
                      reads=[r_kf, r_enb], writes=[r_kd[p2]])
                S.add("pool", lambda e: e.tensor_tensor(kl[p2][:], ktok[:], er[:], ALU.mult), reads=[r_ktok, r_er], writes=[r_kl[p2]])

            def scan(n, s, cc):
                p2 = cc % 2
                ssl = slice(s * 128, (s + 1) * 128)
                vt_, r_vt_ = vtok[p2], r_vtok[p2]
                qd_, kd_, kl_, eb_ = qd[p2], kd[p2], kl[p2], ebp[p2]
                r_qd_, r_kd_, r_kl_, r_eb_ = r_qd[p2], r_kd[p2], r_kl[p2], r_ebp[p2]
                sbr, sbw = Sb[cc % 2], Sb[1 - cc % 2]
                r_sbr, r_sbw = r_Sb[cc % 2], r_Sb[1 - cc % 2]
                for hh in range(4):
                    S.add("pe", (lambda hh: lambda e: e.matmul(pb[5][:, hh * 128:(hh + 1) * 128], kd_[:, hh, :], qd_[:, hh, :],
                                                               start=True, stop=True))(hh), reads=[r_kd_, r_qd_], writes=[r_pb[5]])
                S.add("dve", lambda e: e.tensor_tensor(am[:], pb[5][:, :512], tri4, ALU.mult), reads=[r_pb[5], r_c], writes=[r_am])
                for hh in range(4):
                    bk = hh // 2
                    col = (hh % 2) * 256
                    S.add("pe", (lambda hh, bk, col: lambda e: e.matmul(pb[bk][:, col:col + 256], kl_[:, hh * 128:(hh + 1) * 128],
                                                                        vt_[:, hh * 256:(hh + 1) * 256], start=True, stop=True))(hh, bk, col),
                          reads=[r_kl_, r_vt_], writes=[r_pb[bk]])
                for hh in range(4):
                    for vc in range(2):
                        bk = 6 + hh // 2
                        col = ((hh % 2) * 2 + vc) * 128
                        S.add("pe", (lambda hh, vc, bk, col: lambda e: e.matmul(
                            pb[bk][:, col:col + 128], vt_[:, hh * 256 + vc * 128:hh * 256 + (vc + 1) * 128], am[:, hh * 128:(hh + 1) * 128],
                            start=True, stop=False))(hh, vc, bk, col), reads=[r_vt_, r_am], writes=[r_pb[bk]])
                        S.add("pe", (lambda hh, vc, bk, col: lambda e: e.matmul(
                            pb[bk][:, col:col + 128], sbr[:, hh, vc * 128:(vc + 1) * 128], qd_[:, hh, :],
                            start=False, stop=True))(hh, vc, bk, col), reads=[r_sbr[hh], r_qd_], writes=[r_pb[bk]])
                for hh in range(4):
                    bk = hh // 2
                    col = (hh % 2) * 256
                    S.add("dve", (lambda hh, bk, col: lambda e: e.scalar_tensor_tensor(
                        Sf[:, hh, :], Sf[:, hh, :], eb_[:, hh * 128 + 127:hh * 128 + 128], pb[bk][:, col:col + 256], ALU.mult, ALU.add))(hh, bk, col),
                        reads=[r_pb[bk], r_eb_], writes=[r_Sf[hh]])
                    S.add("act", (lambda hh: lambda e: e.copy(sbw[:, hh, :], Sf[:, hh, :]))(hh), reads=[r_Sf[hh]], writes=[r_sbw[hh]])
                for q2 in range(2):
                    S.add("act", (lambda q2: lambda e: e.activation(osq[:, q2 * 4:(q2 + 1) * 4, :],
                                                                    pb[6 + q2][:, :512].rearrange("p (c t) -> p c t", c=4), AF.Square))(q2),
                          reads=[r_pb[6 + q2]], writes=[r_osq])
                for hh in range(4):
                    for vc in range(2):
                        S.add("pe", (lambda hh, vc: lambda e: e.matmul(pb[5][:, hh * 128:(hh + 1) * 128], ones, osq[:, hh * 2 + vc, :],
                                                                       start=(vc == 0), stop=(vc == 1)))(hh, vc), reads=[r_osq, r_c], writes=[r_pb[5]])
                S.add("act", lambda e: e.activation(sdn[:], pb[5][:, :512], AF.Ln, bias=float(EPS), scale=1.0 / 256), reads=[r_pb[5]], writes=[r_sdn])
                rsn_v = rsn[:].rearrange("p (h v) t -> p h v t", v=2)
                for vc in range(2):
                    S.add("act", (lambda vc: lambda e: e.activation(rsn_v[:, :, vc, :], sdn[:].rearrange("p (h t) -> p h t", h=4),
                                                                    AF.Exp, scale=-0.5))(vc), reads=[r_sdn], writes=[r_rsn])
                S.add("pool", lambda e: e.tensor_tensor(tg[:], gate[:, :, ssl], rsn[:], ALU.mult), reads=[r_gate, r_rsn], writes=[r_tg])
                for q2 in range(2):
                    S.add("dve", (lambda q2: lambda e: e.tensor_tensor(
                        og[:, q2 * 4:(q2 + 1) * 4, ssl], pb[6 + q2][:, :512].rearrange("p (c t) -> p c t", c=4), tg[:, q2 * 4:(q2 + 1) * 4, :],
                        ALU.mult))(q2), reads=[r_pb[6 + q2], r_tg], writes=[r_og])

            def outproj(n):
                b = n % 2
                t0 = n * NT
                for oc in range(8):
                    bk = oc % 2
                    for c in range(8):
                        S.add("pe", (lambda oc, c, bk: lambda e: e.matmul(pb[bk][:, :NT], wout[:, c, oc * 128:(oc + 1) * 128], og[:, c, :],
                                                                           start=(c == 0), stop=(c == 7)))(oc, c, bk),
                              reads=[r_wout, r_og], writes=[r_pb[bk]])
                    S.add("dve", (lambda oc, bk: lambda e: e.tensor_tensor(xo[:, oc, :], pb[bk][:, :NT], xt[b][:, oc, :], ALU.add))(oc, bk),
                          reads=[r_pb[bk], r_xt[b]], writes=[r_qf, r_kf])
                S.dma("pool", yv[:, :, t0:t0 + NT], xo[:], reads=[r_qf, r_kf], writes=[r_xout])

            load(0)
            for n in range(ntile):
                if n + 1 < ntile:
                    load(n + 1)
                if n == 0:
                    prenorm(0)
                    hmul(0)
                proj_fm(n)
                base = 4 * n
                prep(n, 0, base)
                prep(n, 1, base + 1)
                scan(n, 0, base)
                if n + 1 < ntile:
                    prenorm(n + 1)
                prep(n, 2, base + 2)
                scan(n, 1, base + 1)
                prep(n, 3, base + 3)
                scan(n, 2, base + 2)
                if n + 1 < ntile:
                    hmul(n + 1)
                scan(n, 3, base + 3)
                outproj(n)
            S.flush()

    def build(self):
        nc = self.nc
        with ExitStack() as st:
            S = Sched(nc, st)
            self.r_scr = S.res()
            r_in = S.res()
            r_a, r_b = S.res(), S.res()
            cur, r_cur = self.x_in, r_in
            nl = len(self.layers)
            for idx, li in enumerate(self.layers):
                last = (idx == nl - 1)
                if li % 2 == 0:
                    self.a1_sweep(S, li, cur, r_cur)
                    self.a2_sweep(S, li, cur, r_cur, self.xa, r_a)
                else:
                    self.gla_sweep(S, li, cur, r_cur, self.xa, r_a)
                if last:
                    self.ffn_sweep(S, li, self.xa, r_a, self.y_out, S.res(), final=self.final)
                else:
                    self.ffn_sweep(S, li, self.xa, r_a, self.xb, r_b, final=False)
                    cur, r_cur = self.xb, r_b
            self.ninst = S.ninst
        return nc


def host_inputs(inp, T_sl=None):
    cst = make_cst()
    vecs = make_vecs(inp)
    common = {
        "cst": cst, "vecs": vecs,
        "ab_w_in": np.ascontiguousarray(inp["ab_w_in"], np.float32),
        "ab_lambda": np.ascontiguousarray(np.asarray(inp["ab_lambda"], np.float32).reshape(2, 256)),
        "pool_w": np.ascontiguousarray(inp["pool_w"], np.float32),
        "ab_w_out": np.ascontiguousarray(inp["ab_w_out"], np.float32),
        "gla_w_in": np.ascontiguousarray(inp["gla_w_in"], np.float32),
        "gla_w_gk_up": np.ascontiguousarray(inp["gla_w_gk_up"], np.float32),
        "gla_b_gk": np.ascontiguousarray(np.asarray(inp["gla_b_gk"], np.float32).reshape(2, 1, 512)),
        "gla_w_out": np.ascontiguousarray(inp["gla_w_out"], np.float32),
        "ffn_w1": np.ascontiguousarray(inp["ffn_w1"], np.float32),
        "ffn_w2": np.ascontiguousarray(inp["ffn_w2"], np.float32),
    }
    return common


_CACHE = {}


def kernel(**inputs):
    x = np.asarray(inputs["x"], np.float32)
    B, T, _ = x.shape
    key = (T,)
    if key not in _CACHE:
        _CACHE[key] = Builder(T).build()
    nc = _CACHE[key]
    common = host_inputs(inputs)
    zeros = {k: np.zeros_like(v) for k, v in common.items()}
    zeros["xT"] = np.zeros((D, T), np.float32)
    in_maps = []
    for c in range(NCORES):
        if c in ACTIVE:
            m = dict(common)
            m["xT"] = np.ascontiguousarray(x[ACTIVE.index(c)].T)
        else:
            m = zeros
        in_maps.append(m)
    res = run_bass_kernel_spmd(nc, in_maps, core_ids=list(range(NCORES)))
    out = np.empty((B, T, D), np.float32)
    for b in range(B):
        out[b] = res.results[ACTIVE[b]]["yT"].T
    return out
```

```python
import math
from contextlib import ExitStack

import numpy as np
import concourse.bass as bass
import concourse.mybir as mybir
from concourse.bass_utils import run_bass_kernel_spmd

F32 = mybir.dt.float32
BF16 = mybir.dt.bfloat16
ALU = mybir.AluOpType
AF = mybir.ActivationFunctionType
AX = mybir.AxisListType

D = 1024
DFF = 4096
EPS = 1e-6
DEPTH = 4
NCORES = 8
ACTIVE = (0, 1, 4, 5)

ENGS = ("pe", "act", "dve", "pool", "sp")
NDMA_SEMS = 8


class Res:
    __slots__ = ("writer", "readers")

    def __init__(self):
        self.writer = None
        self.readers = []


class Op:
    __slots__ = ("eng", "fn", "deps", "is_dma", "sig", "count", "dsem", "dval", "waits", "snap", "done")

    def __init__(self, eng, fn, is_dma):
        self.eng = eng
        self.fn = fn
        self.is_dma = is_dma
        self.deps = []
        self.sig = False
        self.count = 0
        self.dsem = None
        self.dval = 0
        self.waits = []
        self.snap = None
        self.done = False


class Sched:
    def __init__(self, nc, stack):
        self.nc = nc
        self.ops = {e: [] for e in ENGS}
        self.cnt = {e: 0 for e in ENGS}
        self.ndma = {e: 0 for e in ENGS}
        self.dma_hist = {e: [] for e in ENGS}
        self.known = {e: [0] * len(ENGS) for e in ENGS}
        self.kdma = {e: {} for e in ENGS}
        self.esem = {e: stack.enter_context(nc.semaphore(f"s_{e}")) for e in ENGS if e != "sp"}
        self.dsem = {}
        for e in ("sp", "pool", "act"):
            for k in range(NDMA_SEMS):
                self.dsem[(e, k)] = stack.enter_context(nc.semaphore(f"d_{e}{k}"))
        self.ninst = 0

    def res(self):
        return Res()

    def add(self, eng, fn, reads=(), writes=(), is_dma=False):
        op = Op(eng, fn, is_dma)
        deps = []
        for r in reads:
            if r.writer is not None:
                deps.append(r.writer)
            r.readers.append(op)
        for w in writes:
            if w.writer is not None:
                deps.append(w.writer)
            deps.extend(w.readers)
            w.writer = op
            w.readers = []
        seen = set()
        for d in deps:
            if d is op or d.done or id(d) in seen:
                continue
            seen.add(id(d))
            op.deps.append(d)
        self.ops[eng].append(op)
        return op

    def dma(self, q, out, in_, reads=(), writes=()):
        return self.add(q, lambda e: e.dma_start(out=out, in_=in_), reads, writes, is_dma=True)

    def barrier(self):
        last = []
        for e in ENGS:
            for o in reversed(self.ops[e]):
                if o.fn is not None and not o.is_dma:
                    last.append(o)
                    break
        rec = []
        for e in ENGS:
            dm = [o for o in self.ops[e] if o.is_dma][-NDMA_SEMS:]
            rec.extend(dm)
        for e in ENGS:
            op = Op(e, None, False)
            op.deps = list(last) + list(rec)
            self.ops[e].append(op)

    def flush(self):
        self.barrier()
        for e in ENGS:
            for op in self.ops[e]:
                for d in op.deps:
                    d.sig = True
        for e in ENGS:
            for op in self.ops[e]:
                if op.is_dma:
                    n = self.ndma[e]
                    op.dsem = (e, n % NDMA_SEMS)
                    op.dval = 16 * (n // NDMA_SEMS + 1)
                    self.dma_hist[e].append(op)
                    self.ndma[e] += 1
                elif op.sig:
                    self.cnt[e] += 1
                    op.count = self.cnt[e]
        eidx = {e: i for i, e in enumerate(ENGS)}
        for e in ENGS:
            known = self.known[e]
            kdma = self.kdma[e]
            hist = self.dma_hist[e]
            nd = len(hist) - sum(1 for o in self.ops[e] if o.is_dma)
            for op in self.ops[e]:
                deps = list(op.deps)
                if op.is_dma:
                    if nd >= NDMA_SEMS:
                        deps.append(hist[nd - NDMA_SEMS])
                    nd += 1
                best = {}
                for d in deps:
                    if d.is_dma:
                        if kdma.get(d.dsem, 0) >= d.dval:
                            continue
                        kdma[d.dsem] = d.dval
                        best[("dma", d.dsem)] = d.dval
                    else:
                        if d.eng == "pe" and e == "pe":
                            continue
                        j = eidx[d.eng]
                        if known[j] >= d.count:
                            continue
                        known[j] = d.count
                        best[("eng", d.eng)] = max(best.get(("eng", d.eng), 0), d.count)
                        if d.snap is not None:
                            for k in range(len(ENGS)):
                                if d.snap[k] > known[k]:
                                    known[k] = d.snap[k]
                op.waits = [(k[0], k[1], v) for k, v in best.items()]
                op.snap = tuple(known)
        nc = self.nc
        ops = self.ops
        esem, dsem = self.esem, self.dsem

        def run(eng_name):
            lst = ops[eng_name]

            def body(eng):
                for op in lst:
                    for kind, key, val in op.waits:
                        eng.wait_ge(dsem[key] if kind == "dma" else esem[key], val)
                    if op.fn is None:
                        continue
                    ins = op.fn(eng)
                    if op.is_dma:
                        ins.then_inc(dsem[op.dsem], 16)
                    elif op.sig:
                        ins.then_inc(esem[eng_name], 1)
            return body

        with nc.Block() as block:
            block.tensor(run("pe"))
            block.scalar(run("act"))
            block.vector(run("dve"))
            block.gpsimd(run("pool"))
            block.sync(run("sp"))
        for e in ENGS:
            self.ninst += len(self.ops[e])
            for op in self.ops[e]:
                op.done = True
                op.fn = None
                op.deps = []
            self.ops[e] = []


C_ONES = 0
C_TRI = 128
C_SU = 256
C_TRI4 = 384
C_INVC = 896
NCST = 960


def make_cst():
    c = np.zeros((128, NCST), np.float32)
    c[:, C_ONES:C_ONES + 128] = 1.0
    s = np.arange(128)
    tri = (s[:, None] <= s[None, :]).astype(np.float32)
    c[:, C_TRI:C_TRI + 128] = tri
    c[:, C_SU:C_SU + 128] = (s[:, None] > s[None, :]).astype(np.float32)
    for h in range(4):
        c[:, C_TRI4 + h * 128:C_TRI4 + (h + 1) * 128] = tri
    for g, w in enumerate((2, 4, 8, 16)):
        t = np.arange(16)
        c[:, C_INVC + g * 16:C_INVC + (g + 1) * 16] = 1.0 / np.minimum(t + 1, w)
    return c


V_NMIX = 0
V_NFFN = 32
V_NFIN = 64
V_PSCALE = 72
V_SUBLN = 80
V_GNORM = 82
NVEC = 98


def make_vecs(inp):
    v = np.zeros((128, NVEC), np.float32)

    def put(col, arr):
        a = np.asarray(arr, np.float32).reshape(-1, 128).T
        v[:, col:col + a.shape[1]] = a

    for i in range(DEPTH):
        put(V_NMIX + 8 * i, inp["norm_mix"][i])
        put(V_NFFN + 8 * i, inp["norm_ffn"][i])
    put(V_NFIN, inp["norm_final"])
    for e in range(2):
        put(V_PSCALE + 4 * e, inp["pool_scale"][e])
        put(V_SUBLN + e, inp["ab_subln"][e])
        put(V_GNORM + 8 * e, inp["gla_norm"][e].reshape(-1))
    return v


class Builder:
    def __init__(self, T, layers=(0, 1, 2, 3), final=True, dbg=False):
        self.T = T
        self.layers = tuple(layers)
        self.final = final
        nc = self.nc = bass.Bass("TRN2", target_bir_lowering=False)
        dt = nc.dram_tensor
        self.x_in = dt("xT", [D, T], F32, kind="ExternalInput").ap()
        self.y_out = dt("yT", [D, T], F32, kind="ExternalOutput").ap()
        self.cst_d = dt("cst", [128, NCST], F32, kind="ExternalInput").ap()
        self.vecs_d = dt("vecs", [128, NVEC], F32, kind="ExternalInput").ap()
        self.w = {}
        for name, shape in (("ab_w_in", [2, D, 2048]), ("ab_lambda", [2, 256]), ("pool_w", [2, 4, 128, 128]),
                            ("ab_w_out", [2, D, D]), ("gla_w_in", [2, D, 3088]), ("gla_w_gk_up", [2, 16, 512]),
                            ("gla_b_gk", [2, 1, 512]), ("gla_w_out", [2, D, D]), ("ffn_w1", [4, D, DFF]),
                            ("ffn_w2", [4, DFF, D])):
            self.w[name] = dt(name, shape, F32, kind="ExternalInput").ap()
        kw = {"kind": "ExternalOutput"} if dbg else {}
        self.xa = dt("xa", [D, T], F32, **kw).ap()
        self.xb = dt("xb", [D, T], F32, **kw).ap()
        self.qT = dt("qTs", [4, 128, T], BF16, **kw).ap()
        self.kT = dt("kTs", [4, 128, T], BF16, **kw).ap()
        self.Vs = dt("Vs", [4, 128, T // 128, 128], BF16, **kw).ap()
        self.mT = dt("mTs", [4, 128, T], BF16, **kw).ap()
        self.r_x = {}
        self.nsb = 0

    def sb(self, st, shape, dtp):
        self.nsb += 1
        return st.enter_context(self.nc.sbuf_tensor(f"sb{self.nsb}", shape, dtp))

    def psum(self, st):
        self.nsb += 1
        return [st.enter_context(self.nc.psum_tensor(f"ps{self.nsb}_{i}", [128, 512], F32)) for i in range(8)]

    def load_consts(self, S, st):
        cst = self.sb(st, [128, NCST], F32)
        vecs = self.sb(st, [128, NVEC], F32)
        r = S.res()
        S.dma("sp", cst[:], self.cst_d, writes=[r])
        S.dma("sp", vecs[:], self.vecs_d, writes=[r])
        return cst, vecs, r

    def load_weight(self, S, st_tmp, dst, src_rows, ncols, rres, wres, scale_cols=None, cw=2048):
        NB = 6
        key = (id(st_tmp), cw)
        if getattr(self, "_stg_key", None) != key:
            self._stg_key = key
            self._stg = ([self.sb(st_tmp, [128, cw], F32) for _ in range(NB)], [S.res() for _ in range(NB)])
        stg, r_stg = self._stg
        engs = ["act", "dve", "act", "dve", "pool", "dve"]
        ci = 0
        for r, src in enumerate(src_rows):
            for c0 in range(0, ncols, cw):
                c1 = min(ncols, c0 + cw)
                b = ci % NB
                S.dma("sp" if ci % 2 == 0 else "act", stg[b][:, :c1 - c0], src[:, c0:c1], writes=[r_stg[b]])
                out = dst[:, r, c0:c1]
                in_ = stg[b][:, :c1 - c0]
                eng = engs[ci % len(engs)]
                sc = None if scale_cols is None else scale_cols[r]
                if sc is None:
                    if eng == "act":
                        S.add("act", (lambda o, i: lambda e: e.copy(o, i))(out, in_), reads=[r_stg[b]] + rres, writes=[wres])
                    else:
                        S.add(eng, (lambda o, i: lambda e: e.tensor_copy(o, i))(out, in_), reads=[r_stg[b]] + rres, writes=[wres])
                else:
                    if eng == "act":
                        S.add("act", (lambda o, i, s: lambda e: e.activation(o, i, AF.Copy, scale=s))(out, in_, sc),
                              reads=[r_stg[b]] + rres, writes=[wres])
                    else:
                        S.add(eng, (lambda o, i, s: lambda e: e.tensor_scalar(o, i, s, None, ALU.mult))(out, in_, sc),
                              reads=[r_stg[b]] + rres, writes=[wres])
                ci += 1

    def rmsnorm_a(self, S, xt_c, r_xt, sq, r_sq, ssum, r_ssum):
        S.add("act", lambda e: e.activation(sq[:], xt_c[:], AF.Square), reads=[r_xt], writes=[r_sq])
        S.add("dve", lambda e: e.tensor_reduce(ssum[:], sq[:].rearrange("p c t -> p t c"), AX.X, ALU.add),
              reads=[r_sq], writes=[r_ssum])

    def rmsnorm_b(self, S, xt_c, r_xt, NT, ssum, r_ssum, rstd, r_rstd, psb, r_psb, ones, r_c, h, r_h, nfeat_inv=1.0 / D,
                  engs=("dve", "pool")):
        S.add("pe", lambda e: e.matmul(psb[:, :NT], ones, ssum[:], start=True, stop=True),
              reads=[r_c, r_ssum], writes=[r_psb])
        S.add("act", lambda e: e.activation(ssum[:], psb[:, :NT], AF.Ln, bias=float(EPS), scale=nfeat_inv),
              reads=[r_psb], writes=[r_ssum])
        S.add("act", lambda e: e.activation(rstd[:], ssum[:], AF.Exp, scale=-0.5), reads=[r_ssum], writes=[r_rstd])
        for c in range(8):
            eng = engs[c % len(engs)]
            S.add(eng, (lambda c: lambda e: e.tensor_tensor(h[:, c, :], xt_c[:, c, :], rstd[:], ALU.mult))(c),
                  reads=[r_xt, r_rstd], writes=[r_h[c]])

    def ffn_sweep(self, S, li, xin, r_xin, xout, r_xout, final):
        T = self.T
        FNT = 256
        nc = self.nc
        with ExitStack() as st:
            cst, vecs, r_c = self.load_consts(S, st)
            ones = cst[:, C_ONES:C_ONES + 128]
            w1sb = self.sb(st, [128, 8, DFF], BF16)
            w2sb = self.sb(st, [128, 32, D], BF16)
            r_w1, r_w2 = S.res(), S.res()
            with ExitStack() as st2:
                w1v = self.w["ffn_w1"][li].rearrange("(kc p) n -> p kc n", p=128)
                w2v = self.w["ffn_w2"][li].rearrange("(hc p) n -> p hc n", p=128)
                self.load_weight(S, st2, w1sb, [w1v[:, kc, :] for kc in range(8)], DFF, [r_c], r_w1,
                                 scale_cols=[vecs[:, V_NFFN + 8 * li + kc:V_NFFN + 8 * li + kc + 1] for kc in range(8)])
                self.load_weight(S, st2, w2sb, [w2v[:, hc, :] for hc in range(32)], D, [r_c], r_w2)
                S.flush()
            xt = [self.sb(st, [128, 8, FNT], F32) for _ in range(2)]
            xo = self.sb(st, [128, 8, FNT], F32)
            sq = self.sb(st, [128, 8, FNT], F32)
            ssum = self.sb(st, [128, FNT], F32)
            rstd = self.sb(st, [128, FNT], F32)
            h = self.sb(st, [128, 8, FNT], BF16)
            hid = self.sb(st, [128, 32, FNT], BF16)
            rl = [self.sb(st, [128, FNT], F32) for _ in range(4)]
            if final:
                sq2 = self.sb(st, [128, 8, FNT], F32)
                ssum2 = self.sb(st, [128, FNT], F32)
                rstd2 = self.sb(st, [128, FNT], F32)
                r_sq2, r_ssum2, r_rstd2 = S.res(), S.res(), S.res()
            pb = self.psum(st)
            R = S.res
            r_xt = [R(), R()]; r_xo = R(); r_sq = R(); r_ssum = R(); r_rstd = R()
            r_h = [R() for _ in range(8)]; r_hid = [R() for _ in range(32)]
            r_pb = [R() for _ in range(8)]; r_rl = [R() for _ in range(4)]
            xv = xin.rearrange("(c p) t -> p c t", p=128)
            yv = xout.rearrange("(c p) t -> p c t", p=128)
            ntile = T // FNT

            def load(n):
                S.dma("sp", xt[n % 2][:], xv[:, :, n * FNT:(n + 1) * FNT], reads=[r_xin], writes=[r_xt[n % 2]])

            def norm_a(n):
                b = n % 2
                self.rmsnorm_a(S, xt[b], r_xt[b], sq, r_sq, ssum, r_ssum)

            def norm_b(n):
                b = n % 2
                self.rmsnorm_b(S, xt[b], r_xt[b], FNT, ssum, r_ssum, rstd, r_rstd, pb[7], r_pb[7], ones, r_c, h, r_h)

            def up(n):
                for j in range(32):
                    bk = j % 4
                    for kc in range(8):
                        S.add("pe", (lambda j, kc, bk: lambda e: e.matmul(
                            pb[bk][:, :FNT], w1sb[:, kc, j * 128:(j + 1) * 128], h[:, kc, :],
                            start=(kc == 0), stop=(kc == 7)))(j, kc, bk), reads=[r_w1, r_h[kc]], writes=[r_pb[bk]])
                    S.add("act", (lambda j, bk: lambda e: e.activation(rl[j % 4][:], pb[bk][:, :FNT], AF.Relu))(j, bk),
                          reads=[r_pb[bk]], writes=[r_rl[j % 4]])
                    S.add("pool", (lambda j: lambda e: e.tensor_tensor(hid[:, j, :], rl[j % 4][:], rl[j % 4][:], ALU.mult))(j),
                          reads=[r_rl[j % 4]], writes=[r_hid[j]])

            def down(n):
                b = n % 2
                for oc in range(8):
                    bk = 4 + oc % 2
                    for hc in range(32):
                        S.add("pe", (lambda oc, hc, bk: lambda e: e.matmul(
                            pb[bk][:, :FNT], w2sb[:, hc, oc * 128:(oc + 1) * 128], hid[:, hc, :],
                            start=(hc == 0), stop=(hc == 31)))(oc, hc, bk), reads=[r_w2, r_hid[hc]], writes=[r_pb[bk]])
                    S.add("dve", (lambda oc, bk: lambda e: e.tensor_tensor(
                        xo[:, oc, :], pb[bk][:, :FNT], xt[b][:, oc, :], ALU.add))(oc, bk),
                        reads=[r_pb[bk], r_xt[b]], writes=[r_xo])
                if final:
                    S.add("act", lambda e: e.activation(sq2[:], xo[:], AF.Square), reads=[r_xo], writes=[r_sq2])
                    S.add("dve", lambda e: e.tensor_reduce(ssum2[:], sq2[:].rearrange("p c t -> p t c"), AX.X, ALU.add),
                          reads=[r_sq2], writes=[r_ssum2])
                    S.add("pe", lambda e: e.matmul(pb[6][:, :FNT], ones, ssum2[:], start=True, stop=True),
                          reads=[r_c, r_ssum2], writes=[r_pb[6]])
                    S.add("act", lambda e: e.activation(ssum2[:], pb[6][:, :FNT], AF.Ln, bias=float(EPS), scale=1.0 / D),
                          reads=[r_pb[6]], writes=[r_ssum2])
                    S.add("act", lambda e: e.activation(rstd2[:], ssum2[:], AF.Exp, scale=-0.5), reads=[r_ssum2], writes=[r_rstd2])
                    for c in range(8):
                        S.add("dve", (lambda c: lambda e: e.scalar_tensor_tensor(
                            sq2[:, c, :], xo[:, c, :], vecs[:, V_NFIN + c:V_NFIN + c + 1], rstd2[:], ALU.mult, ALU.mult))(c),
                            reads=[r_xo, r_rstd2, r_c], writes=[r_sq2])
                    S.dma("pool", yv[:, :, n * FNT:(n + 1) * FNT], sq2[:], reads=[r_sq2], writes=[r_xout])
                else:
                    S.dma("pool", yv[:, :, n * FNT:(n + 1) * FNT], xo[:], reads=[r_xo], writes=[r_xout])

            load(0)
            if ntile > 1:
                load(1)
            norm_a(0)
            norm_b(0)
            for n in range(ntile):
                if n + 1 < ntile:
                    norm_a(n + 1)
                up(n)
                if n + 1 < ntile:
                    norm_b(n + 1)
                down(n)
                if n + 2 < ntile:
                    load(n + 2)
            S.flush()

    def a1_sweep(self, S, li, xin, r_xin):
        T = self.T
        NT = 512
        e_ = li // 2
        with ExitStack() as st:
            cst, vecs, r_c = self.load_consts(S, st)
            ones = cst[:, C_ONES:C_ONES + 128]
            win = self.sb(st, [128, 8, 2048], BF16)
            pw = self.sb(st, [128, 4, 128], BF16)
            r_win, r_pw = S.res(), S.res()
            with ExitStack() as st2:
                wv = self.w["ab_w_in"][e_].rearrange("(kc p) n -> p kc n", p=128)
                self.load_weight(S, st2, win, [wv[:, kc, :] for kc in range(8)], 2048, [r_c], r_win,
                                 scale_cols=[vecs[:, V_NMIX + 8 * li + kc:V_NMIX + 8 * li + kc + 1] for kc in range(8)])
                pwv = self.w["pool_w"][e_].rearrange("g c d -> c g d")
                self.load_weight(S, st2, pw, [pwv[:, g, :] for g in range(4)], 128, [r_c], r_pw, cw=128)
                S.flush()
            xt = [self.sb(st, [128, 8, NT], F32) for _ in range(2)]
            sq = self.sb(st, [128, 8, NT], F32)
            ssum = self.sb(st, [128, NT], F32)
            rstd = self.sb(st, [128, NT], F32)
            hh_ = [self.sb(st, [128, 8, NT], BF16) for _ in range(2)]
            qk = [self.sb(st, [128, 8, NT], BF16) for _ in range(2)]
            vt = [self.sb(st, [128, 4, 512], BF16) for _ in range(2)]
            uext = [self.sb(st, [128, 4, 16 + NT], F32) for _ in range(2)]
            ta = self.sb(st, [128, 16 + NT], F32)
            tb = self.sb(st, [128, 16 + NT], F32)
            rr = self.sb(st, [128, 4, NT], BF16)
            mo = [self.sb(st, [128, 4, NT], BF16) for _ in range(2)]
            pb = self.psum(st)
            R = S.res
            r_xt = [R(), R()]; r_sq = R(); r_ssum = R(); r_rstd = R()
            r_hh = [[R() for _ in range(8)] for _ in range(2)]; r_pb = [R() for _ in range(8)]
            r_qk = [R(), R()]; r_vt = [R(), R()]; r_ue = [[R() for _ in range(4)] for _ in range(2)]
            r_ta, r_tb = R(), R(); r_rr = [R() for _ in range(4)]; r_mo = [R(), R()]
            r_sc = self.r_scr
            xv = xin.rearrange("(c p) t -> p c t", p=128)
            ntile = T // NT
            qTv = self.qT.rearrange("h p t -> p h t")
            kTv = self.kT.rearrange("h p t -> p h t")
            mTv = self.mT.rearrange("g p t -> p g t")
            Vv = self.Vs.rearrange("h p k v -> p h k v")

            def load(n):
                S.dma("sp", xt[n % 2][:], xv[:, :, n * NT:(n + 1) * NT], reads=[r_xin], writes=[r_xt[n % 2]])

            def norm_a(n):
                b = n % 2
                self.rmsnorm_a(S, xt[b], r_xt[b], sq, r_sq, ssum, r_ssum)

            def norm_b(n):
                b = n % 2
                self.rmsnorm_b(S, xt[b], r_xt[b], NT, ssum, r_ssum, rstd, r_rstd, pb[7], r_pb[7], ones, r_c, hh_[b], r_hh[b],
                               engs=("dve",))

            for g in range(4):
                S.add("pool", (lambda g: lambda e: e.memset(uext[0][:, g, 0:16], 0.0))(g), writes=[r_ue[0][g]])
            S.add("pool", lambda e: e.memset(ta[:], 0.0), writes=[r_ta])
            S.add("pool", lambda e: e.memset(tb[:], 0.0), writes=[r_tb])

            def proj(n, part):
                b = n % 2
                t0 = n * NT
                h = hh_[b]
                r_h = r_hh[b]
                if part == 1:
                    proj_b(n, b, t0, h, r_h)
                    return
                for oc in range(8):
                    bk = oc % 3
                    for kc in range(8):
                        S.add("pe", (lambda oc, kc, bk: lambda e: e.matmul(
                            pb[bk][:, :NT], win[:, kc, oc * 128:(oc + 1) * 128], h[:, kc, :],
                            start=(kc == 0), stop=(kc == 7)))(oc, kc, bk), reads=[r_win, r_h[kc]], writes=[r_pb[bk]])
                    S.add("act", (lambda oc, bk: lambda e: e.copy(qk[b][:, oc, :], pb[bk][:, :NT]))(oc, bk),
                          reads=[r_pb[bk]], writes=[r_qk[b]])
                S.dma("pool", qTv[:, :, t0:t0 + NT], qk[b][:, 0:4, :], reads=[r_qk[b]], writes=[r_sc])
                S.dma("pool", kTv[:, :, t0:t0 + NT], qk[b][:, 4:8, :], reads=[r_qk[b]], writes=[r_sc])

            def proj_b(n, b, t0, h, r_h):
                for s in range(4):
                    bk = 3 + s % 2
                    for kc in range(8):
                        S.add("pe", (lambda s, kc, bk: lambda e: e.matmul(
                            pb[bk][:, :512], h[:, kc, s * 128:(s + 1) * 128], win[:, kc, 1024:1536],
                            start=(kc == 0), stop=(kc == 7)))(s, kc, bk), reads=[r_win, r_h[kc]], writes=[r_pb[bk]])
                    S.add("dve", (lambda s, bk: lambda e: e.tensor_copy(vt[b][:, s, :], pb[bk][:, :512]))(s, bk),
                          reads=[r_pb[bk]], writes=[r_vt[b]])
                for hh in range(4):
                    S.dma("pool", self.Vs[hh, :, 4 * n:4 * n + 4, :], vt[b][:, :, hh * 128:(hh + 1) * 128],
                          reads=[r_vt[b]], writes=[r_sc])
                for g in range(4):
                    bk = 5 + g % 2
                    oc = 12 + g
                    for kc in range(8):
                        S.add("pe", (lambda oc, kc, bk: lambda e: e.matmul(
                            pb[bk][:, :NT], win[:, kc, oc * 128:(oc + 1) * 128], h[:, kc, :],
                            start=(kc == 0), stop=(kc == 7)))(oc, kc, bk), reads=[r_win, r_h[kc]], writes=[r_pb[bk]])
                    S.add("act", (lambda g, bk: lambda e: e.copy(uext[b][:, g, 16:16 + NT], pb[bk][:, :NT]))(g, bk),
                          reads=[r_pb[bk]], writes=[r_ue[b][g]])

            def pool(n):
                b = n % 2
                t0 = n * NT
                W = 16 + NT
                for g in range(4):
                    w = 2 ** (g + 1)
                    src = uext[b][:, g, :]
                    r_src = r_ue[b][g]
                    bufs = [(ta, r_ta), (tb, r_tb)]
                    sh = 1
                    k = 0
                    cur, r_cur = src, r_src
                    while sh < w:
                        dst, r_dst = bufs[k % 2]
                        S.add("pool", (lambda dst, cur, sh: lambda e: e.tensor_tensor(
                            dst[:, sh:W], cur[:, sh:W], cur[:, 0:W - sh], ALU.add))(dst, cur, sh),
                            reads=[r_cur], writes=[r_dst])
                        cur, r_cur = dst[:], r_dst
                        sh *= 2
                        k += 1
                    S.add("dve", (lambda g, cur, w: lambda e: e.scalar_tensor_tensor(
                        rr[:, g, :], cur[:, 16:W], 1.0 / w, uext[b][:, g, 16:W], ALU.mult, ALU.subtract))(g, cur, w),
                        reads=[r_cur, r_ue[b][g]], writes=[r_rr[g]])
                    if n == 0:
                        S.add("dve", (lambda g, cur: lambda e: e.tensor_tensor(
                            ta[:, 0:16], cur[:, 16:32], cst[:, C_INVC + g * 16:C_INVC + (g + 1) * 16], ALU.mult))(g, cur),
                            reads=[r_cur, r_c], writes=[r_ta])
                        S.add("dve", (lambda g: lambda e: e.tensor_tensor(
                            rr[:, g, 0:16], ta[:, 0:16], uext[b][:, g, 16:32], ALU.subtract))(g),
                            reads=[r_ta, r_ue[b][g]], writes=[r_rr[g]])
                    if n + 1 < ntile:
                        S.add("pool", (lambda g: lambda e: e.tensor_copy(uext[1 - b][:, g, 0:16], uext[b][:, g, NT:NT + 16]))(g),
                              reads=[r_ue[b][g]], writes=[r_ue[1 - b][g]])

            def pool_mm(n):
                b = n % 2
                t0 = n * NT
                for g in range(4):
                    bk = 5 + g % 2
                    S.add("pe", (lambda g, bk: lambda e: e.matmul(pb[bk][:, :NT], pw[:, g, :], rr[:, g, :], start=True, stop=True))(g, bk),
                          reads=[r_pw, r_rr[g]], writes=[r_pb[bk]])
                    S.add("act", (lambda g, bk: lambda e: e.activation(
                        mo[b][:, g, :], pb[bk][:, :NT], AF.Copy, scale=vecs[:, V_PSCALE + 4 * e_ + g:V_PSCALE + 4 * e_ + g + 1]))(g, bk),
                        reads=[r_pb[bk], r_c], writes=[r_mo[b]])
                S.dma("pool", mTv[:, :, t0:t0 + NT], mo[b][:], reads=[r_mo[b]], writes=[r_sc])

            load(0)
            if ntile > 1:
                load(1)
            norm_a(0)
            norm_b(0)
            for n in range(ntile):
                if n + 1 < ntile:
                    norm_a(n + 1)
                proj(n, 0)
                if n + 1 < ntile:
                    norm_b(n + 1)
                proj(n, 1)
                if n > 0:
                    pool_mm(n - 1)
                pool(n)
                if n + 2 < ntile:
                    load(n + 2)
            pool_mm(ntile - 1)
            S.flush()

    def a2_sweep(self, S, li, xin, r_xin, xout, r_xout):
        T = self.T
        NT = 512
        e_ = li // 2
        lam_init = 0.8 - 0.6 * math.exp(-0.3 * li)
        KG = 16
        with ExitStack() as st:
            cst, vecs, r_c = self.load_consts(S, st)
            ones = cst[:, C_ONES:C_ONES + 128]
            wout = self.sb(st, [128, 8, D], BF16)
            r_wout = S.res()
            onesb = self.sb(st, [128, 128], BF16)
            lamt = self.sb(st, [128, 256], F32)
            lprod = self.sb(st, [128, 128], F32)
            lsum = self.sb(st, [128, 2], F32)
            neglam = self.sb(st, [128, 1], F32)
            gsub = self.sb(st, [128, 1], F32)
            r_l = S.res()
            ediag = [[self.sb(st, [128, NT], BF16) for _ in range(2)] for _ in range(4)]
            r_ed = [[S.res() for _ in range(2)] for _ in range(4)]
            with ExitStack() as st2:
                wv = self.w["ab_w_out"][e_].rearrange("(kc p) n -> p kc n", p=128)
                self.load_weight(S, st2, wout, [wv[:, kc, :] for kc in range(8)], D, [r_c], r_wout)
                S.add("dve", lambda e: e.tensor_copy(onesb[:], ones), reads=[r_c], writes=[r_l])
                S.dma("sp", lamt[:], self.w["ab_lambda"][e_:e_ + 1, :].partition_broadcast(128), writes=[r_l])
                S.add("dve", lambda e: e.tensor_tensor(lprod[:, 0:64], lamt[:, 0:64], lamt[:, 64:128], ALU.mult), reads=[r_l], writes=[r_l])
                S.add("dve", lambda e: e.tensor_tensor(lprod[:, 64:128], lamt[:, 128:192], lamt[:, 192:256], ALU.mult), reads=[r_l], writes=[r_l])
                S.add("dve", lambda e: e.tensor_reduce(lsum[:], lprod[:].rearrange("p (a d) -> p a d", a=2), AX.X, ALU.add), reads=[r_l], writes=[r_l])
                S.add("act", lambda e: e.activation(lsum[:], lsum[:], AF.Exp), reads=[r_l], writes=[r_l])
                S.add("dve", lambda e: e.tensor_tensor(neglam[:], lsum[:, 1:2], lsum[:, 0:1], ALU.subtract), reads=[r_l], writes=[r_l])
                S.add("dve", lambda e: e.tensor_scalar(neglam[:], neglam[:], -float(lam_init), None, ALU.add), reads=[r_l], writes=[r_l])
                S.add("dve", lambda e: e.tensor_scalar(gsub[:], vecs[:, V_SUBLN + e_:V_SUBLN + e_ + 1], float(1.0 - lam_init), None, ALU.mult),
                      reads=[r_c], writes=[r_l])
                for j in range(4):
                    for a in range(2):
                        S.add("pool", (lambda j, a: lambda e: e.memset(ediag[j][a][:], 0.0))(j, a), writes=[r_ed[j][a]])
                S.flush()
            xt = [self.sb(st, [128, 8, NT], F32) for _ in range(2)]
            xo = self.sb(st, [128, 8, NT], F32)
            qt = [self.sb(st, [128, 4, NT], BF16) for _ in range(2)]
            mo = [self.sb(st, [128, 4, NT], BF16) for _ in range(2)]
            ao = self.sb(st, [128, 4, NT], BF16)
            kb = [self.sb(st, [128, KG * 128], BF16) for _ in range(3)]
            vb = [self.sb(st, [128, KG, 128], BF16) for _ in range(3)]
            eb = [[self.sb(st, [128, NT], BF16) for _ in range(2)] for _ in range(3)]
            rl1 = self.sb(st, [128, NT], F32); rl2 = self.sb(st, [128, NT], F32)
            t1 = self.sb(st, [128, NT], F32); t2 = self.sb(st, [128, NT], F32)
            A = self.sb(st, [128, NT], F32); asq = self.sb(st, [128, NT], F32)
            sd = self.sb(st, [128, NT], F32); rs = self.sb(st, [128, NT], F32)
            acc2 = self.sb(st, [128, NT], F32)
            r_acc2 = S.res()
            pb = self.psum(st)
            R = S.res
            r_xt = [R(), R()]; r_xo = R(); r_qt = [R(), R()]; r_mo = [R(), R()]; r_ao = [R() for _ in range(4)]
            r_kb = [R() for _ in range(3)]; r_vb = [R() for _ in range(3)]
            r_eb = [[R(), R()] for _ in range(3)]
            r_ep = R()
            r_pb = [R() for _ in range(8)]
            r_sc = self.r_scr
            xv = xin.rearrange("(c p) t -> p c t", p=128)
            yv = xout.rearrange("(c p) t -> p c t", p=128)
            qTv = self.qT.rearrange("h p t -> p h t")
            mTv = self.mT.rearrange("g p t -> p g t")
            nblk = T // NT
            PS_S = [(0, 1), (2, 3)]
            PO1, PO2, PL1, PL2 = 4, 5, 6, 7
            grp_ctr = [0]

            def load(n):
                t0 = n * NT
                b = n % 2
                S.dma("sp", xt[b][:], xv[:, :, t0:t0 + NT], reads=[r_xin], writes=[r_xt[b]])
                S.dma("sp", qt[b][:], qTv[:, :, t0:t0 + NT], reads=[r_sc], writes=[r_qt[b]])
                S.dma("sp", mo[b][:], mTv[:, :, t0:t0 + NT], reads=[r_sc], writes=[r_mo[b]])

            def load_kv(h, g0, ng):
                i = grp_ctr[0] % 3
                grp_ctr[0] += 1
                S.dma("sp", kb[i][:, :ng * 128], self.kT[h, :, g0 * 128:(g0 + ng) * 128], reads=[r_sc], writes=[r_kb[i]])
                S.dma("sp", vb[i][:, :ng, :], self.Vs[h, :, g0:g0 + ng, :], reads=[r_sc], writes=[r_vb[i]])
                return i

            def attn_head(qb, h, pending):
                b = qb % 2
                nk = 4 * (qb + 1)
                groups = []
                for g0 in range(0, nk, KG):
                    groups.append((g0, min(KG, nk - g0)))
                gbuf = {}
                for gi in range(min(2, len(groups))):
                    gbuf[gi] = load_kv(h, *groups[gi])
                ectr = [0]
                pend = []

                def qk(kt):
                    gi, lk = kt // KG, kt % KG
                    if gi not in gbuf:
                        gbuf[gi] = load_kv(h, *groups[gi])
                    i = gbuf[gi]
                    j = kt - 4 * qb
                    c0 = 128 * j if j >= 0 else 0
                    sa, sb_ = PS_S[kt % 2]
                    S.add("pe", lambda e: e.matmul(pb[sa][:, c0:NT], kb[i][0:64, lk * 128:(lk + 1) * 128], qt[b][0:64, h, c0:NT],
                                                   start=True, stop=True), reads=[r_kb[i], r_qt[b]], writes=[r_pb[sa]])
                    S.add("pe", lambda e: e.matmul(pb[sb_][:, c0:NT], kb[i][64:128, lk * 128:(lk + 1) * 128], qt[b][64:128, h, c0:NT],
                                                   start=True, stop=True), reads=[r_kb[i], r_qt[b]], writes=[r_pb[sb_]])
                    if j < 0:
                        k3 = ectr[0] % 3
                        ectr[0] += 1
                        e1, e2 = eb[k3][0], eb[k3][1]
                        re1, re2 = r_eb[k3][0], r_eb[k3][1]
                        S.add("act", lambda e: e.activation(e1[:, :], pb[sa][:, :NT], AF.Exp, scale=0.125), reads=[r_pb[sa]], writes=[re1])
                        S.add("act", lambda e: e.activation(e2[:, :], pb[sb_][:, :NT], AF.Exp, scale=0.125), reads=[r_pb[sb_]], writes=[re2])
                    else:
                        e1, e2 = ediag[j][0], ediag[j][1]
                        re1, re2 = r_ed[j][0], r_ed[j][1]
                        for (et, re_, pbn) in ((e1, re1, sa), (e2, re2, sb_)):
                            S.add("act", (lambda et, pbn: lambda e: e.activation(et[0:64, c0:NT], pb[pbn][0:64, c0:NT], AF.Exp, scale=0.125))(et, pbn),
                                  reads=[r_pb[pbn]], writes=[re_])
                            S.add("act", (lambda et, pbn: lambda e: e.activation(et[64:128, c0 + 64:NT], pb[pbn][64:128, c0 + 64:NT], AF.Exp, scale=0.125))(et, pbn),
                                  reads=[r_pb[pbn]], writes=[re_])
                    return (kt, c0, e1, e2, re1, re2, i, lk)

                def pv(item):
                    kt, c0, e1, e2, re1, re2, i, lk = item
                    first, last = (kt == 0), (kt == nk - 1)
                    rngs = [(c0, NT)]
                    for (a0, a1) in rngs:
                        S.add("pe", (lambda a0, a1: lambda e: e.matmul(pb[PO1][:, a0:a1], vb[i][:, lk, :], e1[:, a0:a1], start=first, stop=last, skip_group_check=True))(a0, a1),
                              reads=[r_vb[i], re1], writes=[r_pb[PO1]])
                    for (a0, a1) in rngs:
                        S.add("pe", (lambda a0, a1: lambda e: e.matmul(pb[PO2][:, a0:a1], vb[i][:, lk, :], e2[:, a0:a1], start=first, stop=last, skip_group_check=True))(a0, a1),
                              reads=[r_vb[i], re2], writes=[r_pb[PO2]])
                    for (a0, a1) in rngs:
                        S.add("pe", (lambda a0, a1: lambda e: e.matmul(pb[PL1][:, a0:a1], onesb[:], e1[:, a0:a1], start=first, stop=last, skip_group_check=True))(a0, a1),
                              reads=[r_l, re1], writes=[r_pb[PL1]])
                    if first:
                        S.add("dve", lambda e: e.tensor_copy(acc2[:, c0:NT], e2[:, c0:NT]), reads=[re2], writes=[r_acc2])
                    else:
                        S.add("dve", lambda e: e.tensor_tensor(acc2[:, c0:NT], acc2[:, c0:NT], e2[:, c0:NT], ALU.add),
                              reads=[re2, r_acc2], writes=[r_acc2])

                prev = qk(0)
                for kt in range(1, nk):
                    cur = qk(kt)
                    if kt == 1 and pending is not None:
                        pending[0]()
                    pv(prev)
                    if kt == 3 and pending is not None:
                        pending[1]()
                    prev = cur
                pv(prev)
                return (lambda: ep_a(qb, h, nk), lambda: ep_b(qb, h, nk))

            def ep_a(qb, h, nk):
                S.add("pe", lambda e: e.matmul(pb[PL2][:, :NT], ones, acc2[:], start=True, stop=True),
                      reads=[r_c, r_acc2], writes=[r_pb[PL2]])
                S.add("act", lambda e: e.activation(rl1[:], pb[PL1][:, :NT], AF.Ln), reads=[r_pb[PL1]], writes=[r_ep])
                S.add("act", lambda e: e.activation(rl1[:], rl1[:], AF.Exp, scale=-1.0), reads=[r_ep], writes=[r_ep])
                S.add("act", lambda e: e.activation(rl2[:], pb[PL2][:, :NT], AF.Ln), reads=[r_pb[PL2]], writes=[r_ep])
                S.add("act", lambda e: e.activation(rl2[:], rl2[:], AF.Exp, scale=-1.0), reads=[r_ep], writes=[r_ep])
                S.add("dve", lambda e: e.tensor_tensor(t1[:], pb[PO1][:, :NT], rl1[:], ALU.mult), reads=[r_pb[PO1], r_ep], writes=[r_ep])
                S.add("dve", lambda e: e.tensor_tensor(t2[:], pb[PO2][:, :NT], rl2[:], ALU.mult), reads=[r_pb[PO2], r_ep], writes=[r_ep])
                S.add("dve", lambda e: e.scalar_tensor_tensor(A[:], t2[:], neglam[:, 0:1], t1[:], ALU.mult, ALU.add), reads=[r_ep, r_l], writes=[r_ep])
                S.add("pool", lambda e: e.tensor_tensor(asq[:], A[:], A[:], ALU.mult), reads=[r_ep], writes=[r_ep])

            def ep_b(qb, h, nk):
                sa = PS_S[0][0]
                S.add("pe", lambda e: e.matmul(pb[sa][:, :NT], ones, asq[:], start=True, stop=True), reads=[r_c, r_ep], writes=[r_pb[sa]])
                S.add("act", lambda e: e.activation(sd[:], pb[sa][:, :NT], AF.Ln, bias=float(EPS), scale=1.0 / 128), reads=[r_pb[sa]], writes=[r_ep])
                S.add("act", lambda e: e.activation(rs[:], sd[:], AF.Exp, scale=-0.5), reads=[r_ep], writes=[r_ep])
                S.add("dve", lambda e: e.scalar_tensor_tensor(ao[:, h, :], A[:], gsub[:, 0:1], rs[:], ALU.mult, ALU.mult),
                      reads=[r_ep, r_l], writes=[r_ao[h]])
                if h == 3:
                    outproj(qb)

            def outproj(qb):
                b = qb % 2
                t0 = qb * NT
                for oc in range(8):
                    bk = PS_S[oc % 2][1]
                    for c in range(8):
                        if c < 4:
                            S.add("pe", (lambda oc, c, bk: lambda e: e.matmul(pb[bk][:, :NT], wout[:, c, oc * 128:(oc + 1) * 128], ao[:, c, :],
                                                                               start=(c == 0), stop=False))(oc, c, bk),
                                  reads=[r_wout, r_ao[c]], writes=[r_pb[bk]])
                        else:
                            S.add("pe", (lambda oc, c, bk: lambda e: e.matmul(pb[bk][:, :NT], wout[:, c, oc * 128:(oc + 1) * 128], mo[b][:, c - 4, :],
                                                                               start=False, stop=(c == 7)))(oc, c, bk),
                                  reads=[r_wout, r_mo[b]], writes=[r_pb[bk]])
                    S.add("dve", (lambda oc, bk: lambda e: e.tensor_tensor(xo[:, oc, :], pb[bk][:, :NT], xt[b][:, oc, :], ALU.add))(oc, bk),
                          reads=[r_pb[bk], r_xt[b]], writes=[r_xo])
                S.dma("pool", yv[:, :, t0:t0 + NT], xo[:], reads=[r_xo], writes=[r_xout])

            load(0)
            pending = None
            for qb in range(nblk):
                for h in range(4):
                    if h == 1 and qb + 1 < nblk:
                        load(qb + 1)
                    pending = attn_head(qb, h, pending)
            pending[0]()
            pending[1]()
            S.flush()

    def gla_sweep(self, S, li, xin, r_xin, xout, r_xout):
        T = self.T
        NT = 512
        o_ = li // 2
        scale = 128.0 ** -0.5
        with ExitStack() as st:
            cst, vecs, r_c = self.load_consts(S, st)
            ones = cst[:, C_ONES:C_ONES + 128]
            tri = cst[:, C_TRI:C_TRI + 128]
            su = cst[:, C_SU:C_SU + 128]
            tri4 = cst[:, C_TRI4:C_TRI4 + 512]
            win = self.sb(st, [128, 8, 3104], BF16)
            wout = self.sb(st, [128, 8, D], BF16)
            wgk = self.sb(st, [64, 1, 512], BF16)
            r_win, r_wout, r_wgk = S.res(), S.res(), S.res()
            with ExitStack() as st2:
                wv = self.w["gla_w_in"][o_].rearrange("(kc p) n -> p kc n", p=128)
                self.load_weight(S, st2, win, [wv[:, kc, :] for kc in range(8)], 3088, [r_c], r_win,
                                 scale_cols=[vecs[:, V_NMIX + 8 * li + kc:V_NMIX + 8 * li + kc + 1] for kc in range(8)])
                wo = self.w["gla_w_out"][o_].rearrange("(kc p) n -> p kc n", p=128)
                self.load_weight(S, st2, wout, [wo[:, kc, :] for kc in range(8)], D, [r_c], r_wout,
                                 scale_cols=[vecs[:, V_GNORM + 8 * o_ + kc:V_GNORM + 8 * o_ + kc + 1] for kc in range(8)])
                stg = self.sb(st2, [64, 512], F32)
                r_s = S.res()
                S.add("pool", lambda e: e.memset(stg[:], 0.0), writes=[r_s])
                S.add("pool", lambda e: e.memset(win[:, :, 3088:3104], 0.0), writes=[r_win])
                S.dma("sp", stg[0:16, :], self.w["gla_w_gk_up"][o_], reads=[r_s], writes=[r_s])
                S.dma("sp", stg[32:33, :], self.w["gla_b_gk"][o_], reads=[r_s], writes=[r_s])
                S.add("dve", lambda e: e.tensor_copy(wgk[:, 0, :], stg[:]), reads=[r_s], writes=[r_wgk])
                S.flush()
            xt = [self.sb(st, [128, 8, NT], F32) for _ in range(2)]
            sq = self.sb(st, [128, 8, NT], BF16)
            ssum = self.sb(st, [128, NT], F32)
            rstd = self.sb(st, [128, NT], F32)
            h = self.sb(st, [128, 8, NT], BF16)
            qkf = self.sb(st, [128, 8, NT], F32)
            qf = qkf[:, 0:4, :]
            kf = qkf[:, 4:8, :]
            xo = qkf
            gate = self.sb(st, [128, 8, NT], BF16)
            gl = self.sb(st, [64, NT], BF16)
            ktok = self.sb(st, [128, 512], F32)
            vtok = [self.sb(st, [128, 1024], BF16) for _ in range(2)]
            ez = self.sb(st, [128, 512], F32)
            gtok = self.sb(st, [128, 512], F32)
            ebp = [self.sb(st, [128, 512], F32) for _ in range(2)]
            enb = gtok
            er = ez
            qd = [self.sb(st, [128, 4, 128], BF16) for _ in range(2)]
            kd = [self.sb(st, [128, 4, 128], BF16) for _ in range(2)]
            kl = [self.sb(st, [128, 512], BF16) for _ in range(2)]
            am = self.sb(st, [128, 512], BF16)
            Sf = self.sb(st, [128, 4, 256], F32)
            Sb = [self.sb(st, [128, 4, 256], BF16) for _ in range(2)]
            osq = self.sb(st, [128, 8, 128], F32)
            sdn = self.sb(st, [128, 512], F32)
            rsn = self.sb(st, [128, 8, 128], F32)
            tg = self.sb(st, [128, 8, 128], F32)
            og = self.sb(st, [128, 8, NT], BF16)
            pb = self.psum(st)
            R = S.res
            r_xt = [R(), R()]; r_sq = R(); r_ssum = R(); r_rstd = R()
            r_h = [R() for _ in range(8)]; r_pb = [R() for _ in range(8)]
            r_qf, r_kf, r_gate, r_gl = R(), R(), R(), R()
            r_ktok, r_ez, r_gtok = R(), R(), R(); r_enb = r_gtok; r_er = r_ez
            r_vtok = [R(), R()]; r_ebp = [R(), R()]
            r_qd = [R(), R()]; r_kd = [R(), R()]; r_kl = [R(), R()]; r_am = R()
            r_Sf = [R() for _ in range(4)]; r_Sb = [[R() for _ in range(4)] for _ in range(2)]
            r_osq, r_tg, r_og = R(), R(), R(); r_sdn = R(); r_rsn = R()
            r_xoc = [R() for _ in range(8)]
            xv = xin.rearrange("(c p) t -> p c t", p=128)
            yv = xout.rearrange("(c p) t -> p c t", p=128)
            ntile = T // NT
            S.add("pool", lambda e: e.memset(Sf[:], 0.0), writes=r_Sf)
            S.add("pool", lambda e: e.memset(Sb[0][:], 0.0), writes=r_Sb[0])
            S.add("pool", lambda e: e.memset(Sb[1][:], 0.0), writes=r_Sb[1])
            S.add("pool", lambda e: e.memset(gl[:], 0.0), writes=[r_gl])
            S.add("pool", lambda e: e.memset(gl[32:64, :], 1.0), writes=[r_gl])

            def load(n):
                S.dma("sp", xt[n % 2][:], xv[:, :, n * NT:(n + 1) * NT], reads=[r_xin], writes=[r_xt[n % 2]])

            def prenorm_a(n):
                b = n % 2
                S.add("act", lambda e: e.activation(sq[:], xt[b][:], AF.Square), reads=[r_xt[b]], writes=[r_sq])
                S.add("dve", lambda e: e.tensor_reduce(ssum[:], sq[:].rearrange("p c t -> p t c"), AX.X, ALU.add),
                      reads=[r_sq], writes=[r_ssum])

            def prenorm_b(n):
                S.add("pe", lambda e: e.matmul(pb[1][:, :NT], ones, ssum[:], start=True, stop=True),
                      reads=[r_c, r_ssum], writes=[r_pb[1]])
                S.add("act", lambda e: e.activation(ssum[:], pb[1][:, :NT], AF.Ln, bias=float(EPS), scale=1.0 / D),
                      reads=[r_pb[1]], writes=[r_ssum])
                S.add("act", lambda e: e.activation(rstd[:], ssum[:], AF.Exp, scale=-0.5), reads=[r_ssum], writes=[r_rstd])

            def hmul(n):
                b = n % 2
                for c in range(8):
                    eng = "dve" if c % 2 == 0 else "pool"
                    S.add(eng, (lambda c: lambda e: e.tensor_tensor(h[:, c, :], xt[b][:, c, :], rstd[:], ALU.mult))(c),
                          reads=[r_xt[b], r_rstd], writes=[r_h[c]])

            def fm(oc0, ncols, dst_fn, bk):
                for kc in range(8):
                    S.add("pe", (lambda kc: lambda e: e.matmul(pb[bk][:ncols, :NT], win[:, kc, oc0:oc0 + ncols], h[:, kc, :],
                                                               start=(kc == 0), stop=(kc == 7)))(kc),
                          reads=[r_win, r_h[kc]], writes=[r_pb[bk]])
                dst_fn(bk)

            def proj_fm(n):
                k = 0
                for c in range(4):
                    bk = k % 2; k += 1
                    fm(c * 128, 128, (lambda c: lambda bk: S.add("act", lambda e: e.copy(qf[:, c, :], pb[bk][:, :NT]),
                                                                  reads=[r_pb[bk]], writes=[r_qf, r_xoc[c]]))(c), bk)
                for c in range(4):
                    bk = k % 2; k += 1
                    fm(512 + c * 128, 128, (lambda c: lambda bk: S.add("dve", lambda e: e.tensor_copy(kf[:, c, :], pb[bk][:, :NT]),
                                                                        reads=[r_pb[bk]], writes=[r_kf, r_xoc[4 + c]]))(c), bk)
                for c in range(8):
                    bk = k % 2; k += 1
                    fm(2048 + c * 128, 128, (lambda c: lambda bk: S.add("act", lambda e: e.activation(gate[:, c, :], pb[bk][:, :NT], AF.Silu),
                                                                         reads=[r_pb[bk]], writes=[r_gate]))(c), bk)
                bk = k % 2; k += 1
                fm(3072, 32, lambda bk: S.add("dve", lambda e: e.tensor_copy(gl[0:16, :], pb[bk][0:16, :NT]),
                                              reads=[r_pb[bk]], writes=[r_gl]), bk)

            def prep(n, s, cc):
                p2 = cc % 2
                ssl = slice(s * 128, (s + 1) * 128)
                vt_, r_vt_ = vtok[p2], r_vtok[p2]
                for kc in range(8):
                    S.add("pe", (lambda kc: lambda e: e.matmul(pb[2][:, :512], h[:, kc, ssl], win[:, kc, 512:1024],
                                                               start=(kc == 0), stop=(kc == 7)))(kc), reads=[r_win, r_h[kc]], writes=[r_pb[2]])
                S.add("act", lambda e: e.copy(ktok[:], pb[2][:, :512]), reads=[r_pb[2]], writes=[r_ktok])
                for hf in range(2):
                    bk = 3 + hf
                    for kc in range(8):
                        S.add("pe", (lambda kc, hf, bk: lambda e: e.matmul(pb[bk][:, :512], h[:, kc, ssl], win[:, kc, 1024 + hf * 512:1536 + hf * 512],
                                                                           start=(kc == 0), stop=(kc == 7)))(kc, hf, bk),
                              reads=[r_win, r_h[kc]], writes=[r_pb[bk]])
                    S.add("dve", (lambda hf, bk: lambda e: e.tensor_copy(vt_[:, hf * 512:(hf + 1) * 512], pb[bk][:, :512]))(hf, bk),
                          reads=[r_pb[bk]], writes=[r_vt_])
                S.add("pe", lambda e: e.matmul(pb[2][:, :512], gl[:, ssl], wgk[:, 0, :], start=True, stop=True),
                      reads=[r_gl, r_wgk], writes=[r_pb[2]])
                S.add("act", lambda e: e.activation(ez[:], pb[2][:, :512], AF.Exp, scale=-1.0), reads=[r_pb[2]], writes=[r_ez])
                S.add("act", lambda e: e.activation(ez[:], ez[:], AF.Ln, bias=1.0), reads=[r_ez], writes=[r_ez])
                S.add("act", lambda e: e.activation(gtok[:], ez[:], AF.Copy, scale=-1.0 / 16.0), reads=[r_ez], writes=[r_gtok])
                for hh in range(4):
                    S.add("pe", (lambda hh: lambda e: e.matmul(pb[3][:, hh * 128:(hh + 1) * 128], gtok[:, hh * 128:(hh + 1) * 128], tri,
                                                               start=True, stop=True))(hh), reads=[r_gtok, r_c], writes=[r_pb[3]])
                S.add("pe", lambda e: e.matmul(pb[4][:, :512], su, gtok[:], start=True, stop=True), reads=[r_gtok, r_c], writes=[r_pb[4]])
                S.add("act", lambda e: e.activation(ebp[p2][:], pb[3][:, :512], AF.Exp), reads=[r_pb[3]], writes=[r_ebp[p2]])
                S.add("act", lambda e: e.activation(enb[:], pb[3][:, :512], AF.Exp, scale=-1.0), reads=[r_pb[3]], writes=[r_enb])
                S.add("act", lambda e: e.activation(er[:], pb[4][:, :512], AF.Exp), reads=[r_pb[4]], writes=[r_er])
                S.add("dve", lambda e: e.scalar_tensor_tensor(qd[p2][:], qf[:, :, ssl], float(scale), ebp[p2][:].rearrange("p (h t) -> p h t", h=4),
                                                              ALU.mult, ALU.mult), reads=[r_qf, r_ebp[p2]], writes=[r_qd[p2]])
                S.add("dve", lambda e: e.tensor_tensor(kd[p2][:], kf[:, :, ssl], enb[:].rearrange("p (h t) -> p h t", h=4), ALU.mult),
                      reads=[r_kf, r_enb], writes=[r_kd[p2]])
                S.add("pool", lambda e: e.tensor_tensor(kl[p2][:], ktok[:], er[:], ALU.mult), reads=[r_ktok, r_er], writes=[r_kl[p2]])

            def scan(n, s, cc):
                p2 = cc % 2
                ssl = slice(s * 128, (s + 1) * 128)
                vt_, r_vt_ = vtok[p2], r_vtok[p2]
                qd_, kd_, kl_, eb_ = qd[p2], kd[p2], kl[p2], ebp[p2]
                r_qd_, r_kd_, r_kl_, r_eb_ = r_qd[p2], r_kd[p2], r_kl[p2], r_ebp[p2]
                sbr, sbw = Sb[cc % 2], Sb[1 - cc % 2]
                r_sbr, r_sbw = r_Sb[cc % 2], r_Sb[1 - cc % 2]
                for hh in range(4):
                    S.add("pe", (lambda hh: lambda e: e.matmul(pb[5][:, hh * 128:(hh + 1) * 128], kd_[:, hh, :], qd_[:, hh, :],
                                                               start=True, stop=True))(hh), reads=[r_kd_, r_qd_], writes=[r_pb[5]])
                S.add("dve", lambda e: e.tensor_tensor(am[:], pb[5][:, :512], tri4, ALU.mult), reads=[r_pb[5], r_c], writes=[r_am])
                for hh in range(4):
                    bk = hh // 2
                    col = (hh % 2) * 256
                    S.add("pe", (lambda hh, bk, col: lambda e: e.matmul(pb[bk][:, col:col + 256], kl_[:, hh * 128:(hh + 1) * 128],
                                                                        vt_[:, hh * 256:(hh + 1) * 256], start=True, stop=True))(hh, bk, col),
                          reads=[r_kl_, r_vt_], writes=[r_pb[bk]])
                for hh in range(4):
                    for vc in range(2):
                        bk = 6 + hh // 2
                        col = ((hh % 2) * 2 + vc) * 128
                        S.add("pe", (lambda hh, vc, bk, col: lambda e: e.matmul(
                            pb[bk][:, col:col + 128], vt_[:, hh * 256 + vc * 128:hh * 256 + (vc + 1) * 128], am[:, hh * 128:(hh + 1) * 128],
                            start=True, stop=False))(hh, vc, bk, col), reads=[r_vt_, r_am], writes=[r_pb[bk]])
                        S.add("pe", (lambda hh, vc, bk, col: lambda e: e.matmul(
                            pb[bk][:, col:col + 128], sbr[:, hh, vc * 128:(vc + 1) * 128], qd_[:, hh, :],
                            start=False, stop=True))(hh, vc, bk, col), reads=[r_sbr[hh], r_qd_], writes=[r_pb[bk]])
                for hh in range(4):
                    bk = hh // 2
                    col = (hh % 2) * 256
                    S.add("dve", (lambda hh, bk, col: lambda e: e.scalar_tensor_tensor(
                        Sf[:, hh, :], Sf[:, hh, :], eb_[:, hh * 128 + 127:hh * 128 + 128], pb[bk][:, col:col + 256], ALU.mult, ALU.add))(hh, bk, col),
                        reads=[r_pb[bk], r_eb_], writes=[r_Sf[hh]])
                    S.add("act", (lambda hh: lambda e: e.copy(sbw[:, hh, :], Sf[:, hh, :]))(hh), reads=[r_Sf[hh]], writes=[r_sbw[hh]])
                for q2 in range(2):
                    S.add("act", (lambda q2: lambda e: e.activation(osq[:, q2 * 4:(q2 + 1) * 4, :],
                                                                    pb[6 + q2][:, :512].rearrange("p (c t) -> p c t", c=4), AF.Square))(q2),
                          reads=[r_pb[6 + q2]], writes=[r_osq])
                for hh in range(4):
                    for vc in range(2):
                        S.add("pe", (lambda hh, vc: lambda e: e.matmul(pb[5][:, hh * 128:(hh + 1) * 128], ones, osq[:, hh * 2 + vc, :],
                                                                       start=(vc == 0), stop=(vc == 1)))(hh, vc), reads=[r_osq, r_c], writes=[r_pb[5]])
                S.add("act", lambda e: e.activation(sdn[:], pb[5][:, :512], AF.Ln, bias=float(EPS), scale=1.0 / 256), reads=[r_pb[5]], writes=[r_sdn])
                rsn_v = rsn[:].rearrange("p (h v) t -> p h v t", v=2)
                for vc in range(2):
                    S.add("act", (lambda vc: lambda e: e.activation(rsn_v[:, :, vc, :], sdn[:].rearrange("p (h t) -> p h t", h=4),
                                                                    AF.Exp, scale=-0.5))(vc), reads=[r_sdn], writes=[r_rsn])
                S.add("pool", lambda e: e.tensor_tensor(tg[:], gate[:, :, ssl], rsn[:], ALU.mult), reads=[r_gate, r_rsn], writes=[r_tg])
                for q2 in range(2):
                    S.add("dve", (lambda q2: lambda e: e.tensor_tensor(
                        og[:, q2 * 4:(q2 + 1) * 4, ssl], pb[6 + q2][:, :512].rearrange("p (c t) -> p c t", c=4), tg[:, q2 * 4:(q2 + 1) * 4, :],
                        ALU.mult))(q2), reads=[r_pb[6 + q2], r_tg], writes=[r_og])

            def outproj(n):
                b = n % 2
                t0 = n * NT
                for oc in range(8):
                    bk = oc % 2
                    for c in range(8):
                        S.add("pe", (lambda oc, c, bk: lambda e: e.matmul(pb[bk][:, :NT], wout[:, c, oc * 128:(oc + 1) * 128], og[:, c, :],
                                                                           start=(c == 0), stop=(c == 7)))(oc, c, bk),
                              reads=[r_wout, r_og], writes=[r_pb[bk]])
                    S.add("dve", (lambda oc, bk: lambda e: e.tensor_tensor(xo[:, oc, :], pb[bk][:, :NT], xt[b][:, oc, :], ALU.add))(oc, bk),
                          reads=[r_pb[bk], r_xt[b]], writes=[r_xoc[oc], r_qf if oc < 4 else r_kf])
                    S.dma("pool", yv[:, oc, t0:t0 + NT], xo[:, oc, :], reads=[r_xoc[oc]], writes=[r_xout])

            load(0)
            for n in range(ntile):
                if n + 1 < ntile:
                    load(n + 1)
                if n == 0:
                    prenorm_a(0)
                    prenorm_b(0)
                    hmul(0)
                proj_fm(n)
                if n + 1 < ntile:
                    prenorm_a(n + 1)
                base = 4 * n
                prep(n, 0, base)
                prep(n, 1, base + 1)
                scan(n, 0, base)
                prep(n, 2, base + 2)
                scan(n, 1, base + 1)
                if n + 1 < ntile:
                    prenorm_b(n + 1)
                prep(n, 3, base + 3)
                scan(n, 2, base + 2)
                scan(n, 3, base + 3)
                if n + 1 < ntile:
                    hmul(n + 1)
                outproj(n)
            S.flush()

    def build(self):
        nc = self.nc
        with ExitStack() as st:
            S = Sched(nc, st)
            self.r_scr = S.res()
            r_in = S.res()
            r_a, r_b = S.res(), S.res()
            cur, r_cur = self.x_in, r_in
            nl = len(self.layers)
            for idx, li in enumerate(self.layers):
                last = (idx == nl - 1)
                if li % 2 == 0:
                    self.a1_sweep(S, li, cur, r_cur)
                    self.a2_sweep(S, li, cur, r_cur, self.xa, r_a)
                else:
                    self.gla_sweep(S, li, cur, r_cur, self.xa, r_a)
                if last:
                    self.ffn_sweep(S, li, self.xa, r_a, self.y_out, S.res(), final=self.final)
                else:
                    self.ffn_sweep(S, li, self.xa, r_a, self.xb, r_b, final=False)
                    cur, r_cur = self.xb, r_b
            self.ninst = S.ninst
        return nc


def host_inputs(inp, T_sl=None):
    cst = make_cst()
    vecs = make_vecs(inp)
    common = {
        "cst": cst, "vecs": vecs,
        "ab_w_in": np.ascontiguousarray(inp["ab_w_in"], np.float32),
        "ab_lambda": np.ascontiguousarray(np.asarray(inp["ab_lambda"], np.float32).reshape(2, 256)),
        "pool_w": np.ascontiguousarray(inp["pool_w"], np.float32),
        "ab_w_out": np.ascontiguousarray(inp["ab_w_out"], np.float32),
        "gla_w_in": np.ascontiguousarray(inp["gla_w_in"], np.float32),
        "gla_w_gk_up": np.ascontiguousarray(inp["gla_w_gk_up"], np.float32),
        "gla_b_gk": np.ascontiguousarray(np.asarray(inp["gla_b_gk"], np.float32).reshape(2, 1, 512)),
        "gla_w_out": np.ascontiguousarray(inp["gla_w_out"], np.float32),
        "ffn_w1": np.ascontiguousarray(inp["ffn_w1"], np.float32),
        "ffn_w2": np.ascontiguousarray(inp["ffn_w2"], np.float32),
    }
    return common


_CACHE = {}


def kernel(**inputs):
    x = np.asarray(inputs["x"], np.float32)
    B, T, _ = x.shape
    key = (T,)
    if key not in _CACHE:
        _CACHE[key] = Builder(T).build()
    nc = _CACHE[key]
    common = host_inputs(inputs)
    zeros = {k: np.zeros_like(v) for k, v in common.items()}
    zeros["xT"] = np.zeros((D, T), np.float32)
    in_maps = []
    for c in range(NCORES):
        if c in ACTIVE:
            m = dict(common)
            m["xT"] = np.ascontiguousarray(x[ACTIVE.index(c)].T)
        else:
            m = zeros
        in_maps.append(m)
    res = run_bass_kernel_spmd(nc, in_maps, core_ids=list(range(NCORES)))
    out = np.empty((B, T, D), np.float32)
    for b in range(B):
        out[b] = res.results[ACTIVE[b]]["yT"].T
    return out
```
